# Optimizing a Trainium2 kernel written in Bass

```python
import math
import jax
import jax.numpy as jnp
from jax import lax
import numpy as np

D_MODEL = 1024
BATCH = 2
SEQ = 8192
DEPTH = 4

MLA_HEADS = 8
MLA_Q_RANK = 256
MLA_KV_RANK = 128
MLA_NOPE = 64
MLA_ROPE = 32
MLA_V = 64
ATTN_QBLOCK = 128
RET_HEADS = 4
RET_DK = 64
RET_DV = 128
RET_CHUNK = 128
S5_GROUP_CH = 16
S5_WIDTH = 512
S5_GROUPS = S5_WIDTH // S5_GROUP_CH
S5_STATE = 64
D_FF = ((8 * D_MODEL // 3 + 127) // 128) * 128
FFN_CONV = 3
ROPE_BASE = 10000.0
LN_EPS = 1e-5
RMS_EPS = 1e-6
GN_EPS = 1e-5
DEEPNORM_ALPHA = (2 * DEPTH) ** 0.25
DEEPNORM_BETA = (8 * DEPTH) ** -0.25
N_BRANCH = 3
IN_WIDTH = (MLA_Q_RANK + MLA_KV_RANK + MLA_ROPE + 2 * RET_HEADS * RET_DK
            + 2 * RET_HEADS * RET_DV + S5_WIDTH + N_BRANCH * D_MODEL)

kernel_name = "hybrid_mla_retention_s5_convffn_encoder"


def _in_splits():
    widths = [MLA_Q_RANK, MLA_KV_RANK, MLA_ROPE, RET_HEADS * RET_DK, RET_HEADS * RET_DK,
              RET_HEADS * RET_DV, RET_HEADS * RET_DV, S5_WIDTH]
    return [int(v) for v in np.cumsum(widths)]


def _layer_norm(x):
    xf = x.astype(jnp.float32)
    mu = jnp.mean(xf, axis=-1, keepdims=True)
    var = jnp.mean(jnp.square(xf - mu), axis=-1, keepdims=True)
    return (xf - mu) * lax.rsqrt(var + LN_EPS)


def _post_norm(x, g, b):
    return (_layer_norm(x) * g + b).astype(x.dtype)


def _rms_norm(x, g):
    xf = x.astype(jnp.float32)
    y = xf * lax.rsqrt(jnp.mean(jnp.square(xf), axis=-1, keepdims=True) + RMS_EPS)
    return (y * g).astype(x.dtype)


def _rope(x, positions):
    half = x.shape[-1] // 2
    inv_freq = ROPE_BASE ** (-jnp.arange(half, dtype=jnp.float32) / half)
    ang = positions.astype(jnp.float32)[..., None] * inv_freq
    ang = ang.reshape(ang.shape[:2] + (1,) * (x.ndim - 3) + (half,))
    cos, sin = jnp.cos(ang), jnp.sin(ang)
    x1 = x[..., :half].astype(jnp.float32)
    x2 = x[..., half:].astype(jnp.float32)
    return jnp.concatenate([x1 * cos - x2 * sin, x1 * sin + x2 * cos], axis=-1).astype(x.dtype)


def _mla(q_c, kv_c, k_rope, positions, q_norm, w_uq, kv_norm, w_ukv):
    B, S, _ = q_c.shape
    q = (_rms_norm(q_c, q_norm) @ w_uq).reshape(B, S, MLA_HEADS, MLA_NOPE + MLA_ROPE)
    q_nope = q[..., :MLA_NOPE]
    q_rope = _rope(q[..., MLA_NOPE:], positions)
    kv = (_rms_norm(kv_c, kv_norm) @ w_ukv).reshape(B, S, MLA_HEADS, MLA_NOPE + MLA_V)
    k_nope, v = kv[..., :MLA_NOPE], kv[..., MLA_NOPE:]
    k_rope = _rope(k_rope, positions)
    scale = (MLA_NOPE + MLA_ROPE) ** -0.5
    nb = S // ATTN_QBLOCK

    def to_blocks(t):
        return jnp.moveaxis(t.reshape((B, nb, ATTN_QBLOCK) + t.shape[2:]), 1, 0)

    def attend(blk):
        qn, qr = blk
        s = (jnp.einsum('bqhd,bkhd->bhqk', qn, k_nope)
             + jnp.einsum('bqhd,bkd->bhqk', qr, k_rope))
        p = jax.nn.softmax(s.astype(jnp.float32) * scale, axis=-1).astype(v.dtype)
        return jnp.einsum('bhqk,bkhd->bqhd', p, v)

    o = lax.map(attend, (to_blocks(q_nope), to_blocks(q_rope)))
    return jnp.moveaxis(o, 0, 1).reshape(B, S, MLA_HEADS * MLA_V)


def _retention_one_dir(q, k, v, log_gamma, include_diag):
    B, S, H, dk = q.shape
    dv = v.shape[-1]
    n, c = S // RET_CHUNK, RET_CHUNK
    qc = q.reshape(B, n, c, H, dk)
    kc = k.reshape(B, n, c, H, dk)
    vc = v.reshape(B, n, c, H, dv)
    idx = jnp.arange(c, dtype=jnp.float32)
    diff = idx[:, None] - idx[None, :]
    mask = (diff >= 0) if include_diag else (diff > 0)
    decay = jnp.where(mask, jnp.exp(log_gamma[:, None, None] * jnp.where(mask, diff, 0.0)), 0.0)
    scores = jnp.einsum('bnihd,bnjhd->bnhij', qc, kc) * decay
    y_intra = jnp.einsum('bnhij,bnjhe->bnihe', scores, vc)
    zeta = jnp.exp(log_gamma[:, None] * (c - 1 - idx))
    chunk_kv = jnp.einsum('bnjhd,hj,bnjhe->nbhde', kc, zeta, vc)
    carry_decay = jnp.exp(log_gamma * c)[None, :, None, None]

    def step(state, kv_n):
        return carry_decay * state + kv_n, state

    _, prev = lax.scan(step, jnp.zeros_like(chunk_kv[0]), chunk_kv)
    xi = jnp.exp(log_gamma[:, None] * (idx + 1))
    y_cross = jnp.einsum('bnihd,nbhde,hi->bnihe', qc, prev, xi)
    return (y_intra + y_cross).reshape(B, S, H, dv)


def _retention(r_q, r_k, r_v, r_g, positions, log_decay_comp):
    B, S, _ = r_q.shape
    q = _rope(r_q.reshape(B, S, RET_HEADS, RET_DK), positions) * (RET_DK ** -0.5)
    k = _rope(r_k.reshape(B, S, RET_HEADS, RET_DK), positions)
    v = r_v.reshape(B, S, RET_HEADS, RET_DV)
    log_gamma = jnp.log1p(-jnp.exp(log_decay_comp.astype(jnp.float32)))
    fwd = _retention_one_dir(q, k, v, log_gamma[0], True)
    bwd = jnp.flip(_retention_one_dir(jnp.flip(q, 1), jnp.flip(k, 1), jnp.flip(v, 1),
                                      log_gamma[1], False), 1)
    y = (fwd + bwd).astype(jnp.float32)
    mu = jnp.mean(y, axis=-1, keepdims=True)
    var = jnp.mean(jnp.square(y - mu), axis=-1, keepdims=True)
    y = ((y - mu) * lax.rsqrt(var + GN_EPS)).reshape(B, S, RET_HEADS * RET_DV)
    return (jax.nn.silu(r_g.astype(jnp.float32)) * y).astype(r_g.dtype)


def _complex_affine_combine(e1, e2):
    a1r, a1i, b1r, b1i = e1
    a2r, a2i, b2r, b2i = e2
    return (a2r * a1r - a2i * a1i,
            a2r * a1i + a2i * a1r,
            a2r * b1r - a2i * b1i + b2r,
            a2r * b1i + a2i * b1r + b2i)


def _s5_one_dir(u, lam_re, lam_im, log_step, b_re, b_im, c_re, c_im, reverse):
    step = jnp.exp(log_step)[:, None]
    mag = jnp.exp(lam_re * step)
    a_re = mag * jnp.cos(lam_im * step)
    a_im = mag * jnp.sin(lam_im * step)
    den = jnp.square(lam_re) + jnp.square(lam_im)
    f_re = ((a_re - 1.0) * lam_re + a_im * lam_im) / den
    f_im = (a_im * lam_re - (a_re - 1.0) * lam_im) / den
    bb_re = f_re[..., None] * b_re - f_im[..., None] * b_im
    bb_im = f_re[..., None] * b_im + f_im[..., None] * b_re
    bu_re = jnp.einsum('bsgc,gpc->bsgp', u, bb_re)
    bu_im = jnp.einsum('bsgc,gpc->bsgp', u, bb_im)
    elems = (jnp.broadcast_to(a_re, bu_re.shape), jnp.broadcast_to(a_im, bu_re.shape), bu_re, bu_im)
    _, _, x_re, x_im = lax.associative_scan(_complex_affine_combine, elems, reverse=reverse, axis=1)
    return jnp.einsum('bsgp,gcp->bsgc', x_re, c_re) - jnp.einsum('bsgp,gcp->bsgc', x_im, c_im)


def _s5(s5_u, lam_re, lam_im, log_step, b_re, b_im, c_re, c_im, d_skip, w_glu):
    B, S, _ = s5_u.shape
    f32 = jnp.float32
    u = s5_u.astype(f32).reshape(B, S, S5_GROUPS, S5_GROUP_CH)
    y = d_skip.astype(f32) * u
    for d in range(2):
        y = y + _s5_one_dir(u, lam_re[d].astype(f32), lam_im[d].astype(f32), log_step[d].astype(f32),
                            b_re[d].astype(f32), b_im[d].astype(f32),
                            c_re[d].astype(f32), c_im[d].astype(f32), reverse=(d == 1))
    y = jax.nn.gelu(y.reshape(B, S, S5_WIDTH)).astype(s5_u.dtype)
    return y * jax.nn.sigmoid(y @ w_glu)


def _conv_ffn(h, w_up, conv_w, conv_b, w_down):
    up = h @ w_up
    up = lax.conv_general_dilated(up, conv_w[:, None, :].astype(up.dtype), window_strides=(1,),
                                  padding='SAME', dimension_numbers=('NWC', 'WIO', 'NWC'),
                                  feature_group_count=up.shape[-1]) + conv_b
    a, g = jnp.split(up, 2, axis=-1)
    return (jax.nn.silu(g) * a) @ w_down


def setup_inputs(seed: int = 0) -> dict:
    key = jax.random.key(seed)
    ks = iter(jax.random.split(key, 48))
    L = DEPTH
    f32 = jnp.float32

    def nrm(shape, scale):
        return scale * jax.random.normal(next(ks), shape, f32)

    beta = DEEPNORM_BETA
    G, P, CH = S5_GROUPS, S5_STATE, S5_GROUP_CH
    lam_re = -0.5 * (1.0 + 0.02 * jax.random.normal(next(ks), (L, 2, G, P), f32))
    lam_im = jnp.broadcast_to(math.pi * jnp.arange(P, dtype=f32), (L, 2, G, P))
    log_step = jax.random.uniform(next(ks), (L, 2, G), f32, math.log(1e-3), math.log(1e-1))
    ret_log_decay = (-(5.0 + jnp.arange(RET_HEADS, dtype=f32)) * math.log(2.0)
                     + 0.01 * jax.random.normal(next(ks), (L, 2, RET_HEADS), f32))
    return {
        "x": nrm((BATCH, SEQ, D_MODEL), 1.0),
        "c": nrm((BATCH, D_MODEL), 1.0),
        "positions": jnp.broadcast_to(jnp.arange(SEQ, dtype=jnp.int32), (BATCH, SEQ)),
        "w_in": nrm((L, D_MODEL, IN_WIDTH), D_MODEL ** -0.5),
        "mla_q_norm": 1.0 + nrm((L, MLA_Q_RANK), 0.05),
        "mla_w_uq": nrm((L, MLA_Q_RANK, MLA_HEADS * (MLA_NOPE + MLA_ROPE)), MLA_Q_RANK ** -0.5),
        "mla_kv_norm": 1.0 + nrm((L, MLA_KV_RANK), 0.05),
        "mla_w_ukv": nrm((L, MLA_KV_RANK, MLA_HEADS * (MLA_NOPE + MLA_V)), MLA_KV_RANK ** -0.5),
        "ret_log_decay": ret_log_decay,
        "s5_lam_re": lam_re,
        "s5_lam_im": lam_im,
        "s5_log_step": log_step,
        "s5_b_re": nrm((L, 2, G, P, CH), (2 * CH) ** -0.5),
        "s5_b_im": nrm((L, 2, G, P, CH), (2 * CH) ** -0.5),
        "s5_c_re": nrm((L, 2, G, CH, P), P ** -0.5),
        "s5_c_im": nrm((L, 2, G, CH, P), P ** -0.5),
        "s5_d": nrm((L, G, CH), 1.0),
        "s5_w_glu": nrm((L, S5_WIDTH, S5_WIDTH), S5_WIDTH ** -0.5),
        "w_branch_mla": nrm((L, MLA_HEADS * MLA_V, D_MODEL), beta * (MLA_HEADS * MLA_V) ** -0.5),
        "w_branch_ret": nrm((L, RET_HEADS * RET_DV, D_MODEL), beta * (RET_HEADS * RET_DV) ** -0.5),
        "w_branch_s5": nrm((L, S5_WIDTH, D_MODEL), beta * S5_WIDTH ** -0.5),
        "w_o": nrm((L, D_MODEL, D_MODEL), beta * D_MODEL ** -0.5),
        "ffn_w_up": nrm((L, D_MODEL, 2 * D_FF), D_MODEL ** -0.5),
        "ffn_conv_w": nrm((L, FFN_CONV, 2 * D_FF), FFN_CONV ** -0.5),
        "ffn_conv_b": nrm((L, 2 * D_FF), 0.02),
        "ffn_w_down": nrm((L, D_FF, D_MODEL), beta * D_FF ** -0.5),
        "ln1_g": 1.0 + nrm((L, D_MODEL), 0.05),
        "ln1_b": nrm((L, D_MODEL), 0.02),
        "ln2_g": 1.0 + nrm((L, D_MODEL), 0.05),
        "ln2_b": nrm((L, D_MODEL), 0.02),
        "w_ada": nrm((L, D_MODEL, 6 * D_MODEL), D_MODEL ** -0.5),
        "b_ada": nrm((L, 6 * D_MODEL), 0.02),
    }


def reference(x, c, positions, w_in, mla_q_norm, mla_w_uq, mla_kv_norm, mla_w_ukv,
              ret_log_decay, s5_lam_re, s5_lam_im, s5_log_step, s5_b_re, s5_b_im,
              s5_c_re, s5_c_im, s5_d, s5_w_glu, w_branch_mla, w_branch_ret, w_branch_s5,
              w_o, ffn_w_up, ffn_conv_w, ffn_conv_b, ffn_w_down, ln1_g, ln1_b, ln2_g, ln2_b,
              w_ada, b_ada):
    splits = _in_splits()
    cond = jax.nn.silu(c)
    for l in range(DEPTH):
        mod = cond @ w_ada[l] + b_ada[l]
        shift1, scale1, gate1, shift2, scale2, gate2 = [m[:, None, :] for m in jnp.split(mod, 6, axis=-1)]

        h = (_layer_norm(x) * (1.0 + scale1) + shift1).astype(x.dtype)
        proj = h @ w_in[l]
        q_c, kv_c, k_r, r_q, r_k, r_v, r_g, s5_u, gates = jnp.split(proj, splits, axis=-1)
        o_mla = _mla(q_c, kv_c, k_r, positions, mla_q_norm[l], mla_w_uq[l], mla_kv_norm[l], mla_w_ukv[l])
        o_ret = _retention(r_q, r_k, r_v, r_g, positions, ret_log_decay[l])
        o_s5 = _s5(s5_u, s5_lam_re[l], s5_lam_im[l], s5_log_step[l], s5_b_re[l], s5_b_im[l],
                   s5_c_re[l], s5_c_im[l], s5_d[l], s5_w_glu[l])
        g_mla, g_ret, g_s5 = jnp.split(jax.nn.sigmoid(gates), N_BRANCH, axis=-1)
        merged = (g_mla * (o_mla @ w_branch_mla[l])
                  + g_ret * (o_ret @ w_branch_ret[l])
                  + g_s5 * (o_s5 @ w_branch_s5[l]))
        x = _post_norm(DEEPNORM_ALPHA * x + gate1 * (merged @ w_o[l]), ln1_g[l], ln1_b[l])

        h = (_layer_norm(x) * (1.0 + scale2) + shift2).astype(x.dtype)
        f = _conv_ffn(h, ffn_w_up[l], ffn_conv_w[l], ffn_conv_b[l], ffn_w_down[l])
        x = _post_norm(DEEPNORM_ALPHA * x + gate2 * f, ln2_g[l], ln2_b[l])
    return x
```

```python
import math
import numpy as np
import ml_dtypes
import concourse.bass as bass
import concourse.mybir as mybir
from concourse.bass_utils import run_bass_kernel_spmd

F32 = mybir.dt.float32
BF16 = mybir.dt.bfloat16
I32 = mybir.dt.int32
ALU = mybir.AluOpType
AF = mybir.ActivationFunctionType

ENGS = ("pe", "act", "dve", "pool", "sp")
D = 1024
NH = 8
DFF = 2816
C_QC, C_KV, C_KR, C_RQ, C_RK, C_RV, C_RG, C_S5, C_G = 0, 256, 384, 416, 672, 928, 1440, 1952, 2464
ALPHA = float((2 * 4) ** 0.25)
LN_EPS, RMS_EPS, GN_EPS = 1e-5, 1e-6, 1e-5
TWO_PI = 2.0 * math.pi
SB_BASE, SB_END = 16512, 229376
BIGD = 40.0


class FW:
    def __init__(self, nc, n_dma_sems=(("sp", 40), ("pool", 16))):
        self.nc = nc
        self.stream = {e: [] for e in ENGS}
        self.ctr = {e: nc.alloc_semaphore(name=f"ctr_{e}") for e in ENGS}
        self.bar = {e: nc.alloc_semaphore(name=f"bar_{e}") for e in ENGS}
        self.count = {e: 0 for e in ENGS}
        self.known = {e: {} for e in ENGS}
        self.keys = {}
        self.dpool = {}
        for e, n in n_dma_sems:
            self.dpool[e] = dict(sems=[nc.alloc_semaphore(name=f"dq_{e}_{i}") for i in range(n)], vals=[0] * n, nxt=0)
        self.nbar = 0
        self.n_inst = 0
        self.customs = {}

    def _wait(self, eng, tok):
        sem, val = tok
        if val <= 0:
            return
        k = self.known[eng]
        if k.get(id(sem), 0) >= val:
            return
        k[id(sem)] = val
        self.stream[eng].append(("wait", sem, val))

    def _deps(self, eng, reads, writes, skip_same):
        toks = []
        for key in reads:
            st = self.keys.get(key)
            if st and st["w"] is not None:
                toks.append(st["w"])
        for key in writes:
            st = self.keys.get(key)
            if st:
                if st["w"] is not None:
                    toks.append(st["w"])
                toks.extend(st["r"])
        own = self.ctr[eng]
        for t in toks:
            if t[0] is own and skip_same:
                continue
            self._wait(eng, t)

    def _record(self, tok, reads, writes):
        for key in reads:
            st = self.keys.setdefault(key, {"w": None, "r": []})
            st["r"] = [t for t in st["r"] if t[0] is not tok[0]] + [tok]
        for key in writes:
            self.keys[key] = {"w": tok, "r": []}

    def op(self, eng, fn, reads=(), writes=(), skip_same=False):
        if eng == "pe":
            skip_same = True
        self._deps(eng, reads, writes, skip_same)
        self.count[eng] += 1
        tok = (self.ctr[eng], self.count[eng])
        self.stream[eng].append(("inst", fn, self.ctr[eng], 1))
        self._record(tok, reads, writes)
        self.n_inst += 1
        return tok

    def dma(self, eng, out, in_, reads=(), writes=()):
        p = self.dpool[eng]
        j = p["nxt"]
        p["nxt"] = (j + 1) % len(p["sems"])
        sem = p["sems"][j]
        self._wait(eng, (sem, p["vals"][j]))
        self._deps(eng, reads, writes, False)
        p["vals"][j] += 16
        tok = (sem, p["vals"][j])
        self.stream[eng].append(("inst", lambda e, o=out, i=in_: e.dma_start(out=o, in_=i), sem, 16))
        self._record(tok, reads, writes)
        self.n_inst += 1
        return tok

    def custom(self, eng, fn, sem, newval, reads=(), writes=()):
        self._deps(eng, reads, writes, False)
        self.stream[eng].append(("inst", fn, sem, 1))
        tok = (sem, newval)
        self.customs[eng] = tok
        self._record(tok, reads, writes)
        return tok

    def barrier(self):
        self.nbar += 1
        for e in ENGS:
            if e in self.dpool:
                p = self.dpool[e]
                for s, v in zip(p["sems"], p["vals"]):
                    self._wait(e, (s, v))
            if e in self.customs:
                self._wait(e, self.customs[e])
            self._wait(e, (self.ctr[e], self.count[e]))
            self.stream[e].append(("inst", lambda en: en.nop(), self.bar[e], 1))
        for e in ENGS:
            for e2 in ENGS:
                if e2 != e:
                    self._wait(e, (self.bar[e2], self.nbar))
        self.keys = {}

    def emit(self):
        nc = self.nc
        engobj = {"pe": "tensor", "act": "scalar", "dve": "vector", "pool": "gpsimd", "sp": "sync"}
        with nc.Block() as block:
            for e in ENGS:
                stream = self.stream[e]

                def body(eng, stream=stream):
                    for it in stream:
                        if it[0] == "wait":
                            eng.wait_ge(it[1], it[2])
                        else:
                            it[1](eng).then_inc(it[2], it[3])

                getattr(block, engobj[e])(body)


def _dsize(dt):
    return {F32: 4, BF16: 2, I32: 4}[dt]


def rev_ap(ap, n):
    a = ap.ap
    assert len(a) == 2 and a[1][0] == 1 and a[1][1] == n, a
    return bass.AP(ap.tensor, ap.offset + (n - 1), [[a[0][0], a[0][1]], [-1, n]])


class Builder:
    def __init__(self, S, L, debug=()):
        self.S, self.L = S, L
        self.T = S // 4
        self.NT = self.T // 128
        self.debug = set(debug)
        self.nc = bass.Bass("TRN2", target_bir_lowering=False)
        self.fw = FW(self.nc)
        self.pers_off = SB_BASE
        self.arena_off = SB_BASE
        self.uid = 0
        self.inputs = {}
        self.dbg_out = []
        self.ps = [self.nc.alloc_psum_tensor(f"ps{i}", [128, 512], F32).ap() for i in range(8)]
        self.psi = 0
        self.cc_sem = self.nc.alloc_semaphore(name="cc_sem")
        self.cc_n = 0
        self.stage_i = 0
        self._cache = {}

    def _alloc(self, name, shape, dt, off):
        nbytes = int(np.prod(shape[1:])) * _dsize(dt)
        nbytes = (nbytes + 31) // 32 * 32
        assert off + nbytes <= SB_END, f"SBUF overflow allocating {name} {shape}: {off + nbytes - SB_END} bytes over"
        self.uid += 1
        h = self.nc.alloc_sbuf_tensor_at(f"{name}_{self.uid}", list(shape), dt, offset=off)
        return h.ap(), off + nbytes

    def pers(self, name, shape, dt=F32):
        ap, self.pers_off = self._alloc(name, shape, dt, self.pers_off)
        self.arena_off = max(self.arena_off, self.pers_off)
        return ap

    def sb(self, name, shape, dt=F32):
        ap, self.arena_off = self._alloc(name, shape, dt, self.arena_off)
        return ap

    def phase(self, name):
        self.fw.barrier()
        self.arena_off = self.pers_off
        self.pname = name

    def subphase(self, mark):
        self.fw.barrier()
        self.arena_off = mark

    def inp(self, name, shape, dt=F32):
        ap = self.nc.dram_tensor(name, list(shape), dt, kind="ExternalInput").ap()
        self.inputs[name] = (tuple(shape), dt)
        return ap

    def dram(self, name, shape, dt=F32):
        return self.nc.dram_tensor(name, list(shape), dt).ap()

    def nextps(self, cyc=None):
        cyc = cyc or list(range(8))
        i = cyc[self.psi % len(cyc)]
        self.psi += 1
        return self.ps[i], f"ps{i}"

    def op(self, eng, meth, r=(), w=(), **kw):
        return self.fw.op(eng, lambda e: getattr(e, meth)(**kw), reads=list(r), writes=list(w))

    def dma(self, out, in_, r=(), w=(), q="sp"):
        return self.fw.dma(q, out, in_, reads=list(r), writes=list(w))

    def mm(self, out, lhsT, rhs, start, stop, r, w):
        return self.fw.op("pe", lambda e: e.matmul(out, lhsT=lhsT, rhs=rhs, start=start, stop=stop), reads=list(r), writes=list(w))

    def tr(self, out, in_, r, w):
        p = in_.shape[0]
        idn = self.ident[0:p, 0:p]
        return self.fw.op("pe", lambda e: e.transpose(out=out, in_=in_, identity=idn), reads=list(r) + ["ident"], writes=list(w))

    def act(self, out, in_, func, r, w, bias=0.0, scale=1.0, **kw):
        return self.fw.op("act", lambda e: e.activation(out=out, in_=in_, func=func, bias=bias, scale=scale, **kw), reads=list(r), writes=list(w))

    def allgather(self, out_ap, in_ap, r, w):
        self.cc_n += 1
        n = self.cc_n
        return self.fw.custom("pool", lambda e: e.collective_compute("AllGather", ALU.bypass, replica_groups=[[0, 1, 2, 3], [4, 5, 6, 7]],
                                                                     ins=[in_ap.opt()], outs=[out_ap.opt()]),
                              self.cc_sem, n, reads=list(r), writes=list(w))

    def dbg(self, name, dram_ap, key):
        if name in self.debug:
            o = self.nc.dram_tensor("dbg_" + name, list(dram_ap.shape), dram_ap.dtype, kind="ExternalOutput").ap()
            self.dma(o, dram_ap, r=[key])
            self.dbg_out.append("dbg_" + name)

    def dbg_sb(self, name, sb_ap, key):
        if name in self.debug:
            o = self.nc.dram_tensor("dbg_" + name, list(sb_ap.shape), sb_ap.dtype, kind="ExternalOutput").ap()
            self.dma(o, sb_ap, r=[key])
            self.dbg_out.append("dbg_" + name)

    def wload(self, dst, src, dst_key, mul=None, mul_key=None, src_key=None):
        shp = list(src.shape)
        rows = shp[0]
        cols = int(np.prod(shp[1:]))
        si = self.stage_i % len(self.wstage)
        self.stage_i += 1
        skey = f"wst{si}"
        st = self.wstage[si][0:rows, 0:cols]
        if len(shp) == 3:
            st = st.rearrange("p (k n) -> p k n", k=shp[1])
        self.dma(st, src, r=[src_key] if src_key else [], w=[skey])
        if mul is None:
            self.op("pool", "tensor_copy", r=[skey], w=[dst_key], out=dst, in_=st)
        else:
            self.op("pool", "tensor_tensor", r=[skey, mul_key], w=[dst_key], out=dst, in0=st, in1=mul, op=ALU.mult)

    def mk_wstage(self, cols, n=3):
        self.wstage = [self.sb(f"wst{i}", [128, cols], F32) for i in range(n)]
        self.stage_i = 0

    def declare_inputs(self):
        L, T, NT, S = self.L, self.T, self.NT, self.S
        I = self.inp
        self.x_in = I("x", [T, D])
        self.cT_in = I("cT", [128, 8])
        self.pos_in = I("pos", [128, NT], I32)
        self.invf16_in = I("invf16", [128, 16])
        self.invf32_in = I("invf32", [128, 32])
        self.cmask_in = I("cmask", [4, 128, 128])
        self.ek_in = I("ek", [128, 4])
        self.etab_in = I("etab", [128, NT])
        self.io_in = I("io", [128, 512])
        self.dist_in = I("dist", [128, 4])
        self.sel_in = I("sel", [128, 4])
        self.selH_in = I("selH", [8, 2])
        self.flag_in = I("flag", [2, 1])
        self.w_in = I("w_in", [L, D, 5536])
        self.qnorm_in = I("qnorm", [L, 128, 2])
        self.w_uq = I("w_uq", [L, 256, 768])
        self.kvnorm_in = I("kvnorm", [L, 128, 1])
        self.w_ukv = I("w_ukv", [L, 128, 1024])
        self.ldc_in = I("ldc", [L, 8])
        self.lamS_re = I("lamS_re", [L, 2, 128, 4])
        self.lamS_im = I("lamS_im", [L, 2, 128, 4])
        self.lsS = I("lsS", [L, 2, 128, 4])
        self.lamB_re = I("lamB_re", [L, 2, 512])
        self.lamB_im = I("lamB_im", [L, 2, 512])
        self.lsB = I("lsB", [L, 2, 512])
        self.Bblk_re = I("Bblk_re", [L, 2, 128, 512])
        self.Bblk_im = I("Bblk_im", [L, 2, 128, 512])
        self.Cblk_re = I("Cblk_re", [L, 2, 128, 512])
        self.Cblk_im = I("Cblk_im", [L, 2, 128, 512])
        self.dskip_in = I("dskip", [L, 128, 4])
        self.w_glu = I("w_glu", [L, 512, 512])
        self.w_br = [I("w_br_mla", [L, 512, D]), I("w_br_ret", [L, 512, D]), I("w_br_s5", [L, 512, D])]
        self.w_o = I("w_o", [L, D, D])
        self.w_up = I("w_up", [L, D, 2 * DFF])
        self.convw_in = I("convw", [L, 128, 44 * 3])
        self.convb_in = I("convb", [L, 128, 44])
        self.w_down = I("w_down", [L, DFF, D])
        self.ln_in = I("ln", [L, 4, D])
        self.w_ada = I("w_ada", [L, D, 6 * D])
        self.b_ada = I("b_ada", [L, 6 * D])
        self.out = self.nc.dram_tensor("out", [T, D], F32, kind="ExternalOutput").ap()
        Dm = self.dram
        self.lat_loc = Dm("lat_loc", [160, T], BF16)
        self.lat_all = Dm("lat_all", [4 * 160, T], BF16)
        self.qT_d = Dm("qT_d", [NH, 96, T], BF16)
        self.rq_d = Dm("rq_d", [T, 256], BF16)
        self.rk_d = Dm("rk_d", [T, 256], BF16)
        self.rv_d = Dm("rv_d", [T, 512], BF16)
        self.rg_d = Dm("rg_d", [T, 512], BF16)
        self.u_loc = Dm("u_loc", [512, T], F32)
        self.u_all = Dm("u_all", [16 * 128, T], F32)
        self.agg_loc = Dm("agg_loc", [4 * 128, 128], F32)
        self.agg_all = Dm("agg_all", [16 * 128, 128], F32)
        self.ys_loc = Dm("ys_loc", [4 * 128, T], F32)
        self.ys_all = Dm("ys_all", [16 * 128, T], F32)
        self.ob_d = [Dm(f"ob{i}_d", [512, T], BF16) for i in range(3)]
        self.mT_d = Dm("mT_d", [D, T], BF16)
        self.halo_loc = Dm("halo_loc", [2, D], F32)
        self.halo_all = Dm("halo_all", [8, D], F32)

    def setup(self):
        T, NT = self.T, self.NT
        P = self.pers
        self.x_res = P("x_res", [128, NT, D])
        self.ident = P("ident", [128, 128], BF16)
        self.ones_f = P("ones_f", [128, 128])
        self.zeros_f = P("zeros_f", [128, 128])
        self.condrep = P("condrep", [128, 8, 128])
        self.cos16 = P("cos16", [128, NT, 16]); self.sin16 = P("sin16", [128, NT, 16])
        self.cos32 = P("cos32", [128, NT, 32]); self.sin32 = P("sin32", [128, NT, 32])
        self.cos32q = P("cos32q", [128, NT, 32]); self.sin32q = P("sin32q", [128, NT, 32])
        self.sel = P("sel", [128, 4]); self.dist = P("dist", [128, 4])
        self.selH = P("selH", [8, 2]); self.flag = P("flag", [2, 1])
        self.ek = P("ek", [128, 4]); self.etab = P("etab", [128, NT])
        self.constc = P("constc", [128, 4])
        self.epsc = self.constc[:, 0:1]; self.rmseps = self.constc[:, 1:2]; self.halfpi = self.constc[:, 2:3]
        self.fw.barrier()
        self.arena_off = self.pers_off
        op, dma = self.op, self.dma
        dma(self.x_res, self.x_in.rearrange("(n p) d -> p n d", p=128), w=["x_res"])
        op("pool", "memset", w=["ident"], ap=self.ident, constant=0.0)
        op("pool", "affine_select", r=["ident"], w=["ident"], out=self.ident, in_=self.ident, pattern=[[-1, 128]],
           compare_op=ALU.not_equal, fill=1.0, base=0, channel_multiplier=1)
        op("dve", "memset", w=["ones_f"], ap=self.ones_f, constant=1.0)
        op("dve", "memset", w=["zeros_f"], ap=self.zeros_f, constant=0.0)
        op("dve", "memset", w=["constc"], ap=self.constc[:, 0:1], constant=LN_EPS)
        op("dve", "memset", w=["constc"], ap=self.constc[:, 1:2], constant=RMS_EPS)
        op("dve", "memset", w=["constc"], ap=self.constc[:, 2:3], constant=math.pi / 2)
        op("dve", "memset", w=["constc"], ap=self.constc[:, 3:4], constant=GN_EPS)
        for nm, src in (("sel", self.sel_in), ("dist", self.dist_in), ("selH", self.selH_in), ("flag", self.flag_in),
                        ("ek", self.ek_in), ("etab", self.etab_in)):
            dma(getattr(self, nm), src, w=[nm])
        cT = self.sb("cT", [128, 8]); cs = self.sb("cs", [128, 8])
        dma(cT, self.cT_in, w=["cT"])
        self.act(cs, cT, AF.Silu, r=["cT"], w=["cs"])
        op("dve", "tensor_copy", r=["cs"], w=["condrep"], out=self.condrep, in_=cs.unsqueeze(2).to_broadcast([128, 8, 128]))
        posi = self.sb("posi", [128, NT], I32); posf = self.sb("posf", [128, NT])
        dma(posi, self.pos_in, w=["posi"])
        op("dve", "tensor_copy", r=["posi"], w=["posf"], out=posf, in_=posi)
        for nf, src, cosT, sinT in ((16, self.invf16_in, self.cos16, self.sin16), (32, self.invf32_in, self.cos32, self.sin32)):
            inv = self.sb(f"inv{nf}", [128, nf]); u = self.sb(f"u{nf}", [128, NT, nf]); ki = self.sb(f"ki{nf}", [128, NT, nf], I32)
            kf = self.sb(f"kf{nf}", [128, NT, nf]); fr = self.sb(f"fr{nf}", [128, NT, nf]); ab = self.sb(f"ab{nf}", [128, NT, nf])
            k = f"rp{nf}"
            dma(inv, src, w=[k + "inv"])
            op("dve", "tensor_scalar", r=[k + "inv"], w=[k + "inv"], out=inv, in0=inv, scalar1=1.0 / TWO_PI, scalar2=None, op0=ALU.mult)
            op("dve", "tensor_tensor", r=["posf", k + "inv"], w=[k + "u"], out=u, in0=posf.unsqueeze(2).to_broadcast([128, NT, nf]),
               in1=inv.unsqueeze(1).to_broadcast([128, NT, nf]), op=ALU.mult)
            op("dve", "tensor_copy", r=[k + "u"], w=[k + "ki"], out=ki, in_=u)
            op("dve", "tensor_copy", r=[k + "ki"], w=[k + "kf"], out=kf, in_=ki)
            op("dve", "tensor_tensor", r=[k + "u", k + "kf"], w=[k + "fr"], out=fr, in0=u, in1=kf, op=ALU.subtract)
            self.act(sinT, fr, AF.Sin, r=[k + "fr"], w=[k + "sin"], scale=TWO_PI)
            self.act(ab, fr, AF.Abs, r=[k + "fr"], w=[k + "ab"])
            self.act(cosT, ab, AF.Sin, r=[k + "ab"], w=[k + "cos"], scale=-TWO_PI, bias=self.halfpi)
        op("dve", "tensor_scalar", r=["rp32cos"], w=["c32q"], out=self.cos32q, in0=self.cos32, scalar1=0.125, scalar2=None, op0=ALU.mult)
        op("dve", "tensor_scalar", r=["rp32sin"], w=["s32q"], out=self.sin32q, in0=self.sin32, scalar1=0.125, scalar2=None, op0=ALU.mult)

    def modvec(self, l, idx, dst, dkey, plus_one=False):
        for qd in range(4):
            c0 = idx * D + qd * 256
            si = self.stage_i % len(self.wstage)
            self.stage_i += 1
            wst = self.wstage[si][:, 0:2048].rearrange("p (k n) -> p k n", k=8)
            brow = self.wstage[si][0:1, 2048:2304]
            sk = f"wst{si}"
            self.dma(wst, self.w_ada[l, :, c0:c0 + 256].rearrange("(k p) n -> p k n", p=128), w=[sk])
            self.dma(brow, self.b_ada[l:l + 1, c0:c0 + 256], w=[sk])
            ps, pk = self.nextps()
            for k in range(8):
                self.mm(ps[:, 0:256], self.condrep[:, k, :], wst[:, k, :], k == 0, False, r=["condrep", sk], w=[pk])
            self.mm(ps[:, 0:256], self.ones_f[0:1, :], brow, False, True, r=["ones_f", sk], w=[pk])
            self.act(dst[:, qd * 256:(qd + 1) * 256], ps[:, 0:256], AF.Identity, r=[], w=[pk, dkey], bias=1.0 if plus_one else 0.0)

    def mk_lnscr(self, tag):
        return dict(st=self.sb(f"lnst{tag}", [128, 2, 6]), mv=self.sb(f"lnmv{tag}", [128, 2]), rs=self.sb(f"lnrs{tag}", [128, 2]), k=f"ln{tag}")

    def ln_stats(self, xt, xkey, p, scr):
        st, mv, rs, k = scr["st"], scr["mv"], scr["rs"], scr["k"]
        for c in range(2):
            self.op("dve", "bn_stats", r=[xkey], w=[k + "st"], out=st[0:p, c, :], in_=xt[:, c * 512:(c + 1) * 512])
        self.op("dve", "bn_aggr", r=[k + "st"], w=[k + "mv"], out=mv[0:p, :], in_=st[0:p].rearrange("p a b -> p (a b)"))
        self.act(rs[0:p, 0:1], mv[0:p, 1:2], AF.Ln, r=[k + "mv"], w=[k + "rs"], bias=self.epsc[0:p, 0:1])
        self.act(rs[0:p, 0:1], rs[0:p, 0:1], AF.Exp, r=[k + "rs"], w=[k + "rs"], scale=-0.5)
        self.op("dve", "scalar_tensor_tensor", r=[k + "mv", k + "rs"], w=[k + "rs2"], out=rs[0:p, 1:2], in0=mv[0:p, 0:1], scalar=-1.0,
                in1=rs[0:p, 0:1], op0=ALU.mult, op1=ALU.mult)
        return rs, [k + "rs", k + "rs2"]

    def phase_A(self, l, which, hT, col0):
        NT = self.NT
        shB = self.sb("shB", [128, D]); scB = self.sb("scB", [128, D])
        self.modvec(l, 3 * which + 0, shB, "shB")
        self.modvec(l, 3 * which + 1, scB, "scB", plus_one=True)
        xn1 = self.sb("xn0", [128, D])
        xn = [xn1, xn1]
        hb = [self.sb(f"hb{i}", [128, D], BF16) for i in range(2)]
        lns = [self.mk_lnscr(i) for i in range(2)]
        for i in range(NT):
            j = i % 2
            xt = self.x_res[:, i, :]
            rs, rk = self.ln_stats(xt, "x_res", 128, lns[j])
            self.act(xn[j], xt, AF.Identity, r=["x_res"] + rk, w=["xn0"], scale=rs[:, 0:1], bias=rs[:, 1:2])
            self.op("dve", "tensor_tensor", r=["xn0", "scB"], w=["xn0"], out=xn[j], in0=xn[j], in1=scB, op=ALU.mult)
            self.op("dve", "tensor_tensor", r=["xn0", "shB"], w=[f"hb{j}"], out=hb[j], in0=xn[j], in1=shB, op=ALU.add)
            ps, pk = self.nextps()
            psb = ps.bitcast(BF16)
            for k in range(8):
                self.tr(psb[:, k * 128:(k + 1) * 128], hb[j][:, k * 128:(k + 1) * 128], r=[f"hb{j}"], w=[pk])
            self.op("dve", "tensor_copy", r=[], w=[pk, f"hT{i}"], out=hT[:, :, col0 + i * 128:col0 + (i + 1) * 128],
                    in_=psb.rearrange("p (k t) -> p k t", k=8))
        return shB, scB

    def rope(self, x1, x2, cosT, sinT, d1, d2, H, n, pk, dkey, tmp):
        c = cosT.unsqueeze(1).to_broadcast([128, H, n]); s = sinT.unsqueeze(1).to_broadcast([128, H, n])
        t1, t2 = tmp[0][:, 0:H * n].rearrange("p (h n) -> p h n", h=H), tmp[1][:, 0:H * n].rearrange("p (h n) -> p h n", h=H)
        o = self.op
        o("dve", "tensor_tensor", r=["rope"], w=[pk, "rt1"], out=t1, in0=x1, in1=c, op=ALU.mult)
        o("dve", "tensor_tensor", r=["rope"], w=[pk, "rt2"], out=t2, in0=x2, in1=s, op=ALU.mult)
        o("dve", "tensor_tensor", r=["rt1", "rt2"], w=[dkey], out=d1, in0=t1, in1=t2, op=ALU.subtract)
        o("dve", "tensor_tensor", r=["rope"], w=[pk, "rt1"], out=t1, in0=x1, in1=s, op=ALU.mult)
        o("dve", "tensor_tensor", r=["rope"], w=[pk, "rt2"], out=t2, in0=x2, in1=c, op=ALU.mult)
        o("dve", "tensor_tensor", r=["rt1", "rt2"], w=[dkey], out=d2, in0=t1, in1=t2, op=ALU.add)

    def phase_B(self, l):
        T, NT = self.T, self.NT
        op, dma, mm, tr, act = self.op, self.dma, self.mm, self.tr, self.act
        self.phase("B")
        hT = self.sb("hT", [128, 8, T], BF16)
        W1 = self.sb("W1", [128, 8, C_S5], BF16)
        Ws5 = self.sb("Ws5", [128, 8, 512], BF16)
        Wuq = self.sb("Wuq", [128, 2, 768], BF16)
        qg = self.sb("qg", [128, 2])
        self.mk_wstage(C_S5 + 512, n=2)
        dma(qg, self.qnorm_in[l], w=["qg"])
        for k in range(8):
            st = self.wstage[k % 2]
            dma(st, self.w_in[l, k * 128:(k + 1) * 128, 0:C_G], w=[f"wst{k % 2}"])
            op("pool", "tensor_copy", r=[f"wst{k % 2}"], w=["W1"], out=W1[:, k, :], in_=st[:, 0:C_S5])
            op("pool", "tensor_copy", r=[f"wst{k % 2}"], w=["Ws5"], out=Ws5[:, k, :], in_=st[:, C_S5:C_G])
        for k in range(2):
            st = self.wstage[k % 2]
            dma(st[:, 0:768], self.w_uq[l, k * 128:(k + 1) * 128, :], w=[f"wst{k % 2}"])
            op("dve", "tensor_scalar", r=[f"wst{k % 2}", "qg"], w=["Wuq"], out=Wuq[:, k, :], in0=st[:, 0:768], scalar1=qg[:, k:k + 1],
               scalar2=None, op0=ALU.mult)
        self.phase_A(l, 0, hT, 0)
        sq = self.sb("sq", [128, 384]); ssq = self.sb("ssq", [128, 2]); rr = self.sb("rr", [128, 2])
        qcn = self.sb("qcn", [128, 256], BF16); lat = self.sb("lat", [128, 160], BF16)
        qcnT = self.sb("qcnT", [128, 2, 128], BF16)
        latT = [self.sb(f"latT{i}", [128, 128], BF16) for i in range(2)]
        krT = [self.sb(f"krT{i}", [32, 128], BF16) for i in range(2)]
        Qr = self.sb("Qr", [128, 8, 96], BF16)
        QT = [self.sb(f"QT{i}", [96, 8, 128], BF16) for i in range(2)]
        rtmp = [self.sb(f"rtmp{i}", [128, 128]) for i in range(2)]
        rqk = [self.sb(f"rqk{i}", [128, 512], BF16) for i in range(2)]
        rvb = [self.sb(f"rvb{i}", [128, 512], BF16) for i in range(1)] * 2
        rgb = [self.sb(f"rgb{i}", [128, 512], BF16) for i in range(1)] * 2
        TB = min(512, T)
        ub = [self.sb(f"ub{i}", [128, TB]) for i in range(2)]
        for i in range(NT):
            j = i % 2
            tsl = slice(i * 128, (i + 1) * 128)
            hk = f"hT{i}"
            ps, pk = self.nextps()
            for k in range(8):
                mm(ps[:, 0:416], hT[:, k, tsl], W1[:, k, 0:416], k == 0, k == 7, r=[hk, "W1"], w=[pk])
            act(sq[:, 0:384], ps[:, 0:384], AF.Square, r=[], w=[pk, "sq"])
            op("dve", "tensor_reduce", r=["sq"], w=["ssq"], out=ssq[:, 0:1], in_=sq[:, 0:256], axis=mybir.AxisListType.X, op=ALU.add)
            op("dve", "tensor_reduce", r=["sq"], w=["ssq"], out=ssq[:, 1:2], in_=sq[:, 256:384], axis=mybir.AxisListType.X, op=ALU.add)
            act(rr[:, 0:1], ssq[:, 0:1], AF.Ln, r=["ssq"], w=["rr"], scale=1.0 / 256, bias=self.rmseps)
            act(rr[:, 1:2], ssq[:, 1:2], AF.Ln, r=["ssq"], w=["rr"], scale=1.0 / 128, bias=self.rmseps)
            act(rr, rr, AF.Exp, r=["rr"], w=["rr"], scale=-0.5)
            op("dve", "tensor_scalar", r=["rr"], w=[pk, "qcn"], out=qcn, in0=ps[:, 0:256], scalar1=rr[:, 0:1], scalar2=None, op0=ALU.mult)
            op("dve", "tensor_scalar", r=["rr"], w=[pk, "lat"], out=lat[:, 0:128], in0=ps[:, 256:384], scalar1=rr[:, 1:2], scalar2=None, op0=ALU.mult)
            self.rope(ps[:, 384:400].unsqueeze(1), ps[:, 400:416].unsqueeze(1), self.cos16[:, i, :], self.sin16[:, i, :],
                      lat[:, 128:144].unsqueeze(1), lat[:, 144:160].unsqueeze(1), 1, 16, pk, "lat", rtmp)
            pt, ptk = self.nextps()
            ptb = pt.bitcast(BF16)
            tr(ptb[:, 0:128], qcn[:, 0:128], r=["qcn"], w=[ptk])
            tr(ptb[:, 128:256], qcn[:, 128:256], r=["qcn"], w=[ptk])
            tr(ptb[:, 256:384], lat[:, 0:128], r=["lat"], w=[ptk])
            tr(ptb[0:32, 384:512], lat[:, 128:160], r=["lat"], w=[ptk])
            op("dve", "tensor_copy", r=[], w=[ptk, "qcnT"], out=qcnT, in_=ptb[:, 0:256].rearrange("p (k t) -> p k t", k=2))
            act(latT[j], ptb[:, 256:384], AF.Identity, r=[], w=[ptk, f"latT{j}"])
            act(krT[j], ptb[0:32, 384:512], AF.Identity, r=[], w=[ptk, f"krT{j}"])
            dma(self.lat_loc[0:128, tsl], latT[j], r=[f"latT{j}"], w=["lat_loc"])
            dma(self.lat_loc[128:160, tsl], krT[j], r=[f"krT{j}"], w=["lat_loc"])
            for g in range(2):
                ps, pk = self.nextps()
                for k in range(2):
                    mm(ps[:, 0:384], qcnT[:, k, :], Wuq[:, k, g * 384:(g + 1) * 384], k == 0, k == 1, r=["qcnT", "Wuq"], w=[pk])
                pv = ps[:, 0:384].rearrange("p (h d) -> p h d", h=4)
                act(Qr[:, g * 4:(g + 1) * 4, 0:64], pv[:, :, 0:64], AF.Identity, r=[], w=[pk, "Qr"])
                self.rope(pv[:, :, 64:80], pv[:, :, 80:96], self.cos16[:, i, :], self.sin16[:, i, :],
                          Qr[:, g * 4:(g + 1) * 4, 64:80], Qr[:, g * 4:(g + 1) * 4, 80:96], 4, 16, pk, "Qr", rtmp)
            pt, ptk = self.nextps()
            ptb = pt.bitcast(BF16)
            for h in range(8):
                tr(ptb[0:96, h * 128:(h + 1) * 128], Qr[:, h, :], r=["Qr"], w=[ptk])
            op("dve", "tensor_copy", r=[], w=[ptk, f"QT{j}"], out=QT[j], in_=ptb[0:96, :].rearrange("p (h t) -> p h t", h=8))
            dma(self.qT_d[:, :, tsl].rearrange("h p t -> p h t"), QT[j], r=[f"QT{j}"], w=["qT_d"])
            ps, pk = self.nextps()
            for k in range(8):
                mm(ps, hT[:, k, tsl], W1[:, k, C_RQ:C_RV], k == 0, k == 7, r=[hk, "W1"], w=[pk])
            for qk in range(2):
                pv = ps[:, qk * 256:(qk + 1) * 256].rearrange("p (h d) -> p h d", h=4)
                dv = rqk[j][:, qk * 256:(qk + 1) * 256].rearrange("p (h d) -> p h d", h=4)
                cT_, sT_ = (self.cos32q, self.sin32q) if qk == 0 else (self.cos32, self.sin32)
                self.rope(pv[:, :, 0:32], pv[:, :, 32:64], cT_[:, i, :], sT_[:, i, :], dv[:, :, 0:32], dv[:, :, 32:64], 4, 32, pk, f"rqk{j}", rtmp)
            dma(self.rq_d[tsl, :], rqk[j][:, 0:256], r=[f"rqk{j}"], w=["rq_d"])
            dma(self.rk_d[tsl, :], rqk[j][:, 256:512], r=[f"rqk{j}"], w=["rk_d"])
            ps, pk = self.nextps()
            for k in range(8):
                mm(ps, hT[:, k, tsl], W1[:, k, C_RV:C_RG], k == 0, k == 7, r=[hk, "W1"], w=[pk])
            act(rvb[j], ps, AF.Identity, r=[], w=[pk, "rvb0"])
            dma(self.rv_d[tsl, :], rvb[j], r=["rvb0"], w=["rv_d"])
            ps, pk = self.nextps()
            for k in range(8):
                mm(ps, hT[:, k, tsl], W1[:, k, C_RG:C_S5], k == 0, k == 7, r=[hk, "W1"], w=[pk])
            act(rgb[j], ps, AF.Silu, r=[], w=[pk, "rgb0"])
            dma(self.rg_d[tsl, :], rgb[j], r=["rgb0"], w=["rg_d"])
        n = 0
        for b in range(T // TB):
            bsl = slice(b * TB, (b + 1) * TB)
            hks = [f"hT{i}" for i in range(b * TB // 128, (b + 1) * TB // 128)]
            for fc in range(4):
                ps, pk = self.nextps()
                for k in range(8):
                    mm(ps[:, 0:TB], Ws5[:, k, fc * 128:(fc + 1) * 128], hT[:, k, bsl], k == 0, k == 7, r=hks + ["Ws5"], w=[pk])
                j = n % 2; n += 1
                act(ub[j], ps[:, 0:TB], AF.Identity, r=[], w=[pk, f"ub{j}"])
                dma(self.u_loc[fc * 128:(fc + 1) * 128, bsl], ub[j], r=[f"ub{j}"], w=["u_loc"])
        self.allgather(self.lat_all, self.lat_loc, r=["lat_loc"], w=["lat_all"])
        for c in range(4):
            self.allgather(self.u_all[c * 512:(c + 1) * 512, :], self.u_loc[c * 128:(c + 1) * 128, :], r=["u_loc"], w=["u_all"])
        for nm in ("lat_loc", "qT_d", "rq_d", "rk_d", "rv_d", "rg_d", "u_loc", "lat_all"):
            self.dbg(nm, getattr(self, nm), nm)

    def phase_R(self, l):
        T, NT = self.T, self.NT
        op, dma, mm, tr, act = self.op, self.dma, self.mm, self.tr, self.act
        self.phase("R")
        rq = self.sb("rq", [128, NT, 256], BF16); rk = self.sb("rk", [128, NT, 256], BF16)
        rv = self.sb("rv", [128, NT, 512], BF16); sg = self.sb("sg", [128, NT, 512], BF16)
        dma(rq, self.rq_d.rearrange("(n p) f -> p n f", p=128), w=["rq"])
        dma(rk, self.rk_d.rearrange("(n p) f -> p n f", p=128), w=["rk"])
        dma(rv, self.rv_d.rearrange("(n p) f -> p n f", p=128), w=["rv"])
        dma(sg, self.rg_d.rearrange("(n p) f -> p n f", p=128), w=["sg"])
        lgall = self.sb("lgall", [128, 8]); lgsel = self.sb("lgsel", [128, 4])
        dma(lgall, self.ldc_in[l:l + 1, :].partition_broadcast(128), w=["lgall"])
        dma(lgsel[0:64, :], self.ldc_in[l:l + 1, 0:4].partition_broadcast(64), w=["lgsel"])
        dma(lgsel[64:128, :], self.ldc_in[l:l + 1, 4:8].partition_broadcast(64), w=["lgsel"])
        for t_, k_ in ((lgall, "lgall"), (lgsel, "lgsel")):
            act(t_, t_, AF.Exp, r=[k_], w=[k_])
            act(t_, t_, AF.Ln, r=[k_], w=[k_], scale=-1.0, bias=1.0)
        zx = self.sb("zx", [128, 4, 4])
        for kind, (ecol, lo) in enumerate(((0, 0), (1, 4), (2, 0), (3, 4))):
            act(zx[:, kind, :], lgall[:, lo:lo + 4], AF.Exp, r=["lgall", "ek"], w=["zx"], scale=self.ek[:, ecol:ecol + 1])
        cm = self.sb("cm", [128, 4, 128])
        dma(cm, self.cmask_in.rearrange("a p f -> p a f"), w=["cm"])
        DT = self.sb("DT", [128, 4, 128]); dtmp = self.sb("dtmp", [128, 2, 128])
        for h in range(4):
            act(dtmp[:, 0, :], cm[:, 0, :], AF.Exp, r=["cm", "lgall"], w=["dtmp0"], scale=lgall[:, h:h + 1])
            act(dtmp[:, 1, :], cm[:, 1, :], AF.Exp, r=["cm", "lgall"], w=["dtmp1"], scale=lgall[:, 4 + h:5 + h])
            op("dve", "tensor_tensor", r=["dtmp0", "cm"], w=["dtmp0"], out=dtmp[:, 0, :], in0=dtmp[:, 0, :], in1=cm[:, 2, :], op=ALU.mult)
            op("dve", "tensor_tensor", r=["dtmp1", "cm"], w=["dtmp1"], out=dtmp[:, 1, :], in0=dtmp[:, 1, :], in1=cm[:, 3, :], op=ALU.mult)
            op("dve", "tensor_tensor", r=["dtmp0", "dtmp1"], w=["DT"], out=DT[:, h, :], in0=dtmp[:, 0, :], in1=dtmp[:, 1, :], op=ALU.add)
        lg128 = self.sb("lg128", [128, 4]); lgT = self.sb("lgT", [128, 4])
        op("dve", "tensor_scalar", r=["lgsel"], w=["lg128"], out=lg128, in0=lgsel, scalar1=128.0, scalar2=None, op0=ALU.mult)
        op("dve", "tensor_scalar", r=["lgsel"], w=["lgT"], out=lgT, in0=lgsel, scalar1=float(T), scalar2=None, op0=ALU.mult)
        cdec = self.sb("cdec", [128, 4])
        act(cdec, lg128, AF.Exp, r=["lg128"], w=["cdec"])
        coefn = self.sb("coefn", [128, 4, NT]); coefr = self.sb("coefr", [128, 4, 4])
        for h in range(4):
            act(coefn[:, h, :], self.etab, AF.Exp, r=["lg128"], w=["coefn"], scale=lg128[:, h:h + 1])
            act(coefr[:, h, :], self.dist, AF.Exp, r=["lgT"], w=["coefr"], scale=lgT[:, h:h + 1])
        E = self.sb("E", [128, 4, NT, 128])
        kk = [self.sb(f"kk{i}", [128, 128], BF16) for i in range(2)]
        n_ = 0
        for n in range(NT):
            for h in range(4):
                j = n_ % 2; n_ += 1
                kh = rk[:, n, h * 64:(h + 1) * 64]
                op("dve", "tensor_scalar", r=["rk", "zx"], w=[f"kk{j}"], out=kk[j][:, 0:64], in0=kh, scalar1=zx[:, 0, h:h + 1], scalar2=None, op0=ALU.mult)
                op("dve", "tensor_scalar", r=["rk", "zx"], w=[f"kk{j}"], out=kk[j][:, 64:128], in0=kh, scalar1=zx[:, 1, h:h + 1], scalar2=None, op0=ALU.mult)
                ps, pk = self.nextps()
                mm(ps[:, 0:128], kk[j], rv[:, n, h * 128:(h + 1) * 128], True, True, r=[f"kk{j}", "rv"], w=[pk])
                act(E[:, h, n, :], ps[:, 0:128], AF.Identity, r=[], w=[pk, f"E{h}_{n}"])
        for h in range(4):
            for n in range(1, NT):
                op("dve", "scalar_tensor_tensor", r=[f"E{h}_{n - 1}"], w=[f"E{h}_{n}"], out=E[0:64, h, n, :], in0=E[0:64, h, n - 1, :],
                   scalar=cdec[0:64, h:h + 1], in1=E[0:64, h, n, :], op0=ALU.mult, op1=ALU.add)
            for n in range(NT - 2, -1, -1):
                op("dve", "scalar_tensor_tensor", r=[f"E{h}_{n + 1}"], w=[f"E{h}_{n}"], out=E[64:128, h, n, :], in0=E[64:128, h, n + 1, :],
                   scalar=cdec[64:128, h:h + 1], in1=E[64:128, h, n, :], op0=ALU.mult, op1=ALU.add)
        ekeys = [f"E{h}_{n}" for h in range(4) for n in range(NT)]
        dma(self.agg_loc.rearrange("(h p) e -> p h e", p=128)[0:64], E[0:64, :, NT - 1, :], r=ekeys, w=["agg_loc"])
        dma(self.agg_loc.rearrange("(h p) e -> p h e", p=128)[64:128], E[64:128, :, 0, :], r=ekeys, w=["agg_loc"])
        self.allgather(self.agg_all, self.agg_loc, r=["agg_loc"], w=["agg_all"])
        agg = self.sb("agg", [128, 4, 4, 128])
        dma(agg, self.agg_all.rearrange("(r h p) e -> p r h e", r=4, h=4), r=["agg_all"], w=["agg"])
        Cin = self.sb("Cin", [128, 4, 128])
        for h in range(4):
            op("dve", "tensor_scalar", r=["agg", "coefr"], w=[f"Cin{h}"], out=Cin[:, h, :], in0=agg[:, 0, h, :], scalar1=coefr[:, h, 0:1], scalar2=None, op0=ALU.mult)
            for r_ in range(1, 4):
                op("dve", "scalar_tensor_tensor", r=["agg", "coefr"], w=[f"Cin{h}"], out=Cin[:, h, :], in0=agg[:, r_, h, :], scalar=coefr[:, h, r_:r_ + 1],
                   in1=Cin[:, h, :], op0=ALU.mult, op1=ALU.add)
        Pb = self.sb("Pb", [128, 4, NT, 128], BF16)
        for h in range(4):
            for n in range(NT):
                src_f = E[0:64, h, n - 1, :] if n >= 1 else self.zeros_f[0:64, :]
                src_b = E[64:128, h, n + 1, :] if n <= NT - 2 else self.zeros_f[64:128, :]
                op("dve", "scalar_tensor_tensor", r=ekeys + [f"Cin{h}", "coefn"], w=[f"Pb{n}"], out=Pb[0:64, h, n, :], in0=Cin[0:64, h, :],
                   scalar=coefn[0:64, h, n:n + 1], in1=src_f, op0=ALU.mult, op1=ALU.add)
                op("dve", "scalar_tensor_tensor", r=ekeys + [f"Cin{h}", "coefn"], w=[f"Pb{n}"], out=Pb[64:128, h, n, :], in0=Cin[64:128, h, :],
                   scalar=coefn[64:128, h, n:n + 1], in1=src_b, op0=ALU.mult, op1=ALU.add)
        for nm_, ap_, k_ in (("zx", zx, ["zx"]), ("DT", DT, ["DT"]), ("E", E, ekeys), ("Cin", Cin, [f"Cin{h}" for h in range(4)]), ("coefr", coefr, ["coefr"]), ("lgall", lgall, ["lgall"])):
            if nm_ in self.debug:
                o_ = self.nc.dram_tensor("dbg_" + nm_, list(ap_.shape), ap_.dtype, kind="ExternalOutput").ap()
                self.dma(o_, ap_, r=k_)
                self.dbg_out.append("dbg_" + nm_)
        qq = [self.sb(f"qq{i}", [128, 128], BF16) for i in range(2)]
        tqk = [self.sb(f"tqk{i}", [64, 256], BF16) for i in range(2)]
        tqq = [self.sb(f"tqq{i}", [128, 128], BF16) for i in range(2)]
        AT = [self.sb(f"AT{i}", [128, 128], BF16) for i in range(2)]
        gst = self.sb("gst", [128, 4, 6]); gmv = self.sb("gmv", [128, 4, 2]); grs = self.sb("grs", [128, 4, 2])
        yn = self.sb("yn", [128, 512]); orb = self.sb("orb", [128, 512], BF16)
        oT = [self.sb(f"oT{i}", [128, 4, 128], BF16) for i in range(2)]
        n_ = 0
        for n in range(NT):
            yps, ypk = self.nextps([6, 7])
            for h in range(4):
                j = n_ % 2; n_ += 1
                qh = rq[:, n, h * 64:(h + 1) * 64]
                op("dve", "tensor_scalar", r=["rq", "zx"], w=[f"qq{j}"], out=qq[j][:, 0:64], in0=qh, scalar1=zx[:, 2, h:h + 1], scalar2=None, op0=ALU.mult)
                op("dve", "tensor_scalar", r=["rq", "zx"], w=[f"qq{j}"], out=qq[j][:, 64:128], in0=qh, scalar1=zx[:, 3, h:h + 1], scalar2=None, op0=ALU.mult)
                pt, ptk = self.nextps([0, 1, 2, 3, 4, 5])
                ptb = pt.bitcast(BF16)
                tr(ptb[0:64, 0:128], qh, r=["rq"], w=[ptk])
                tr(ptb[0:64, 128:256], rk[:, n, h * 64:(h + 1) * 64], r=["rk"], w=[ptk])
                tr(ptb[:, 256:384], qq[j], r=[f"qq{j}"], w=[ptk])
                op("dve", "tensor_copy", r=[], w=[ptk, f"tqk{j}"], out=tqk[j], in_=ptb[0:64, 0:256])
                act(tqq[j], ptb[:, 256:384], AF.Identity, r=[], w=[ptk, f"tqq{j}"])
                sps, spk = self.nextps([0, 1, 2, 3, 4, 5])
                mm(sps[:, 0:128], tqk[j][:, 128:256], tqk[j][:, 0:128], True, True, r=[f"tqk{j}"], w=[spk])
                op("dve", "tensor_tensor", r=["DT"], w=[spk, f"AT{j}"], out=AT[j], in0=sps[:, 0:128], in1=DT[:, h, :], op=ALU.mult)
                ysl = yps[:, h * 128:(h + 1) * 128]
                mm(ysl, AT[j], rv[:, n, h * 128:(h + 1) * 128], True, False, r=[f"AT{j}", "rv"], w=[ypk])
                mm(ysl, tqq[j], Pb[:, h, n, :], False, True, r=[f"tqq{j}", f"Pb{n}"], w=[ypk])
            for h in range(4):
                op("dve", "bn_stats", r=[], w=[ypk, "gst"], out=gst[:, h, :], in_=yps[:, h * 128:(h + 1) * 128])
            for h in range(4):
                op("dve", "bn_aggr", r=["gst"], w=["gmv"], out=gmv[:, h, :], in_=gst[:, h, :])
            act(grs[:, :, 0], gmv[:, :, 1], AF.Ln, r=["gmv"], w=["grs"], bias=self.constc[:, 3:4])
            act(grs[:, :, 0], grs[:, :, 0], AF.Exp, r=["grs"], w=["grs"], scale=-0.5)
            op("dve", "scalar_tensor_tensor", r=["gmv", "grs"], w=["grs2"], out=grs[:, :, 1], in0=gmv[:, :, 0], scalar=-1.0, in1=grs[:, :, 0],
               op0=ALU.mult, op1=ALU.mult)
            for h in range(4):
                act(yn[:, h * 128:(h + 1) * 128], yps[:, h * 128:(h + 1) * 128], AF.Identity, r=["grs", "grs2"], w=[ypk, "yn"],
                    scale=grs[:, h, 0:1], bias=grs[:, h, 1:2])
            op("dve", "tensor_tensor", r=["yn", "sg"], w=["orb"], out=orb, in0=yn, in1=sg[:, n, :], op=ALU.mult)
            pt, ptk = self.nextps([0, 1, 2, 3, 4, 5])
            ptb = pt.bitcast(BF16)
            for h in range(4):
                tr(ptb[:, h * 128:(h + 1) * 128], orb[:, h * 128:(h + 1) * 128], r=["orb"], w=[ptk])
            jo = n % 2
            op("dve", "tensor_copy", r=[], w=[ptk, f"oT{jo}"], out=oT[jo], in_=ptb[:, 0:512].rearrange("p (h t) -> p h t", h=4))
            dma(self.ob_d[1][:, n * 128:(n + 1) * 128].rearrange("(h p) t -> p h t", p=128), oT[jo], r=[f"oT{jo}"], w=["ob1_d"])
        self.dbg("ob1_d", self.ob_d[1], "ob1_d")

    def angle_tables(self, u, ukey, shape, tag, sinT, cosT, skey, ckey, negsin=False):
        op, act = self.op, self.act
        ki = self.cached(f"at_ki{tag}", shape, I32); kf = self.cached(f"at_kf{tag}", shape); ab = self.cached(f"at_ab{tag}", shape)
        k = f"at{tag}"
        op("dve", "tensor_copy", r=[ukey], w=[k + "ki"], out=ki, in_=u)
        op("dve", "tensor_copy", r=[k + "ki"], w=[k + "kf"], out=kf, in_=ki)
        op("dve", "tensor_tensor", r=[ukey, k + "kf"], w=[ukey], out=u, in0=u, in1=kf, op=ALU.subtract)
        act(sinT, u, AF.Sin, r=[ukey], w=[skey], scale=-TWO_PI if negsin else TWO_PI)
        act(ab, u, AF.Abs, r=[ukey], w=[k + "ab"])
        act(cosT, ab, AF.Sin, r=[k + "ab"], w=[ckey], scale=-TWO_PI, bias=self.halfpi[0:shape[0], :])

    def cached(self, name, shape, dt=F32):
        key = (self.fw.nbar, name)
        if key not in self._cache:
            self._cache[key] = self.sb(name, shape, dt)
        return self._cache[key]

    def phase_S(self, l):
        S, T = self.S, self.T
        op, dma, mm, act = self.op, self.dma, self.mm, self.act
        self.phase("S")
        SBk = min(512, S)
        NB = S // SBk
        TB = min(2048, T)
        uT = self.sb("uT", [128, S], BF16)
        ys = self.sb("ys", [128, S])
        io = self.sb("io", [128, 512])
        lamS = self.sb("lamS", [128, 2, 3, 4])
        stepS = self.sb("stepS", [128, 2, 4]); magS = self.sb("magS", [128, 2, 4]); thS = self.sb("thS", [128, 2, 4])
        Bb = self.sb("Bb", [128, 2, 2, 512], BF16)
        Cb = self.sb("Cb", [128, 2, 2, 512], BF16)
        wc = self.sb("wc", [128, 2, 4, 2])
        mark = self.arena_off
        ust = [self.sb(f"ust{i}", [128, TB]) for i in range(3)]
        uacc = self.sb("uacc", [128, TB])
        n_ = 0
        for q in range(4):
            for tb in range(T // TB):
                for c in range(4):
                    j = n_ % 3; n_ += 1
                    dma(ust[j], self.u_all[(c * 4 + q) * 128:(c * 4 + q + 1) * 128, tb * TB:(tb + 1) * TB], w=[f"ust{j}"])
                    dst = uT[:, q * T + tb * TB:q * T + (tb + 1) * TB] if c == 3 else uacc
                    dk = "uT" if c == 3 else "uacc"
                    if c == 0:
                        op("dve", "tensor_scalar", r=[f"ust{j}"], w=["uacc"], out=uacc, in0=ust[j], scalar1=self.sel[:, 0:1], scalar2=None, op0=ALU.mult)
                    else:
                        op("dve", "scalar_tensor_tensor", r=[f"ust{j}", "uacc"], w=[dk], out=dst, in0=ust[j], scalar=self.sel[:, c:c + 1], in1=uacc,
                           op0=ALU.mult, op1=ALU.add)
        dma(io, self.io_in, w=["io"])
        for d in range(2):
            dma(lamS[:, d, 0, :], self.lamS_re[l, d], w=["lamS"])
            dma(lamS[:, d, 1, :], self.lamS_im[l, d], w=["lamS"])
            dma(lamS[:, d, 2, :], self.lsS[l, d], w=["lamS"])
        act(stepS, lamS[:, :, 2, :], AF.Exp, r=["lamS"], w=["stepS"])
        op("dve", "tensor_tensor", r=["lamS", "stepS"], w=["magS"], out=magS, in0=lamS[:, :, 0, :], in1=stepS, op=ALU.mult)
        act(magS, magS, AF.Exp, r=["magS"], w=["magS"])
        op("dve", "tensor_tensor", r=["lamS", "stepS"], w=["thS"], out=thS, in0=lamS[:, :, 1, :], in1=stepS, op=ALU.mult)
        op("dve", "tensor_scalar", r=["thS"], w=["thS"], out=thS, in0=thS, scalar1=1.0 / TWO_PI, scalar2=None, op0=ALU.mult)
        for d in range(2):
            self.subphase(mark)
            lB = self.sb("lB", [128, 3, 512])
            dma(lB[:, 0, :], self.lamB_re[l, d:d + 1, :].partition_broadcast(128), w=["lB"])
            dma(lB[:, 1, :], self.lamB_im[l, d:d + 1, :].partition_broadcast(128), w=["lB"])
            dma(lB[:, 2, :], self.lsB[l, d:d + 1, :].partition_broadcast(128), w=["lB"])
            sh = [128, 512]
            stB = self.sb("stB", sh); mgB = self.sb("mgB", sh); tu = self.sb("tu", sh); sn = self.sb("sn", sh); cs = self.sb("cs", sh)
            are = self.sb("are", sh); aim = self.sb("aim", sh); den = self.sb("den", sh); fre = self.sb("fre", sh); fim = self.sb("fim", sh)
            t1 = self.sb("t1s", sh)
            lre, lim = lB[:, 0, :], lB[:, 1, :]
            act(stB, lB[:, 2, :], AF.Exp, r=["lB"], w=["stB"])
            op("dve", "tensor_tensor", r=["lB", "stB"], w=["mgB"], out=mgB, in0=lre, in1=stB, op=ALU.mult)
            act(mgB, mgB, AF.Exp, r=["mgB"], w=["mgB"])
            op("dve", "tensor_tensor", r=["lB", "stB"], w=["tu"], out=tu, in0=lim, in1=stB, op=ALU.mult)
            op("dve", "tensor_scalar", r=["tu"], w=["tu"], out=tu, in0=tu, scalar1=1.0 / TWO_PI, scalar2=None, op0=ALU.mult)
            self.angle_tables(tu, "tu", sh, "B", sn, cs, "sn", "cs")
            op("dve", "tensor_tensor", r=["mgB", "cs"], w=["are"], out=are, in0=mgB, in1=cs, op=ALU.mult)
            op("dve", "tensor_tensor", r=["mgB", "sn"], w=["aim"], out=aim, in0=mgB, in1=sn, op=ALU.mult)
            op("dve", "tensor_scalar", r=["are"], w=["are"], out=are, in0=are, scalar1=-1.0, scalar2=None, op0=ALU.add)
            op("dve", "tensor_tensor", r=["lB"], w=["den"], out=den, in0=lre, in1=lre, op=ALU.mult)
            op("dve", "tensor_tensor", r=["lB"], w=["t1s"], out=t1, in0=lim, in1=lim, op=ALU.mult)
            op("dve", "tensor_tensor", r=["den", "t1s"], w=["den"], out=den, in0=den, in1=t1, op=ALU.add)
            op("dve", "reciprocal", r=["den"], w=["den"], out=den, in_=den)
            op("dve", "tensor_tensor", r=["are", "lB"], w=["fre"], out=fre, in0=are, in1=lre, op=ALU.mult)
            op("dve", "tensor_tensor", r=["aim", "lB"], w=["t1s"], out=t1, in0=aim, in1=lim, op=ALU.mult)
            op("dve", "tensor_tensor", r=["fre", "t1s"], w=["fre"], out=fre, in0=fre, in1=t1, op=ALU.add)
            op("dve", "tensor_tensor", r=["fre", "den"], w=["fre"], out=fre, in0=fre, in1=den, op=ALU.mult)
            op("dve", "tensor_tensor", r=["aim", "lB"], w=["fim"], out=fim, in0=aim, in1=lre, op=ALU.mult)
            op("dve", "tensor_tensor", r=["are", "lB"], w=["t1s"], out=t1, in0=are, in1=lim, op=ALU.mult)
            op("dve", "tensor_tensor", r=["fim", "t1s"], w=["fim"], out=fim, in0=fim, in1=t1, op=ALU.subtract)
            op("dve", "tensor_tensor", r=["fim", "den"], w=["fim"], out=fim, in0=fim, in1=den, op=ALU.mult)
            braw = self.sb("braw", [128, 2, 512]); t2 = self.sb("t2s", [128, 512]); t3 = self.sb("t3s", [128, 512])
            dma(braw[:, 0, :], self.Bblk_re[l, d], w=["braw"])
            dma(braw[:, 1, :], self.Bblk_im[l, d], w=["braw"])
            op("dve", "tensor_tensor", r=["braw", "fre"], w=["t2s"], out=t2, in0=braw[:, 0, :], in1=fre, op=ALU.mult)
            op("dve", "tensor_tensor", r=["braw", "fim"], w=["t3s"], out=t3, in0=braw[:, 1, :], in1=fim, op=ALU.mult)
            op("dve", "tensor_tensor", r=["t2s", "t3s"], w=["Bb"], out=Bb[:, d, 0, :], in0=t2, in1=t3, op=ALU.subtract)
            op("dve", "tensor_tensor", r=["braw", "fre"], w=["t2s"], out=t2, in0=braw[:, 1, :], in1=fre, op=ALU.mult)
            op("dve", "tensor_tensor", r=["braw", "fim"], w=["t3s"], out=t3, in0=braw[:, 0, :], in1=fim, op=ALU.mult)
            op("dve", "tensor_tensor", r=["t2s", "t3s"], w=["Bb"], out=Bb[:, d, 1, :], in0=t2, in1=t3, op=ALU.add)
            dma(braw[:, 0, :], self.Cblk_re[l, d], r=["Bb"], w=["braw"])
            dma(braw[:, 1, :], self.Cblk_im[l, d], r=["Bb"], w=["braw"])
            op("pool", "tensor_copy", r=["braw"], w=["Cb"], out=Cb[:, d, :, :], in_=braw)
        self.subphase(mark)
        op("dve", "memset", w=["wc"], ap=wc, constant=0.0)
        W = [128, SBk]
        tun = [self.sb(f"tun{i}", W) for i in range(2)]
        nsin = [self.sb(f"nsin{i}", W) for i in range(2)]; cosb = [self.sb(f"cosb{i}", W) for i in range(2)]
        ta = self.sb("ta", W); tb_ = self.sb("tb", W); tc = self.sb("tc", W); td = self.sb("td", W)
        vre = self.sb("vre", W); vim = self.sb("vim", W); wre = self.sb("wre", W); wim = self.sb("wim", W)
        xre = [self.sb(f"xre{i}", W, BF16) for i in range(2)]; nxi = [self.sb(f"nxi{i}", W, BF16) for i in range(2)]
        n_ = 0
        for d in range(2):
            for sbi in range(NB):
                tau0 = sbi * SBk
                t0 = tau0 if d == 0 else S - tau0 - SBk
                aps, apk = self.nextps([6, 7])
                for st in range(4):
                    j = n_ % 2; n_ += 1
                    op("dve", "tensor_scalar", r=["io", "thS"], w=[f"tun{j}"], out=tun[j], in0=io[:, 0:SBk], scalar1=float(tau0), scalar2=thS[:, d, st:st + 1],
                       op0=ALU.add, op1=ALU.mult)
                    self.angle_tables(tun[j], f"tun{j}", W, "M", nsin[j], cosb[j], f"nsin{j}", f"cosb{j}", negsin=True)
                    pre, prk = self.nextps([0, 1, 2, 3, 4, 5])
                    pim, pik = self.nextps([0, 1, 2, 3, 4, 5])
                    mm(pre[:, 0:SBk], Bb[:, d, 0, st * 128:(st + 1) * 128], uT[:, t0:t0 + SBk], True, True, r=["Bb", "uT"], w=[prk])
                    mm(pim[:, 0:SBk], Bb[:, d, 1, st * 128:(st + 1) * 128], uT[:, t0:t0 + SBk], True, True, r=["Bb", "uT"], w=[pik])
                    bre = pre[:, 0:SBk] if d == 0 else rev_ap(pre[:, 0:SBk], SBk)
                    bim = pim[:, 0:SBk] if d == 0 else rev_ap(pim[:, 0:SBk], SBk)
                    ns, cb = nsin[j], cosb[j]
                    op("dve", "tensor_tensor", r=[f"cosb{j}"], w=[prk, "ta"], out=ta, in0=bre, in1=cb, op=ALU.mult)
                    op("dve", "tensor_tensor", r=[f"nsin{j}"], w=[pik, "tb"], out=tb_, in0=bim, in1=ns, op=ALU.mult)
                    op("dve", "tensor_tensor", r=["ta", "tb"], w=["vre"], out=vre, in0=ta, in1=tb_, op=ALU.subtract)
                    op("dve", "tensor_tensor", r=[f"cosb{j}"], w=[pik, "tc"], out=tc, in0=bim, in1=cb, op=ALU.mult)
                    op("dve", "tensor_tensor", r=[f"nsin{j}"], w=[prk, "td"], out=td, in0=bre, in1=ns, op=ALU.mult)
                    op("dve", "tensor_tensor", r=["tc", "td"], w=["vim"], out=vim, in0=tc, in1=td, op=ALU.add)
                    mg = magS[:, d, st:st + 1].to_broadcast(W)
                    op("dve", "tensor_tensor_scan", r=["vre", "magS", "wc"], w=["wre"], out=wre, data0=mg, data1=vre, initial=wc[:, d, st, 0:1], op0=ALU.mult, op1=ALU.add)
                    op("dve", "tensor_tensor_scan", r=["vim", "magS", "wc"], w=["wim"], out=wim, data0=mg, data1=vim, initial=wc[:, d, st, 1:2], op0=ALU.mult, op1=ALU.add)
                    op("dve", "tensor_copy", r=["wre"], w=["wc"], out=wc[:, d, st, 0:1], in_=wre[:, SBk - 1:SBk])
                    op("dve", "tensor_copy", r=["wim"], w=["wc"], out=wc[:, d, st, 1:2], in_=wim[:, SBk - 1:SBk])
                    op("dve", "tensor_tensor", r=["wre", f"cosb{j}"], w=["ta"], out=ta, in0=wre, in1=cb, op=ALU.mult)
                    op("dve", "tensor_tensor", r=["wim", f"nsin{j}"], w=["tb"], out=tb_, in0=wim, in1=ns, op=ALU.mult)
                    op("dve", "tensor_tensor", r=["ta", "tb"], w=[f"xre{j}"], out=xre[j], in0=ta, in1=tb_, op=ALU.add)
                    op("dve", "tensor_tensor", r=["wre", f"nsin{j}"], w=["tc"], out=tc, in0=wre, in1=ns, op=ALU.mult)
                    op("dve", "tensor_tensor", r=["wim", f"cosb{j}"], w=["td"], out=td, in0=wim, in1=cb, op=ALU.mult)
                    op("dve", "tensor_tensor", r=["tc", "td"], w=[f"nxi{j}"], out=nxi[j], in0=tc, in1=td, op=ALU.subtract)
                    mm(aps[:, 0:SBk], Cb[:, d, 0, st * 128:(st + 1) * 128], xre[j], st == 0, False, r=["Cb", f"xre{j}"], w=[apk])
                    mm(aps[:, 0:SBk], Cb[:, d, 1, st * 128:(st + 1) * 128], nxi[j], False, st == 3, r=["Cb", f"nxi{j}"], w=[apk])
                if d == 0:
                    act(ys[:, t0:t0 + SBk], aps[:, 0:SBk], AF.Identity, r=[], w=[apk, "ys"])
                else:
                    yr = rev_ap(ys[:, t0:t0 + SBk], SBk)
                    op("dve", "tensor_tensor", r=[], w=[apk, "ys"], out=yr, in0=aps[:, 0:SBk], in1=yr, op=ALU.add)
        for q in range(4):
            dma(self.ys_loc[q * 128:(q + 1) * 128, :], ys[:, q * T:(q + 1) * T], r=["ys"], w=["ys_loc"])
        for q in range(4):
            self.allgather(self.ys_all[q * 512:(q + 1) * 512, :], self.ys_loc[q * 128:(q + 1) * 128, :], r=["ys_loc"], w=["ys_all"])
        self.dbg("ys_loc", self.ys_loc, "ys_loc")

    def phase_AT(self, l):
        S, T = self.S, self.T
        op, dma, mm, act = self.op, self.dma, self.mm, self.act
        self.phase("AT")
        NKT = S // 128
        QB = min(512, T)
        KB = min(512, S)
        kvnT = self.sb("kvnT", [128, S], BF16); krT = self.sb("krT", [32, S], BF16)
        for r_ in range(4):
            dma(kvnT[:, r_ * T:(r_ + 1) * T], self.lat_all[r_ * 160:r_ * 160 + 128, :], r=["lat_all"], w=["kvnT"])
            dma(krT[:, r_ * T:(r_ + 1) * T], self.lat_all[r_ * 160 + 128:r_ * 160 + 160, :], r=["lat_all"], w=["krT"])
        kvg = self.sb("kvg", [128, 1]); wst = self.sb("wukv_st", [128, 1024])
        Wk = self.sb("Wk", [128, 8, 96], BF16); Wv = self.sb("Wv", [128, 8, 64], BF16); Sel = self.sb("Sel", [32, 96], BF16)
        dma(kvg, self.kvnorm_in[l], w=["kvg"])
        dma(wst, self.w_ukv[l], w=["wukv_st"])
        op("dve", "memset", w=["Wk"], ap=Wk, constant=0.0)
        op("dve", "memset", w=["Sel"], ap=Sel, constant=0.0)
        wv_ = wst.rearrange("p (h d) -> p h d", h=8)
        op("dve", "tensor_scalar", r=["wukv_st", "kvg", "Wk"], w=["Wk"], out=Wk[:, :, 0:64], in0=wv_[:, :, 0:64], scalar1=kvg[:, 0:1], scalar2=None, op0=ALU.mult)
        op("dve", "tensor_scalar", r=["wukv_st", "kvg"], w=["Wv"], out=Wv, in0=wv_[:, :, 64:128], scalar1=kvg[:, 0:1], scalar2=None, op0=ALU.mult)
        op("dve", "tensor_copy", r=["Sel", "ident"], w=["Sel"], out=Sel[:, 64:96], in_=self.ident[0:32, 0:32])
        KT = [self.sb(f"KT{i}", [96, S], BF16) for i in range(2)]
        Vp = [self.sb(f"Vp{i}", [128, NKT, 65], BF16) for i in range(2)]
        QTh = [self.sb(f"QTh{i}", [96, T], BF16) for i in range(2)]
        pT = [self.sb(f"pT{i}", [128, QB], BF16) for i in range(4)]
        rec = self.sb("rec", [128, QB]); Rb = self.sb("Rb", [64, QB])
        oh = [self.sb(f"oh{i}", [64, QB], BF16) for i in range(2)]
        for i in range(2):
            op("dve", "memset", w=[f"Vp{i}"], ap=Vp[i], constant=1.0)
        scale = 96.0 ** -0.5
        np_ = 0; no_ = 0; ns_ = [0]
        for h in range(NH):
            hb = h % 2
            dma(QTh[hb], self.qT_d[h], r=["qT_d"], w=[f"QTh{hb}"])
            for kb in range(S // KB):
                ksl = slice(kb * KB, (kb + 1) * KB)
                ps, pk = self.nextps([0])
                mm(ps[0:96, 0:KB], Wk[:, h, :], kvnT[:, ksl], True, False, r=["Wk", "kvnT"], w=[pk])
                mm(ps[0:96, 0:KB], Sel, krT[:, ksl], False, True, r=["Sel", "krT"], w=[pk])
                op("dve", "tensor_copy", r=[], w=[pk, f"KT{hb}"], out=KT[hb][:, ksl], in_=ps[0:96, 0:KB])
            for g in range(0, NKT, 8):
                ng = min(8, NKT - g)
                ps, pk = self.nextps([0])
                for t in range(ng):
                    mm(ps[:, t * 64:(t + 1) * 64], kvnT[:, (g + t) * 128:(g + t + 1) * 128], Wv[:, h, :], True, True, r=["kvnT", "Wv"], w=[pk])
                act(Vp[hb][:, g:g + ng, 0:64], ps[:, 0:ng * 64].rearrange("p (t d) -> p t d", t=ng), AF.Identity, r=[], w=[pk, f"Vp{hb}"])
            for qb in range(T // QB):
                qsl = slice(qb * QB, (qb + 1) * QB)
                acc, ak = self.nextps([6, 7])
                LOOK = 3

                def issue_s(kt_):
                    sps_, spk_ = self.ps[1 + (ns_[0] % 4)], f"ps{1 + (ns_[0] % 4)}"
                    ns_[0] += 1
                    mm(sps_[:, 0:QB], KT[hb][:, kt_ * 128:(kt_ + 1) * 128], QTh[hb][:, qsl], True, True, r=[f"KT{hb}", f"QTh{hb}"], w=[spk_])
                    return sps_, spk_
                pend = [issue_s(k_) for k_ in range(min(LOOK, NKT))]
                for kt in range(NKT):
                    sps, spk = pend.pop(0)
                    j = np_ % 4; np_ += 1
                    act(pT[j], sps[:, 0:QB], AF.Exp, r=[], w=[spk, f"pT{j}"], scale=scale)
                    if kt + LOOK < NKT:
                        pend.append(issue_s(kt + LOOK))
                    mm(acc[0:65, 0:QB], Vp[hb][:, kt, :], pT[j], kt == 0, kt == NKT - 1, r=[f"Vp{hb}", f"pT{j}"], w=[ak])
                op("dve", "reciprocal", r=[], w=[ak, "rec"], out=rec[64:65, :], in_=acc[64:65, 0:QB])
                rp, rpk = self.nextps([5])
                mm(rp[0:64, 0:QB], self.ones_f[64:65, 0:64], rec[64:65, :], True, True, r=["rec", "ones_f"], w=[rpk])
                act(Rb, rp[0:64, 0:QB], AF.Identity, r=[], w=[rpk, "Rb"])
                jo = no_ % 2; no_ += 1
                op("dve", "tensor_tensor", r=["Rb"], w=[ak, f"oh{jo}"], out=oh[jo], in0=acc[0:64, 0:QB], in1=Rb, op=ALU.mult)
                dma(self.ob_d[0][h * 64:(h + 1) * 64, qsl], oh[jo], r=[f"oh{jo}"], w=["ob0_d"])
        self.dbg("ob0_d", self.ob_d[0], "ob0_d")

    def phase_M0(self, l):
        S, T = self.S, self.T
        op, dma, mm, act = self.op, self.dma, self.mm, self.act
        self.phase("M0")
        TB = min(512, T)
        Wg = self.sb("Wglu", [128, 4, 512], BF16)
        self.mk_wstage(512, n=2)
        for k in range(4):
            self.wload(Wg[:, k, :], self.w_glu[l, k * 128:(k + 1) * 128, :], "Wglu")
        dsk = self.sb("dsk", [128, 4])
        dma(dsk, self.dskip_in[l], w=["dsk"])
        cand = [self.sb(f"cand{i}", [128, TB]) for i in range(3)]
        ub = [self.sb(f"ub{i}", [128, TB]) for i in range(2)]
        yacc = self.sb("yacc", [128, TB]); y = self.sb("yM0", [128, 4, TB]); t1 = self.sb("t1m", [128, TB]); t2 = self.sb("t2m", [128, TB])
        yg = self.sb("yg", [128, 4, TB]); ygb = self.sb("ygb", [128, 4, TB], BF16); sgm = self.sb("sgm", [128, TB])
        ob = [self.sb(f"obs{i}", [128, TB], BF16) for i in range(2)]
        n_ = 0; nu = 0; no_ = 0
        for tb in range(T // TB):
            bsl = slice(tb * TB, (tb + 1) * TB)
            for fc in range(4):
                for q in range(4):
                    j = n_ % 3; n_ += 1
                    dma(cand[j], self.ys_all[(q * 4 + fc) * 128:(q * 4 + fc + 1) * 128, tb * TB:(tb + 1) * TB], r=["ys_all"], w=[f"cand{j}"])
                    if q == 0:
                        op("dve", "tensor_scalar", r=[f"cand{j}"], w=["yacc"], out=yacc, in0=cand[j], scalar1=self.sel[:, 0:1], scalar2=None, op0=ALU.mult)
                    else:
                        op("dve", "scalar_tensor_tensor", r=[f"cand{j}", "yacc"], w=["yacc"], out=yacc, in0=cand[j], scalar=self.sel[:, q:q + 1], in1=yacc,
                           op0=ALU.mult, op1=ALU.add)
                ju = nu % 2; nu += 1
                dma(ub[ju], self.u_loc[fc * 128:(fc + 1) * 128, bsl], r=["u_loc"], w=[f"ub{ju}"])
                yk = f"y{fc}"
                op("dve", "scalar_tensor_tensor", r=[f"ub{ju}", "yacc", "dsk"], w=[yk], out=y[:, fc, :], in0=ub[ju], scalar=dsk[:, fc:fc + 1], in1=yacc,
                   op0=ALU.mult, op1=ALU.add)
                op("dve", "tensor_tensor", r=[yk], w=["t1m"], out=t1, in0=y[:, fc, :], in1=y[:, fc, :], op=ALU.mult)
                op("dve", "tensor_scalar", r=["t1m"], w=["t1m"], out=t1, in0=t1, scalar1=0.044715, scalar2=1.0, op0=ALU.mult, op1=ALU.add)
                op("dve", "tensor_tensor", r=["t1m", yk], w=["t2m"], out=t2, in0=t1, in1=y[:, fc, :], op=ALU.mult)
                act(t2, t2, AF.Sigmoid, r=["t2m"], w=["t2m"], scale=2.0 * math.sqrt(2.0 / math.pi))
                op("dve", "tensor_tensor", r=["t2m", yk], w=[f"yg{fc}"], out=yg[:, fc, :], in0=t2, in1=y[:, fc, :], op=ALU.mult)
                act(ygb[:, fc, :], yg[:, fc, :], AF.Identity, r=[f"yg{fc}"], w=[f"ygb{fc}"])
            for fo in range(4):
                ps, pk = self.nextps()
                for k in range(4):
                    mm(ps[:, 0:TB], Wg[:, k, fo * 128:(fo + 1) * 128], ygb[:, k, :], k == 0, k == 3, r=["Wglu"] + [f"ygb{k}"], w=[pk])
                act(sgm, ps[:, 0:TB], AF.Sigmoid, r=[], w=[pk, "sgm"])
                jo = no_ % 2; no_ += 1
                op("dve", "tensor_tensor", r=["sgm", f"yg{fo}"], w=[f"obs{jo}"], out=ob[jo], in0=sgm, in1=yg[:, fo, :], op=ALU.mult)
                dma(self.ob_d[2][fo * 128:(fo + 1) * 128, bsl], ob[jo], r=[f"obs{jo}"], w=["ob2_d"])
        self.dbg("ob2_d", self.ob_d[2], "ob2_d")

    def phase_M1(self, l):
        T = self.T
        op, dma, mm, act = self.op, self.dma, self.mm, self.act
        self.phase("M1")
        TB = min(512, T)
        hT = self.sb("hT", [128, 8, T], BF16)
        self.mk_wstage(2304, n=3)
        self.phase_A(l, 0, hT, 0)
        hks = [f"hT{i}" for i in range(self.NT)]
        obT = [self.sb(f"obT{i}", [128, 4, TB], BF16) for i in range(3)]
        nob = 0
        Wg = [self.sb(f"Wg{i}", [128, 3, 8, 128], BF16) for i in range(2)]
        Wb = [self.sb(f"Wb{i}", [128, 3, 4, 128], BF16) for i in range(2)]
        sg = [self.sb(f"sgt{i}", [128, TB]) for i in range(2)]
        m = self.sb("macc", [128, TB]); tm = self.sb("tmM", [128, TB])
        mo = [self.sb(f"mo{i}", [128, TB], BF16) for i in range(2)]
        ns = 0; no_ = 0
        for f in range(8):
            jw = f % 2
            for b in range(3):
                c0 = C_G + b * D + f * 128
                self.wload(Wg[jw][:, b, :, :], self.w_in[l, :, c0:c0 + 128].rearrange("(k p) n -> p k n", p=128), f"Wg{jw}")
                self.wload(Wb[jw][:, b, :, :], self.w_br[b][l, :, f * 128:(f + 1) * 128].rearrange("(k p) n -> p k n", p=128), f"Wb{jw}")
            for tb in range(T // TB):
                bsl = slice(tb * TB, (tb + 1) * TB)
                for b in range(3):
                    gp, gk = self.nextps()
                    for k in range(8):
                        mm(gp[:, 0:TB], Wg[jw][:, b, k, :], hT[:, k, bsl], k == 0, k == 7, r=[f"Wg{jw}"] + hks, w=[gk])
                    js = ns % 2; ns += 1
                    act(sg[js], gp[:, 0:TB], AF.Sigmoid, r=[], w=[gk, f"sgt{js}"])
                    jb = nob % 3; nob += 1
                    dma(obT[jb], self.ob_d[b][:, bsl].rearrange("(k p) t -> p k t", p=128), r=[f"ob{b}_d"], w=[f"obT{jb}"])
                    bp, bk = self.nextps()
                    for k in range(4):
                        mm(bp[:, 0:TB], Wb[jw][:, b, k, :], obT[jb][:, k, :], k == 0, k == 3, r=[f"Wb{jw}", f"obT{jb}"], w=[bk])
                    if b == 0:
                        op("dve", "tensor_tensor", r=[f"sgt{js}"], w=[bk, "macc"], out=m, in0=bp[:, 0:TB], in1=sg[js], op=ALU.mult)
                    else:
                        op("dve", "tensor_tensor", r=[f"sgt{js}"], w=[bk, "tmM"], out=tm, in0=bp[:, 0:TB], in1=sg[js], op=ALU.mult)
                        if b == 1:
                            op("dve", "tensor_tensor", r=["tmM", "macc"], w=["macc"], out=m, in0=m, in1=tm, op=ALU.add)
                        else:
                            jo = no_ % 2; no_ += 1
                            op("dve", "tensor_tensor", r=["tmM", "macc"], w=[f"mo{jo}"], out=mo[jo], in0=m, in1=tm, op=ALU.add)
                            dma(self.mT_d[f * 128:(f + 1) * 128, bsl], mo[jo], r=[f"mo{jo}"], w=["mT_d"])
        self.dbg("mT_d", self.mT_d, "mT_d")

    def phase_M2(self, l):
        T, NT = self.T, self.NT
        op, dma, mm, act = self.op, self.dma, self.mm, self.act
        self.phase("M2")
        self.mk_wstage(2304, n=2)
        gB = self.sb("gateB", [128, D])
        self.modvec(l, 2, gB, "gateB")
        Wo = self.sb("Wo", [128, 8, D], BF16)
        for k in range(8):
            self.wload(Wo[:, k, :], self.w_o[l, k * 128:(k + 1) * 128, :], "Wo", mul=gB, mul_key="gateB")
        mT = self.sb("mT", [128, 8, T], BF16)
        dma(mT, self.mT_d.rearrange("(k p) t -> p k t", p=128), r=["mT_d"], w=["mT"])
        for i in range(NT):
            for half in range(2):
                ps, pk = self.nextps()
                for k in range(8):
                    mm(ps, mT[:, k, i * 128:(i + 1) * 128], Wo[:, k, half * 512:(half + 1) * 512], k == 0, k == 7, r=["mT", "Wo"], w=[pk])
                xs = self.x_res[:, i, half * 512:(half + 1) * 512]
                op("dve", "scalar_tensor_tensor", r=[], w=[pk, f"x{i}"], out=xs, in0=xs, scalar=ALPHA, in1=ps, op0=ALU.mult, op1=ALU.add)
        self.post_norm(l, 0)

    def post_norm(self, l, which):
        NT = self.NT
        op, dma, act = self.op, self.dma, self.act
        gB = self.sb("lngB", [128, D]); bB = self.sb("lnbB", [128, D])
        dma(gB, self.ln_in[l, 2 * which:2 * which + 1, :].partition_broadcast(128), w=["lngB"])
        dma(bB, self.ln_in[l, 2 * which + 1:2 * which + 2, :].partition_broadcast(128), w=["lnbB"])
        lns = [self.mk_lnscr(f"p{i}") for i in range(2)]
        xn = [self.sb(f"pxn{i}", [128, D]) for i in range(2)]
        for i in range(NT):
            j = i % 2
            xt = self.x_res[:, i, :]
            rs, rk = self.ln_stats(xt, f"x{i}", 128, lns[j])
            act(xn[j], xt, AF.Identity, r=[f"x{i}"] + rk, w=[f"pxn{j}"], scale=rs[:, 0:1], bias=rs[:, 1:2])
            op("dve", "tensor_tensor", r=[f"pxn{j}", "lngB"], w=[f"pxn{j}"], out=xn[j], in0=xn[j], in1=gB, op=ALU.mult)
            op("dve", "tensor_tensor", r=[f"pxn{j}", "lnbB"], w=[f"x{i}"], out=xt, in0=xn[j], in1=bB, op=ALU.add)

    def phase_F(self, l, last):
        T, NT = self.T, self.NT
        op, dma, mm, act, tr = self.op, self.dma, self.mm, self.act, self.tr
        self.phase("H")
        dma(self.halo_loc[0:1, :], self.x_res[0:1, 0, :], w=["halo_loc"])
        dma(self.halo_loc[1:2, :], self.x_res[127:128, NT - 1, :], w=["halo_loc"])
        self.allgather(self.halo_all, self.halo_loc, r=["halo_loc"], w=["halo_all"])
        self.phase("F")
        hT = self.sb("hT2", [128, 8, T + 2], BF16)
        self.mk_wstage(2304, n=3)
        shB, scB = self.phase_A(l, 1, hT, 1)
        rows = self.sb("hrows", [8, D]); xh = self.sb("xh", [2, D]); hh = self.sb("hh", [2, D], BF16)
        dma(rows, self.halo_all, r=["halo_all"], w=["hrows"])
        for half in range(2):
            ps, pk = self.nextps()
            mm(ps[0:2, :], self.selH, rows[:, half * 512:(half + 1) * 512], True, True, r=["hrows"], w=[pk])
            act(xh[:, half * 512:(half + 1) * 512], ps[0:2, :], AF.Identity, r=[], w=[pk, "xh"])
        lsc = self.mk_lnscr("h")
        rs, rk = self.ln_stats(xh, "xh", 2, lsc)
        act(xh, xh, AF.Identity, r=["xh"] + rk, w=["xh"], scale=rs[0:2, 0:1], bias=rs[0:2, 1:2])
        op("dve", "tensor_tensor", r=["xh", "scB"], w=["xh"], out=xh, in0=xh, in1=scB[0:2, :], op=ALU.mult)
        op("dve", "tensor_tensor", r=["xh", "shB"], w=["xh"], out=xh, in0=xh, in1=shB[0:2, :], op=ALU.add)
        op("dve", "tensor_scalar", r=["xh"], w=["hh"], out=hh, in0=xh, scalar1=self.flag[0:2, 0:1], scalar2=None, op0=ALU.mult)
        ps, pk = self.nextps()
        psb = ps.bitcast(BF16)
        for k in range(8):
            tr(psb[:, k * 2:(k + 1) * 2], hh[:, k * 128:(k + 1) * 128], r=["hh"], w=[pk])
        pv = psb[:, 0:16].rearrange("p (k t) -> p k t", k=8)
        op("dve", "tensor_copy", r=[], w=[pk, "hTh0"], out=hT[:, :, 0:1], in_=pv[:, :, 0:1])
        op("dve", "tensor_copy", r=[], w=[pk, "hTh1"], out=hT[:, :, T + 1:T + 2], in_=pv[:, :, 1:2])
        hks = [f"hT{i}" for i in range(NT)] + ["hTh0", "hTh1"]
        gB = self.sb("gate2B", [128, D])
        self.modvec(l, 5, gB, "gate2B")
        cw = self.sb("cw", [128, 44, 3]); cb = self.sb("cb", [128, 44])
        dma(cw, self.convw_in[l].rearrange("p (c k) -> p c k", k=3), w=["cw"])
        dma(cb, self.convb_in[l], w=["cb"])
        FB = min(256, T)
        Wa = [self.sb(f"Wa{i}", [128, 8, 128], BF16) for i in range(2)]
        Wgt = [self.sb(f"Wgt{i}", [128, 8, 128], BF16) for i in range(2)]
        Wd = [self.sb(f"Wd{i}", [128, D], BF16) for i in range(2)]
        ca_ = [self.sb(f"ca{i}", [128, FB]) for i in range(1)]; cg_ = [self.sb(f"cg{i}", [128, FB]) for i in range(1)]
        ca, cg = ca_[0], cg_[0]
        av = [self.sb(f"av{i}", [128, FB], BF16) for i in range(2)]
        na = 0
        NTB = T // FB

        def issue_up(jf, tb):
            jw = jf % 2
            if tb == 0:
                self.wload(Wa[jw], self.w_up[l, :, jf * 128:(jf + 1) * 128].rearrange("(k p) n -> p k n", p=128), f"Wa{jw}")
                self.wload(Wgt[jw], self.w_up[l, :, DFF + jf * 128:DFF + (jf + 1) * 128].rearrange("(k p) n -> p k n", p=128), f"Wgt{jw}")
                self.wload(Wd[jw], self.w_down[l, jf * 128:(jf + 1) * 128, :], f"Wd{jw}", mul=gB, mul_key="gate2B")
            c0 = tb * FB
            ui = (jf * NTB + tb) % 2
            pa, pak = self.ps[ui], f"ps{ui}"
            pg, pgk = self.ps[2 + ui], f"ps{2 + ui}"
            for k in range(8):
                mm(pa[:, 0:FB + 2], Wa[jw][:, k, :], hT[:, k, c0:c0 + FB + 2], k == 0, k == 7, r=[f"Wa{jw}"] + hks, w=[pak])
            for k in range(8):
                mm(pg[:, 0:FB + 2], Wgt[jw][:, k, :], hT[:, k, c0:c0 + FB + 2], k == 0, k == 7, r=[f"Wgt{jw}"] + hks, w=[pgk])
            return pa, pak, pg, pgk

        def finish(jf, tb, pa, pak, pg, pgk):
            nonlocal na
            jw = jf % 2
            c0 = tb * FB
            for (pp, ppk, dst, dk, ch) in ((pa, pak, ca, "ca", jf), (pg, pgk, cg, "cg", 22 + jf)):
                act(dst, pp[:, 1:FB + 1], AF.Identity, r=["cw", "cb"], w=[ppk, dk], scale=cw[:, ch, 1:2], bias=cb[:, ch:ch + 1])
                op("dve", "scalar_tensor_tensor", r=["cw"], w=[ppk, dk], out=dst, in0=pp[:, 0:FB], scalar=cw[:, ch, 0:1], in1=dst, op0=ALU.mult, op1=ALU.add)
                op("dve", "scalar_tensor_tensor", r=["cw"], w=[ppk, dk], out=dst, in0=pp[:, 2:FB + 2], scalar=cw[:, ch, 2:3], in1=dst, op0=ALU.mult, op1=ALU.add)
            act(cg, cg, AF.Silu, r=["cg"], w=["cg"])
            ja = na % 2; na += 1
            op("dve", "tensor_tensor", r=["ca", "cg"], w=[f"av{ja}"], out=av[ja], in0=ca, in1=cg, op=ALU.mult)
            for ti in range(FB // 128):
                i = (c0 // 128) + ti
                for half in range(2):
                    ps, pk = self.nextps([4, 5, 6, 7])
                    mm(ps, av[ja][:, ti * 128:(ti + 1) * 128], Wd[jw][:, half * 512:(half + 1) * 512], True, True, r=[f"av{ja}", f"Wd{jw}"], w=[pk])
                    xs = self.x_res[:, i, half * 512:(half + 1) * 512]
                    if jf == 0:
                        op("dve", "scalar_tensor_tensor", r=[], w=[pk, f"x{i}"], out=xs, in0=xs, scalar=ALPHA, in1=ps, op0=ALU.mult, op1=ALU.add)
                    else:
                        op("dve", "tensor_tensor", r=[], w=[pk, f"x{i}"], out=xs, in0=xs, in1=ps, op=ALU.add)

        prev = None
        for jf in range(DFF // 128):
            for tb in range(NTB):
                cur = (jf, tb) + issue_up(jf, tb)
                if prev is not None:
                    finish(*prev)
                prev = cur
        finish(*prev)
        self.post_norm(l, 1)
        if last:
            dma(self.out.rearrange("(n p) d -> p n d", p=128), self.x_res, r=[f"x{i}" for i in range(NT)], w=["out"])

    def build(self, stop=None):
        self.declare_inputs()
        self.setup()
        for l in range(self.L):
            self.phase_B(l)
            self.phase_R(l)
            self.phase_S(l)
            self.phase_AT(l)
            self.phase_M0(l)
            self.phase_M1(l)
            self.phase_M2(l)
            self.phase_F(l, last=(l == self.L - 1))
        self.fw.barrier()
        self.fw.emit()
        return self.nc


def _consts(S):
    T = S // 4
    NT = T // 128
    f32 = np.float32
    p = np.arange(128, dtype=f32)
    c = {}
    c["invf16"] = np.broadcast_to((f32(10000.0) ** (-np.arange(16, dtype=f32) / f32(16))).astype(f32), (128, 16)).copy()
    c["invf32"] = np.broadcast_to((f32(10000.0) ** (-np.arange(32, dtype=f32) / f32(32))).astype(f32), (128, 32)).copy()
    jj, ii = np.meshgrid(p, p, indexing="ij")
    EF = np.maximum(ii - jj, 0); EB = np.maximum(jj - ii, 0)
    MF = (ii >= jj).astype(f32); MB = (jj > ii).astype(f32)
    c["cmask"] = np.stack([EF, EB, MF, MB]).astype(f32)
    c["ek"] = np.stack([127 - p, p, p + 1, 128 - p], 1).astype(f32)
    n = np.arange(NT, dtype=f32)
    et = np.zeros((128, NT), f32); et[:64] = n[None, :]; et[64:] = (NT - 1 - n)[None, :]
    c["etab"] = et
    c["io"] = np.broadcast_to(np.arange(1, 513, dtype=f32), (128, 512)).copy()
    return c


def prep_inputs(inp, S, L):
    T = S // 4
    NT = T // 128
    f32 = np.float32
    A = lambda a: np.ascontiguousarray(np.asarray(a))
    cst = _consts(S)
    sh = {}
    sh["w_in"] = A(inp["w_in"][:L]); sh["w_uq"] = A(inp["mla_w_uq"][:L]); sh["w_ukv"] = A(inp["mla_w_ukv"][:L])
    sh["qnorm"] = A(np.asarray(inp["mla_q_norm"])[:L].reshape(L, 2, 128).transpose(0, 2, 1))
    sh["kvnorm"] = A(np.asarray(inp["mla_kv_norm"])[:L].reshape(L, 128, 1))
    sh["ldc"] = A(np.asarray(inp["ret_log_decay"])[:L].reshape(L, 8))
    sh["dskip"] = A(np.asarray(inp["s5_d"])[:L].reshape(L, 4, 128).transpose(0, 2, 1))
    sh["w_glu"] = A(inp["s5_w_glu"][:L])
    sh["w_br_mla"] = A(inp["w_branch_mla"][:L]); sh["w_br_ret"] = A(inp["w_branch_ret"][:L]); sh["w_br_s5"] = A(inp["w_branch_s5"][:L])
    sh["w_o"] = A(inp["w_o"][:L]); sh["w_up"] = A(inp["ffn_w_up"][:L]); sh["w_down"] = A(inp["ffn_w_down"][:L])
    cw = np.asarray(inp["ffn_conv_w"])[:L]
    sh["convw"] = A(cw.reshape(L, 3, 44, 128).transpose(0, 3, 2, 1).reshape(L, 128, 132))
    sh["convb"] = A(np.asarray(inp["ffn_conv_b"])[:L].reshape(L, 44, 128).transpose(0, 2, 1))
    sh["ln"] = A(np.stack([np.asarray(inp[k])[:L] for k in ("ln1_g", "ln1_b", "ln2_g", "ln2_b")], 1))
    sh["w_ada"] = A(inp["w_ada"][:L]); sh["b_ada"] = A(inp["b_ada"][:L])
    sh.update(cst)
    x = np.asarray(inp["x"]); cc = np.asarray(inp["c"]); pos = np.asarray(inp["positions"])
    s5 = {k: np.asarray(inp[k])[:L] for k in ("s5_lam_re", "s5_lam_im", "s5_log_step", "s5_b_re", "s5_b_im", "s5_c_re", "s5_c_im")}
    maps = []
    for core in range(8):
        b, r = core // 4, core % 4
        m = dict(sh)
        m["x"] = A(x[b, r * T:(r + 1) * T, :])
        m["cT"] = A(cc[b].reshape(8, 128).T)
        m["pos"] = A(pos[b, r * T:(r + 1) * T].reshape(NT, 128).T.astype(np.int32))
        dist = np.full((128, 4), BIGD, f32)
        for rr in range(4):
            if rr < r:
                dist[:64, rr] = r - 1 - rr
            if rr > r:
                dist[64:, rr] = rr - 1 - r
        m["dist"] = dist
        sel = np.zeros((128, 4), f32); sel[:, r] = 1.0
        m["sel"] = sel
        selH = np.zeros((8, 2), f32); flag = np.zeros((2, 1), f32)
        if r > 0:
            selH[2 * (r - 1) + 1, 0] = 1.0; flag[0, 0] = 1.0
        if r < 3:
            selH[2 * (r + 1), 1] = 1.0; flag[1, 0] = 1.0
        m["selH"] = selH; m["flag"] = flag
        g0 = 8 * r
        lamS_re = np.zeros((L, 2, 128, 4), f32); lamS_im = np.zeros_like(lamS_re); lsS = np.zeros_like(lamS_re)
        Bre = np.zeros((L, 2, 128, 4, 128), f32); Bim = np.zeros_like(Bre); Cre = np.zeros_like(Bre); Cim = np.zeros_like(Bre)
        for st in range(4):
            for gg in range(2):
                g = g0 + 2 * st + gg
                gl = 2 * st + gg
                lamS_re[:, :, gg * 64:(gg + 1) * 64, st] = s5["s5_lam_re"][:, :, g, :]
                lamS_im[:, :, gg * 64:(gg + 1) * 64, st] = s5["s5_lam_im"][:, :, g, :]
                lsS[:, :, gg * 64:(gg + 1) * 64, st] = s5["s5_log_step"][:, :, g, None]
                Bre[:, :, gl * 16:(gl + 1) * 16, st, gg * 64:(gg + 1) * 64] = s5["s5_b_re"][:, :, g].transpose(0, 1, 3, 2)
                Bim[:, :, gl * 16:(gl + 1) * 16, st, gg * 64:(gg + 1) * 64] = s5["s5_b_im"][:, :, g].transpose(0, 1, 3, 2)
                Cre[:, :, gg * 64:(gg + 1) * 64, st, gl * 16:(gl + 1) * 16] = s5["s5_c_re"][:, :, g].transpose(0, 1, 3, 2)
                Cim[:, :, gg * 64:(gg + 1) * 64, st, gl * 16:(gl + 1) * 16] = s5["s5_c_im"][:, :, g].transpose(0, 1, 3, 2)
        m["lamS_re"] = lamS_re; m["lamS_im"] = lamS_im; m["lsS"] = lsS
        m["lamB_re"] = A(lamS_re.transpose(0, 1, 3, 2).reshape(L, 2, 512))
        m["lamB_im"] = A(lamS_im.transpose(0, 1, 3, 2).reshape(L, 2, 512))
        m["lsB"] = A(lsS.transpose(0, 1, 3, 2).reshape(L, 2, 512))
        m["Bblk_re"] = Bre.reshape(L, 2, 128, 512); m["Bblk_im"] = Bim.reshape(L, 2, 128, 512)
        m["Cblk_re"] = Cre.reshape(L, 2, 128, 512); m["Cblk_im"] = Cim.reshape(L, 2, 128, 512)
        maps.append(m)
    return maps


_CACHE = {}


def kernel(**inputs):
    S = int(np.asarray(inputs["x"]).shape[1])
    L = int(np.asarray(inputs["w_in"]).shape[0])
    T = S // 4
    key = (S, L)
    if key not in _CACHE:
        b = Builder(S, L)
        b.build()
        _CACHE[key] = b
    b = _CACHE[key]
    maps = prep_inputs(inputs, S, L)
    maps = [{k: m[k] for k in b.inputs} for m in maps]
    res = run_bass_kernel_spmd(b.nc, maps, core_ids=list(range(8)))
    out = np.zeros((2, S, D), np.float32)
    for c in range(8):
        out[c // 4, (c % 4) * T:(c % 4 + 1) * T, :] = np.asarray(res.results[c]["out"]).reshape(T, D)
    return out
```

```python
import math
import numpy as np
import ml_dtypes
import concourse.bass as bass
import concourse.mybir as mybir
from concourse.bass_utils import run_bass_kernel_spmd

F32 = mybir.dt.float32
BF16 = mybir.dt.bfloat16
I32 = mybir.dt.int32
ALU = mybir.AluOpType
AF = mybir.ActivationFunctionType

ENGS = ("pe", "act", "dve", "pool", "sp")
D = 1024
NH = 8
DFF = 2816
C_QC, C_KV, C_KR, C_RQ, C_RK, C_RV, C_RG, C_S5, C_G = 0, 256, 384, 416, 672, 928, 1440, 1952, 2464
ALPHA = float((2 * 4) ** 0.25)
LN_EPS, RMS_EPS, GN_EPS = 1e-5, 1e-6, 1e-5
TWO_PI = 2.0 * math.pi
SB_BASE, SB_END = 16512, 229376
BIGD = 40.0


class FW:
    def __init__(self, nc, n_dma_sems=(("sp", 40), ("pool", 16))):
        self.nc = nc
        self.stream = {e: [] for e in ENGS}
        self.ctr = {e: nc.alloc_semaphore(name=f"ctr_{e}") for e in ENGS}
        self.bar = {e: nc.alloc_semaphore(name=f"bar_{e}") for e in ENGS}
        self.count = {e: 0 for e in ENGS}
        self.known = {e: {} for e in ENGS}
        self.keys = {}
        self.dpool = {}
        for e, n in n_dma_sems:
            self.dpool[e] = dict(sems=[nc.alloc_semaphore(name=f"dq_{e}_{i}") for i in range(n)], vals=[0] * n, nxt=0)
        self.nbar = 0
        self.n_inst = 0
        self.customs = {}

    def _wait(self, eng, tok):
        sem, val = tok
        if val <= 0:
            return
        k = self.known[eng]
        if k.get(id(sem), 0) >= val:
            return
        k[id(sem)] = val
        self.stream[eng].append(("wait", sem, val))

    def _deps(self, eng, reads, writes, skip_same):
        toks = []
        for key in reads:
            st = self.keys.get(key)
            if st and st["w"] is not None:
                toks.append(st["w"])
        for key in writes:
            st = self.keys.get(key)
            if st:
                if st["w"] is not None:
                    toks.append(st["w"])
                toks.extend(st["r"])
        own = self.ctr[eng]
        for t in toks:
            if t[0] is own and skip_same:
                continue
            self._wait(eng, t)

    def _record(self, tok, reads, writes):
        for key in reads:
            st = self.keys.setdefault(key, {"w": None, "r": []})
            st["r"] = [t for t in st["r"] if t[0] is not tok[0]] + [tok]
        for key in writes:
            self.keys[key] = {"w": tok, "r": []}

    def op(self, eng, fn, reads=(), writes=(), skip_same=False):
        if eng == "pe":
            skip_same = True
        self._deps(eng, reads, writes, skip_same)
        self.count[eng] += 1
        tok = (self.ctr[eng], self.count[eng])
        self.stream[eng].append(("inst", fn, self.ctr[eng], 1))
        self._record(tok, reads, writes)
        self.n_inst += 1
        return tok

    def dma(self, eng, out, in_, reads=(), writes=()):
        p = self.dpool[eng]
        j = p["nxt"]
        p["nxt"] = (j + 1) % len(p["sems"])
        sem = p["sems"][j]
        self._wait(eng, (sem, p["vals"][j]))
        self._deps(eng, reads, writes, False)
        p["vals"][j] += 16
        tok = (sem, p["vals"][j])
        self.stream[eng].append(("inst", lambda e, o=out, i=in_: e.dma_start(out=o, in_=i), sem, 16))
        self._record(tok, reads, writes)
        self.n_inst += 1
        return tok

    def custom(self, eng, fn, sem, newval, reads=(), writes=()):
        self._deps(eng, reads, writes, False)
        self.stream[eng].append(("inst", fn, sem, 1))
        tok = (sem, newval)
        self.customs[eng] = tok
        self._record(tok, reads, writes)
        return tok

    def barrier(self):
        self.nbar += 1
        for e in ENGS:
            if e in self.dpool:
                p = self.dpool[e]
                for s, v in zip(p["sems"], p["vals"]):
                    self._wait(e, (s, v))
            if e in self.customs:
                self._wait(e, self.customs[e])
            self._wait(e, (self.ctr[e], self.count[e]))
            self.stream[e].append(("inst", lambda en: en.nop(), self.bar[e], 1))
        for e in ENGS:
            for e2 in ENGS:
                if e2 != e:
                    self._wait(e, (self.bar[e2], self.nbar))
        self.keys = {}

    def emit(self):
        nc = self.nc
        engobj = {"pe": "tensor", "act": "scalar", "dve": "vector", "pool": "gpsimd", "sp": "sync"}
        with nc.Block() as block:
            for e in ENGS:
                stream = self.stream[e]

                def body(eng, stream=stream):
                    for it in stream:
                        if it[0] == "wait":
                            eng.wait_ge(it[1], it[2])
                        else:
                            it[1](eng).then_inc(it[2], it[3])

                getattr(block, engobj[e])(body)


def _dsize(dt):
    return {F32: 4, BF16: 2, I32: 4}[dt]


def rev_ap(ap, n):
    a = ap.ap
    assert len(a) == 2 and a[1][0] == 1 and a[1][1] == n, a
    return bass.AP(ap.tensor, ap.offset + (n - 1), [[a[0][0], a[0][1]], [-1, n]])


class Builder:
    def __init__(self, S, L, debug=()):
        self.S, self.L = S, L
        self.T = S // 4
        self.NT = self.T // 128
        self.debug = set(debug)
        self.nc = bass.Bass("TRN2", target_bir_lowering=False)
        self.fw = FW(self.nc)
        self.pers_off = SB_BASE
        self.arena_off = SB_BASE
        self.uid = 0
        self.inputs = {}
        self.dbg_out = []
        self.ps = [self.nc.alloc_psum_tensor(f"ps{i}", [128, 512], F32).ap() for i in range(8)]
        self.psi = 0
        self.cc_sem = self.nc.alloc_semaphore(name="cc_sem")
        self.cc_n = 0
        self.stage_i = 0
        self._cache = {}

    def _alloc(self, name, shape, dt, off):
        nbytes = int(np.prod(shape[1:])) * _dsize(dt)
        nbytes = (nbytes + 31) // 32 * 32
        assert off + nbytes <= SB_END, f"SBUF overflow allocating {name} {shape}: {off + nbytes - SB_END} bytes over"
        self.uid += 1
        h = self.nc.alloc_sbuf_tensor_at(f"{name}_{self.uid}", list(shape), dt, offset=off)
        return h.ap(), off + nbytes

    def pers(self, name, shape, dt=F32):
        ap, self.pers_off = self._alloc(name, shape, dt, self.pers_off)
        self.arena_off = max(self.arena_off, self.pers_off)
        return ap

    def sb(self, name, shape, dt=F32):
        ap, self.arena_off = self._alloc(name, shape, dt, self.arena_off)
        return ap

    def phase(self, name):
        self.fw.barrier()
        self.arena_off = self.pers_off
        self.pname = name

    def subphase(self, mark):
        self.fw.barrier()
        self.arena_off = mark

    def inp(self, name, shape, dt=F32):
        ap = self.nc.dram_tensor(name, list(shape), dt, kind="ExternalInput").ap()
        self.inputs[name] = (tuple(shape), dt)
        return ap

    def dram(self, name, shape, dt=F32):
        return self.nc.dram_tensor(name, list(shape), dt).ap()

    def nextps(self, cyc=None):
        cyc = cyc or list(range(8))
        i = cyc[self.psi % len(cyc)]
        self.psi += 1
        return self.ps[i], f"ps{i}"

    def op(self, eng, meth, r=(), w=(), **kw):
        return self.fw.op(eng, lambda e: getattr(e, meth)(**kw), reads=list(r), writes=list(w))

    def dma(self, out, in_, r=(), w=(), q="sp"):
        return self.fw.dma(q, out, in_, reads=list(r), writes=list(w))

    def mm(self, out, lhsT, rhs, start, stop, r, w):
        return self.fw.op("pe", lambda e: e.matmul(out, lhsT=lhsT, rhs=rhs, start=start, stop=stop), reads=list(r), writes=list(w))

    def tr(self, out, in_, r, w):
        p = in_.shape[0]
        idn = self.ident[0:p, 0:p]
        return self.fw.op("pe", lambda e: e.transpose(out=out, in_=in_, identity=idn), reads=list(r) + ["ident"], writes=list(w))

    def act(self, out, in_, func, r, w, bias=0.0, scale=1.0, **kw):
        return self.fw.op("act", lambda e: e.activation(out=out, in_=in_, func=func, bias=bias, scale=scale, **kw), reads=list(r), writes=list(w))

    def allgather(self, out_ap, in_ap, r, w):
        self.cc_n += 1
        n = self.cc_n
        return self.fw.custom("pool", lambda e: e.collective_compute("AllGather", ALU.bypass, replica_groups=[[0, 1, 2, 3], [4, 5, 6, 7]],
                                                                     ins=[in_ap.opt()], outs=[out_ap.opt()]),
                              self.cc_sem, n, reads=list(r), writes=list(w))

    def dbg(self, name, dram_ap, key):
        if name in self.debug:
            o = self.nc.dram_tensor("dbg_" + name, list(dram_ap.shape), dram_ap.dtype, kind="ExternalOutput").ap()
            self.dma(o, dram_ap, r=[key])
            self.dbg_out.append("dbg_" + name)

    def dbg_sb(self, name, sb_ap, key):
        if name in self.debug:
            o = self.nc.dram_tensor("dbg_" + name, list(sb_ap.shape), sb_ap.dtype, kind="ExternalOutput").ap()
            self.dma(o, sb_ap, r=[key])
            self.dbg_out.append("dbg_" + name)

    def wload(self, dst, src, dst_key, mul=None, mul_key=None, src_key=None):
        shp = list(src.shape)
        rows = shp[0]
        cols = int(np.prod(shp[1:]))
        si = self.stage_i % len(self.wstage)
        self.stage_i += 1
        skey = f"wst{si}"
        st = self.wstage[si][0:rows, 0:cols]
        if len(shp) == 3:
            st = st.rearrange("p (k n) -> p k n", k=shp[1])
        self.dma(st, src, r=[src_key] if src_key else [], w=[skey])
        if mul is None:
            self.op("pool", "tensor_copy", r=[skey], w=[dst_key], out=dst, in_=st)
        else:
            self.op("pool", "tensor_tensor", r=[skey, mul_key], w=[dst_key], out=dst, in0=st, in1=mul, op=ALU.mult)

    def mk_wstage(self, cols, n=3):
        self.wstage = [self.sb(f"wst{i}", [128, cols], F32) for i in range(n)]
        self.stage_i = 0

    def declare_inputs(self):
        L, T, NT, S = self.L, self.T, self.NT, self.S
        I = self.inp
        self.x_in = I("x", [T, D])
        self.cT_in = I("cT", [128, 8])
        self.pos_in = I("pos", [128, NT], I32)
        self.invf16_in = I("invf16", [128, 16])
        self.invf32_in = I("invf32", [128, 32])
        self.cmask_in = I("cmask", [4, 128, 128])
        self.ek_in = I("ek", [128, 4])
        self.etab_in = I("etab", [128, NT])
        self.io_in = I("io", [128, 512])
        self.dist_in = I("dist", [128, 4])
        self.sel_in = I("sel", [128, 4])
        self.selH_in = I("selH", [8, 2])
        self.flag_in = I("flag", [2, 1])
        self.w_in = I("w_in", [L, D, 5536])
        self.qnorm_in = I("qnorm", [L, 128, 2])
        self.w_uq = I("w_uq", [L, 256, 768])
        self.kvnorm_in = I("kvnorm", [L, 128, 1])
        self.w_ukv = I("w_ukv", [L, 128, 1024])
        self.ldc_in = I("ldc", [L, 8])
        self.lamS_re = I("lamS_re", [L, 2, 128, 4])
        self.lamS_im = I("lamS_im", [L, 2, 128, 4])
        self.lsS = I("lsS", [L, 2, 128, 4])
        self.lamB_re = I("lamB_re", [L, 2, 512])
        self.lamB_im = I("lamB_im", [L, 2, 512])
        self.lsB = I("lsB", [L, 2, 512])
        self.Bblk_re = I("Bblk_re", [L, 2, 128, 512])
        self.Bblk_im = I("Bblk_im", [L, 2, 128, 512])
        self.Cblk_re = I("Cblk_re", [L, 2, 128, 512])
        self.Cblk_im = I("Cblk_im", [L, 2, 128, 512])
        self.dskip_in = I("dskip", [L, 128, 4])
        self.w_glu = I("w_glu", [L, 512, 512])
        self.w_br = [I("w_br_mla", [L, 512, D]), I("w_br_ret", [L, 512, D]), I("w_br_s5", [L, 512, D])]
        self.w_o = I("w_o", [L, D, D])
        self.w_up = I("w_up", [L, D, 2 * DFF])
        self.convw_in = I("convw", [L, 128, 44 * 3])
        self.convb_in = I("convb", [L, 128, 44])
        self.w_down = I("w_down", [L, DFF, D])
        self.ln_in = I("ln", [L, 4, D])
        self.w_ada = I("w_ada", [L, D, 6 * D])
        self.b_ada = I("b_ada", [L, 6 * D])
        self.out = self.nc.dram_tensor("out", [T, D], F32, kind="ExternalOutput").ap()
        Dm = self.dram
        self.lat_loc = Dm("lat_loc", [160, T], BF16)
        self.lat_all = Dm("lat_all", [4 * 160, T], BF16)
        self.qT_d = Dm("qT_d", [NH, 96, T], BF16)
        self.rq_d = Dm("rq_d", [T, 256], BF16)
        self.rk_d = Dm("rk_d", [T, 256], BF16)
        self.rv_d = Dm("rv_d", [T, 512], BF16)
        self.rg_d = Dm("rg_d", [T, 512], BF16)
        self.u_loc = Dm("u_loc", [512, T], F32)
        self.u_all = Dm("u_all", [16 * 128, T], F32)
        self.agg_loc = Dm("agg_loc", [4 * 128, 128], F32)
        self.agg_all = Dm("agg_all", [16 * 128, 128], F32)
        self.ys_loc = Dm("ys_loc", [4 * 128, T], F32)
        self.ys_all = Dm("ys_all", [16 * 128, T], F32)
        self.ob_d = [Dm(f"ob{i}_d", [512, T], BF16) for i in range(3)]
        self.mT_d = Dm("mT_d", [D, T], BF16)
        self.halo_loc = Dm("halo_loc", [2, D], F32)
        self.halo_all = Dm("halo_all", [8, D], F32)

    def setup(self):
        T, NT = self.T, self.NT
        P = self.pers
        self.x_res = P("x_res", [128, NT, D])
        self.ident = P("ident", [128, 128], BF16)
        self.ones_f = P("ones_f", [128, 128])
        self.zeros_f = P("zeros_f", [128, 128])
        self.condrep = P("condrep", [128, 8, 128])
        self.cos16 = P("cos16", [128, NT, 16]); self.sin16 = P("sin16", [128, NT, 16])
        self.cos32 = P("cos32", [128, NT, 32]); self.sin32 = P("sin32", [128, NT, 32])
        self.cos32q = P("cos32q", [128, NT, 32]); self.sin32q = P("sin32q", [128, NT, 32])
        self.sel = P("sel", [128, 4]); self.dist = P("dist", [128, 4])
        self.selH = P("selH", [8, 2]); self.flag = P("flag", [2, 1])
        self.ek = P("ek", [128, 4]); self.etab = P("etab", [128, NT])
        self.constc = P("constc", [128, 4])
        self.epsc = self.constc[:, 0:1]; self.rmseps = self.constc[:, 1:2]; self.halfpi = self.constc[:, 2:3]
        self.fw.barrier()
        self.arena_off = self.pers_off
        op, dma = self.op, self.dma
        dma(self.x_res, self.x_in.rearrange("(n p) d -> p n d", p=128), w=["x_res"])
        op("pool", "memset", w=["ident"], ap=self.ident, constant=0.0)
        op("pool", "affine_select", r=["ident"], w=["ident"], out=self.ident, in_=self.ident, pattern=[[-1, 128]],
           compare_op=ALU.not_equal, fill=1.0, base=0, channel_multiplier=1)
        op("dve", "memset", w=["ones_f"], ap=self.ones_f, constant=1.0)
        op("dve", "memset", w=["zeros_f"], ap=self.zeros_f, constant=0.0)
        op("dve", "memset", w=["constc"], ap=self.constc[:, 0:1], constant=LN_EPS)
        op("dve", "memset", w=["constc"], ap=self.constc[:, 1:2], constant=RMS_EPS)
        op("dve", "memset", w=["constc"], ap=self.constc[:, 2:3], constant=math.pi / 2)
        op("dve", "memset", w=["constc"], ap=self.constc[:, 3:4], constant=GN_EPS)
        for nm, src in (("sel", self.sel_in), ("dist", self.dist_in), ("selH", self.selH_in), ("flag", self.flag_in),
                        ("ek", self.ek_in), ("etab", self.etab_in)):
            dma(getattr(self, nm), src, w=[nm])
        cT = self.sb("cT", [128, 8]); cs = self.sb("cs", [128, 8])
        dma(cT, self.cT_in, w=["cT"])
        self.act(cs, cT, AF.Silu, r=["cT"], w=["cs"])
        op("dve", "tensor_copy", r=["cs"], w=["condrep"], out=self.condrep, in_=cs.unsqueeze(2).to_broadcast([128, 8, 128]))
        posi = self.sb("posi", [128, NT], I32); posf = self.sb("posf", [128, NT])
        dma(posi, self.pos_in, w=["posi"])
        op("dve", "tensor_copy", r=["posi"], w=["posf"], out=posf, in_=posi)
        for nf, src, cosT, sinT in ((16, self.invf16_in, self.cos16, self.sin16), (32, self.invf32_in, self.cos32, self.sin32)):
            inv = self.sb(f"inv{nf}", [128, nf]); u = self.sb(f"u{nf}", [128, NT, nf]); ki = self.sb(f"ki{nf}", [128, NT, nf], I32)
            kf = self.sb(f"kf{nf}", [128, NT, nf]); fr = self.sb(f"fr{nf}", [128, NT, nf]); ab = self.sb(f"ab{nf}", [128, NT, nf])
            k = f"rp{nf}"
            dma(inv, src, w=[k + "inv"])
            op("dve", "tensor_scalar", r=[k + "inv"], w=[k + "inv"], out=inv, in0=inv, scalar1=1.0 / TWO_PI, scalar2=None, op0=ALU.mult)
            op("dve", "tensor_tensor", r=["posf", k + "inv"], w=[k + "u"], out=u, in0=posf.unsqueeze(2).to_broadcast([128, NT, nf]),
               in1=inv.unsqueeze(1).to_broadcast([128, NT, nf]), op=ALU.mult)
            op("dve", "tensor_copy", r=[k + "u"], w=[k + "ki"], out=ki, in_=u)
            op("dve", "tensor_copy", r=[k + "ki"], w=[k + "kf"], out=kf, in_=ki)
            op("dve", "tensor_tensor", r=[k + "u", k + "kf"], w=[k + "fr"], out=fr, in0=u, in1=kf, op=ALU.subtract)
            self.act(sinT, fr, AF.Sin, r=[k + "fr"], w=[k + "sin"], scale=TWO_PI)
            self.act(ab, fr, AF.Abs, r=[k + "fr"], w=[k + "ab"])
            self.act(cosT, ab, AF.Sin, r=[k + "ab"], w=[k + "cos"], scale=-TWO_PI, bias=self.halfpi)
        op("dve", "tensor_scalar", r=["rp32cos"], w=["c32q"], out=self.cos32q, in0=self.cos32, scalar1=0.125, scalar2=None, op0=ALU.mult)
        op("dve", "tensor_scalar", r=["rp32sin"], w=["s32q"], out=self.sin32q, in0=self.sin32, scalar1=0.125, scalar2=None, op0=ALU.mult)

    def modvec(self, l, idx, dst, dkey, plus_one=False):
        for qd in range(4):
            c0 = idx * D + qd * 256
            si = self.stage_i % len(self.wstage)
            self.stage_i += 1
            wst = self.wstage[si][:, 0:2048].rearrange("p (k n) -> p k n", k=8)
            brow = self.wstage[si][0:1, 2048:2304]
            sk = f"wst{si}"
            self.dma(wst, self.w_ada[l, :, c0:c0 + 256].rearrange("(k p) n -> p k n", p=128), w=[sk])
            self.dma(brow, self.b_ada[l:l + 1, c0:c0 + 256], w=[sk])
            ps, pk = self.nextps()
            for k in range(8):
                self.mm(ps[:, 0:256], self.condrep[:, k, :], wst[:, k, :], k == 0, False, r=["condrep", sk], w=[pk])
            self.mm(ps[:, 0:256], self.ones_f[0:1, :], brow, False, True, r=["ones_f", sk], w=[pk])
            self.act(dst[:, qd * 256:(qd + 1) * 256], ps[:, 0:256], AF.Identity, r=[], w=[pk, dkey], bias=1.0 if plus_one else 0.0)

    def mk_lnscr(self, tag):
        return dict(st=self.sb(f"lnst{tag}", [128, 2, 6]), mv=self.sb(f"lnmv{tag}", [128, 2]), rs=self.sb(f"lnrs{tag}", [128, 2]), k=f"ln{tag}")

    def ln_stats(self, xt, xkey, p, scr):
        st, mv, rs, k = scr["st"], scr["mv"], scr["rs"], scr["k"]
        for c in range(2):
            self.op("dve", "bn_stats", r=[xkey], w=[k + "st"], out=st[0:p, c, :], in_=xt[:, c * 512:(c + 1) * 512])
        self.op("dve", "bn_aggr", r=[k + "st"], w=[k + "mv"], out=mv[0:p, :], in_=st[0:p].rearrange("p a b -> p (a b)"))
        self.act(rs[0:p, 0:1], mv[0:p, 1:2], AF.Ln, r=[k + "mv"], w=[k + "rs"], bias=self.epsc[0:p, 0:1])
        self.act(rs[0:p, 0:1], rs[0:p, 0:1], AF.Exp, r=[k + "rs"], w=[k + "rs"], scale=-0.5)
        self.op("dve", "scalar_tensor_tensor", r=[k + "mv", k + "rs"], w=[k + "rs2"], out=rs[0:p, 1:2], in0=mv[0:p, 0:1], scalar=-1.0,
                in1=rs[0:p, 0:1], op0=ALU.mult, op1=ALU.mult)
        return rs, [k + "rs", k + "rs2"]

    def phase_A(self, l, which, hT, col0):
        NT = self.NT
        shB = self.sb("shB", [128, D]); scB = self.sb("scB", [128, D])
        self.modvec(l, 3 * which + 0, shB, "shB")
        self.modvec(l, 3 * which + 1, scB, "scB", plus_one=True)
        xn1 = self.sb("xn0", [128, D])
        xn = [xn1, xn1]
        hb = [self.sb(f"hb{i}", [128, D], BF16) for i in range(2)]
        lns = [self.mk_lnscr(i) for i in range(2)]
        for i in range(NT):
            j = i % 2
            xt = self.x_res[:, i, :]
            rs, rk = self.ln_stats(xt, "x_res", 128, lns[j])
            self.act(xn[j], xt, AF.Identity, r=["x_res"] + rk, w=["xn0"], scale=rs[:, 0:1], bias=rs[:, 1:2])
            self.op("dve", "tensor_tensor", r=["xn0", "scB"], w=["xn0"], out=xn[j], in0=xn[j], in1=scB, op=ALU.mult)
            self.op("dve", "tensor_tensor", r=["xn0", "shB"], w=[f"hb{j}"], out=hb[j], in0=xn[j], in1=shB, op=ALU.add)
            ps, pk = self.nextps()
            psb = ps.bitcast(BF16)
            for k in range(8):
                self.tr(psb[:, k * 128:(k + 1) * 128], hb[j][:, k * 128:(k + 1) * 128], r=[f"hb{j}"], w=[pk])
            self.op("dve", "tensor_copy", r=[], w=[pk, f"hT{i}"], out=hT[:, :, col0 + i * 128:col0 + (i + 1) * 128],
                    in_=psb.rearrange("p (k t) -> p k t", k=8))
        return shB, scB

    def rope(self, x1, x2, cosT, sinT, d1, d2, H, n, pk, dkey, tmp):
        c = cosT.unsqueeze(1).to_broadcast([128, H, n]); s = sinT.unsqueeze(1).to_broadcast([128, H, n])
        t1, t2 = tmp[0][:, 0:H * n].rearrange("p (h n) -> p h n", h=H), tmp[1][:, 0:H * n].rearrange("p (h n) -> p h n", h=H)
        o = self.op
        o("dve", "tensor_tensor", r=["rope"], w=[pk, "rt1"], out=t1, in0=x1, in1=c, op=ALU.mult)
        o("dve", "tensor_tensor", r=["rope"], w=[pk, "rt2"], out=t2, in0=x2, in1=s, op=ALU.mult)
        o("dve", "tensor_tensor", r=["rt1", "rt2"], w=[dkey], out=d1, in0=t1, in1=t2, op=ALU.subtract)
        o("dve", "tensor_tensor", r=["rope"], w=[pk, "rt1"], out=t1, in0=x1, in1=s, op=ALU.mult)
        o("dve", "tensor_tensor", r=["rope"], w=[pk, "rt2"], out=t2, in0=x2, in1=c, op=ALU.mult)
        o("dve", "tensor_tensor", r=["rt1", "rt2"], w=[dkey], out=d2, in0=t1, in1=t2, op=ALU.add)

    def phase_B(self, l):
        T, NT = self.T, self.NT
        op, dma, mm, tr, act = self.op, self.dma, self.mm, self.tr, self.act
        self.phase("B")
        hT = self.sb("hT", [128, 8, T], BF16)
        W1 = self.sb("W1", [128, 8, C_S5], BF16)
        Ws5 = self.sb("Ws5", [128, 8, 512], BF16)
        Wuq = self.sb("Wuq", [128, 2, 768], BF16)
        qg = self.sb("qg", [128, 2])
        self.mk_wstage(C_S5 + 512, n=2)
        dma(qg, self.qnorm_in[l], w=["qg"])
        for k in range(8):
            st = self.wstage[k % 2]
            dma(st, self.w_in[l, k * 128:(k + 1) * 128, 0:C_G], w=[f"wst{k % 2}"])
            op("pool", "tensor_copy", r=[f"wst{k % 2}"], w=["W1"], out=W1[:, k, :], in_=st[:, 0:C_S5])
            op("pool", "tensor_copy", r=[f"wst{k % 2}"], w=["Ws5"], out=Ws5[:, k, :], in_=st[:, C_S5:C_G])
        for k in range(2):
            st = self.wstage[k % 2]
            dma(st[:, 0:768], self.w_uq[l, k * 128:(k + 1) * 128, :], w=[f"wst{k % 2}"])
            op("dve", "tensor_scalar", r=[f"wst{k % 2}", "qg"], w=["Wuq"], out=Wuq[:, k, :], in0=st[:, 0:768], scalar1=qg[:, k:k + 1],
               scalar2=None, op0=ALU.mult)
        self.phase_A(l, 0, hT, 0)
        sq = self.sb("sq", [128, 384]); ssq = self.sb("ssq", [128, 2]); rr = self.sb("rr", [128, 2])
        qcn = self.sb("qcn", [128, 256], BF16); lat = self.sb("lat", [128, 160], BF16)
        qcnT = self.sb("qcnT", [128, 2, 128], BF16)
        latT = [self.sb(f"latT{i}", [128, 128], BF16) for i in range(2)]
        krT = [self.sb(f"krT{i}", [32, 128], BF16) for i in range(2)]
        Qr = self.sb("Qr", [128, 8, 96], BF16)
        QT = [self.sb(f"QT{i}", [96, 8, 128], BF16) for i in range(2)]
        rtmp = [self.sb(f"rtmp{i}", [128, 128]) for i in range(2)]
        rqk = [self.sb(f"rqk{i}", [128, 512], BF16) for i in range(2)]
        rvb = [self.sb(f"rvb{i}", [128, 512], BF16) for i in range(1)] * 2
        rgb = [self.sb(f"rgb{i}", [128, 512], BF16) for i in range(1)] * 2
        TB = min(512, T)
        ub = [self.sb(f"ub{i}", [128, TB]) for i in range(2)]
        for i in range(NT):
            j = i % 2
            tsl = slice(i * 128, (i + 1) * 128)
            hk = f"hT{i}"
            ps, pk = self.nextps()
            for k in range(8):
                mm(ps[:, 0:416], hT[:, k, tsl], W1[:, k, 0:416], k == 0, k == 7, r=[hk, "W1"], w=[pk])
            act(sq[:, 0:384], ps[:, 0:384], AF.Square, r=[], w=[pk, "sq"])
            op("dve", "tensor_reduce", r=["sq"], w=["ssq"], out=ssq[:, 0:1], in_=sq[:, 0:256], axis=mybir.AxisListType.X, op=ALU.add)
            op("dve", "tensor_reduce", r=["sq"], w=["ssq"], out=ssq[:, 1:2], in_=sq[:, 256:384], axis=mybir.AxisListType.X, op=ALU.add)
            act(rr[:, 0:1], ssq[:, 0:1], AF.Ln, r=["ssq"], w=["rr"], scale=1.0 / 256, bias=self.rmseps)
            act(rr[:, 1:2], ssq[:, 1:2], AF.Ln, r=["ssq"], w=["rr"], scale=1.0 / 128, bias=self.rmseps)
            act(rr, rr, AF.Exp, r=["rr"], w=["rr"], scale=-0.5)
            op("dve", "tensor_scalar", r=["rr"], w=[pk, "qcn"], out=qcn, in0=ps[:, 0:256], scalar1=rr[:, 0:1], scalar2=None, op0=ALU.mult)
            op("dve", "tensor_scalar", r=["rr"], w=[pk, "lat"], out=lat[:, 0:128], in0=ps[:, 256:384], scalar1=rr[:, 1:2], scalar2=None, op0=ALU.mult)
            self.rope(ps[:, 384:400].unsqueeze(1), ps[:, 400:416].unsqueeze(1), self.cos16[:, i, :], self.sin16[:, i, :],
                      lat[:, 128:144].unsqueeze(1), lat[:, 144:160].unsqueeze(1), 1, 16, pk, "lat", rtmp)
            pt, ptk = self.nextps()
            ptb = pt.bitcast(BF16)
            tr(ptb[:, 0:128], qcn[:, 0:128], r=["qcn"], w=[ptk])
            tr(ptb[:, 128:256], qcn[:, 128:256], r=["qcn"], w=[ptk])
            tr(ptb[:, 256:384], lat[:, 0:128], r=["lat"], w=[ptk])
            tr(ptb[0:32, 384:512], lat[:, 128:160], r=["lat"], w=[ptk])
            op("dve", "tensor_copy", r=[], w=[ptk, "qcnT"], out=qcnT, in_=ptb[:, 0:256].rearrange("p (k t) -> p k t", k=2))
            act(latT[j], ptb[:, 256:384], AF.Identity, r=[], w=[ptk, f"latT{j}"])
            act(krT[j], ptb[0:32, 384:512], AF.Identity, r=[], w=[ptk, f"krT{j}"])
            dma(self.lat_loc[0:128, tsl], latT[j], r=[f"latT{j}"], w=["lat_loc"])
            dma(self.lat_loc[128:160, tsl], krT[j], r=[f"krT{j}"], w=["lat_loc"])
            for g in range(2):
                ps, pk = self.nextps()
                for k in range(2):
                    mm(ps[:, 0:384], qcnT[:, k, :], Wuq[:, k, g * 384:(g + 1) * 384], k == 0, k == 1, r=["qcnT", "Wuq"], w=[pk])
                pv = ps[:, 0:384].rearrange("p (h d) -> p h d", h=4)
                act(Qr[:, g * 4:(g + 1) * 4, 0:64], pv[:, :, 0:64], AF.Identity, r=[], w=[pk, "Qr"])
                self.rope(pv[:, :, 64:80], pv[:, :, 80:96], self.cos16[:, i, :], self.sin16[:, i, :],
                          Qr[:, g * 4:(g + 1) * 4, 64:80], Qr[:, g * 4:(g + 1) * 4, 80:96], 4, 16, pk, "Qr", rtmp)
            pt, ptk = self.nextps()
            ptb = pt.bitcast(BF16)
            for h in range(8):
                tr(ptb[0:96, h * 128:(h + 1) * 128], Qr[:, h, :], r=["Qr"], w=[ptk])
            op("dve", "tensor_copy", r=[], w=[ptk, f"QT{j}"], out=QT[j], in_=ptb[0:96, :].rearrange("p (h t) -> p h t", h=8))
            dma(self.qT_d[:, :, tsl].rearrange("h p t -> p h t"), QT[j], r=[f"QT{j}"], w=["qT_d"])
            ps, pk = self.nextps()
            for k in range(8):
                mm(ps, hT[:, k, tsl], W1[:, k, C_RQ:C_RV], k == 0, k == 7, r=[hk, "W1"], w=[pk])
            for qk in range(2):
                pv = ps[:, qk * 256:(qk + 1) * 256].rearrange("p (h d) -> p h d", h=4)
                dv = rqk[j][:, qk * 256:(qk + 1) * 256].rearrange("p (h d) -> p h d", h=4)
                cT_, sT_ = (self.cos32q, self.sin32q) if qk == 0 else (self.cos32, self.sin32)
                self.rope(pv[:, :, 0:32], pv[:, :, 32:64], cT_[:, i, :], sT_[:, i, :], dv[:, :, 0:32], dv[:, :, 32:64], 4, 32, pk, f"rqk{j}", rtmp)
            dma(self.rq_d[tsl, :], rqk[j][:, 0:256], r=[f"rqk{j}"], w=["rq_d"])
            dma(self.rk_d[tsl, :], rqk[j][:, 256:512], r=[f"rqk{j}"], w=["rk_d"])
            ps, pk = self.nextps()
            for k in range(8):
                mm(ps, hT[:, k, tsl], W1[:, k, C_RV:C_RG], k == 0, k == 7, r=[hk, "W1"], w=[pk])
            act(rvb[j], ps, AF.Identity, r=[], w=[pk, "rvb0"])
            dma(self.rv_d[tsl, :], rvb[j], r=["rvb0"], w=["rv_d"])
            ps, pk = self.nextps()
            for k in range(8):
                mm(ps, hT[:, k, tsl], W1[:, k, C_RG:C_S5], k == 0, k == 7, r=[hk, "W1"], w=[pk])
            act(rgb[j], ps, AF.Silu, r=[], w=[pk, "rgb0"])
            dma(self.rg_d[tsl, :], rgb[j], r=["rgb0"], w=["rg_d"])
        n = 0
        for b in range(T // TB):
            bsl = slice(b * TB, (b + 1) * TB)
            hks = [f"hT{i}" for i in range(b * TB // 128, (b + 1) * TB // 128)]
            for fc in range(4):
                ps, pk = self.nextps()
                for k in range(8):
                    mm(ps[:, 0:TB], Ws5[:, k, fc * 128:(fc + 1) * 128], hT[:, k, bsl], k == 0, k == 7, r=hks + ["Ws5"], w=[pk])
                j = n % 2; n += 1
                act(ub[j], ps[:, 0:TB], AF.Identity, r=[], w=[pk, f"ub{j}"])
                dma(self.u_loc[fc * 128:(fc + 1) * 128, bsl], ub[j], r=[f"ub{j}"], w=["u_loc"])
        self.allgather(self.lat_all, self.lat_loc, r=["lat_loc"], w=["lat_all"])
        for c in range(4):
            self.allgather(self.u_all[c * 512:(c + 1) * 512, :], self.u_loc[c * 128:(c + 1) * 128, :], r=["u_loc"], w=["u_all"])
        for nm in ("lat_loc", "qT_d", "rq_d", "rk_d", "rv_d", "rg_d", "u_loc", "lat_all"):
            self.dbg(nm, getattr(self, nm), nm)

    def phase_R(self, l):
        T, NT = self.T, self.NT
        op, dma, mm, tr, act = self.op, self.dma, self.mm, self.tr, self.act
        self.phase("R")
        rq = self.sb("rq", [128, NT, 256], BF16); rk = self.sb("rk", [128, NT, 256], BF16)
        rv = self.sb("rv", [128, NT, 512], BF16); sg = self.sb("sg", [128, NT, 512], BF16)
        dma(rq, self.rq_d.rearrange("(n p) f -> p n f", p=128), w=["rq"])
        dma(rk, self.rk_d.rearrange("(n p) f -> p n f", p=128), w=["rk"])
        dma(rv, self.rv_d.rearrange("(n p) f -> p n f", p=128), w=["rv"])
        dma(sg, self.rg_d.rearrange("(n p) f -> p n f", p=128), w=["sg"])
        lgall = self.sb("lgall", [128, 8]); lgsel = self.sb("lgsel", [128, 4])
        dma(lgall, self.ldc_in[l:l + 1, :].partition_broadcast(128), w=["lgall"])
        dma(lgsel[0:64, :], self.ldc_in[l:l + 1, 0:4].partition_broadcast(64), w=["lgsel"])
        dma(lgsel[64:128, :], self.ldc_in[l:l + 1, 4:8].partition_broadcast(64), w=["lgsel"])
        for t_, k_ in ((lgall, "lgall"), (lgsel, "lgsel")):
            act(t_, t_, AF.Exp, r=[k_], w=[k_])
            act(t_, t_, AF.Ln, r=[k_], w=[k_], scale=-1.0, bias=1.0)
        zx = self.sb("zx", [128, 4, 4])
        for kind, (ecol, lo) in enumerate(((0, 0), (1, 4), (2, 0), (3, 4))):
            act(zx[:, kind, :], lgall[:, lo:lo + 4], AF.Exp, r=["lgall", "ek"], w=["zx"], scale=self.ek[:, ecol:ecol + 1])
        cm = self.sb("cm", [128, 4, 128])
        dma(cm, self.cmask_in.rearrange("a p f -> p a f"), w=["cm"])
        DT = self.sb("DT", [128, 4, 128]); dtmp = self.sb("dtmp", [128, 2, 128])
        for h in range(4):
            act(dtmp[:, 0, :], cm[:, 0, :], AF.Exp, r=["cm", "lgall"], w=["dtmp0"], scale=lgall[:, h:h + 1])
            act(dtmp[:, 1, :], cm[:, 1, :], AF.Exp, r=["cm", "lgall"], w=["dtmp1"], scale=lgall[:, 4 + h:5 + h])
            op("dve", "tensor_tensor", r=["dtmp0", "cm"], w=["dtmp0"], out=dtmp[:, 0, :], in0=dtmp[:, 0, :], in1=cm[:, 2, :], op=ALU.mult)
            op("dve", "tensor_tensor", r=["dtmp1", "cm"], w=["dtmp1"], out=dtmp[:, 1, :], in0=dtmp[:, 1, :], in1=cm[:, 3, :], op=ALU.mult)
            op("dve", "tensor_tensor", r=["dtmp0", "dtmp1"], w=["DT"], out=DT[:, h, :], in0=dtmp[:, 0, :], in1=dtmp[:, 1, :], op=ALU.add)
        lg128 = self.sb("lg128", [128, 4]); lgT = self.sb("lgT", [128, 4])
        op("dve", "tensor_scalar", r=["lgsel"], w=["lg128"], out=lg128, in0=lgsel, scalar1=128.0, scalar2=None, op0=ALU.mult)
        op("dve", "tensor_scalar", r=["lgsel"], w=["lgT"], out=lgT, in0=lgsel, scalar1=float(T), scalar2=None, op0=ALU.mult)
        cdec = self.sb("cdec", [128, 4])
        act(cdec, lg128, AF.Exp, r=["lg128"], w=["cdec"])
        coefn = self.sb("coefn", [128, 4, NT]); coefr = self.sb("coefr", [128, 4, 4])
        for h in range(4):
            act(coefn[:, h, :], self.etab, AF.Exp, r=["lg128"], w=["coefn"], scale=lg128[:, h:h + 1])
            act(coefr[:, h, :], self.dist, AF.Exp, r=["lgT"], w=["coefr"], scale=lgT[:, h:h + 1])
        E = self.sb("E", [128, 4, NT, 128])
        kk = [self.sb(f"kk{i}", [128, 128], BF16) for i in range(2)]
        n_ = 0
        for n in range(NT):
            for h in range(4):
                j = n_ % 2; n_ += 1
                kh = rk[:, n, h * 64:(h + 1) * 64]
                op("dve", "tensor_scalar", r=["rk", "zx"], w=[f"kk{j}"], out=kk[j][:, 0:64], in0=kh, scalar1=zx[:, 0, h:h + 1], scalar2=None, op0=ALU.mult)
                op("dve", "tensor_scalar", r=["rk", "zx"], w=[f"kk{j}"], out=kk[j][:, 64:128], in0=kh, scalar1=zx[:, 1, h:h + 1], scalar2=None, op0=ALU.mult)
                ps, pk = self.nextps()
                mm(ps[:, 0:128], kk[j], rv[:, n, h * 128:(h + 1) * 128], True, True, r=[f"kk{j}", "rv"], w=[pk])
                act(E[:, h, n, :], ps[:, 0:128], AF.Identity, r=[], w=[pk, f"E{h}_{n}"])
        for h in range(4):
            for n in range(1, NT):
                op("dve", "scalar_tensor_tensor", r=[f"E{h}_{n - 1}"], w=[f"E{h}_{n}"], out=E[0:64, h, n, :], in0=E[0:64, h, n - 1, :],
                   scalar=cdec[0:64, h:h + 1], in1=E[0:64, h, n, :], op0=ALU.mult, op1=ALU.add)
            for n in range(NT - 2, -1, -1):
                op("dve", "scalar_tensor_tensor", r=[f"E{h}_{n + 1}"], w=[f"E{h}_{n}"], out=E[64:128, h, n, :], in0=E[64:128, h, n + 1, :],
                   scalar=cdec[64:128, h:h + 1], in1=E[64:128, h, n, :], op0=ALU.mult, op1=ALU.add)
        ekeys = [f"E{h}_{n}" for h in range(4) for n in range(NT)]
        dma(self.agg_loc.rearrange("(h p) e -> p h e", p=128)[0:64], E[0:64, :, NT - 1, :], r=ekeys, w=["agg_loc"])
        dma(self.agg_loc.rearrange("(h p) e -> p h e", p=128)[64:128], E[64:128, :, 0, :], r=ekeys, w=["agg_loc"])
        self.allgather(self.agg_all, self.agg_loc, r=["agg_loc"], w=["agg_all"])
        agg = self.sb("agg", [128, 4, 4, 128])
        dma(agg, self.agg_all.rearrange("(r h p) e -> p r h e", r=4, h=4), r=["agg_all"], w=["agg"])
        Cin = self.sb("Cin", [128, 4, 128])
        for h in range(4):
            op("dve", "tensor_scalar", r=["agg", "coefr"], w=[f"Cin{h}"], out=Cin[:, h, :], in0=agg[:, 0, h, :], scalar1=coefr[:, h, 0:1], scalar2=None, op0=ALU.mult)
            for r_ in range(1, 4):
                op("dve", "scalar_tensor_tensor", r=["agg", "coefr"], w=[f"Cin{h}"], out=Cin[:, h, :], in0=agg[:, r_, h, :], scalar=coefr[:, h, r_:r_ + 1],
                   in1=Cin[:, h, :], op0=ALU.mult, op1=ALU.add)
        Pb = self.sb("Pb", [128, 4, NT, 128], BF16)
        for h in range(4):
            for n in range(NT):
                src_f = E[0:64, h, n - 1, :] if n >= 1 else self.zeros_f[0:64, :]
                src_b = E[64:128, h, n + 1, :] if n <= NT - 2 else self.zeros_f[64:128, :]
                op("dve", "scalar_tensor_tensor", r=ekeys + [f"Cin{h}", "coefn"], w=[f"Pb{n}"], out=Pb[0:64, h, n, :], in0=Cin[0:64, h, :],
                   scalar=coefn[0:64, h, n:n + 1], in1=src_f, op0=ALU.mult, op1=ALU.add)
                op("dve", "scalar_tensor_tensor", r=ekeys + [f"Cin{h}", "coefn"], w=[f"Pb{n}"], out=Pb[64:128, h, n, :], in0=Cin[64:128, h, :],
                   scalar=coefn[64:128, h, n:n + 1], in1=src_b, op0=ALU.mult, op1=ALU.add)
        for nm_, ap_, k_ in (("zx", zx, ["zx"]), ("DT", DT, ["DT"]), ("E", E, ekeys), ("Cin", Cin, [f"Cin{h}" for h in range(4)]), ("coefr", coefr, ["coefr"]), ("lgall", lgall, ["lgall"])):
            if nm_ in self.debug:
                o_ = self.nc.dram_tensor("dbg_" + nm_, list(ap_.shape), ap_.dtype, kind="ExternalOutput").ap()
                self.dma(o_, ap_, r=k_)
                self.dbg_out.append("dbg_" + nm_)
        qq = [self.sb(f"qq{i}", [128, 128], BF16) for i in range(2)]
        tqk = [self.sb(f"tqk{i}", [64, 256], BF16) for i in range(2)]
        tqq = [self.sb(f"tqq{i}", [128, 128], BF16) for i in range(2)]
        AT = [self.sb(f"AT{i}", [128, 128], BF16) for i in range(2)]
        gst = self.sb("gst", [128, 4, 6]); gmv = self.sb("gmv", [128, 4, 2]); grs = self.sb("grs", [128, 4, 2])
        yn = self.sb("yn", [128, 512]); orb = self.sb("orb", [128, 512], BF16)
        oT = [self.sb(f"oT{i}", [128, 4, 128], BF16) for i in range(2)]
        n_ = 0
        for n in range(NT):
            yps, ypk = self.nextps([6, 7])
            for h in range(4):
                j = n_ % 2; n_ += 1
                qh = rq[:, n, h * 64:(h + 1) * 64]
                op("dve", "tensor_scalar", r=["rq", "zx"], w=[f"qq{j}"], out=qq[j][:, 0:64], in0=qh, scalar1=zx[:, 2, h:h + 1], scalar2=None, op0=ALU.mult)
                op("dve", "tensor_scalar", r=["rq", "zx"], w=[f"qq{j}"], out=qq[j][:, 64:128], in0=qh, scalar1=zx[:, 3, h:h + 1], scalar2=None, op0=ALU.mult)
                pt, ptk = self.nextps([0, 1, 2, 3, 4, 5])
                ptb = pt.bitcast(BF16)
                tr(ptb[0:64, 0:128], qh, r=["rq"], w=[ptk])
                tr(ptb[0:64, 128:256], rk[:, n, h * 64:(h + 1) * 64], r=["rk"], w=[ptk])
                tr(ptb[:, 256:384], qq[j], r=[f"qq{j}"], w=[ptk])
                op("dve", "tensor_copy", r=[], w=[ptk, f"tqk{j}"], out=tqk[j], in_=ptb[0:64, 0:256])
                act(tqq[j], ptb[:, 256:384], AF.Identity, r=[], w=[ptk, f"tqq{j}"])
                sps, spk = self.nextps([0, 1, 2, 3, 4, 5])
                mm(sps[:, 0:128], tqk[j][:, 128:256], tqk[j][:, 0:128], True, True, r=[f"tqk{j}"], w=[spk])
                op("dve", "tensor_tensor", r=["DT"], w=[spk, f"AT{j}"], out=AT[j], in0=sps[:, 0:128], in1=DT[:, h, :], op=ALU.mult)
                ysl = yps[:, h * 128:(h + 1) * 128]
                mm(ysl, AT[j], rv[:, n, h * 128:(h + 1) * 128], True, False, r=[f"AT{j}", "rv"], w=[ypk])
                mm(ysl, tqq[j], Pb[:, h, n, :], False, True, r=[f"tqq{j}", f"Pb{n}"], w=[ypk])
            for h in range(4):
                op("dve", "bn_stats", r=[], w=[ypk, "gst"], out=gst[:, h, :], in_=yps[:, h * 128:(h + 1) * 128])
            for h in range(4):
                op("dve", "bn_aggr", r=["gst"], w=["gmv"], out=gmv[:, h, :], in_=gst[:, h, :])
            act(grs[:, :, 0], gmv[:, :, 1], AF.Ln, r=["gmv"], w=["grs"], bias=self.constc[:, 3:4])
            act(grs[:, :, 0], grs[:, :, 0], AF.Exp, r=["grs"], w=["grs"], scale=-0.5)
            op("dve", "scalar_tensor_tensor", r=["gmv", "grs"], w=["grs2"], out=grs[:, :, 1], in0=gmv[:, :, 0], scalar=-1.0, in1=grs[:, :, 0],
               op0=ALU.mult, op1=ALU.mult)
            for h in range(4):
                act(yn[:, h * 128:(h + 1) * 128], yps[:, h * 128:(h + 1) * 128], AF.Identity, r=["grs", "grs2"], w=[ypk, "yn"],
                    scale=grs[:, h, 0:1], bias=grs[:, h, 1:2])
            op("dve", "tensor_tensor", r=["yn", "sg"], w=["orb"], out=orb, in0=yn, in1=sg[:, n, :], op=ALU.mult)
            pt, ptk = self.nextps([0, 1, 2, 3, 4, 5])
            ptb = pt.bitcast(BF16)
            for h in range(4):
                tr(ptb[:, h * 128:(h + 1) * 128], orb[:, h * 128:(h + 1) * 128], r=["orb"], w=[ptk])
            jo = n % 2
            op("dve", "tensor_copy", r=[], w=[ptk, f"oT{jo}"], out=oT[jo], in_=ptb[:, 0:512].rearrange("p (h t) -> p h t", h=4))
            dma(self.ob_d[1][:, n * 128:(n + 1) * 128].rearrange("(h p) t -> p h t", p=128), oT[jo], r=[f"oT{jo}"], w=["ob1_d"])
        self.dbg("ob1_d", self.ob_d[1], "ob1_d")

    def angle_tables(self, u, ukey, shape, tag, sinT, cosT, skey, ckey, negsin=False):
        op, act = self.op, self.act
        ki = self.cached(f"at_ki{tag}", shape, I32); kf = self.cached(f"at_kf{tag}", shape); ab = self.cached(f"at_ab{tag}", shape)
        k = f"at{tag}"
        op("dve", "tensor_copy", r=[ukey], w=[k + "ki"], out=ki, in_=u)
        op("dve", "tensor_copy", r=[k + "ki"], w=[k + "kf"], out=kf, in_=ki)
        op("dve", "tensor_tensor", r=[ukey, k + "kf"], w=[ukey], out=u, in0=u, in1=kf, op=ALU.subtract)
        act(sinT, u, AF.Sin, r=[ukey], w=[skey], scale=-TWO_PI if negsin else TWO_PI)
        act(ab, u, AF.Abs, r=[ukey], w=[k + "ab"])
        act(cosT, ab, AF.Sin, r=[k + "ab"], w=[ckey], scale=-TWO_PI, bias=self.halfpi[0:shape[0], :])

    def cached(self, name, shape, dt=F32):
        key = (self.fw.nbar, name)
        if key not in self._cache:
            self._cache[key] = self.sb(name, shape, dt)
        return self._cache[key]

    def phase_S(self, l):
        S, T = self.S, self.T
        op, dma, mm, act = self.op, self.dma, self.mm, self.act
        self.phase("S")
        SBk = min(512, S)
        NB = S // SBk
        TB = min(2048, T)
        uT = self.sb("uT", [128, S], BF16)
        ys = self.sb("ys", [128, S])
        io = self.sb("io", [128, 512])
        lamS = self.sb("lamS", [128, 2, 3, 4])
        stepS = self.sb("stepS", [128, 2, 4]); magS = self.sb("magS", [128, 2, 4]); thS = self.sb("thS", [128, 2, 4])
        Bb = self.sb("Bb", [128, 2, 2, 512], BF16)
        Cb = self.sb("Cb", [128, 2, 2, 512], BF16)
        wc = self.sb("wc", [128, 2, 4, 2])
        mark = self.arena_off
        ust = [self.sb(f"ust{i}", [128, TB]) for i in range(3)]
        uacc = self.sb("uacc", [128, TB])
        n_ = 0
        for q in range(4):
            for tb in range(T // TB):
                for c in range(4):
                    j = n_ % 3; n_ += 1
                    dma(ust[j], self.u_all[(c * 4 + q) * 128:(c * 4 + q + 1) * 128, tb * TB:(tb + 1) * TB], w=[f"ust{j}"])
                    dst = uT[:, q * T + tb * TB:q * T + (tb + 1) * TB] if c == 3 else uacc
                    dk = "uT" if c == 3 else "uacc"
                    if c == 0:
                        op("dve", "tensor_scalar", r=[f"ust{j}"], w=["uacc"], out=uacc, in0=ust[j], scalar1=self.sel[:, 0:1], scalar2=None, op0=ALU.mult)
                    else:
                        op("dve", "scalar_tensor_tensor", r=[f"ust{j}", "uacc"], w=[dk], out=dst, in0=ust[j], scalar=self.sel[:, c:c + 1], in1=uacc,
                           op0=ALU.mult, op1=ALU.add)
        dma(io, self.io_in, w=["io"])
        for d in range(2):
            dma(lamS[:, d, 0, :], self.lamS_re[l, d], w=["lamS"])
            dma(lamS[:, d, 1, :], self.lamS_im[l, d], w=["lamS"])
            dma(lamS[:, d, 2, :], self.lsS[l, d], w=["lamS"])
        act(stepS, lamS[:, :, 2, :], AF.Exp, r=["lamS"], w=["stepS"])
        op("dve", "tensor_tensor", r=["lamS", "stepS"], w=["magS"], out=magS, in0=lamS[:, :, 0, :], in1=stepS, op=ALU.mult)
        act(magS, magS, AF.Exp, r=["magS"], w=["magS"])
        op("dve", "tensor_tensor", r=["lamS", "stepS"], w=["thS"], out=thS, in0=lamS[:, :, 1, :], in1=stepS, op=ALU.mult)
        op("dve", "tensor_scalar", r=["thS"], w=["thS"], out=thS, in0=thS, scalar1=1.0 / TWO_PI, scalar2=None, op0=ALU.mult)
        for d in range(2):
            self.subphase(mark)
            lB = self.sb("lB", [128, 3, 512])
            dma(lB[:, 0, :], self.lamB_re[l, d:d + 1, :].partition_broadcast(128), w=["lB"])
            dma(lB[:, 1, :], self.lamB_im[l, d:d + 1, :].partition_broadcast(128), w=["lB"])
            dma(lB[:, 2, :], self.lsB[l, d:d + 1, :].partition_broadcast(128), w=["lB"])
            sh = [128, 512]
            stB = self.sb("stB", sh); mgB = self.sb("mgB", sh); tu = self.sb("tu", sh); sn = self.sb("sn", sh); cs = self.sb("cs", sh)
            are = self.sb("are", sh); aim = self.sb("aim", sh); den = self.sb("den", sh); fre = self.sb("fre", sh); fim = self.sb("fim", sh)
            t1 = self.sb("t1s", sh)
            lre, lim = lB[:, 0, :], lB[:, 1, :]
            act(stB, lB[:, 2, :], AF.Exp, r=["lB"], w=["stB"])
            op("dve", "tensor_tensor", r=["lB", "stB"], w=["mgB"], out=mgB, in0=lre, in1=stB, op=ALU.mult)
            act(mgB, mgB, AF.Exp, r=["mgB"], w=["mgB"])
            op("dve", "tensor_tensor", r=["lB", "stB"], w=["tu"], out=tu, in0=lim, in1=stB, op=ALU.mult)
            op("dve", "tensor_scalar", r=["tu"], w=["tu"], out=tu, in0=tu, scalar1=1.0 / TWO_PI, scalar2=None, op0=ALU.mult)
            self.angle_tables(tu, "tu", sh, "B", sn, cs, "sn", "cs")
            op("dve", "tensor_tensor", r=["mgB", "cs"], w=["are"], out=are, in0=mgB, in1=cs, op=ALU.mult)
            op("dve", "tensor_tensor", r=["mgB", "sn"], w=["aim"], out=aim, in0=mgB, in1=sn, op=ALU.mult)
            op("dve", "tensor_scalar", r=["are"], w=["are"], out=are, in0=are, scalar1=-1.0, scalar2=None, op0=ALU.add)
            op("dve", "tensor_tensor", r=["lB"], w=["den"], out=den, in0=lre, in1=lre, op=ALU.mult)
            op("dve", "tensor_tensor", r=["lB"], w=["t1s"], out=t1, in0=lim, in1=lim, op=ALU.mult)
            op("dve", "tensor_tensor", r=["den", "t1s"], w=["den"], out=den, in0=den, in1=t1, op=ALU.add)
            op("dve", "reciprocal", r=["den"], w=["den"], out=den, in_=den)
            op("dve", "tensor_tensor", r=["are", "lB"], w=["fre"], out=fre, in0=are, in1=lre, op=ALU.mult)
            op("dve", "tensor_tensor", r=["aim", "lB"], w=["t1s"], out=t1, in0=aim, in1=lim, op=ALU.mult)
            op("dve", "tensor_tensor", r=["fre", "t1s"], w=["fre"], out=fre, in0=fre, in1=t1, op=ALU.add)
            op("dve", "tensor_tensor", r=["fre", "den"], w=["fre"], out=fre, in0=fre, in1=den, op=ALU.mult)
            op("dve", "tensor_tensor", r=["aim", "lB"], w=["fim"], out=fim, in0=aim, in1=lre, op=ALU.mult)
            op("dve", "tensor_tensor", r=["are", "lB"], w=["t1s"], out=t1, in0=are, in1=lim, op=ALU.mult)
            op("dve", "tensor_tensor", r=["fim", "t1s"], w=["fim"], out=fim, in0=fim, in1=t1, op=ALU.subtract)
            op("dve", "tensor_tensor", r=["fim", "den"], w=["fim"], out=fim, in0=fim, in1=den, op=ALU.mult)
            braw = self.sb("braw", [128, 2, 512]); t2 = self.sb("t2s", [128, 512]); t3 = self.sb("t3s", [128, 512])
            dma(braw[:, 0, :], self.Bblk_re[l, d], w=["braw"])
            dma(braw[:, 1, :], self.Bblk_im[l, d], w=["braw"])
            op("dve", "tensor_tensor", r=["braw", "fre"], w=["t2s"], out=t2, in0=braw[:, 0, :], in1=fre, op=ALU.mult)
            op("dve", "tensor_tensor", r=["braw", "fim"], w=["t3s"], out=t3, in0=braw[:, 1, :], in1=fim, op=ALU.mult)
            op("dve", "tensor_tensor", r=["t2s", "t3s"], w=["Bb"], out=Bb[:, d, 0, :], in0=t2, in1=t3, op=ALU.subtract)
            op("dve", "tensor_tensor", r=["braw", "fre"], w=["t2s"], out=t2, in0=braw[:, 1, :], in1=fre, op=ALU.mult)
            op("dve", "tensor_tensor", r=["braw", "fim"], w=["t3s"], out=t3, in0=braw[:, 0, :], in1=fim, op=ALU.mult)
            op("dve", "tensor_tensor", r=["t2s", "t3s"], w=["Bb"], out=Bb[:, d, 1, :], in0=t2, in1=t3, op=ALU.add)
            dma(braw[:, 0, :], self.Cblk_re[l, d], r=["Bb"], w=["braw"])
            dma(braw[:, 1, :], self.Cblk_im[l, d], r=["Bb"], w=["braw"])
            op("pool", "tensor_copy", r=["braw"], w=["Cb"], out=Cb[:, d, :, :], in_=braw)
        self.subphase(mark)
        op("dve", "memset", w=["wc"], ap=wc, constant=0.0)
        W = [128, SBk]
        tun = [self.sb(f"tun{i}", W) for i in range(2)]
        nsin = [self.sb(f"nsin{i}", W) for i in range(2)]; cosb = [self.sb(f"cosb{i}", W) for i in range(2)]
        ta = self.sb("ta", W); tb_ = self.sb("tb", W); tc = self.sb("tc", W); td = self.sb("td", W)
        vre = self.sb("vre", W); vim = self.sb("vim", W); wre = self.sb("wre", W); wim = self.sb("wim", W)
        xre = [self.sb(f"xre{i}", W, BF16) for i in range(2)]; nxi = [self.sb(f"nxi{i}", W, BF16) for i in range(2)]
        its = [(d, sbi, st) for d in range(2) for sbi in range(NB) for st in range(4)]

        def front(i):
            d, sbi, st = its[i]
            j = i % 2
            tau0 = sbi * SBk
            t0 = tau0 if d == 0 else S - tau0 - SBk
            op("dve", "tensor_scalar", r=["io", "thS"], w=[f"tun{j}"], out=tun[j], in0=io[:, 0:SBk], scalar1=float(tau0), scalar2=thS[:, d, st:st + 1],
               op0=ALU.add, op1=ALU.mult)
            self.angle_tables(tun[j], f"tun{j}", W, "M", nsin[j], cosb[j], f"nsin{j}", f"cosb{j}", negsin=True)
            bi = (2 * i) % 6
            pre, prk = self.ps[bi], f"ps{bi}"
            pim, pik = self.ps[bi + 1], f"ps{bi + 1}"
            mm(pre[:, 0:SBk], Bb[:, d, 0, st * 128:(st + 1) * 128], uT[:, t0:t0 + SBk], True, True, r=["Bb", "uT"], w=[prk])
            mm(pim[:, 0:SBk], Bb[:, d, 1, st * 128:(st + 1) * 128], uT[:, t0:t0 + SBk], True, True, r=["Bb", "uT"], w=[pik])
            return pre, prk, pim, pik

        def back(i, pre, prk, pim, pik, aps, apk):
            d, sbi, st = its[i]
            j = i % 2
            bre = pre[:, 0:SBk] if d == 0 else rev_ap(pre[:, 0:SBk], SBk)
            bim = pim[:, 0:SBk] if d == 0 else rev_ap(pim[:, 0:SBk], SBk)
            ns, cb = nsin[j], cosb[j]
            op("dve", "tensor_tensor", r=[f"cosb{j}"], w=[prk, "ta"], out=ta, in0=bre, in1=cb, op=ALU.mult)
            op("dve", "tensor_tensor", r=[f"nsin{j}"], w=[pik, "tb"], out=tb_, in0=bim, in1=ns, op=ALU.mult)
            op("dve", "tensor_tensor", r=["ta", "tb"], w=["vre"], out=vre, in0=ta, in1=tb_, op=ALU.subtract)
            op("dve", "tensor_tensor", r=[f"cosb{j}"], w=[pik, "tc"], out=tc, in0=bim, in1=cb, op=ALU.mult)
            op("dve", "tensor_tensor", r=[f"nsin{j}"], w=[prk, "td"], out=td, in0=bre, in1=ns, op=ALU.mult)
            op("dve", "tensor_tensor", r=["tc", "td"], w=["vim"], out=vim, in0=tc, in1=td, op=ALU.add)
            mg = magS[:, d, st:st + 1].to_broadcast(W)
            op("dve", "tensor_tensor_scan", r=["vre", "magS", "wc"], w=["wre"], out=wre, data0=mg, data1=vre, initial=wc[:, d, st, 0:1], op0=ALU.mult, op1=ALU.add)
            op("dve", "tensor_tensor_scan", r=["vim", "magS", "wc"], w=["wim"], out=wim, data0=mg, data1=vim, initial=wc[:, d, st, 1:2], op0=ALU.mult, op1=ALU.add)
            act(wc[:, d, st, 0:1], wre[:, SBk - 1:SBk], AF.Identity, r=["wre"], w=["wc"])
            act(wc[:, d, st, 1:2], wim[:, SBk - 1:SBk], AF.Identity, r=["wim"], w=["wc"])
            op("dve", "tensor_tensor", r=["wre", f"cosb{j}"], w=["ta"], out=ta, in0=wre, in1=cb, op=ALU.mult)
            op("dve", "tensor_tensor", r=["wim", f"nsin{j}"], w=["tb"], out=tb_, in0=wim, in1=ns, op=ALU.mult)
            op("dve", "tensor_tensor", r=["ta", "tb"], w=[f"xre{j}"], out=xre[j], in0=ta, in1=tb_, op=ALU.add)
            op("dve", "tensor_tensor", r=["wre", f"nsin{j}"], w=["tc"], out=tc, in0=wre, in1=ns, op=ALU.mult)
            op("dve", "tensor_tensor", r=["wim", f"cosb{j}"], w=["td"], out=td, in0=wim, in1=cb, op=ALU.mult)
            op("dve", "tensor_tensor", r=["tc", "td"], w=[f"nxi{j}"], out=nxi[j], in0=tc, in1=td, op=ALU.subtract)
            mm(aps[:, 0:SBk], Cb[:, d, 0, st * 128:(st + 1) * 128], xre[j], st == 0, False, r=["Cb", f"xre{j}"], w=[apk])
            mm(aps[:, 0:SBk], Cb[:, d, 1, st * 128:(st + 1) * 128], nxi[j], False, st == 3, r=["Cb", f"nxi{j}"], w=[apk])
            if st == 3:
                tau0 = sbi * SBk
                t0 = tau0 if d == 0 else S - tau0 - SBk
                if d == 0:
                    act(ys[:, t0:t0 + SBk], aps[:, 0:SBk], AF.Identity, r=[], w=[apk, "ys"])
                else:
                    yr = rev_ap(ys[:, t0:t0 + SBk], SBk)
                    op("dve", "tensor_tensor", r=[], w=[apk, "ys"], out=yr, in0=aps[:, 0:SBk], in1=yr, op=ALU.add)

        nxt = front(0)
        aps = apk = None
        for i in range(len(its)):
            cur = nxt
            if i + 1 < len(its):
                nxt = front(i + 1)
            if its[i][2] == 0:
                ai = 6 + (i // 4) % 2
                aps, apk = self.ps[ai], f"ps{ai}"
            back(i, *cur, aps, apk)
        for q in range(4):
            dma(self.ys_loc[q * 128:(q + 1) * 128, :], ys[:, q * T:(q + 1) * T], r=["ys"], w=["ys_loc"])
        for q in range(4):
            self.allgather(self.ys_all[q * 512:(q + 1) * 512, :], self.ys_loc[q * 128:(q + 1) * 128, :], r=["ys_loc"], w=["ys_all"])
        self.dbg("ys_loc", self.ys_loc, "ys_loc")

    def phase_AT(self, l):
        S, T = self.S, self.T
        op, dma, mm, act = self.op, self.dma, self.mm, self.act
        self.phase("AT")
        NKT = S // 128
        QB = min(512, T)
        KB = min(512, S)
        kvnT = self.sb("kvnT", [128, S], BF16); krT = self.sb("krT", [32, S], BF16)
        for r_ in range(4):
            dma(kvnT[:, r_ * T:(r_ + 1) * T], self.lat_all[r_ * 160:r_ * 160 + 128, :], r=["lat_all"], w=["kvnT"])
            dma(krT[:, r_ * T:(r_ + 1) * T], self.lat_all[r_ * 160 + 128:r_ * 160 + 160, :], r=["lat_all"], w=["krT"])
        kvg = self.sb("kvg", [128, 1]); wst = self.sb("wukv_st", [128, 1024])
        Wk = self.sb("Wk", [128, 8, 96], BF16); Wv = self.sb("Wv", [128, 8, 64], BF16); Sel = self.sb("Sel", [32, 96], BF16)
        dma(kvg, self.kvnorm_in[l], w=["kvg"])
        dma(wst, self.w_ukv[l], w=["wukv_st"])
        op("dve", "memset", w=["Wk"], ap=Wk, constant=0.0)
        op("dve", "memset", w=["Sel"], ap=Sel, constant=0.0)
        wv_ = wst.rearrange("p (h d) -> p h d", h=8)
        op("dve", "tensor_scalar", r=["wukv_st", "kvg", "Wk"], w=["Wk"], out=Wk[:, :, 0:64], in0=wv_[:, :, 0:64], scalar1=kvg[:, 0:1], scalar2=None, op0=ALU.mult)
        op("dve", "tensor_scalar", r=["wukv_st", "kvg"], w=["Wv"], out=Wv, in0=wv_[:, :, 64:128], scalar1=kvg[:, 0:1], scalar2=None, op0=ALU.mult)
        op("dve", "tensor_copy", r=["Sel", "ident"], w=["Sel"], out=Sel[:, 64:96], in_=self.ident[0:32, 0:32])
        KT = [self.sb(f"KT{i}", [96, S], BF16) for i in range(2)]
        Vp = [self.sb(f"Vp{i}", [128, NKT, 65], BF16) for i in range(2)]
        QTh = [self.sb(f"QTh{i}", [96, T], BF16) for i in range(2)]
        pT = [self.sb(f"pT{i}", [128, QB], BF16) for i in range(3)]
        rec = self.sb("rec", [128, QB]); Rb = self.sb("Rb", [64, QB])
        oh = [self.sb(f"oh{i}", [64, QB], BF16) for i in range(2)]
        for i in range(2):
            op("dve", "memset", w=[f"Vp{i}"], ap=Vp[i], constant=1.0)
        scale = 96.0 ** -0.5
        np_ = 0; no_ = 0
        def build_kv(h):
            hb = h % 2
            dma(QTh[hb], self.qT_d[h], r=["qT_d"], w=[f"QTh{hb}"])
            for kb in range(S // KB):
                ksl = slice(kb * KB, (kb + 1) * KB)
                ps, pk = self.ps[0], "ps0"
                mm(ps[0:96, 0:KB], Wk[:, h, :], kvnT[:, ksl], True, False, r=["Wk", "kvnT"], w=[pk])
                mm(ps[0:96, 0:KB], Sel, krT[:, ksl], False, True, r=["Sel", "krT"], w=[pk])
                op("dve", "tensor_copy", r=[], w=[pk, f"KT{hb}"], out=KT[hb][:, ksl], in_=ps[0:96, 0:KB])
            for g in range(0, NKT, 8):
                ng = min(8, NKT - g)
                ps, pk = self.ps[1], "ps1"
                for t in range(ng):
                    mm(ps[:, t * 64:(t + 1) * 64], kvnT[:, (g + t) * 128:(g + t + 1) * 128], Wv[:, h, :], True, True, r=["kvnT", "Wv"], w=[pk])
                op("dve", "tensor_copy", r=[], w=[pk, f"Vp{hb}"], out=Vp[hb][:, g:g + ng, 0:64], in_=ps[:, 0:ng * 64].rearrange("p (t d) -> p t d", t=ng))

        build_kv(0)
        for h in range(NH):
            hb = h % 2
            for qb in range(T // QB):
                qsl = slice(qb * QB, (qb + 1) * QB)
                acc, ak = self.nextps([6, 7])
                LOOK = 2

                def issue_s(kt_):
                    sps_, spk_ = self.nextps([2, 3, 4])
                    mm(sps_[:, 0:QB], KT[hb][:, kt_ * 128:(kt_ + 1) * 128], QTh[hb][:, qsl], True, True, r=[f"KT{hb}", f"QTh{hb}"], w=[spk_])
                    return sps_, spk_
                pend = [issue_s(k_) for k_ in range(min(LOOK, NKT))]
                for kt in range(NKT):
                    sps, spk = pend.pop(0)
                    j = np_ % 3; np_ += 1
                    act(pT[j], sps[:, 0:QB], AF.Exp, r=[], w=[spk, f"pT{j}"], scale=scale)
                    if kt + LOOK < NKT:
                        pend.append(issue_s(kt + LOOK))
                    mm(acc[0:65, 0:QB], Vp[hb][:, kt, :], pT[j], kt == 0, kt == NKT - 1, r=[f"Vp{hb}", f"pT{j}"], w=[ak])
                op("dve", "reciprocal", r=[], w=[ak, "rec"], out=rec[64:65, :], in_=acc[64:65, 0:QB])
                rp, rpk = self.nextps([5])
                mm(rp[0:64, 0:QB], self.ones_f[64:65, 0:64], rec[64:65, :], True, True, r=["rec", "ones_f"], w=[rpk])
                act(Rb, rp[0:64, 0:QB], AF.Identity, r=[], w=[rpk, "Rb"])
                jo = no_ % 2; no_ += 1
                op("dve", "tensor_tensor", r=["Rb"], w=[ak, f"oh{jo}"], out=oh[jo], in0=acc[0:64, 0:QB], in1=Rb, op=ALU.mult)
                dma(self.ob_d[0][h * 64:(h + 1) * 64, qsl], oh[jo], r=[f"oh{jo}"], w=["ob0_d"])
                if qb == 0 and h + 1 < NH:
                    build_kv(h + 1)
        self.dbg("ob0_d", self.ob_d[0], "ob0_d")

    def phase_M0(self, l):
        S, T = self.S, self.T
        op, dma, mm, act = self.op, self.dma, self.mm, self.act
        self.phase("M0")
        TB = min(512, T)
        Wg = self.sb("Wglu", [128, 4, 512], BF16)
        self.mk_wstage(512, n=2)
        for k in range(4):
            self.wload(Wg[:, k, :], self.w_glu[l, k * 128:(k + 1) * 128, :], "Wglu")
        dsk = self.sb("dsk", [128, 4])
        dma(dsk, self.dskip_in[l], w=["dsk"])
        cand = [self.sb(f"cand{i}", [128, TB]) for i in range(3)]
        ub = [self.sb(f"ub{i}", [128, TB]) for i in range(2)]
        yacc = self.sb("yacc", [128, TB]); y = self.sb("yM0", [128, 4, TB]); t1 = self.sb("t1m", [128, TB]); t2 = self.sb("t2m", [128, TB])
        yg = self.sb("yg", [128, 4, TB]); ygb = self.sb("ygb", [128, 4, TB], BF16); sgm = self.sb("sgm", [128, TB])
        ob = [self.sb(f"obs{i}", [128, TB], BF16) for i in range(2)]
        n_ = 0; nu = 0; no_ = 0
        for tb in range(T // TB):
            bsl = slice(tb * TB, (tb + 1) * TB)
            for fc in range(4):
                for q in range(4):
                    j = n_ % 3; n_ += 1
                    dma(cand[j], self.ys_all[(q * 4 + fc) * 128:(q * 4 + fc + 1) * 128, tb * TB:(tb + 1) * TB], r=["ys_all"], w=[f"cand{j}"])
                    if q == 0:
                        op("dve", "tensor_scalar", r=[f"cand{j}"], w=["yacc"], out=yacc, in0=cand[j], scalar1=self.sel[:, 0:1], scalar2=None, op0=ALU.mult)
                    else:
                        op("dve", "scalar_tensor_tensor", r=[f"cand{j}", "yacc"], w=["yacc"], out=yacc, in0=cand[j], scalar=self.sel[:, q:q + 1], in1=yacc,
                           op0=ALU.mult, op1=ALU.add)
                ju = nu % 2; nu += 1
                dma(ub[ju], self.u_loc[fc * 128:(fc + 1) * 128, bsl], r=["u_loc"], w=[f"ub{ju}"])
                yk = f"y{fc}"
                op("dve", "scalar_tensor_tensor", r=[f"ub{ju}", "yacc", "dsk"], w=[yk], out=y[:, fc, :], in0=ub[ju], scalar=dsk[:, fc:fc + 1], in1=yacc,
                   op0=ALU.mult, op1=ALU.add)
                op("dve", "tensor_tensor", r=[yk], w=["t1m"], out=t1, in0=y[:, fc, :], in1=y[:, fc, :], op=ALU.mult)
                op("dve", "tensor_scalar", r=["t1m"], w=["t1m"], out=t1, in0=t1, scalar1=0.044715, scalar2=1.0, op0=ALU.mult, op1=ALU.add)
                op("dve", "tensor_tensor", r=["t1m", yk], w=["t2m"], out=t2, in0=t1, in1=y[:, fc, :], op=ALU.mult)
                act(t2, t2, AF.Sigmoid, r=["t2m"], w=["t2m"], scale=2.0 * math.sqrt(2.0 / math.pi))
                op("dve", "tensor_tensor", r=["t2m", yk], w=[f"yg{fc}"], out=yg[:, fc, :], in0=t2, in1=y[:, fc, :], op=ALU.mult)
                act(ygb[:, fc, :], yg[:, fc, :], AF.Identity, r=[f"yg{fc}"], w=[f"ygb{fc}"])
            for fo in range(4):
                ps, pk = self.nextps()
                for k in range(4):
                    mm(ps[:, 0:TB], Wg[:, k, fo * 128:(fo + 1) * 128], ygb[:, k, :], k == 0, k == 3, r=["Wglu"] + [f"ygb{k}"], w=[pk])
                act(sgm, ps[:, 0:TB], AF.Sigmoid, r=[], w=[pk, "sgm"])
                jo = no_ % 2; no_ += 1
                op("dve", "tensor_tensor", r=["sgm", f"yg{fo}"], w=[f"obs{jo}"], out=ob[jo], in0=sgm, in1=yg[:, fo, :], op=ALU.mult)
                dma(self.ob_d[2][fo * 128:(fo + 1) * 128, bsl], ob[jo], r=[f"obs{jo}"], w=["ob2_d"])
        self.dbg("ob2_d", self.ob_d[2], "ob2_d")

    def phase_M1(self, l):
        T = self.T
        op, dma, mm, act = self.op, self.dma, self.mm, self.act
        self.phase("M1")
        TB = min(512, T)
        hT = self.sb("hT", [128, 8, T], BF16)
        self.mk_wstage(2304, n=3)
        self.phase_A(l, 0, hT, 0)
        hks = [f"hT{i}" for i in range(self.NT)]
        obT = [self.sb(f"obT{i}", [128, 4, TB], BF16) for i in range(3)]
        nob = 0
        Wg = [self.sb(f"Wg{i}", [128, 3, 8, 128], BF16) for i in range(2)]
        Wb = [self.sb(f"Wb{i}", [128, 3, 4, 128], BF16) for i in range(2)]
        sg = [self.sb(f"sgt{i}", [128, TB]) for i in range(2)]
        m = self.sb("macc", [128, TB]); tm = self.sb("tmM", [128, TB])
        mo = [self.sb(f"mo{i}", [128, TB], BF16) for i in range(2)]
        ns = 0; no_ = 0
        for f in range(8):
            jw = f % 2
            for b in range(3):
                c0 = C_G + b * D + f * 128
                self.wload(Wg[jw][:, b, :, :], self.w_in[l, :, c0:c0 + 128].rearrange("(k p) n -> p k n", p=128), f"Wg{jw}")
                self.wload(Wb[jw][:, b, :, :], self.w_br[b][l, :, f * 128:(f + 1) * 128].rearrange("(k p) n -> p k n", p=128), f"Wb{jw}")
            for tb in range(T // TB):
                bsl = slice(tb * TB, (tb + 1) * TB)
                for b in range(3):
                    gp, gk = self.nextps()
                    for k in range(8):
                        mm(gp[:, 0:TB], Wg[jw][:, b, k, :], hT[:, k, bsl], k == 0, k == 7, r=[f"Wg{jw}"] + hks, w=[gk])
                    js = ns % 2; ns += 1
                    act(sg[js], gp[:, 0:TB], AF.Sigmoid, r=[], w=[gk, f"sgt{js}"])
                    jb = nob % 3; nob += 1
                    dma(obT[jb], self.ob_d[b][:, bsl].rearrange("(k p) t -> p k t", p=128), r=[f"ob{b}_d"], w=[f"obT{jb}"])
                    bp, bk = self.nextps()
                    for k in range(4):
                        mm(bp[:, 0:TB], Wb[jw][:, b, k, :], obT[jb][:, k, :], k == 0, k == 3, r=[f"Wb{jw}", f"obT{jb}"], w=[bk])
                    if b == 0:
                        op("dve", "tensor_tensor", r=[f"sgt{js}"], w=[bk, "macc"], out=m, in0=bp[:, 0:TB], in1=sg[js], op=ALU.mult)
                    else:
                        op("dve", "tensor_tensor", r=[f"sgt{js}"], w=[bk, "tmM"], out=tm, in0=bp[:, 0:TB], in1=sg[js], op=ALU.mult)
                        if b == 1:
                            op("dve", "tensor_tensor", r=["tmM", "macc"], w=["macc"], out=m, in0=m, in1=tm, op=ALU.add)
                        else:
                            jo = no_ % 2; no_ += 1
                            op("dve", "tensor_tensor", r=["tmM", "macc"], w=[f"mo{jo}"], out=mo[jo], in0=m, in1=tm, op=ALU.add)
                            dma(self.mT_d[f * 128:(f + 1) * 128, bsl], mo[jo], r=[f"mo{jo}"], w=["mT_d"])
        self.dbg("mT_d", self.mT_d, "mT_d")

    def phase_M2(self, l):
        T, NT = self.T, self.NT
        op, dma, mm, act = self.op, self.dma, self.mm, self.act
        self.phase("M2")
        self.mk_wstage(2304, n=2)
        gB = self.sb("gateB", [128, D])
        self.modvec(l, 2, gB, "gateB")
        Wo = self.sb("Wo", [128, 8, D], BF16)
        for k in range(8):
            self.wload(Wo[:, k, :], self.w_o[l, k * 128:(k + 1) * 128, :], "Wo", mul=gB, mul_key="gateB")
        mT = self.sb("mT", [128, 8, T], BF16)
        dma(mT, self.mT_d.rearrange("(k p) t -> p k t", p=128), r=["mT_d"], w=["mT"])
        for i in range(NT):
            for half in range(2):
                ps, pk = self.nextps()
                for k in range(8):
                    mm(ps, mT[:, k, i * 128:(i + 1) * 128], Wo[:, k, half * 512:(half + 1) * 512], k == 0, k == 7, r=["mT", "Wo"], w=[pk])
                xs = self.x_res[:, i, half * 512:(half + 1) * 512]
                op("dve", "scalar_tensor_tensor", r=[], w=[pk, f"x{i}"], out=xs, in0=xs, scalar=ALPHA, in1=ps, op0=ALU.mult, op1=ALU.add)
        self.post_norm(l, 0)

    def post_norm(self, l, which):
        NT = self.NT
        op, dma, act = self.op, self.dma, self.act
        gB = self.sb("lngB", [128, D]); bB = self.sb("lnbB", [128, D])
        dma(gB, self.ln_in[l, 2 * which:2 * which + 1, :].partition_broadcast(128), w=["lngB"])
        dma(bB, self.ln_in[l, 2 * which + 1:2 * which + 2, :].partition_broadcast(128), w=["lnbB"])
        lns = [self.mk_lnscr(f"p{i}") for i in range(2)]
        xn = [self.sb(f"pxn{i}", [128, D]) for i in range(2)]
        for i in range(NT):
            j = i % 2
            xt = self.x_res[:, i, :]
            rs, rk = self.ln_stats(xt, f"x{i}", 128, lns[j])
            act(xn[j], xt, AF.Identity, r=[f"x{i}"] + rk, w=[f"pxn{j}"], scale=rs[:, 0:1], bias=rs[:, 1:2])
            op("dve", "tensor_tensor", r=[f"pxn{j}", "lngB"], w=[f"pxn{j}"], out=xn[j], in0=xn[j], in1=gB, op=ALU.mult)
            op("dve", "tensor_tensor", r=[f"pxn{j}", "lnbB"], w=[f"x{i}"], out=xt, in0=xn[j], in1=bB, op=ALU.add)

    def phase_F(self, l, last):
        T, NT = self.T, self.NT
        op, dma, mm, act, tr = self.op, self.dma, self.mm, self.act, self.tr
        self.phase("H")
        dma(self.halo_loc[0:1, :], self.x_res[0:1, 0, :], w=["halo_loc"])
        dma(self.halo_loc[1:2, :], self.x_res[127:128, NT - 1, :], w=["halo_loc"])
        self.allgather(self.halo_all, self.halo_loc, r=["halo_loc"], w=["halo_all"])
        self.phase("F")
        hT = self.sb("hT2", [128, 8, T + 2], BF16)
        self.mk_wstage(2304, n=3)
        shB, scB = self.phase_A(l, 1, hT, 1)
        rows = self.sb("hrows", [8, D]); xh = self.sb("xh", [2, D]); hh = self.sb("hh", [2, D], BF16)
        dma(rows, self.halo_all, r=["halo_all"], w=["hrows"])
        for half in range(2):
            ps, pk = self.nextps()
            mm(ps[0:2, :], self.selH, rows[:, half * 512:(half + 1) * 512], True, True, r=["hrows"], w=[pk])
            act(xh[:, half * 512:(half + 1) * 512], ps[0:2, :], AF.Identity, r=[], w=[pk, "xh"])
        lsc = self.mk_lnscr("h")
        rs, rk = self.ln_stats(xh, "xh", 2, lsc)
        act(xh, xh, AF.Identity, r=["xh"] + rk, w=["xh"], scale=rs[0:2, 0:1], bias=rs[0:2, 1:2])
        op("dve", "tensor_tensor", r=["xh", "scB"], w=["xh"], out=xh, in0=xh, in1=scB[0:2, :], op=ALU.mult)
        op("dve", "tensor_tensor", r=["xh", "shB"], w=["xh"], out=xh, in0=xh, in1=shB[0:2, :], op=ALU.add)
        op("dve", "tensor_scalar", r=["xh"], w=["hh"], out=hh, in0=xh, scalar1=self.flag[0:2, 0:1], scalar2=None, op0=ALU.mult)
        ps, pk = self.nextps()
        psb = ps.bitcast(BF16)
        for k in range(8):
            tr(psb[:, k * 2:(k + 1) * 2], hh[:, k * 128:(k + 1) * 128], r=["hh"], w=[pk])
        pv = psb[:, 0:16].rearrange("p (k t) -> p k t", k=8)
        op("dve", "tensor_copy", r=[], w=[pk, "hTh0"], out=hT[:, :, 0:1], in_=pv[:, :, 0:1])
        op("dve", "tensor_copy", r=[], w=[pk, "hTh1"], out=hT[:, :, T + 1:T + 2], in_=pv[:, :, 1:2])
        hks = [f"hT{i}" for i in range(NT)] + ["hTh0", "hTh1"]
        gB = self.sb("gate2B", [128, D])
        self.modvec(l, 5, gB, "gate2B")
        cw = self.sb("cw", [128, 44, 3]); cb = self.sb("cb", [128, 44])
        dma(cw, self.convw_in[l].rearrange("p (c k) -> p c k", k=3), w=["cw"])
        dma(cb, self.convb_in[l], w=["cb"])
        FB = min(256, T)
        Wa = [self.sb(f"Wa{i}", [128, 8, 128], BF16) for i in range(2)]
        Wgt = [self.sb(f"Wgt{i}", [128, 8, 128], BF16) for i in range(2)]
        Wd = [self.sb(f"Wd{i}", [128, D], BF16) for i in range(2)]
        ca_ = [self.sb(f"ca{i}", [128, FB]) for i in range(1)]; cg_ = [self.sb(f"cg{i}", [128, FB]) for i in range(1)]
        ca, cg = ca_[0], cg_[0]
        av = [self.sb(f"av{i}", [128, FB], BF16) for i in range(2)]
        na = 0
        NTB = T // FB

        def issue_up(jf, tb):
            jw = jf % 2
            if tb == 0:
                self.wload(Wa[jw], self.w_up[l, :, jf * 128:(jf + 1) * 128].rearrange("(k p) n -> p k n", p=128), f"Wa{jw}")
                self.wload(Wgt[jw], self.w_up[l, :, DFF + jf * 128:DFF + (jf + 1) * 128].rearrange("(k p) n -> p k n", p=128), f"Wgt{jw}")
                self.wload(Wd[jw], self.w_down[l, jf * 128:(jf + 1) * 128, :], f"Wd{jw}", mul=gB, mul_key="gate2B")
            c0 = tb * FB
            ui = (jf * NTB + tb) % 2
            pa, pak = self.ps[ui], f"ps{ui}"
            pg, pgk = self.ps[2 + ui], f"ps{2 + ui}"
            for k in range(8):
                mm(pa[:, 0:FB + 2], Wa[jw][:, k, :], hT[:, k, c0:c0 + FB + 2], k == 0, k == 7, r=[f"Wa{jw}"] + hks, w=[pak])
            for k in range(8):
                mm(pg[:, 0:FB + 2], Wgt[jw][:, k, :], hT[:, k, c0:c0 + FB + 2], k == 0, k == 7, r=[f"Wgt{jw}"] + hks, w=[pgk])
            return pa, pak, pg, pgk

        def finish(jf, tb, pa, pak, pg, pgk):
            nonlocal na
            jw = jf % 2
            c0 = tb * FB
            for (pp, ppk, dst, dk, ch) in ((pa, pak, ca, "ca", jf), (pg, pgk, cg, "cg", 22 + jf)):
                act(dst, pp[:, 1:FB + 1], AF.Identity, r=["cw", "cb"], w=[ppk, dk], scale=cw[:, ch, 1:2], bias=cb[:, ch:ch + 1])
                op("dve", "scalar_tensor_tensor", r=["cw"], w=[ppk, dk], out=dst, in0=pp[:, 0:FB], scalar=cw[:, ch, 0:1], in1=dst, op0=ALU.mult, op1=ALU.add)
                op("dve", "scalar_tensor_tensor", r=["cw"], w=[ppk, dk], out=dst, in0=pp[:, 2:FB + 2], scalar=cw[:, ch, 2:3], in1=dst, op0=ALU.mult, op1=ALU.add)
            act(cg, cg, AF.Silu, r=["cg"], w=["cg"])
            ja = na % 2; na += 1
            op("dve", "tensor_tensor", r=["ca", "cg"], w=[f"av{ja}"], out=av[ja], in0=ca, in1=cg, op=ALU.mult)
            for ti in range(FB // 128):
                i = (c0 // 128) + ti
                for half in range(2):
                    ps, pk = self.nextps([4, 5, 6, 7])
                    mm(ps, av[ja][:, ti * 128:(ti + 1) * 128], Wd[jw][:, half * 512:(half + 1) * 512], True, True, r=[f"av{ja}", f"Wd{jw}"], w=[pk])
                    xs = self.x_res[:, i, half * 512:(half + 1) * 512]
                    if jf == 0:
                        op("dve", "scalar_tensor_tensor", r=[], w=[pk, f"x{i}"], out=xs, in0=xs, scalar=ALPHA, in1=ps, op0=ALU.mult, op1=ALU.add)
                    else:
                        op("dve", "tensor_tensor", r=[], w=[pk, f"x{i}"], out=xs, in0=xs, in1=ps, op=ALU.add)

        prev = None
        for jf in range(DFF // 128):
            for tb in range(NTB):
                cur = (jf, tb) + issue_up(jf, tb)
                if prev is not None:
                    finish(*prev)
                prev = cur
        finish(*prev)
        self.post_norm(l, 1)
        if last:
            dma(self.out.rearrange("(n p) d -> p n d", p=128), self.x_res, r=[f"x{i}" for i in range(NT)], w=["out"])

    def build(self, stop=None):
        self.declare_inputs()
        self.setup()
        for l in range(self.L):
            self.phase_B(l)
            self.phase_R(l)
            self.phase_S(l)
            self.phase_AT(l)
            self.phase_M0(l)
            self.phase_M1(l)
            self.phase_M2(l)
            self.phase_F(l, last=(l == self.L - 1))
        self.fw.barrier()
        self.fw.emit()
        return self.nc


def _consts(S):
    T = S // 4
    NT = T // 128
    f32 = np.float32
    p = np.arange(128, dtype=f32)
    c = {}
    c["invf16"] = np.broadcast_to((f32(10000.0) ** (-np.arange(16, dtype=f32) / f32(16))).astype(f32), (128, 16)).copy()
    c["invf32"] = np.broadcast_to((f32(10000.0) ** (-np.arange(32, dtype=f32) / f32(32))).astype(f32), (128, 32)).copy()
    jj, ii = np.meshgrid(p, p, indexing="ij")
    EF = np.maximum(ii - jj, 0); EB = np.maximum(jj - ii, 0)
    MF = (ii >= jj).astype(f32); MB = (jj > ii).astype(f32)
    c["cmask"] = np.stack([EF, EB, MF, MB]).astype(f32)
    c["ek"] = np.stack([127 - p, p, p + 1, 128 - p], 1).astype(f32)
    n = np.arange(NT, dtype=f32)
    et = np.zeros((128, NT), f32); et[:64] = n[None, :]; et[64:] = (NT - 1 - n)[None, :]
    c["etab"] = et
    c["io"] = np.broadcast_to(np.arange(1, 513, dtype=f32), (128, 512)).copy()
    return c


def prep_inputs(inp, S, L):
    T = S // 4
    NT = T // 128
    f32 = np.float32
    A = lambda a: np.ascontiguousarray(np.asarray(a))
    cst = _consts(S)
    sh = {}
    sh["w_in"] = A(inp["w_in"][:L]); sh["w_uq"] = A(inp["mla_w_uq"][:L]); sh["w_ukv"] = A(inp["mla_w_ukv"][:L])
    sh["qnorm"] = A(np.asarray(inp["mla_q_norm"])[:L].reshape(L, 2, 128).transpose(0, 2, 1))
    sh["kvnorm"] = A(np.asarray(inp["mla_kv_norm"])[:L].reshape(L, 128, 1))
    sh["ldc"] = A(np.asarray(inp["ret_log_decay"])[:L].reshape(L, 8))
    sh["dskip"] = A(np.asarray(inp["s5_d"])[:L].reshape(L, 4, 128).transpose(0, 2, 1))
    sh["w_glu"] = A(inp["s5_w_glu"][:L])
    sh["w_br_mla"] = A(inp["w_branch_mla"][:L]); sh["w_br_ret"] = A(inp["w_branch_ret"][:L]); sh["w_br_s5"] = A(inp["w_branch_s5"][:L])
    sh["w_o"] = A(inp["w_o"][:L]); sh["w_up"] = A(inp["ffn_w_up"][:L]); sh["w_down"] = A(inp["ffn_w_down"][:L])
    cw = np.asarray(inp["ffn_conv_w"])[:L]
    sh["convw"] = A(cw.reshape(L, 3, 44, 128).transpose(0, 3, 2, 1).reshape(L, 128, 132))
    sh["convb"] = A(np.asarray(inp["ffn_conv_b"])[:L].reshape(L, 44, 128).transpose(0, 2, 1))
    sh["ln"] = A(np.stack([np.asarray(inp[k])[:L] for k in ("ln1_g", "ln1_b", "ln2_g", "ln2_b")], 1))
    sh["w_ada"] = A(inp["w_ada"][:L]); sh["b_ada"] = A(inp["b_ada"][:L])
    sh.update(cst)
    x = np.asarray(inp["x"]); cc = np.asarray(inp["c"]); pos = np.asarray(inp["positions"])
    s5 = {k: np.asarray(inp[k])[:L] for k in ("s5_lam_re", "s5_lam_im", "s5_log_step", "s5_b_re", "s5_b_im", "s5_c_re", "s5_c_im")}
    maps = []
    for core in range(8):
        b, r = core // 4, core % 4
        m = dict(sh)
        m["x"] = A(x[b, r * T:(r + 1) * T, :])
        m["cT"] = A(cc[b].reshape(8, 128).T)
        m["pos"] = A(pos[b, r * T:(r + 1) * T].reshape(NT, 128).T.astype(np.int32))
        dist = np.full((128, 4), BIGD, f32)
        for rr in range(4):
            if rr < r:
                dist[:64, rr] = r - 1 - rr
            if rr > r:
                dist[64:, rr] = rr - 1 - r
        m["dist"] = dist
        sel = np.zeros((128, 4), f32); sel[:, r] = 1.0
        m["sel"] = sel
        selH = np.zeros((8, 2), f32); flag = np.zeros((2, 1), f32)
        if r > 0:
            selH[2 * (r - 1) + 1, 0] = 1.0; flag[0, 0] = 1.0
        if r < 3:
            selH[2 * (r + 1), 1] = 1.0; flag[1, 0] = 1.0
        m["selH"] = selH; m["flag"] = flag
        g0 = 8 * r
        lamS_re = np.zeros((L, 2, 128, 4), f32); lamS_im = np.zeros_like(lamS_re); lsS = np.zeros_like(lamS_re)
        Bre = np.zeros((L, 2, 128, 4, 128), f32); Bim = np.zeros_like(Bre); Cre = np.zeros_like(Bre); Cim = np.zeros_like(Bre)
        for st in range(4):
            for gg in range(2):
                g = g0 + 2 * st + gg
                gl = 2 * st + gg
                lamS_re[:, :, gg * 64:(gg + 1) * 64, st] = s5["s5_lam_re"][:, :, g, :]
                lamS_im[:, :, gg * 64:(gg + 1) * 64, st] = s5["s5_lam_im"][:, :, g, :]
                lsS[:, :, gg * 64:(gg + 1) * 64, st] = s5["s5_log_step"][:, :, g, None]
                Bre[:, :, gl * 16:(gl + 1) * 16, st, gg * 64:(gg + 1) * 64] = s5["s5_b_re"][:, :, g].transpose(0, 1, 3, 2)
                Bim[:, :, gl * 16:(gl + 1) * 16, st, gg * 64:(gg + 1) * 64] = s5["s5_b_im"][:, :, g].transpose(0, 1, 3, 2)
                Cre[:, :, gg * 64:(gg + 1) * 64, st, gl * 16:(gl + 1) * 16] = s5["s5_c_re"][:, :, g].transpose(0, 1, 3, 2)
                Cim[:, :, gg * 64:(gg + 1) * 64, st, gl * 16:(gl + 1) * 16] = s5["s5_c_im"][:, :, g].transpose(0, 1, 3, 2)
        m["lamS_re"] = lamS_re; m["lamS_im"] = lamS_im; m["lsS"] = lsS
        m["lamB_re"] = A(lamS_re.transpose(0, 1, 3, 2).reshape(L, 2, 512))
        m["lamB_im"] = A(lamS_im.transpose(0, 1, 3, 2).reshape(L, 2, 512))
        m["lsB"] = A(lsS.transpose(0, 1, 3, 2).reshape(L, 2, 512))
        m["Bblk_re"] = Bre.reshape(L, 2, 128, 512); m["Bblk_im"] = Bim.reshape(L, 2, 128, 512)
        m["Cblk_re"] = Cre.reshape(L, 2, 128, 512); m["Cblk_im"] = Cim.reshape(L, 2, 128, 512)
        maps.append(m)
    return maps


_CACHE = {}


def kernel(**inputs):
    S = int(np.asarray(inputs["x"]).shape[1])
    L = int(np.asarray(inputs["w_in"]).shape[0])
    T = S // 4
    key = (S, L)
    if key not in _CACHE:
        b = Builder(S, L)
        b.build()
        _CACHE[key] = b
    b = _CACHE[key]
    maps = prep_inputs(inputs, S, L)
    maps = [{k: m[k] for k in b.inputs} for m in maps]
    res = run_bass_kernel_spmd(b.nc, maps, core_ids=list(range(8)))
    out = np.zeros((2, S, D), np.float32)
    for c in range(8):
        out[c // 4, (c % 4) * T:(c % 4 + 1) * T, :] = np.asarray(res.results[c]["out"]).reshape(T, D)
    return out
```

```python
import math
import numpy as np
import ml_dtypes
import concourse.bass as bass
import concourse.mybir as mybir
from concourse.bass_utils import run_bass_kernel_spmd

F32 = mybir.dt.float32
BF16 = mybir.dt.bfloat16
I32 = mybir.dt.int32
ALU = mybir.AluOpType
AF = mybir.ActivationFunctionType

ENGS = ("pe", "act", "dve", "pool", "sp")
D = 1024
NH = 8
DFF = 2816
C_QC, C_KV, C_KR, C_RQ, C_RK, C_RV, C_RG, C_S5, C_G = 0, 256, 384, 416, 672, 928, 1440, 1952, 2464
ALPHA = float((2 * 4) ** 0.25)
LN_EPS, RMS_EPS, GN_EPS = 1e-5, 1e-6, 1e-5
TWO_PI = 2.0 * math.pi
SB_BASE, SB_END = 16512, 229376
BIGD = 40.0


class FW:
    def __init__(self, nc, n_dma_sems=(("sp", 40), ("pool", 16))):
        self.nc = nc
        self.stream = {e: [] for e in ENGS}
        self.ctr = {e: nc.alloc_semaphore(name=f"ctr_{e}") for e in ENGS}
        self.bar = {e: nc.alloc_semaphore(name=f"bar_{e}") for e in ENGS}
        self.count = {e: 0 for e in ENGS}
        self.known = {e: {} for e in ENGS}
        self.keys = {}
        self.dpool = {}
        for e, n in n_dma_sems:
            self.dpool[e] = dict(sems=[nc.alloc_semaphore(name=f"dq_{e}_{i}") for i in range(n)], vals=[0] * n, nxt=0)
        self.nbar = 0
        self.n_inst = 0
        self.customs = {}

    def _wait(self, eng, tok):
        sem, val = tok
        if val <= 0:
            return
        k = self.known[eng]
        if k.get(id(sem), 0) >= val:
            return
        k[id(sem)] = val
        self.stream[eng].append(("wait", sem, val))

    def _deps(self, eng, reads, writes, skip_same):
        toks = []
        for key in reads:
            st = self.keys.get(key)
            if st and st["w"] is not None:
                toks.append(st["w"])
        for key in writes:
            st = self.keys.get(key)
            if st:
                if st["w"] is not None:
                    toks.append(st["w"])
                toks.extend(st["r"])
        own = self.ctr[eng]
        for t in toks:
            if t[0] is own and skip_same:
                continue
            self._wait(eng, t)

    def _record(self, tok, reads, writes):
        for key in reads:
            st = self.keys.setdefault(key, {"w": None, "r": []})
            st["r"] = [t for t in st["r"] if t[0] is not tok[0]] + [tok]
        for key in writes:
            self.keys[key] = {"w": tok, "r": []}

    def op(self, eng, fn, reads=(), writes=(), skip_same=False):
        if eng == "pe":
            skip_same = True
        self._deps(eng, reads, writes, skip_same)
        self.count[eng] += 1
        tok = (self.ctr[eng], self.count[eng])
        self.stream[eng].append(("inst", fn, self.ctr[eng], 1))
        self._record(tok, reads, writes)
        self.n_inst += 1
        return tok

    def dma(self, eng, out, in_, reads=(), writes=()):
        p = self.dpool[eng]
        j = p["nxt"]
        p["nxt"] = (j + 1) % len(p["sems"])
        sem = p["sems"][j]
        self._wait(eng, (sem, p["vals"][j]))
        self._deps(eng, reads, writes, False)
        p["vals"][j] += 16
        tok = (sem, p["vals"][j])
        self.stream[eng].append(("inst", lambda e, o=out, i=in_: e.dma_start(out=o, in_=i), sem, 16))
        self._record(tok, reads, writes)
        self.n_inst += 1
        return tok

    def custom(self, eng, fn, sem, newval, reads=(), writes=()):
        self._deps(eng, reads, writes, False)
        self.stream[eng].append(("inst", fn, sem, 1))
        tok = (sem, newval)
        self.customs[eng] = tok
        self._record(tok, reads, writes)
        return tok

    def barrier(self):
        self.nbar += 1
        for e in ENGS:
            if e in self.dpool:
                p = self.dpool[e]
                for s, v in zip(p["sems"], p["vals"]):
                    self._wait(e, (s, v))
            if e in self.customs:
                self._wait(e, self.customs[e])
            self._wait(e, (self.ctr[e], self.count[e]))
            self.stream[e].append(("inst", lambda en: en.nop(), self.bar[e], 1))
        for e in ENGS:
            for e2 in ENGS:
                if e2 != e:
                    self._wait(e, (self.bar[e2], self.nbar))
        self.keys = {}

    def emit(self):
        nc = self.nc
        engobj = {"pe": "tensor", "act": "scalar", "dve": "vector", "pool": "gpsimd", "sp": "sync"}
        with nc.Block() as block:
            for e in ENGS:
                stream = self.stream[e]

                def body(eng, stream=stream):
                    for it in stream:
                        if it[0] == "wait":
                            eng.wait_ge(it[1], it[2])
                        else:
                            it[1](eng).then_inc(it[2], it[3])

                getattr(block, engobj[e])(body)


def _dsize(dt):
    return {F32: 4, BF16: 2, I32: 4}[dt]


def rev_ap(ap, n):
    a = ap.ap
    assert len(a) == 2 and a[1][0] == 1 and a[1][1] == n, a
    return bass.AP(ap.tensor, ap.offset + (n - 1), [[a[0][0], a[0][1]], [-1, n]])


class Builder:
    def __init__(self, S, L, debug=()):
        self.S, self.L = S, L
        self.T = S // 4
        self.NT = self.T // 128
        self.debug = set(debug)
        self.nc = bass.Bass("TRN2", target_bir_lowering=False)
        self.fw = FW(self.nc)
        self.pers_off = SB_BASE
        self.arena_off = SB_BASE
        self.uid = 0
        self.inputs = {}
        self.dbg_out = []
        self.ps = [self.nc.alloc_psum_tensor(f"ps{i}", [128, 512], F32).ap() for i in range(8)]
        self.psi = 0
        self.cc_sem = self.nc.alloc_semaphore(name="cc_sem")
        self.cc_n = 0
        self.stage_i = 0
        self._cache = {}

    def _alloc(self, name, shape, dt, off):
        nbytes = int(np.prod(shape[1:])) * _dsize(dt)
        nbytes = (nbytes + 31) // 32 * 32
        assert off + nbytes <= SB_END, f"SBUF overflow allocating {name} {shape}: {off + nbytes - SB_END} bytes over"
        self.uid += 1
        h = self.nc.alloc_sbuf_tensor_at(f"{name}_{self.uid}", list(shape), dt, offset=off)
        return h.ap(), off + nbytes

    def pers(self, name, shape, dt=F32):
        ap, self.pers_off = self._alloc(name, shape, dt, self.pers_off)
        self.arena_off = max(self.arena_off, self.pers_off)
        return ap

    def sb(self, name, shape, dt=F32):
        ap, self.arena_off = self._alloc(name, shape, dt, self.arena_off)
        return ap

    def phase(self, name):
        self.fw.barrier()
        self.arena_off = self.pers_off
        self.pname = name

    def subphase(self, mark):
        self.fw.barrier()
        self.arena_off = mark

    def inp(self, name, shape, dt=F32):
        ap = self.nc.dram_tensor(name, list(shape), dt, kind="ExternalInput").ap()
        self.inputs[name] = (tuple(shape), dt)
        return ap

    def dram(self, name, shape, dt=F32):
        return self.nc.dram_tensor(name, list(shape), dt).ap()

    def nextps(self, cyc=None):
        cyc = cyc or list(range(8))
        i = cyc[self.psi % len(cyc)]
        self.psi += 1
        return self.ps[i], f"ps{i}"

    def op(self, eng, meth, r=(), w=(), **kw):
        return self.fw.op(eng, lambda e: getattr(e, meth)(**kw), reads=list(r), writes=list(w))

    def dma(self, out, in_, r=(), w=(), q="sp"):
        return self.fw.dma(q, out, in_, reads=list(r), writes=list(w))

    def mm(self, out, lhsT, rhs, start, stop, r, w):
        return self.fw.op("pe", lambda e: e.matmul(out, lhsT=lhsT, rhs=rhs, start=start, stop=stop), reads=list(r), writes=list(w))

    def tr(self, out, in_, r, w):
        p = in_.shape[0]
        idn = self.ident[0:p, 0:p]
        return self.fw.op("pe", lambda e: e.transpose(out=out, in_=in_, identity=idn), reads=list(r) + ["ident"], writes=list(w))

    def act(self, out, in_, func, r, w, bias=0.0, scale=1.0, **kw):
        return self.fw.op("act", lambda e: e.activation(out=out, in_=in_, func=func, bias=bias, scale=scale, **kw), reads=list(r), writes=list(w))

    def allgather(self, out_ap, in_ap, r, w):
        self.cc_n += 1
        n = self.cc_n
        return self.fw.custom("pool", lambda e: e.collective_compute("AllGather", ALU.bypass, replica_groups=[[0, 1, 2, 3], [4, 5, 6, 7]],
                                                                     ins=[in_ap.opt()], outs=[out_ap.opt()]),
                              self.cc_sem, n, reads=list(r), writes=list(w))

    def dbg(self, name, dram_ap, key):
        if name in self.debug:
            o = self.nc.dram_tensor("dbg_" + name, list(dram_ap.shape), dram_ap.dtype, kind="ExternalOutput").ap()
            self.dma(o, dram_ap, r=[key])
            self.dbg_out.append("dbg_" + name)

    def dbg_sb(self, name, sb_ap, key):
        if name in self.debug:
            o = self.nc.dram_tensor("dbg_" + name, list(sb_ap.shape), sb_ap.dtype, kind="ExternalOutput").ap()
            self.dma(o, sb_ap, r=[key])
            self.dbg_out.append("dbg_" + name)

    def wload(self, dst, src, dst_key, mul=None, mul_key=None, src_key=None):
        shp = list(src.shape)
        rows = shp[0]
        cols = int(np.prod(shp[1:]))
        si = self.stage_i % len(self.wstage)
        self.stage_i += 1
        skey = f"wst{si}"
        st = self.wstage[si][0:rows, 0:cols]
        if len(shp) == 3:
            st = st.rearrange("p (k n) -> p k n", k=shp[1])
        self.dma(st, src, r=[src_key] if src_key else [], w=[skey])
        if mul is None:
            self.op("pool", "tensor_copy", r=[skey], w=[dst_key], out=dst, in_=st)
        else:
            self.op("pool", "tensor_tensor", r=[skey, mul_key], w=[dst_key], out=dst, in0=st, in1=mul, op=ALU.mult)

    def mk_wstage(self, cols, n=3):
        self.wstage = [self.sb(f"wst{i}", [128, cols], F32) for i in range(n)]
        self.stage_i = 0

    def declare_inputs(self):
        L, T, NT, S = self.L, self.T, self.NT, self.S
        I = self.inp
        self.x_in = I("x", [T, D])
        self.cT_in = I("cT", [128, 8])
        self.pos_in = I("pos", [128, NT], I32)
        self.invf16_in = I("invf16", [128, 16])
        self.invf32_in = I("invf32", [128, 32])
        self.cmask_in = I("cmask", [4, 128, 128])
        self.ek_in = I("ek", [128, 4])
        self.etab_in = I("etab", [128, NT])
        self.io_in = I("io", [128, 512])
        self.dist_in = I("dist", [128, 4])
        self.sel_in = I("sel", [128, 4])
        self.selH_in = I("selH", [8, 2])
        self.flag_in = I("flag", [2, 1])
        self.w_in = I("w_in", [L, D, 5536])
        self.qnorm_in = I("qnorm", [L, 128, 2])
        self.w_uq = I("w_uq", [L, 256, 768])
        self.kvnorm_in = I("kvnorm", [L, 128, 1])
        self.w_ukv = I("w_ukv", [L, 128, 1024])
        self.ldc_in = I("ldc", [L, 8])
        self.lamS_re = I("lamS_re", [L, 2, 128, 4])
        self.lamS_im = I("lamS_im", [L, 2, 128, 4])
        self.lsS = I("lsS", [L, 2, 128, 4])
        self.lamB_re = I("lamB_re", [L, 2, 512])
        self.lamB_im = I("lamB_im", [L, 2, 512])
        self.lsB = I("lsB", [L, 2, 512])
        self.Bblk_re = I("Bblk_re", [L, 2, 128, 512])
        self.Bblk_im = I("Bblk_im", [L, 2, 128, 512])
        self.Cblk_re = I("Cblk_re", [L, 2, 128, 512])
        self.Cblk_im = I("Cblk_im", [L, 2, 128, 512])
        self.dskip_in = I("dskip", [L, 128, 4])
        self.w_glu = I("w_glu", [L, 512, 512])
        self.w_br = [I("w_br_mla", [L, 512, D]), I("w_br_ret", [L, 512, D]), I("w_br_s5", [L, 512, D])]
        self.w_o = I("w_o", [L, D, D])
        self.w_up = I("w_up", [L, D, 2 * DFF])
        self.convw_in = I("convw", [L, 128, 44 * 3])
        self.convb_in = I("convb", [L, 128, 44])
        self.w_down = I("w_down", [L, DFF, D])
        self.ln_in = I("ln", [L, 4, D])
        self.w_ada = I("w_ada", [L, D, 6 * D])
        self.b_ada = I("b_ada", [L, 6 * D])
        self.out = self.nc.dram_tensor("out", [T, D], F32, kind="ExternalOutput").ap()
        Dm = self.dram
        self.lat_loc = Dm("lat_loc", [160, T], BF16)
        self.lat_all = Dm("lat_all", [4 * 160, T], BF16)
        self.qT_d = Dm("qT_d", [NH, 96, T], BF16)
        self.rq_d = Dm("rq_d", [T, 256], BF16)
        self.rk_d = Dm("rk_d", [T, 256], BF16)
        self.rv_d = Dm("rv_d", [T, 512], BF16)
        self.rg_d = Dm("rg_d", [T, 512], BF16)
        self.u_loc = Dm("u_loc", [512, T], F32)
        self.u_all = Dm("u_all", [16 * 128, T], F32)
        self.agg_loc = Dm("agg_loc", [4 * 128, 128], F32)
        self.agg_all = Dm("agg_all", [16 * 128, 128], F32)
        self.ys_loc = Dm("ys_loc", [4 * 128, T], F32)
        self.ys_all = Dm("ys_all", [16 * 128, T], F32)
        self.ob_d = [Dm(f"ob{i}_d", [512, T], BF16) for i in range(3)]
        self.mT_d = Dm("mT_d", [D, T], BF16)
        self.halo_loc = Dm("halo_loc", [2, D], F32)
        self.halo_all = Dm("halo_all", [8, D], F32)

    def setup(self):
        T, NT = self.T, self.NT
        P = self.pers
        self.x_res = P("x_res", [128, NT, D])
        self.ident = P("ident", [128, 128], BF16)
        self.ones_f = P("ones_f", [128, 128])
        self.zeros_f = P("zeros_f", [128, 128])
        self.condrep = P("condrep", [128, 8, 128])
        self.cos16 = P("cos16", [128, NT, 16]); self.sin16 = P("sin16", [128, NT, 16])
        self.cos32 = P("cos32", [128, NT, 32]); self.sin32 = P("sin32", [128, NT, 32])
        self.cos32q = P("cos32q", [128, NT, 32]); self.sin32q = P("sin32q", [128, NT, 32])
        self.sel = P("sel", [128, 4]); self.dist = P("dist", [128, 4])
        self.selH = P("selH", [8, 2]); self.flag = P("flag", [2, 1])
        self.ek = P("ek", [128, 4]); self.etab = P("etab", [128, NT])
        self.constc = P("constc", [128, 4])
        self.epsc = self.constc[:, 0:1]; self.rmseps = self.constc[:, 1:2]; self.halfpi = self.constc[:, 2:3]
        self.fw.barrier()
        self.arena_off = self.pers_off
        op, dma = self.op, self.dma
        dma(self.x_res, self.x_in.rearrange("(n p) d -> p n d", p=128), w=["x_res"])
        op("pool", "memset", w=["ident"], ap=self.ident, constant=0.0)
        op("pool", "affine_select", r=["ident"], w=["ident"], out=self.ident, in_=self.ident, pattern=[[-1, 128]],
           compare_op=ALU.not_equal, fill=1.0, base=0, channel_multiplier=1)
        op("dve", "memset", w=["ones_f"], ap=self.ones_f, constant=1.0)
        op("dve", "memset", w=["zeros_f"], ap=self.zeros_f, constant=0.0)
        op("dve", "memset", w=["constc"], ap=self.constc[:, 0:1], constant=LN_EPS)
        op("dve", "memset", w=["constc"], ap=self.constc[:, 1:2], constant=RMS_EPS)
        op("dve", "memset", w=["constc"], ap=self.constc[:, 2:3], constant=math.pi / 2)
        op("dve", "memset", w=["constc"], ap=self.constc[:, 3:4], constant=GN_EPS)
        for nm, src in (("sel", self.sel_in), ("dist", self.dist_in), ("selH", self.selH_in), ("flag", self.flag_in),
                        ("ek", self.ek_in), ("etab", self.etab_in)):
            dma(getattr(self, nm), src, w=[nm])
        cT = self.sb("cT", [128, 8]); cs = self.sb("cs", [128, 8])
        dma(cT, self.cT_in, w=["cT"])
        self.act(cs, cT, AF.Silu, r=["cT"], w=["cs"])
        op("dve", "tensor_copy", r=["cs"], w=["condrep"], out=self.condrep, in_=cs.unsqueeze(2).to_broadcast([128, 8, 128]))
        posi = self.sb("posi", [128, NT], I32); posf = self.sb("posf", [128, NT])
        dma(posi, self.pos_in, w=["posi"])
        op("dve", "tensor_copy", r=["posi"], w=["posf"], out=posf, in_=posi)
        for nf, src, cosT, sinT in ((16, self.invf16_in, self.cos16, self.sin16), (32, self.invf32_in, self.cos32, self.sin32)):
            inv = self.sb(f"inv{nf}", [128, nf]); u = self.sb(f"u{nf}", [128, NT, nf]); ki = self.sb(f"ki{nf}", [128, NT, nf], I32)
            kf = self.sb(f"kf{nf}", [128, NT, nf]); fr = self.sb(f"fr{nf}", [128, NT, nf]); ab = self.sb(f"ab{nf}", [128, NT, nf])
            k = f"rp{nf}"
            dma(inv, src, w=[k + "inv"])
            op("dve", "tensor_scalar", r=[k + "inv"], w=[k + "inv"], out=inv, in0=inv, scalar1=1.0 / TWO_PI, scalar2=None, op0=ALU.mult)
            op("dve", "tensor_tensor", r=["posf", k + "inv"], w=[k + "u"], out=u, in0=posf.unsqueeze(2).to_broadcast([128, NT, nf]),
               in1=inv.unsqueeze(1).to_broadcast([128, NT, nf]), op=ALU.mult)
            op("dve", "tensor_copy", r=[k + "u"], w=[k + "ki"], out=ki, in_=u)
            op("dve", "tensor_copy", r=[k + "ki"], w=[k + "kf"], out=kf, in_=ki)
            op("dve", "tensor_tensor", r=[k + "u", k + "kf"], w=[k + "fr"], out=fr, in0=u, in1=kf, op=ALU.subtract)
            self.act(sinT, fr, AF.Sin, r=[k + "fr"], w=[k + "sin"], scale=TWO_PI)
            self.act(ab, fr, AF.Abs, r=[k + "fr"], w=[k + "ab"])
            self.act(cosT, ab, AF.Sin, r=[k + "ab"], w=[k + "cos"], scale=-TWO_PI, bias=self.halfpi)
        op("dve", "tensor_scalar", r=["rp32cos"], w=["c32q"], out=self.cos32q, in0=self.cos32, scalar1=0.125, scalar2=None, op0=ALU.mult)
        op("dve", "tensor_scalar", r=["rp32sin"], w=["s32q"], out=self.sin32q, in0=self.sin32, scalar1=0.125, scalar2=None, op0=ALU.mult)

    def modvec(self, l, idx, dst, dkey, plus_one=False):
        for qd in range(4):
            c0 = idx * D + qd * 256
            si = self.stage_i % len(self.wstage)
            self.stage_i += 1
            wst = self.wstage[si][:, 0:2048].rearrange("p (k n) -> p k n", k=8)
            brow = self.wstage[si][0:1, 2048:2304]
            sk = f"wst{si}"
            self.dma(wst, self.w_ada[l, :, c0:c0 + 256].rearrange("(k p) n -> p k n", p=128), w=[sk])
            self.dma(brow, self.b_ada[l:l + 1, c0:c0 + 256], w=[sk])
            ps, pk = self.nextps()
            for k in range(8):
                self.mm(ps[:, 0:256], self.condrep[:, k, :], wst[:, k, :], k == 0, False, r=["condrep", sk], w=[pk])
            self.mm(ps[:, 0:256], self.ones_f[0:1, :], brow, False, True, r=["ones_f", sk], w=[pk])
            self.act(dst[:, qd * 256:(qd + 1) * 256], ps[:, 0:256], AF.Identity, r=[], w=[pk, dkey], bias=1.0 if plus_one else 0.0)

    def mk_lnscr(self, tag):
        return dict(st=self.sb(f"lnst{tag}", [128, 2, 6]), mv=self.sb(f"lnmv{tag}", [128, 2]), rs=self.sb(f"lnrs{tag}", [128, 2]), k=f"ln{tag}")

    def ln_stats(self, xt, xkey, p, scr):
        st, mv, rs, k = scr["st"], scr["mv"], scr["rs"], scr["k"]
        for c in range(2):
            self.op("dve", "bn_stats", r=[xkey], w=[k + "st"], out=st[0:p, c, :], in_=xt[:, c * 512:(c + 1) * 512])
        self.op("dve", "bn_aggr", r=[k + "st"], w=[k + "mv"], out=mv[0:p, :], in_=st[0:p].rearrange("p a b -> p (a b)"))
        self.act(rs[0:p, 0:1], mv[0:p, 1:2], AF.Ln, r=[k + "mv"], w=[k + "rs"], bias=self.epsc[0:p, 0:1])
        self.act(rs[0:p, 0:1], rs[0:p, 0:1], AF.Exp, r=[k + "rs"], w=[k + "rs"], scale=-0.5)
        self.op("dve", "scalar_tensor_tensor", r=[k + "mv", k + "rs"], w=[k + "rs2"], out=rs[0:p, 1:2], in0=mv[0:p, 0:1], scalar=-1.0,
                in1=rs[0:p, 0:1], op0=ALU.mult, op1=ALU.mult)
        return rs, [k + "rs", k + "rs2"]

    def phase_A(self, l, which, hT, col0):
        NT = self.NT
        shB = self.sb("shB", [128, D]); scB = self.sb("scB", [128, D])
        self.modvec(l, 3 * which + 0, shB, "shB")
        self.modvec(l, 3 * which + 1, scB, "scB", plus_one=True)
        xn1 = self.sb("xn0", [128, D])
        xn = [xn1, xn1]
        hb = [self.sb(f"hb{i}", [128, D], BF16) for i in range(2)]
        lns = [self.mk_lnscr(i) for i in range(2)]
        for i in range(NT):
            j = i % 2
            xt = self.x_res[:, i, :]
            rs, rk = self.ln_stats(xt, "x_res", 128, lns[j])
            self.act(xn[j], xt, AF.Identity, r=["x_res"] + rk, w=["xn0"], scale=rs[:, 0:1], bias=rs[:, 1:2])
            self.op("dve", "tensor_tensor", r=["xn0", "scB"], w=["xn0"], out=xn[j], in0=xn[j], in1=scB, op=ALU.mult)
            self.op("dve", "tensor_tensor", r=["xn0", "shB"], w=[f"hb{j}"], out=hb[j], in0=xn[j], in1=shB, op=ALU.add)
            ps, pk = self.nextps()
            psb = ps.bitcast(BF16)
            for k in range(8):
                self.tr(psb[:, k * 128:(k + 1) * 128], hb[j][:, k * 128:(k + 1) * 128], r=[f"hb{j}"], w=[pk])
            self.op("dve", "tensor_copy", r=[], w=[pk, f"hT{i}"], out=hT[:, :, col0 + i * 128:col0 + (i + 1) * 128],
                    in_=psb.rearrange("p (k t) -> p k t", k=8))
        return shB, scB

    def rope(self, x1, x2, cosT, sinT, d1, d2, H, n, pk, dkey, tmp):
        c = cosT.unsqueeze(1).to_broadcast([128, H, n]); s = sinT.unsqueeze(1).to_broadcast([128, H, n])
        t1, t2 = tmp[0][:, 0:H * n].rearrange("p (h n) -> p h n", h=H), tmp[1][:, 0:H * n].rearrange("p (h n) -> p h n", h=H)
        o = self.op
        o("dve", "tensor_tensor", r=["rope"], w=[pk, "rt1"], out=t1, in0=x1, in1=c, op=ALU.mult)
        o("dve", "tensor_tensor", r=["rope"], w=[pk, "rt2"], out=t2, in0=x2, in1=s, op=ALU.mult)
        o("dve", "tensor_tensor", r=["rt1", "rt2"], w=[dkey], out=d1, in0=t1, in1=t2, op=ALU.subtract)
        o("dve", "tensor_tensor", r=["rope"], w=[pk, "rt1"], out=t1, in0=x1, in1=s, op=ALU.mult)
        o("dve", "tensor_tensor", r=["rope"], w=[pk, "rt2"], out=t2, in0=x2, in1=c, op=ALU.mult)
        o("dve", "tensor_tensor", r=["rt1", "rt2"], w=[dkey], out=d2, in0=t1, in1=t2, op=ALU.add)

    def phase_B(self, l):
        T, NT = self.T, self.NT
        op, dma, mm, tr, act = self.op, self.dma, self.mm, self.tr, self.act
        self.phase("B")
        hT = self.sb("hT", [128, 8, T], BF16)
        W1 = self.sb("W1", [128, 8, C_S5], BF16)
        Ws5 = self.sb("Ws5", [128, 8, 512], BF16)
        Wuq = self.sb("Wuq", [128, 2, 768], BF16)
        qg = self.sb("qg", [128, 2])
        self.mk_wstage(C_S5 + 512, n=2)
        dma(qg, self.qnorm_in[l], w=["qg"])
        for k in range(8):
            st = self.wstage[k % 2]
            dma(st, self.w_in[l, k * 128:(k + 1) * 128, 0:C_G], w=[f"wst{k % 2}"])
            op("pool", "tensor_copy", r=[f"wst{k % 2}"], w=["W1"], out=W1[:, k, :], in_=st[:, 0:C_S5])
            op("pool", "tensor_copy", r=[f"wst{k % 2}"], w=["Ws5"], out=Ws5[:, k, :], in_=st[:, C_S5:C_G])
        for k in range(2):
            st = self.wstage[k % 2]
            dma(st[:, 0:768], self.w_uq[l, k * 128:(k + 1) * 128, :], w=[f"wst{k % 2}"])
            op("dve", "tensor_scalar", r=[f"wst{k % 2}", "qg"], w=["Wuq"], out=Wuq[:, k, :], in0=st[:, 0:768], scalar1=qg[:, k:k + 1],
               scalar2=None, op0=ALU.mult)
        self.phase_A(l, 0, hT, 0)
        sq = self.sb("sq", [128, 384]); ssq = self.sb("ssq", [128, 2]); rr = self.sb("rr", [128, 2])
        qcn = self.sb("qcn", [128, 256], BF16); lat = self.sb("lat", [128, 160], BF16)
        qcnT = self.sb("qcnT", [128, 2, 128], BF16)
        latT = [self.sb(f"latT{i}", [128, 128], BF16) for i in range(2)]
        krT = [self.sb(f"krT{i}", [32, 128], BF16) for i in range(2)]
        Qr = self.sb("Qr", [128, 8, 96], BF16)
        QT = [self.sb(f"QT{i}", [96, 8, 128], BF16) for i in range(2)]
        rtmp = [self.sb(f"rtmp{i}", [128, 128]) for i in range(2)]
        rqk = [self.sb(f"rqk{i}", [128, 512], BF16) for i in range(2)]
        rvb = [self.sb(f"rvb{i}", [128, 512], BF16) for i in range(1)] * 2
        rgb = [self.sb(f"rgb{i}", [128, 512], BF16) for i in range(1)] * 2
        TB = min(512, T)
        ub = [self.sb(f"ub{i}", [128, TB]) for i in range(2)]
        for i in range(NT):
            j = i % 2
            tsl = slice(i * 128, (i + 1) * 128)
            hk = f"hT{i}"
            ps, pk = self.nextps()
            for k in range(8):
                mm(ps[:, 0:416], hT[:, k, tsl], W1[:, k, 0:416], k == 0, k == 7, r=[hk, "W1"], w=[pk])
            act(sq[:, 0:384], ps[:, 0:384], AF.Square, r=[], w=[pk, "sq"])
            op("dve", "tensor_reduce", r=["sq"], w=["ssq"], out=ssq[:, 0:1], in_=sq[:, 0:256], axis=mybir.AxisListType.X, op=ALU.add)
            op("dve", "tensor_reduce", r=["sq"], w=["ssq"], out=ssq[:, 1:2], in_=sq[:, 256:384], axis=mybir.AxisListType.X, op=ALU.add)
            act(rr[:, 0:1], ssq[:, 0:1], AF.Ln, r=["ssq"], w=["rr"], scale=1.0 / 256, bias=self.rmseps)
            act(rr[:, 1:2], ssq[:, 1:2], AF.Ln, r=["ssq"], w=["rr"], scale=1.0 / 128, bias=self.rmseps)
            act(rr, rr, AF.Exp, r=["rr"], w=["rr"], scale=-0.5)
            op("dve", "tensor_scalar", r=["rr"], w=[pk, "qcn"], out=qcn, in0=ps[:, 0:256], scalar1=rr[:, 0:1], scalar2=None, op0=ALU.mult)
            op("dve", "tensor_scalar", r=["rr"], w=[pk, "lat"], out=lat[:, 0:128], in0=ps[:, 256:384], scalar1=rr[:, 1:2], scalar2=None, op0=ALU.mult)
            self.rope(ps[:, 384:400].unsqueeze(1), ps[:, 400:416].unsqueeze(1), self.cos16[:, i, :], self.sin16[:, i, :],
                      lat[:, 128:144].unsqueeze(1), lat[:, 144:160].unsqueeze(1), 1, 16, pk, "lat", rtmp)
            pt, ptk = self.nextps()
            ptb = pt.bitcast(BF16)
            tr(ptb[:, 0:128], qcn[:, 0:128], r=["qcn"], w=[ptk])
            tr(ptb[:, 128:256], qcn[:, 128:256], r=["qcn"], w=[ptk])
            tr(ptb[:, 256:384], lat[:, 0:128], r=["lat"], w=[ptk])
            tr(ptb[0:32, 384:512], lat[:, 128:160], r=["lat"], w=[ptk])
            op("dve", "tensor_copy", r=[], w=[ptk, "qcnT"], out=qcnT, in_=ptb[:, 0:256].rearrange("p (k t) -> p k t", k=2))
            act(latT[j], ptb[:, 256:384], AF.Identity, r=[], w=[ptk, f"latT{j}"])
            act(krT[j], ptb[0:32, 384:512], AF.Identity, r=[], w=[ptk, f"krT{j}"])
            dma(self.lat_loc[0:128, tsl], latT[j], r=[f"latT{j}"], w=["lat_loc"])
            dma(self.lat_loc[128:160, tsl], krT[j], r=[f"krT{j}"], w=["lat_loc"])
            for g in range(2):
                ps, pk = self.nextps()
                for k in range(2):
                    mm(ps[:, 0:384], qcnT[:, k, :], Wuq[:, k, g * 384:(g + 1) * 384], k == 0, k == 1, r=["qcnT", "Wuq"], w=[pk])
                pv = ps[:, 0:384].rearrange("p (h d) -> p h d", h=4)
                act(Qr[:, g * 4:(g + 1) * 4, 0:64], pv[:, :, 0:64], AF.Identity, r=[], w=[pk, "Qr"])
                self.rope(pv[:, :, 64:80], pv[:, :, 80:96], self.cos16[:, i, :], self.sin16[:, i, :],
                          Qr[:, g * 4:(g + 1) * 4, 64:80], Qr[:, g * 4:(g + 1) * 4, 80:96], 4, 16, pk, "Qr", rtmp)
            pt, ptk = self.nextps()
            ptb = pt.bitcast(BF16)
            for h in range(8):
                tr(ptb[0:96, h * 128:(h + 1) * 128], Qr[:, h, :], r=["Qr"], w=[ptk])
            op("dve", "tensor_copy", r=[], w=[ptk, f"QT{j}"], out=QT[j], in_=ptb[0:96, :].rearrange("p (h t) -> p h t", h=8))
            dma(self.qT_d[:, :, tsl].rearrange("h p t -> p h t"), QT[j], r=[f"QT{j}"], w=["qT_d"])
            ps, pk = self.nextps()
            for k in range(8):
                mm(ps, hT[:, k, tsl], W1[:, k, C_RQ:C_RV], k == 0, k == 7, r=[hk, "W1"], w=[pk])
            for qk in range(2):
                pv = ps[:, qk * 256:(qk + 1) * 256].rearrange("p (h d) -> p h d", h=4)
                dv = rqk[j][:, qk * 256:(qk + 1) * 256].rearrange("p (h d) -> p h d", h=4)
                cT_, sT_ = (self.cos32q, self.sin32q) if qk == 0 else (self.cos32, self.sin32)
                self.rope(pv[:, :, 0:32], pv[:, :, 32:64], cT_[:, i, :], sT_[:, i, :], dv[:, :, 0:32], dv[:, :, 32:64], 4, 32, pk, f"rqk{j}", rtmp)
            dma(self.rq_d[tsl, :], rqk[j][:, 0:256], r=[f"rqk{j}"], w=["rq_d"])
            dma(self.rk_d[tsl, :], rqk[j][:, 256:512], r=[f"rqk{j}"], w=["rk_d"])
            ps, pk = self.nextps()
            for k in range(8):
                mm(ps, hT[:, k, tsl], W1[:, k, C_RV:C_RG], k == 0, k == 7, r=[hk, "W1"], w=[pk])
            act(rvb[j], ps, AF.Identity, r=[], w=[pk, "rvb0"])
            dma(self.rv_d[tsl, :], rvb[j], r=["rvb0"], w=["rv_d"])
            ps, pk = self.nextps()
            for k in range(8):
                mm(ps, hT[:, k, tsl], W1[:, k, C_RG:C_S5], k == 0, k == 7, r=[hk, "W1"], w=[pk])
            act(rgb[j], ps, AF.Silu, r=[], w=[pk, "rgb0"])
            dma(self.rg_d[tsl, :], rgb[j], r=["rgb0"], w=["rg_d"])
        n = 0
        for b in range(T // TB):
            bsl = slice(b * TB, (b + 1) * TB)
            hks = [f"hT{i}" for i in range(b * TB // 128, (b + 1) * TB // 128)]
            for fc in range(4):
                ps, pk = self.nextps()
                for k in range(8):
                    mm(ps[:, 0:TB], Ws5[:, k, fc * 128:(fc + 1) * 128], hT[:, k, bsl], k == 0, k == 7, r=hks + ["Ws5"], w=[pk])
                j = n % 2; n += 1
                act(ub[j], ps[:, 0:TB], AF.Identity, r=[], w=[pk, f"ub{j}"])
                dma(self.u_loc[fc * 128:(fc + 1) * 128, bsl], ub[j], r=[f"ub{j}"], w=["u_loc"])
        self.allgather(self.lat_all, self.lat_loc, r=["lat_loc"], w=["lat_all"])
        for c in range(4):
            self.allgather(self.u_all[c * 512:(c + 1) * 512, :], self.u_loc[c * 128:(c + 1) * 128, :], r=["u_loc"], w=["u_all"])
        for nm in ("lat_loc", "qT_d", "rq_d", "rk_d", "rv_d", "rg_d", "u_loc", "lat_all"):
            self.dbg(nm, getattr(self, nm), nm)

    def phase_R(self, l):
        T, NT = self.T, self.NT
        op, dma, mm, tr, act = self.op, self.dma, self.mm, self.tr, self.act
        self.phase("R")
        rq = self.sb("rq", [128, NT, 256], BF16); rk = self.sb("rk", [128, NT, 256], BF16)
        rv = self.sb("rv", [128, NT, 512], BF16); sg = self.sb("sg", [128, NT, 512], BF16)
        dma(rq, self.rq_d.rearrange("(n p) f -> p n f", p=128), w=["rq"])
        dma(rk, self.rk_d.rearrange("(n p) f -> p n f", p=128), w=["rk"])
        dma(rv, self.rv_d.rearrange("(n p) f -> p n f", p=128), w=["rv"])
        dma(sg, self.rg_d.rearrange("(n p) f -> p n f", p=128), w=["sg"])
        lgall = self.sb("lgall", [128, 8]); lgsel = self.sb("lgsel", [128, 4])
        dma(lgall, self.ldc_in[l:l + 1, :].partition_broadcast(128), w=["lgall"])
        dma(lgsel[0:64, :], self.ldc_in[l:l + 1, 0:4].partition_broadcast(64), w=["lgsel"])
        dma(lgsel[64:128, :], self.ldc_in[l:l + 1, 4:8].partition_broadcast(64), w=["lgsel"])
        for t_, k_ in ((lgall, "lgall"), (lgsel, "lgsel")):
            act(t_, t_, AF.Exp, r=[k_], w=[k_])
            act(t_, t_, AF.Ln, r=[k_], w=[k_], scale=-1.0, bias=1.0)
        zx = self.sb("zx", [128, 4, 4])
        for kind, (ecol, lo) in enumerate(((0, 0), (1, 4), (2, 0), (3, 4))):
            act(zx[:, kind, :], lgall[:, lo:lo + 4], AF.Exp, r=["lgall", "ek"], w=["zx"], scale=self.ek[:, ecol:ecol + 1])
        cm = self.sb("cm", [128, 4, 128])
        dma(cm, self.cmask_in.rearrange("a p f -> p a f"), w=["cm"])
        DT = self.sb("DT", [128, 4, 128]); dtmp = self.sb("dtmp", [128, 2, 128])
        for h in range(4):
            act(dtmp[:, 0, :], cm[:, 0, :], AF.Exp, r=["cm", "lgall"], w=["dtmp0"], scale=lgall[:, h:h + 1])
            act(dtmp[:, 1, :], cm[:, 1, :], AF.Exp, r=["cm", "lgall"], w=["dtmp1"], scale=lgall[:, 4 + h:5 + h])
            op("dve", "tensor_tensor", r=["dtmp0", "cm"], w=["dtmp0"], out=dtmp[:, 0, :], in0=dtmp[:, 0, :], in1=cm[:, 2, :], op=ALU.mult)
            op("dve", "tensor_tensor", r=["dtmp1", "cm"], w=["dtmp1"], out=dtmp[:, 1, :], in0=dtmp[:, 1, :], in1=cm[:, 3, :], op=ALU.mult)
            op("dve", "tensor_tensor", r=["dtmp0", "dtmp1"], w=["DT"], out=DT[:, h, :], in0=dtmp[:, 0, :], in1=dtmp[:, 1, :], op=ALU.add)
        lg128 = self.sb("lg128", [128, 4]); lgT = self.sb("lgT", [128, 4])
        op("dve", "tensor_scalar", r=["lgsel"], w=["lg128"], out=lg128, in0=lgsel, scalar1=128.0, scalar2=None, op0=ALU.mult)
        op("dve", "tensor_scalar", r=["lgsel"], w=["lgT"], out=lgT, in0=lgsel, scalar1=float(T), scalar2=None, op0=ALU.mult)
        cdec = self.sb("cdec", [128, 4])
        act(cdec, lg128, AF.Exp, r=["lg128"], w=["cdec"])
        coefn = self.sb("coefn", [128, 4, NT]); coefr = self.sb("coefr", [128, 4, 4])
        for h in range(4):
            act(coefn[:, h, :], self.etab, AF.Exp, r=["lg128"], w=["coefn"], scale=lg128[:, h:h + 1])
            act(coefr[:, h, :], self.dist, AF.Exp, r=["lgT"], w=["coefr"], scale=lgT[:, h:h + 1])
        E = self.sb("E", [128, 4, NT, 128])
        kk = [self.sb(f"kk{i}", [128, 128], BF16) for i in range(2)]
        n_ = 0
        for n in range(NT):
            for h in range(4):
                j = n_ % 2; n_ += 1
                kh = rk[:, n, h * 64:(h + 1) * 64]
                op("dve", "tensor_scalar", r=["rk", "zx"], w=[f"kk{j}"], out=kk[j][:, 0:64], in0=kh, scalar1=zx[:, 0, h:h + 1], scalar2=None, op0=ALU.mult)
                op("dve", "tensor_scalar", r=["rk", "zx"], w=[f"kk{j}"], out=kk[j][:, 64:128], in0=kh, scalar1=zx[:, 1, h:h + 1], scalar2=None, op0=ALU.mult)
                ps, pk = self.nextps()
                mm(ps[:, 0:128], kk[j], rv[:, n, h * 128:(h + 1) * 128], True, True, r=[f"kk{j}", "rv"], w=[pk])
                act(E[:, h, n, :], ps[:, 0:128], AF.Identity, r=[], w=[pk, f"E{h}_{n}"])
        for h in range(4):
            for n in range(1, NT):
                op("dve", "scalar_tensor_tensor", r=[f"E{h}_{n - 1}"], w=[f"E{h}_{n}"], out=E[0:64, h, n, :], in0=E[0:64, h, n - 1, :],
                   scalar=cdec[0:64, h:h + 1], in1=E[0:64, h, n, :], op0=ALU.mult, op1=ALU.add)
            for n in range(NT - 2, -1, -1):
                op("dve", "scalar_tensor_tensor", r=[f"E{h}_{n + 1}"], w=[f"E{h}_{n}"], out=E[64:128, h, n, :], in0=E[64:128, h, n + 1, :],
                   scalar=cdec[64:128, h:h + 1], in1=E[64:128, h, n, :], op0=ALU.mult, op1=ALU.add)
        ekeys = [f"E{h}_{n}" for h in range(4) for n in range(NT)]
        dma(self.agg_loc.rearrange("(h p) e -> p h e", p=128)[0:64], E[0:64, :, NT - 1, :], r=ekeys, w=["agg_loc"])
        dma(self.agg_loc.rearrange("(h p) e -> p h e", p=128)[64:128], E[64:128, :, 0, :], r=ekeys, w=["agg_loc"])
        self.allgather(self.agg_all, self.agg_loc, r=["agg_loc"], w=["agg_all"])
        agg = self.sb("agg", [128, 4, 4, 128])
        dma(agg, self.agg_all.rearrange("(r h p) e -> p r h e", r=4, h=4), r=["agg_all"], w=["agg"])
        Cin = self.sb("Cin", [128, 4, 128])
        for h in range(4):
            op("dve", "tensor_scalar", r=["agg", "coefr"], w=[f"Cin{h}"], out=Cin[:, h, :], in0=agg[:, 0, h, :], scalar1=coefr[:, h, 0:1], scalar2=None, op0=ALU.mult)
            for r_ in range(1, 4):
                op("dve", "scalar_tensor_tensor", r=["agg", "coefr"], w=[f"Cin{h}"], out=Cin[:, h, :], in0=agg[:, r_, h, :], scalar=coefr[:, h, r_:r_ + 1],
                   in1=Cin[:, h, :], op0=ALU.mult, op1=ALU.add)
        Pb = self.sb("Pb", [128, 4, NT, 128], BF16)
        for h in range(4):
            for n in range(NT):
                src_f = E[0:64, h, n - 1, :] if n >= 1 else self.zeros_f[0:64, :]
                src_b = E[64:128, h, n + 1, :] if n <= NT - 2 else self.zeros_f[64:128, :]
                op("dve", "scalar_tensor_tensor", r=ekeys + [f"Cin{h}", "coefn"], w=[f"Pb{n}"], out=Pb[0:64, h, n, :], in0=Cin[0:64, h, :],
                   scalar=coefn[0:64, h, n:n + 1], in1=src_f, op0=ALU.mult, op1=ALU.add)
                op("dve", "scalar_tensor_tensor", r=ekeys + [f"Cin{h}", "coefn"], w=[f"Pb{n}"], out=Pb[64:128, h, n, :], in0=Cin[64:128, h, :],
                   scalar=coefn[64:128, h, n:n + 1], in1=src_b, op0=ALU.mult, op1=ALU.add)
        for nm_, ap_, k_ in (("zx", zx, ["zx"]), ("DT", DT, ["DT"]), ("E", E, ekeys), ("Cin", Cin, [f"Cin{h}" for h in range(4)]), ("coefr", coefr, ["coefr"]), ("lgall", lgall, ["lgall"])):
            if nm_ in self.debug:
                o_ = self.nc.dram_tensor("dbg_" + nm_, list(ap_.shape), ap_.dtype, kind="ExternalOutput").ap()
                self.dma(o_, ap_, r=k_)
                self.dbg_out.append("dbg_" + nm_)
        qq = [self.sb(f"qq{i}", [128, 128], BF16) for i in range(2)]
        tqk = [self.sb(f"tqk{i}", [64, 256], BF16) for i in range(2)]
        tqq = [self.sb(f"tqq{i}", [128, 128], BF16) for i in range(2)]
        AT = [self.sb(f"AT{i}", [128, 128], BF16) for i in range(2)]
        gst = self.sb("gst", [128, 4, 6]); gmv = self.sb("gmv", [128, 4, 2]); grs = self.sb("grs", [128, 4, 2])
        yn = self.sb("yn", [128, 512]); orb = self.sb("orb", [128, 512], BF16)
        oT = [self.sb(f"oT{i}", [128, 4, 128], BF16) for i in range(2)]
        n_ = 0

        def heads_part(n):
            nonlocal n_
            yps, ypk = self.ps[6 + n % 2], f"ps{6 + n % 2}"
            for h in range(4):
                j = n_ % 2; n_ += 1
                qh = rq[:, n, h * 64:(h + 1) * 64]
                op("dve", "tensor_scalar", r=["rq", "zx"], w=[f"qq{j}"], out=qq[j][:, 0:64], in0=qh, scalar1=zx[:, 2, h:h + 1], scalar2=None, op0=ALU.mult)
                op("dve", "tensor_scalar", r=["rq", "zx"], w=[f"qq{j}"], out=qq[j][:, 64:128], in0=qh, scalar1=zx[:, 3, h:h + 1], scalar2=None, op0=ALU.mult)
                pt, ptk = self.nextps([0, 1, 2, 3, 4, 5])
                ptb = pt.bitcast(BF16)
                tr(ptb[0:64, 0:128], qh, r=["rq"], w=[ptk])
                tr(ptb[0:64, 128:256], rk[:, n, h * 64:(h + 1) * 64], r=["rk"], w=[ptk])
                tr(ptb[:, 256:384], qq[j], r=[f"qq{j}"], w=[ptk])
                op("dve", "tensor_copy", r=[], w=[ptk, f"tqk{j}"], out=tqk[j], in_=ptb[0:64, 0:256])
                act(tqq[j], ptb[:, 256:384], AF.Identity, r=[], w=[ptk, f"tqq{j}"])
                sps, spk = self.nextps([0, 1, 2, 3, 4, 5])
                mm(sps[:, 0:128], tqk[j][:, 128:256], tqk[j][:, 0:128], True, True, r=[f"tqk{j}"], w=[spk])
                op("dve", "tensor_tensor", r=["DT"], w=[spk, f"AT{j}"], out=AT[j], in0=sps[:, 0:128], in1=DT[:, h, :], op=ALU.mult)
                ysl = yps[:, h * 128:(h + 1) * 128]
                mm(ysl, AT[j], rv[:, n, h * 128:(h + 1) * 128], True, False, r=[f"AT{j}", "rv"], w=[ypk])
                mm(ysl, tqq[j], Pb[:, h, n, :], False, True, r=[f"tqq{j}", f"Pb{n}"], w=[ypk])
            return yps, ypk

        def epilogue(n, yps, ypk):
            for h in range(4):
                op("dve", "bn_stats", r=[], w=[ypk, "gst"], out=gst[:, h, :], in_=yps[:, h * 128:(h + 1) * 128])
            for h in range(4):
                op("dve", "bn_aggr", r=["gst"], w=["gmv"], out=gmv[:, h, :], in_=gst[:, h, :])
            act(grs[:, :, 0], gmv[:, :, 1], AF.Ln, r=["gmv"], w=["grs"], bias=self.constc[:, 3:4])
            act(grs[:, :, 0], grs[:, :, 0], AF.Exp, r=["grs"], w=["grs"], scale=-0.5)
            op("dve", "scalar_tensor_tensor", r=["gmv", "grs"], w=["grs2"], out=grs[:, :, 1], in0=gmv[:, :, 0], scalar=-1.0, in1=grs[:, :, 0],
               op0=ALU.mult, op1=ALU.mult)
            for h in range(4):
                act(yn[:, h * 128:(h + 1) * 128], yps[:, h * 128:(h + 1) * 128], AF.Identity, r=["grs", "grs2"], w=[ypk, "yn"],
                    scale=grs[:, h, 0:1], bias=grs[:, h, 1:2])
            op("dve", "tensor_tensor", r=["yn", "sg"], w=["orb"], out=orb, in0=yn, in1=sg[:, n, :], op=ALU.mult)
            pt, ptk = self.nextps([0, 1, 2, 3, 4, 5])
            ptb = pt.bitcast(BF16)
            for h in range(4):
                tr(ptb[:, h * 128:(h + 1) * 128], orb[:, h * 128:(h + 1) * 128], r=["orb"], w=[ptk])
            jo = n % 2
            op("dve", "tensor_copy", r=[], w=[ptk, f"oT{jo}"], out=oT[jo], in_=ptb[:, 0:512].rearrange("p (h t) -> p h t", h=4))
            dma(self.ob_d[1][:, n * 128:(n + 1) * 128].rearrange("(h p) t -> p h t", p=128), oT[jo], r=[f"oT{jo}"], w=["ob1_d"])

        prev = None
        for n in range(NT):
            cur = (n,) + heads_part(n)
            if prev is not None:
                epilogue(*prev)
            prev = cur
        epilogue(*prev)
        self.dbg("ob1_d", self.ob_d[1], "ob1_d")

    def angle_tables(self, u, ukey, shape, tag, sinT, cosT, skey, ckey, negsin=False):
        op, act = self.op, self.act
        ki = self.cached(f"at_ki{tag}", shape, I32); kf = self.cached(f"at_kf{tag}", shape); ab = self.cached(f"at_ab{tag}", shape)
        k = f"at{tag}"
        op("dve", "tensor_copy", r=[ukey], w=[k + "ki"], out=ki, in_=u)
        op("dve", "tensor_copy", r=[k + "ki"], w=[k + "kf"], out=kf, in_=ki)
        op("dve", "tensor_tensor", r=[ukey, k + "kf"], w=[ukey], out=u, in0=u, in1=kf, op=ALU.subtract)
        act(sinT, u, AF.Sin, r=[ukey], w=[skey], scale=-TWO_PI if negsin else TWO_PI)
        act(ab, u, AF.Abs, r=[ukey], w=[k + "ab"])
        act(cosT, ab, AF.Sin, r=[k + "ab"], w=[ckey], scale=-TWO_PI, bias=self.halfpi[0:shape[0], :])

    def cached(self, name, shape, dt=F32):
        key = (self.fw.nbar, name)
        if key not in self._cache:
            self._cache[key] = self.sb(name, shape, dt)
        return self._cache[key]

    def phase_S(self, l):
        S, T = self.S, self.T
        op, dma, mm, act = self.op, self.dma, self.mm, self.act
        self.phase("S")
        SBk = min(512, S)
        NB = S // SBk
        TB = min(2048, T)
        uT = self.sb("uT", [128, S], BF16)
        ys = self.sb("ys", [128, S])
        io = self.sb("io", [128, 512])
        lamS = self.sb("lamS", [128, 2, 3, 4])
        stepS = self.sb("stepS", [128, 2, 4]); magS = self.sb("magS", [128, 2, 4]); thS = self.sb("thS", [128, 2, 4])
        Bb = self.sb("Bb", [128, 2, 2, 512], BF16)
        Cb = self.sb("Cb", [128, 2, 2, 512], BF16)
        wc = self.sb("wc", [128, 2, 4, 2])
        mark = self.arena_off
        ust = [self.sb(f"ust{i}", [128, TB]) for i in range(3)]
        uacc = self.sb("uacc", [128, TB])
        n_ = 0
        for q in range(4):
            for tb in range(T // TB):
                for c in range(4):
                    j = n_ % 3; n_ += 1
                    dma(ust[j], self.u_all[(c * 4 + q) * 128:(c * 4 + q + 1) * 128, tb * TB:(tb + 1) * TB], w=[f"ust{j}"])
                    dst = uT[:, q * T + tb * TB:q * T + (tb + 1) * TB] if c == 3 else uacc
                    dk = "uT" if c == 3 else "uacc"
                    if c == 0:
                        op("dve", "tensor_scalar", r=[f"ust{j}"], w=["uacc"], out=uacc, in0=ust[j], scalar1=self.sel[:, 0:1], scalar2=None, op0=ALU.mult)
                    else:
                        op("dve", "scalar_tensor_tensor", r=[f"ust{j}", "uacc"], w=[dk], out=dst, in0=ust[j], scalar=self.sel[:, c:c + 1], in1=uacc,
                           op0=ALU.mult, op1=ALU.add)
        dma(io, self.io_in, w=["io"])
        for d in range(2):
            dma(lamS[:, d, 0, :], self.lamS_re[l, d], w=["lamS"])
            dma(lamS[:, d, 1, :], self.lamS_im[l, d], w=["lamS"])
            dma(lamS[:, d, 2, :], self.lsS[l, d], w=["lamS"])
        act(stepS, lamS[:, :, 2, :], AF.Exp, r=["lamS"], w=["stepS"])
        op("dve", "tensor_tensor", r=["lamS", "stepS"], w=["magS"], out=magS, in0=lamS[:, :, 0, :], in1=stepS, op=ALU.mult)
        act(magS, magS, AF.Exp, r=["magS"], w=["magS"])
        op("dve", "tensor_tensor", r=["lamS", "stepS"], w=["thS"], out=thS, in0=lamS[:, :, 1, :], in1=stepS, op=ALU.mult)
        op("dve", "tensor_scalar", r=["thS"], w=["thS"], out=thS, in0=thS, scalar1=1.0 / TWO_PI, scalar2=None, op0=ALU.mult)
        for d in range(2):
            self.subphase(mark)
            lB = self.sb("lB", [128, 3, 512])
            dma(lB[:, 0, :], self.lamB_re[l, d:d + 1, :].partition_broadcast(128), w=["lB"])
            dma(lB[:, 1, :], self.lamB_im[l, d:d + 1, :].partition_broadcast(128), w=["lB"])
            dma(lB[:, 2, :], self.lsB[l, d:d + 1, :].partition_broadcast(128), w=["lB"])
            sh = [128, 512]
            stB = self.sb("stB", sh); mgB = self.sb("mgB", sh); tu = self.sb("tu", sh); sn = self.sb("sn", sh); cs = self.sb("cs", sh)
            are = self.sb("are", sh); aim = self.sb("aim", sh); den = self.sb("den", sh); fre = self.sb("fre", sh); fim = self.sb("fim", sh)
            t1 = self.sb("t1s", sh)
            lre, lim = lB[:, 0, :], lB[:, 1, :]
            act(stB, lB[:, 2, :], AF.Exp, r=["lB"], w=["stB"])
            op("dve", "tensor_tensor", r=["lB", "stB"], w=["mgB"], out=mgB, in0=lre, in1=stB, op=ALU.mult)
            act(mgB, mgB, AF.Exp, r=["mgB"], w=["mgB"])
            op("dve", "tensor_tensor", r=["lB", "stB"], w=["tu"], out=tu, in0=lim, in1=stB, op=ALU.mult)
            op("dve", "tensor_scalar", r=["tu"], w=["tu"], out=tu, in0=tu, scalar1=1.0 / TWO_PI, scalar2=None, op0=ALU.mult)
            self.angle_tables(tu, "tu", sh, "B", sn, cs, "sn", "cs")
            op("dve", "tensor_tensor", r=["mgB", "cs"], w=["are"], out=are, in0=mgB, in1=cs, op=ALU.mult)
            op("dve", "tensor_tensor", r=["mgB", "sn"], w=["aim"], out=aim, in0=mgB, in1=sn, op=ALU.mult)
            op("dve", "tensor_scalar", r=["are"], w=["are"], out=are, in0=are, scalar1=-1.0, scalar2=None, op0=ALU.add)
            op("dve", "tensor_tensor", r=["lB"], w=["den"], out=den, in0=lre, in1=lre, op=ALU.mult)
            op("dve", "tensor_tensor", r=["lB"], w=["t1s"], out=t1, in0=lim, in1=lim, op=ALU.mult)
            op("dve", "tensor_tensor", r=["den", "t1s"], w=["den"], out=den, in0=den, in1=t1, op=ALU.add)
            op("dve", "reciprocal", r=["den"], w=["den"], out=den, in_=den)
            op("dve", "tensor_tensor", r=["are", "lB"], w=["fre"], out=fre, in0=are, in1=lre, op=ALU.mult)
            op("dve", "tensor_tensor", r=["aim", "lB"], w=["t1s"], out=t1, in0=aim, in1=lim, op=ALU.mult)
            op("dve", "tensor_tensor", r=["fre", "t1s"], w=["fre"], out=fre, in0=fre, in1=t1, op=ALU.add)
            op("dve", "tensor_tensor", r=["fre", "den"], w=["fre"], out=fre, in0=fre, in1=den, op=ALU.mult)
            op("dve", "tensor_tensor", r=["aim", "lB"], w=["fim"], out=fim, in0=aim, in1=lre, op=ALU.mult)
            op("dve", "tensor_tensor", r=["are", "lB"], w=["t1s"], out=t1, in0=are, in1=lim, op=ALU.mult)
            op("dve", "tensor_tensor", r=["fim", "t1s"], w=["fim"], out=fim, in0=fim, in1=t1, op=ALU.subtract)
            op("dve", "tensor_tensor", r=["fim", "den"], w=["fim"], out=fim, in0=fim, in1=den, op=ALU.mult)
            braw = self.sb("braw", [128, 2, 512]); t2 = self.sb("t2s", [128, 512]); t3 = self.sb("t3s", [128, 512])
            dma(braw[:, 0, :], self.Bblk_re[l, d], w=["braw"])
            dma(braw[:, 1, :], self.Bblk_im[l, d], w=["braw"])
            op("dve", "tensor_tensor", r=["braw", "fre"], w=["t2s"], out=t2, in0=braw[:, 0, :], in1=fre, op=ALU.mult)
            op("dve", "tensor_tensor", r=["braw", "fim"], w=["t3s"], out=t3, in0=braw[:, 1, :], in1=fim, op=ALU.mult)
            op("dve", "tensor_tensor", r=["t2s", "t3s"], w=["Bb"], out=Bb[:, d, 0, :], in0=t2, in1=t3, op=ALU.subtract)
            op("dve", "tensor_tensor", r=["braw", "fre"], w=["t2s"], out=t2, in0=braw[:, 1, :], in1=fre, op=ALU.mult)
            op("dve", "tensor_tensor", r=["braw", "fim"], w=["t3s"], out=t3, in0=braw[:, 0, :], in1=fim, op=ALU.mult)
            op("dve", "tensor_tensor", r=["t2s", "t3s"], w=["Bb"], out=Bb[:, d, 1, :], in0=t2, in1=t3, op=ALU.add)
            dma(braw[:, 0, :], self.Cblk_re[l, d], r=["Bb"], w=["braw"])
            dma(braw[:, 1, :], self.Cblk_im[l, d], r=["Bb"], w=["braw"])
            op("pool", "tensor_copy", r=["braw"], w=["Cb"], out=Cb[:, d, :, :], in_=braw)
        self.subphase(mark)
        op("dve", "memset", w=["wc"], ap=wc, constant=0.0)
        W = [128, SBk]
        tun = [self.sb(f"tun{i}", W) for i in range(2)]
        nsin = [self.sb(f"nsin{i}", W) for i in range(2)]; cosb = [self.sb(f"cosb{i}", W) for i in range(2)]
        ta = self.sb("ta", W); tb_ = self.sb("tb", W); tc = self.sb("tc", W); td = self.sb("td", W)
        vre = self.sb("vre", W); vim = self.sb("vim", W); wre = self.sb("wre", W); wim = self.sb("wim", W)
        xre = [self.sb(f"xre{i}", W, BF16) for i in range(2)]; nxi = [self.sb(f"nxi{i}", W, BF16) for i in range(2)]
        its = [(d, sbi, st) for d in range(2) for sbi in range(NB) for st in range(4)]

        def front(i):
            d, sbi, st = its[i]
            j = i % 2
            tau0 = sbi * SBk
            t0 = tau0 if d == 0 else S - tau0 - SBk
            op("dve", "tensor_scalar", r=["io", "thS"], w=[f"tun{j}"], out=tun[j], in0=io[:, 0:SBk], scalar1=float(tau0), scalar2=thS[:, d, st:st + 1],
               op0=ALU.add, op1=ALU.mult)
            self.angle_tables(tun[j], f"tun{j}", W, "M", nsin[j], cosb[j], f"nsin{j}", f"cosb{j}", negsin=True)
            bi = (2 * i) % 6
            pre, prk = self.ps[bi], f"ps{bi}"
            pim, pik = self.ps[bi + 1], f"ps{bi + 1}"
            mm(pre[:, 0:SBk], Bb[:, d, 0, st * 128:(st + 1) * 128], uT[:, t0:t0 + SBk], True, True, r=["Bb", "uT"], w=[prk])
            mm(pim[:, 0:SBk], Bb[:, d, 1, st * 128:(st + 1) * 128], uT[:, t0:t0 + SBk], True, True, r=["Bb", "uT"], w=[pik])
            return pre, prk, pim, pik

        def back(i, pre, prk, pim, pik, aps, apk):
            d, sbi, st = its[i]
            j = i % 2
            bre = pre[:, 0:SBk] if d == 0 else rev_ap(pre[:, 0:SBk], SBk)
            bim = pim[:, 0:SBk] if d == 0 else rev_ap(pim[:, 0:SBk], SBk)
            ns, cb = nsin[j], cosb[j]
            op("dve", "tensor_tensor", r=[f"cosb{j}"], w=[prk, "ta"], out=ta, in0=bre, in1=cb, op=ALU.mult)
            op("dve", "tensor_tensor", r=[f"nsin{j}"], w=[pik, "tb"], out=tb_, in0=bim, in1=ns, op=ALU.mult)
            op("dve", "tensor_tensor", r=["ta", "tb"], w=["vre"], out=vre, in0=ta, in1=tb_, op=ALU.subtract)
            op("dve", "tensor_tensor", r=[f"cosb{j}"], w=[pik, "tc"], out=tc, in0=bim, in1=cb, op=ALU.mult)
            op("dve", "tensor_tensor", r=[f"nsin{j}"], w=[prk, "td"], out=td, in0=bre, in1=ns, op=ALU.mult)
            op("dve", "tensor_tensor", r=["tc", "td"], w=["vim"], out=vim, in0=tc, in1=td, op=ALU.add)
            mg = magS[:, d, st:st + 1].to_broadcast(W)
            op("dve", "tensor_tensor_scan", r=["vre", "magS", "wc"], w=["wre"], out=wre, data0=mg, data1=vre, initial=wc[:, d, st, 0:1], op0=ALU.mult, op1=ALU.add)
            op("dve", "tensor_tensor_scan", r=["vim", "magS", "wc"], w=["wim"], out=wim, data0=mg, data1=vim, initial=wc[:, d, st, 1:2], op0=ALU.mult, op1=ALU.add)
            act(wc[:, d, st, 0:1], wre[:, SBk - 1:SBk], AF.Identity, r=["wre"], w=["wc"])
            act(wc[:, d, st, 1:2], wim[:, SBk - 1:SBk], AF.Identity, r=["wim"], w=["wc"])
            op("dve", "tensor_tensor", r=["wre", f"cosb{j}"], w=["ta"], out=ta, in0=wre, in1=cb, op=ALU.mult)
            op("dve", "tensor_tensor", r=["wim", f"nsin{j}"], w=["tb"], out=tb_, in0=wim, in1=ns, op=ALU.mult)
            op("dve", "tensor_tensor", r=["ta", "tb"], w=[f"xre{j}"], out=xre[j], in0=ta, in1=tb_, op=ALU.add)
            op("dve", "tensor_tensor", r=["wre", f"nsin{j}"], w=["tc"], out=tc, in0=wre, in1=ns, op=ALU.mult)
            op("dve", "tensor_tensor", r=["wim", f"cosb{j}"], w=["td"], out=td, in0=wim, in1=cb, op=ALU.mult)
            op("dve", "tensor_tensor", r=["tc", "td"], w=[f"nxi{j}"], out=nxi[j], in0=tc, in1=td, op=ALU.subtract)
            mm(aps[:, 0:SBk], Cb[:, d, 0, st * 128:(st + 1) * 128], xre[j], st == 0, False, r=["Cb", f"xre{j}"], w=[apk])
            mm(aps[:, 0:SBk], Cb[:, d, 1, st * 128:(st + 1) * 128], nxi[j], False, st == 3, r=["Cb", f"nxi{j}"], w=[apk])
            if st == 3:
                tau0 = sbi * SBk
                t0 = tau0 if d == 0 else S - tau0 - SBk
                if d == 0:
                    act(ys[:, t0:t0 + SBk], aps[:, 0:SBk], AF.Identity, r=[], w=[apk, "ys"])
                else:
                    yr = rev_ap(ys[:, t0:t0 + SBk], SBk)
                    op("dve", "tensor_tensor", r=[], w=[apk, "ys"], out=yr, in0=aps[:, 0:SBk], in1=yr, op=ALU.add)

        nxt = front(0)
        aps = apk = None
        for i in range(len(its)):
            cur = nxt
            if i + 1 < len(its):
                nxt = front(i + 1)
            if its[i][2] == 0:
                ai = 6 + (i // 4) % 2
                aps, apk = self.ps[ai], f"ps{ai}"
            back(i, *cur, aps, apk)
        for q in range(4):
            dma(self.ys_loc[q * 128:(q + 1) * 128, :], ys[:, q * T:(q + 1) * T], r=["ys"], w=["ys_loc"])
        for q in range(4):
            self.allgather(self.ys_all[q * 512:(q + 1) * 512, :], self.ys_loc[q * 128:(q + 1) * 128, :], r=["ys_loc"], w=["ys_all"])
        self.dbg("ys_loc", self.ys_loc, "ys_loc")

    def phase_AT(self, l):
        S, T = self.S, self.T
        op, dma, mm, act = self.op, self.dma, self.mm, self.act
        self.phase("AT")
        NKT = S // 128
        QB = min(512, T)
        KB = min(512, S)
        kvnT = self.sb("kvnT", [128, S], BF16); krT = self.sb("krT", [32, S], BF16)
        for r_ in range(4):
            dma(kvnT[:, r_ * T:(r_ + 1) * T], self.lat_all[r_ * 160:r_ * 160 + 128, :], r=["lat_all"], w=["kvnT"])
            dma(krT[:, r_ * T:(r_ + 1) * T], self.lat_all[r_ * 160 + 128:r_ * 160 + 160, :], r=["lat_all"], w=["krT"])
        kvg = self.sb("kvg", [128, 1]); wst = self.sb("wukv_st", [128, 1024])
        Wk = self.sb("Wk", [128, 8, 96], BF16); Wv = self.sb("Wv", [128, 8, 64], BF16); Sel = self.sb("Sel", [32, 96], BF16)
        dma(kvg, self.kvnorm_in[l], w=["kvg"])
        dma(wst, self.w_ukv[l], w=["wukv_st"])
        op("dve", "memset", w=["Wk"], ap=Wk, constant=0.0)
        op("dve", "memset", w=["Sel"], ap=Sel, constant=0.0)
        wv_ = wst.rearrange("p (h d) -> p h d", h=8)
        op("dve", "tensor_scalar", r=["wukv_st", "kvg", "Wk"], w=["Wk"], out=Wk[:, :, 0:64], in0=wv_[:, :, 0:64], scalar1=kvg[:, 0:1], scalar2=None, op0=ALU.mult)
        op("dve", "tensor_scalar", r=["wukv_st", "kvg"], w=["Wv"], out=Wv, in0=wv_[:, :, 64:128], scalar1=kvg[:, 0:1], scalar2=None, op0=ALU.mult)
        op("dve", "tensor_copy", r=["Sel", "ident"], w=["Sel"], out=Sel[:, 64:96], in_=self.ident[0:32, 0:32])
        KT = [self.sb(f"KT{i}", [96, S], BF16) for i in range(2)]
        Vp = [self.sb(f"Vp{i}", [128, NKT, 65], BF16) for i in range(2)]
        QTh = [self.sb(f"QTh{i}", [96, T], BF16) for i in range(2)]
        pT = [self.sb(f"pT{i}", [128, QB], BF16) for i in range(3)]
        rec = self.sb("rec", [128, QB]); Rb = self.sb("Rb", [64, QB])
        oh = [self.sb(f"oh{i}", [64, QB], BF16) for i in range(2)]
        for i in range(2):
            op("dve", "memset", w=[f"Vp{i}"], ap=Vp[i], constant=1.0)
        scale = 96.0 ** -0.5
        np_ = 0; no_ = 0
        for h in range(NH):
            hb = h % 2
            dma(QTh[hb], self.qT_d[h], r=["qT_d"], w=[f"QTh{hb}"])
            for kb in range(S // KB):
                ksl = slice(kb * KB, (kb + 1) * KB)
                ps, pk = self.nextps([0, 1])
                mm(ps[0:96, 0:KB], Wk[:, h, :], kvnT[:, ksl], True, False, r=["Wk", "kvnT"], w=[pk])
                mm(ps[0:96, 0:KB], Sel, krT[:, ksl], False, True, r=["Sel", "krT"], w=[pk])
                op("dve", "tensor_copy", r=[], w=[pk, f"KT{hb}"], out=KT[hb][:, ksl], in_=ps[0:96, 0:KB])
            for g in range(0, NKT, 8):
                ng = min(8, NKT - g)
                ps, pk = self.nextps([0, 1])
                for t in range(ng):
                    mm(ps[:, t * 64:(t + 1) * 64], kvnT[:, (g + t) * 128:(g + t + 1) * 128], Wv[:, h, :], True, True, r=["kvnT", "Wv"], w=[pk])
                act(Vp[hb][:, g:g + ng, 0:64], ps[:, 0:ng * 64].rearrange("p (t d) -> p t d", t=ng), AF.Identity, r=[], w=[pk, f"Vp{hb}"])
            for qb in range(T // QB):
                qsl = slice(qb * QB, (qb + 1) * QB)
                acc, ak = self.nextps([6, 7])
                LOOK = 2

                def issue_s(kt_):
                    sps_, spk_ = self.nextps([2, 3, 4])
                    mm(sps_[:, 0:QB], KT[hb][:, kt_ * 128:(kt_ + 1) * 128], QTh[hb][:, qsl], True, True, r=[f"KT{hb}", f"QTh{hb}"], w=[spk_])
                    return sps_, spk_
                pend = [issue_s(k_) for k_ in range(min(LOOK, NKT))]
                for kt in range(NKT):
                    sps, spk = pend.pop(0)
                    j = np_ % 3; np_ += 1
                    act(pT[j], sps[:, 0:QB], AF.Exp, r=[], w=[spk, f"pT{j}"], scale=scale)
                    if kt + LOOK < NKT:
                        pend.append(issue_s(kt + LOOK))
                    mm(acc[0:65, 0:QB], Vp[hb][:, kt, :], pT[j], kt == 0, kt == NKT - 1, r=[f"Vp{hb}", f"pT{j}"], w=[ak])
                op("dve", "reciprocal", r=[], w=[ak, "rec"], out=rec[64:65, :], in_=acc[64:65, 0:QB])
                rp, rpk = self.nextps([5])
                mm(rp[0:64, 0:QB], self.ones_f[64:65, 0:64], rec[64:65, :], True, True, r=["rec", "ones_f"], w=[rpk])
                act(Rb, rp[0:64, 0:QB], AF.Identity, r=[], w=[rpk, "Rb"])
                jo = no_ % 2; no_ += 1
                op("dve", "tensor_tensor", r=["Rb"], w=[ak, f"oh{jo}"], out=oh[jo], in0=acc[0:64, 0:QB], in1=Rb, op=ALU.mult)
                dma(self.ob_d[0][h * 64:(h + 1) * 64, qsl], oh[jo], r=[f"oh{jo}"], w=["ob0_d"])
        self.dbg("ob0_d", self.ob_d[0], "ob0_d")

    def phase_M0(self, l):
        S, T = self.S, self.T
        op, dma, mm, act = self.op, self.dma, self.mm, self.act
        self.phase("M0")
        TB = min(512, T)
        Wg = self.sb("Wglu", [128, 4, 512], BF16)
        self.mk_wstage(512, n=2)
        for k in range(4):
            self.wload(Wg[:, k, :], self.w_glu[l, k * 128:(k + 1) * 128, :], "Wglu")
        dsk = self.sb("dsk", [128, 4])
        dma(dsk, self.dskip_in[l], w=["dsk"])
        cand = [self.sb(f"cand{i}", [128, TB]) for i in range(3)]
        ub = [self.sb(f"ub{i}", [128, TB]) for i in range(2)]
        yacc = self.sb("yacc", [128, TB]); y = self.sb("yM0", [128, 4, TB]); t1 = self.sb("t1m", [128, TB]); t2 = self.sb("t2m", [128, TB])
        yg = self.sb("yg", [128, 4, TB]); ygb = self.sb("ygb", [128, 4, TB], BF16); sgm = self.sb("sgm", [128, TB])
        ob = [self.sb(f"obs{i}", [128, TB], BF16) for i in range(2)]
        n_ = 0; nu = 0; no_ = 0
        for tb in range(T // TB):
            bsl = slice(tb * TB, (tb + 1) * TB)
            for fc in range(4):
                for q in range(4):
                    j = n_ % 3; n_ += 1
                    dma(cand[j], self.ys_all[(q * 4 + fc) * 128:(q * 4 + fc + 1) * 128, tb * TB:(tb + 1) * TB], r=["ys_all"], w=[f"cand{j}"])
                    if q == 0:
                        op("dve", "tensor_scalar", r=[f"cand{j}"], w=["yacc"], out=yacc, in0=cand[j], scalar1=self.sel[:, 0:1], scalar2=None, op0=ALU.mult)
                    else:
                        op("dve", "scalar_tensor_tensor", r=[f"cand{j}", "yacc"], w=["yacc"], out=yacc, in0=cand[j], scalar=self.sel[:, q:q + 1], in1=yacc,
                           op0=ALU.mult, op1=ALU.add)
                ju = nu % 2; nu += 1
                dma(ub[ju], self.u_loc[fc * 128:(fc + 1) * 128, bsl], r=["u_loc"], w=[f"ub{ju}"])
                yk = f"y{fc}"
                op("dve", "scalar_tensor_tensor", r=[f"ub{ju}", "yacc", "dsk"], w=[yk], out=y[:, fc, :], in0=ub[ju], scalar=dsk[:, fc:fc + 1], in1=yacc,
                   op0=ALU.mult, op1=ALU.add)
                op("dve", "tensor_tensor", r=[yk], w=["t1m"], out=t1, in0=y[:, fc, :], in1=y[:, fc, :], op=ALU.mult)
                op("dve", "tensor_scalar", r=["t1m"], w=["t1m"], out=t1, in0=t1, scalar1=0.044715, scalar2=1.0, op0=ALU.mult, op1=ALU.add)
                op("dve", "tensor_tensor", r=["t1m", yk], w=["t2m"], out=t2, in0=t1, in1=y[:, fc, :], op=ALU.mult)
                act(t2, t2, AF.Sigmoid, r=["t2m"], w=["t2m"], scale=2.0 * math.sqrt(2.0 / math.pi))
                op("dve", "tensor_tensor", r=["t2m", yk], w=[f"yg{fc}"], out=yg[:, fc, :], in0=t2, in1=y[:, fc, :], op=ALU.mult)
                act(ygb[:, fc, :], yg[:, fc, :], AF.Identity, r=[f"yg{fc}"], w=[f"ygb{fc}"])
            for fo in range(4):
                ps, pk = self.nextps()
                for k in range(4):
                    mm(ps[:, 0:TB], Wg[:, k, fo * 128:(fo + 1) * 128], ygb[:, k, :], k == 0, k == 3, r=["Wglu"] + [f"ygb{k}"], w=[pk])
                act(sgm, ps[:, 0:TB], AF.Sigmoid, r=[], w=[pk, "sgm"])
                jo = no_ % 2; no_ += 1
                op("dve", "tensor_tensor", r=["sgm", f"yg{fo}"], w=[f"obs{jo}"], out=ob[jo], in0=sgm, in1=yg[:, fo, :], op=ALU.mult)
                dma(self.ob_d[2][fo * 128:(fo + 1) * 128, bsl], ob[jo], r=[f"obs{jo}"], w=["ob2_d"])
        self.dbg("ob2_d", self.ob_d[2], "ob2_d")

    def phase_M1(self, l):
        T = self.T
        op, dma, mm, act = self.op, self.dma, self.mm, self.act
        self.phase("M1")
        TB = min(512, T)
        hT = self.sb("hT", [128, 8, T], BF16)
        self.mk_wstage(2304, n=3)
        self.phase_A(l, 0, hT, 0)
        hks = [f"hT{i}" for i in range(self.NT)]
        obT = [self.sb(f"obT{i}", [128, 4, TB], BF16) for i in range(3)]
        nob = 0
        Wg = [self.sb(f"Wg{i}", [128, 3, 8, 128], BF16) for i in range(2)]
        Wb = [self.sb(f"Wb{i}", [128, 3, 4, 128], BF16) for i in range(2)]
        sg = [self.sb(f"sgt{i}", [128, TB]) for i in range(2)]
        m = self.sb("macc", [128, TB]); tm = self.sb("tmM", [128, TB])
        mo = [self.sb(f"mo{i}", [128, TB], BF16) for i in range(2)]
        ns = 0; no_ = 0
        for f in range(8):
            jw = f % 2
            for b in range(3):
                c0 = C_G + b * D + f * 128
                self.wload(Wg[jw][:, b, :, :], self.w_in[l, :, c0:c0 + 128].rearrange("(k p) n -> p k n", p=128), f"Wg{jw}")
                self.wload(Wb[jw][:, b, :, :], self.w_br[b][l, :, f * 128:(f + 1) * 128].rearrange("(k p) n -> p k n", p=128), f"Wb{jw}")
            for tb in range(T // TB):
                bsl = slice(tb * TB, (tb + 1) * TB)
                for b in range(3):
                    gp, gk = self.nextps()
                    for k in range(8):
                        mm(gp[:, 0:TB], Wg[jw][:, b, k, :], hT[:, k, bsl], k == 0, k == 7, r=[f"Wg{jw}"] + hks, w=[gk])
                    js = ns % 2; ns += 1
                    act(sg[js], gp[:, 0:TB], AF.Sigmoid, r=[], w=[gk, f"sgt{js}"])
                    jb = nob % 3; nob += 1
                    dma(obT[jb], self.ob_d[b][:, bsl].rearrange("(k p) t -> p k t", p=128), r=[f"ob{b}_d"], w=[f"obT{jb}"])
                    bp, bk = self.nextps()
                    for k in range(4):
                        mm(bp[:, 0:TB], Wb[jw][:, b, k, :], obT[jb][:, k, :], k == 0, k == 3, r=[f"Wb{jw}", f"obT{jb}"], w=[bk])
                    if b == 0:
                        op("dve", "tensor_tensor", r=[f"sgt{js}"], w=[bk, "macc"], out=m, in0=bp[:, 0:TB], in1=sg[js], op=ALU.mult)
                    else:
                        op("dve", "tensor_tensor", r=[f"sgt{js}"], w=[bk, "tmM"], out=tm, in0=bp[:, 0:TB], in1=sg[js], op=ALU.mult)
                        if b == 1:
                            op("dve", "tensor_tensor", r=["tmM", "macc"], w=["macc"], out=m, in0=m, in1=tm, op=ALU.add)
                        else:
                            jo = no_ % 2; no_ += 1
                            op("dve", "tensor_tensor", r=["tmM", "macc"], w=[f"mo{jo}"], out=mo[jo], in0=m, in1=tm, op=ALU.add)
                            dma(self.mT_d[f * 128:(f + 1) * 128, bsl], mo[jo], r=[f"mo{jo}"], w=["mT_d"])
        self.dbg("mT_d", self.mT_d, "mT_d")

    def phase_M2(self, l):
        T, NT = self.T, self.NT
        op, dma, mm, act = self.op, self.dma, self.mm, self.act
        self.phase("M2")
        self.mk_wstage(2304, n=2)
        gB = self.sb("gateB", [128, D])
        self.modvec(l, 2, gB, "gateB")
        Wo = self.sb("Wo", [128, 8, D], BF16)
        for k in range(8):
            self.wload(Wo[:, k, :], self.w_o[l, k * 128:(k + 1) * 128, :], "Wo", mul=gB, mul_key="gateB")
        mT = self.sb("mT", [128, 8, T], BF16)
        dma(mT, self.mT_d.rearrange("(k p) t -> p k t", p=128), r=["mT_d"], w=["mT"])
        for i in range(NT):
            for half in range(2):
                ps, pk = self.nextps()
                for k in range(8):
                    mm(ps, mT[:, k, i * 128:(i + 1) * 128], Wo[:, k, half * 512:(half + 1) * 512], k == 0, k == 7, r=["mT", "Wo"], w=[pk])
                xs = self.x_res[:, i, half * 512:(half + 1) * 512]
                op("dve", "scalar_tensor_tensor", r=[], w=[pk, f"x{i}"], out=xs, in0=xs, scalar=ALPHA, in1=ps, op0=ALU.mult, op1=ALU.add)
        self.post_norm(l, 0)

    def post_norm(self, l, which):
        NT = self.NT
        op, dma, act = self.op, self.dma, self.act
        gB = self.sb("lngB", [128, D]); bB = self.sb("lnbB", [128, D])
        dma(gB, self.ln_in[l, 2 * which:2 * which + 1, :].partition_broadcast(128), w=["lngB"])
        dma(bB, self.ln_in[l, 2 * which + 1:2 * which + 2, :].partition_broadcast(128), w=["lnbB"])
        lns = [self.mk_lnscr(f"p{i}") for i in range(2)]
        xn = [self.sb(f"pxn{i}", [128, D]) for i in range(2)]
        for i in range(NT):
            j = i % 2
            xt = self.x_res[:, i, :]
            rs, rk = self.ln_stats(xt, f"x{i}", 128, lns[j])
            act(xn[j], xt, AF.Identity, r=[f"x{i}"] + rk, w=[f"pxn{j}"], scale=rs[:, 0:1], bias=rs[:, 1:2])
            op("dve", "tensor_tensor", r=[f"pxn{j}", "lngB"], w=[f"pxn{j}"], out=xn[j], in0=xn[j], in1=gB, op=ALU.mult)
            op("dve", "tensor_tensor", r=[f"pxn{j}", "lnbB"], w=[f"x{i}"], out=xt, in0=xn[j], in1=bB, op=ALU.add)

    def phase_F(self, l, last):
        T, NT = self.T, self.NT
        op, dma, mm, act, tr = self.op, self.dma, self.mm, self.act, self.tr
        self.phase("H")
        dma(self.halo_loc[0:1, :], self.x_res[0:1, 0, :], w=["halo_loc"])
        dma(self.halo_loc[1:2, :], self.x_res[127:128, NT - 1, :], w=["halo_loc"])
        self.allgather(self.halo_all, self.halo_loc, r=["halo_loc"], w=["halo_all"])
        self.phase("F")
        hT = self.sb("hT2", [128, 8, T + 2], BF16)
        self.mk_wstage(2304, n=3)
        shB, scB = self.phase_A(l, 1, hT, 1)
        rows = self.sb("hrows", [8, D]); xh = self.sb("xh", [2, D]); hh = self.sb("hh", [2, D], BF16)
        dma(rows, self.halo_all, r=["halo_all"], w=["hrows"])
        for half in range(2):
            ps, pk = self.nextps()
            mm(ps[0:2, :], self.selH, rows[:, half * 512:(half + 1) * 512], True, True, r=["hrows"], w=[pk])
            act(xh[:, half * 512:(half + 1) * 512], ps[0:2, :], AF.Identity, r=[], w=[pk, "xh"])
        lsc = self.mk_lnscr("h")
        rs, rk = self.ln_stats(xh, "xh", 2, lsc)
        act(xh, xh, AF.Identity, r=["xh"] + rk, w=["xh"], scale=rs[0:2, 0:1], bias=rs[0:2, 1:2])
        op("dve", "tensor_tensor", r=["xh", "scB"], w=["xh"], out=xh, in0=xh, in1=scB[0:2, :], op=ALU.mult)
        op("dve", "tensor_tensor", r=["xh", "shB"], w=["xh"], out=xh, in0=xh, in1=shB[0:2, :], op=ALU.add)
        op("dve", "tensor_scalar", r=["xh"], w=["hh"], out=hh, in0=xh, scalar1=self.flag[0:2, 0:1], scalar2=None, op0=ALU.mult)
        ps, pk = self.nextps()
        psb = ps.bitcast(BF16)
        for k in range(8):
            tr(psb[:, k * 2:(k + 1) * 2], hh[:, k * 128:(k + 1) * 128], r=["hh"], w=[pk])
        pv = psb[:, 0:16].rearrange("p (k t) -> p k t", k=8)
        op("dve", "tensor_copy", r=[], w=[pk, "hTh0"], out=hT[:, :, 0:1], in_=pv[:, :, 0:1])
        op("dve", "tensor_copy", r=[], w=[pk, "hTh1"], out=hT[:, :, T + 1:T + 2], in_=pv[:, :, 1:2])
        hks = [f"hT{i}" for i in range(NT)] + ["hTh0", "hTh1"]
        gB = self.sb("gate2B", [128, D])
        self.modvec(l, 5, gB, "gate2B")
        cw = self.sb("cw", [128, 44, 3]); cb = self.sb("cb", [128, 44])
        dma(cw, self.convw_in[l].rearrange("p (c k) -> p c k", k=3), w=["cw"])
        dma(cb, self.convb_in[l], w=["cb"])
        FB = min(256, T)
        Wa = [self.sb(f"Wa{i}", [128, 8, 128], BF16) for i in range(2)]
        Wgt = [self.sb(f"Wgt{i}", [128, 8, 128], BF16) for i in range(2)]
        Wd = [self.sb(f"Wd{i}", [128, D], BF16) for i in range(2)]
        ca_ = [self.sb(f"ca{i}", [128, FB]) for i in range(1)]; cg_ = [self.sb(f"cg{i}", [128, FB]) for i in range(1)]
        ca, cg = ca_[0], cg_[0]
        av = [self.sb(f"av{i}", [128, FB], BF16) for i in range(2)]
        na = 0
        NTB = T // FB

        def issue_up(jf, tb):
            jw = jf % 2
            if tb == 0:
                self.wload(Wa[jw], self.w_up[l, :, jf * 128:(jf + 1) * 128].rearrange("(k p) n -> p k n", p=128), f"Wa{jw}")
                self.wload(Wgt[jw], self.w_up[l, :, DFF + jf * 128:DFF + (jf + 1) * 128].rearrange("(k p) n -> p k n", p=128), f"Wgt{jw}")
                self.wload(Wd[jw], self.w_down[l, jf * 128:(jf + 1) * 128, :], f"Wd{jw}", mul=gB, mul_key="gate2B")
            c0 = tb * FB
            ui = (jf * NTB + tb) % 2
            pa, pak = self.ps[ui], f"ps{ui}"
            pg, pgk = self.ps[2 + ui], f"ps{2 + ui}"
            for k in range(8):
                mm(pa[:, 0:FB + 2], Wa[jw][:, k, :], hT[:, k, c0:c0 + FB + 2], k == 0, k == 7, r=[f"Wa{jw}"] + hks, w=[pak])
            for k in range(8):
                mm(pg[:, 0:FB + 2], Wgt[jw][:, k, :], hT[:, k, c0:c0 + FB + 2], k == 0, k == 7, r=[f"Wgt{jw}"] + hks, w=[pgk])
            return pa, pak, pg, pgk

        def finish(jf, tb, pa, pak, pg, pgk):
            nonlocal na
            jw = jf % 2
            c0 = tb * FB
            for (pp, ppk, dst, dk, ch) in ((pa, pak, ca, "ca", jf), (pg, pgk, cg, "cg", 22 + jf)):
                act(dst, pp[:, 1:FB + 1], AF.Identity, r=["cw", "cb"], w=[ppk, dk], scale=cw[:, ch, 1:2], bias=cb[:, ch:ch + 1])
                op("dve", "scalar_tensor_tensor", r=["cw"], w=[ppk, dk], out=dst, in0=pp[:, 0:FB], scalar=cw[:, ch, 0:1], in1=dst, op0=ALU.mult, op1=ALU.add)
                op("dve", "scalar_tensor_tensor", r=["cw"], w=[ppk, dk], out=dst, in0=pp[:, 2:FB + 2], scalar=cw[:, ch, 2:3], in1=dst, op0=ALU.mult, op1=ALU.add)
            act(cg, cg, AF.Silu, r=["cg"], w=["cg"])
            ja = na % 2; na += 1
            op("dve", "tensor_tensor", r=["ca", "cg"], w=[f"av{ja}"], out=av[ja], in0=ca, in1=cg, op=ALU.mult)
            for ti in range(FB // 128):
                i = (c0 // 128) + ti
                for half in range(2):
                    ps, pk = self.nextps([4, 5, 6, 7])
                    mm(ps, av[ja][:, ti * 128:(ti + 1) * 128], Wd[jw][:, half * 512:(half + 1) * 512], True, True, r=[f"av{ja}", f"Wd{jw}"], w=[pk])
                    xs = self.x_res[:, i, half * 512:(half + 1) * 512]
                    if jf == 0:
                        op("dve", "scalar_tensor_tensor", r=[], w=[pk, f"x{i}"], out=xs, in0=xs, scalar=ALPHA, in1=ps, op0=ALU.mult, op1=ALU.add)
                    else:
                        op("dve", "tensor_tensor", r=[], w=[pk, f"x{i}"], out=xs, in0=xs, in1=ps, op=ALU.add)

        prev = None
        for jf in range(DFF // 128):
            for tb in range(NTB):
                cur = (jf, tb) + issue_up(jf, tb)
                if prev is not None:
                    finish(*prev)
                prev = cur
        finish(*prev)
        self.post_norm(l, 1)
        if last:
            dma(self.out.rearrange("(n p) d -> p n d", p=128), self.x_res, r=[f"x{i}" for i in range(NT)], w=["out"])

    def build(self, stop=None):
        self.declare_inputs()
        self.setup()
        for l in range(self.L):
            self.phase_B(l)
            self.phase_R(l)
            self.phase_S(l)
            self.phase_AT(l)
            self.phase_M0(l)
            self.phase_M1(l)
            self.phase_M2(l)
            self.phase_F(l, last=(l == self.L - 1))
        self.fw.barrier()
        self.fw.emit()
        return self.nc


def _consts(S):
    T = S // 4
    NT = T // 128
    f32 = np.float32
    p = np.arange(128, dtype=f32)
    c = {}
    c["invf16"] = np.broadcast_to((f32(10000.0) ** (-np.arange(16, dtype=f32) / f32(16))).astype(f32), (128, 16)).copy()
    c["invf32"] = np.broadcast_to((f32(10000.0) ** (-np.arange(32, dtype=f32) / f32(32))).astype(f32), (128, 32)).copy()
    jj, ii = np.meshgrid(p, p, indexing="ij")
    EF = np.maximum(ii - jj, 0); EB = np.maximum(jj - ii, 0)
    MF = (ii >= jj).astype(f32); MB = (jj > ii).astype(f32)
    c["cmask"] = np.stack([EF, EB, MF, MB]).astype(f32)
    c["ek"] = np.stack([127 - p, p, p + 1, 128 - p], 1).astype(f32)
    n = np.arange(NT, dtype=f32)
    et = np.zeros((128, NT), f32); et[:64] = n[None, :]; et[64:] = (NT - 1 - n)[None, :]
    c["etab"] = et
    c["io"] = np.broadcast_to(np.arange(1, 513, dtype=f32), (128, 512)).copy()
    return c


def prep_inputs(inp, S, L):
    T = S // 4
    NT = T // 128
    f32 = np.float32
    A = lambda a: np.ascontiguousarray(np.asarray(a))
    cst = _consts(S)
    sh = {}
    sh["w_in"] = A(inp["w_in"][:L]); sh["w_uq"] = A(inp["mla_w_uq"][:L]); sh["w_ukv"] = A(inp["mla_w_ukv"][:L])
    sh["qnorm"] = A(np.asarray(inp["mla_q_norm"])[:L].reshape(L, 2, 128).transpose(0, 2, 1))
    sh["kvnorm"] = A(np.asarray(inp["mla_kv_norm"])[:L].reshape(L, 128, 1))
    sh["ldc"] = A(np.asarray(inp["ret_log_decay"])[:L].reshape(L, 8))
    sh["dskip"] = A(np.asarray(inp["s5_d"])[:L].reshape(L, 4, 128).transpose(0, 2, 1))
    sh["w_glu"] = A(inp["s5_w_glu"][:L])
    sh["w_br_mla"] = A(inp["w_branch_mla"][:L]); sh["w_br_ret"] = A(inp["w_branch_ret"][:L]); sh["w_br_s5"] = A(inp["w_branch_s5"][:L])
    sh["w_o"] = A(inp["w_o"][:L]); sh["w_up"] = A(inp["ffn_w_up"][:L]); sh["w_down"] = A(inp["ffn_w_down"][:L])
    cw = np.asarray(inp["ffn_conv_w"])[:L]
    sh["convw"] = A(cw.reshape(L, 3, 44, 128).transpose(0, 3, 2, 1).reshape(L, 128, 132))
    sh["convb"] = A(np.asarray(inp["ffn_conv_b"])[:L].reshape(L, 44, 128).transpose(0, 2, 1))
    sh["ln"] = A(np.stack([np.asarray(inp[k])[:L] for k in ("ln1_g", "ln1_b", "ln2_g", "ln2_b")], 1))
    sh["w_ada"] = A(inp["w_ada"][:L]); sh["b_ada"] = A(inp["b_ada"][:L])
    sh.update(cst)
    x = np.asarray(inp["x"]); cc = np.asarray(inp["c"]); pos = np.asarray(inp["positions"])
    s5 = {k: np.asarray(inp[k])[:L] for k in ("s5_lam_re", "s5_lam_im", "s5_log_step", "s5_b_re", "s5_b_im", "s5_c_re", "s5_c_im")}
    maps = []
    for core in range(8):
        b, r = core // 4, core % 4
        m = dict(sh)
        m["x"] = A(x[b, r * T:(r + 1) * T, :])
        m["cT"] = A(cc[b].reshape(8, 128).T)
        m["pos"] = A(pos[b, r * T:(r + 1) * T].reshape(NT, 128).T.astype(np.int32))
        dist = np.full((128, 4), BIGD, f32)
        for rr in range(4):
            if rr < r:
                dist[:64, rr] = r - 1 - rr
            if rr > r:
                dist[64:, rr] = rr - 1 - r
        m["dist"] = dist
        sel = np.zeros((128, 4), f32); sel[:, r] = 1.0
        m["sel"] = sel
        selH = np.zeros((8, 2), f32); flag = np.zeros((2, 1), f32)
        if r > 0:
            selH[2 * (r - 1) + 1, 0] = 1.0; flag[0, 0] = 1.0
        if r < 3:
            selH[2 * (r + 1), 1] = 1.0; flag[1, 0] = 1.0
        m["selH"] = selH; m["flag"] = flag
        g0 = 8 * r
        lamS_re = np.zeros((L, 2, 128, 4), f32); lamS_im = np.zeros_like(lamS_re); lsS = np.zeros_like(lamS_re)
        Bre = np.zeros((L, 2, 128, 4, 128), f32); Bim = np.zeros_like(Bre); Cre = np.zeros_like(Bre); Cim = np.zeros_like(Bre)
        for st in range(4):
            for gg in range(2):
                g = g0 + 2 * st + gg
                gl = 2 * st + gg
                lamS_re[:, :, gg * 64:(gg + 1) * 64, st] = s5["s5_lam_re"][:, :, g, :]
                lamS_im[:, :, gg * 64:(gg + 1) * 64, st] = s5["s5_lam_im"][:, :, g, :]
                lsS[:, :, gg * 64:(gg + 1) * 64, st] = s5["s5_log_step"][:, :, g, None]
                Bre[:, :, gl * 16:(gl + 1) * 16, st, gg * 64:(gg + 1) * 64] = s5["s5_b_re"][:, :, g].transpose(0, 1, 3, 2)
                Bim[:, :, gl * 16:(gl + 1) * 16, st, gg * 64:(gg + 1) * 64] = s5["s5_b_im"][:, :, g].transpose(0, 1, 3, 2)
                Cre[:, :, gg * 64:(gg + 1) * 64, st, gl * 16:(gl + 1) * 16] = s5["s5_c_re"][:, :, g].transpose(0, 1, 3, 2)
                Cim[:, :, gg * 64:(gg + 1) * 64, st, gl * 16:(gl + 1) * 16] = s5["s5_c_im"][:, :, g].transpose(0, 1, 3, 2)
        m["lamS_re"] = lamS_re; m["lamS_im"] = lamS_im; m["lsS"] = lsS
        m["lamB_re"] = A(lamS_re.transpose(0, 1, 3, 2).reshape(L, 2, 512))
        m["lamB_im"] = A(lamS_im.transpose(0, 1, 3, 2).reshape(L, 2, 512))
        m["lsB"] = A(lsS.transpose(0, 1, 3, 2).reshape(L, 2, 512))
        m["Bblk_re"] = Bre.reshape(L, 2, 128, 512); m["Bblk_im"] = Bim.reshape(L, 2, 128, 512)
        m["Cblk_re"] = Cre.reshape(L, 2, 128, 512); m["Cblk_im"] = Cim.reshape(L, 2, 128, 512)
        maps.append(m)
    return maps


_CACHE = {}


def kernel(**inputs):
    S = int(np.asarray(inputs["x"]).shape[1])
    L = int(np.asarray(inputs["w_in"]).shape[0])
    T = S // 4
    key = (S, L)
    if key not in _CACHE:
        b = Builder(S, L)
        b.build()
        _CACHE[key] = b
    b = _CACHE[key]
    maps = prep_inputs(inputs, S, L)
    maps = [{k: m[k] for k in b.inputs} for m in maps]
    res = run_bass_kernel_spmd(b.nc, maps, core_ids=list(range(8)))
    out = np.zeros((2, S, D), np.float32)
    for c in range(8):
        out[c // 4, (c % 4) * T:(c % 4 + 1) * T, :] = np.asarray(res.results[c]["out"]).reshape(T, D)
    return out
```

```python
import math
import numpy as np
import ml_dtypes
import concourse.bass as bass
import concourse.mybir as mybir
from concourse.bass_utils import run_bass_kernel_spmd

F32 = mybir.dt.float32
BF16 = mybir.dt.bfloat16
I32 = mybir.dt.int32
ALU = mybir.AluOpType
AF = mybir.ActivationFunctionType

ENGS = ("pe", "act", "dve", "pool", "sp")
D = 1024
NH = 8
DFF = 2816
C_QC, C_KV, C_KR, C_RQ, C_RK, C_RV, C_RG, C_S5, C_G = 0, 256, 384, 416, 672, 928, 1440, 1952, 2464
ALPHA = float((2 * 4) ** 0.25)
LN_EPS, RMS_EPS, GN_EPS = 1e-5, 1e-6, 1e-5
TWO_PI = 2.0 * math.pi
SB_BASE, SB_END = 16512, 229376
BIGD = 40.0


class FW:
    def __init__(self, nc, n_dma_sems=(("sp", 40), ("pool", 16))):
        self.nc = nc
        self.stream = {e: [] for e in ENGS}
        self.ctr = {e: nc.alloc_semaphore(name=f"ctr_{e}") for e in ENGS}
        self.bar = {e: nc.alloc_semaphore(name=f"bar_{e}") for e in ENGS}
        self.count = {e: 0 for e in ENGS}
        self.known = {e: {} for e in ENGS}
        self.keys = {}
        self.dpool = {}
        for e, n in n_dma_sems:
            self.dpool[e] = dict(sems=[nc.alloc_semaphore(name=f"dq_{e}_{i}") for i in range(n)], vals=[0] * n, nxt=0)
        self.nbar = 0
        self.n_inst = 0
        self.customs = {}

    def _wait(self, eng, tok):
        sem, val = tok
        if val <= 0:
            return
        k = self.known[eng]
        if k.get(id(sem), 0) >= val:
            return
        k[id(sem)] = val
        self.stream[eng].append(("wait", sem, val))

    def _deps(self, eng, reads, writes, skip_same):
        toks = []
        for key in reads:
            st = self.keys.get(key)
            if st and st["w"] is not None:
                toks.append(st["w"])
        for key in writes:
            st = self.keys.get(key)
            if st:
                if st["w"] is not None:
                    toks.append(st["w"])
                toks.extend(st["r"])
        own = self.ctr[eng]
        for t in toks:
            if t[0] is own and skip_same:
                continue
            self._wait(eng, t)

    def _record(self, tok, reads, writes):
        for key in reads:
            st = self.keys.setdefault(key, {"w": None, "r": []})
            st["r"] = [t for t in st["r"] if t[0] is not tok[0]] + [tok]
        for key in writes:
            self.keys[key] = {"w": tok, "r": []}

    def op(self, eng, fn, reads=(), writes=(), skip_same=False):
        if eng == "pe":
            skip_same = True
        self._deps(eng, reads, writes, skip_same)
        self.count[eng] += 1
        tok = (self.ctr[eng], self.count[eng])
        self.stream[eng].append(("inst", fn, self.ctr[eng], 1))
        self._record(tok, reads, writes)
        self.n_inst += 1
        return tok

    def dma(self, eng, out, in_, reads=(), writes=()):
        p = self.dpool[eng]
        j = p["nxt"]
        p["nxt"] = (j + 1) % len(p["sems"])
        sem = p["sems"][j]
        self._wait(eng, (sem, p["vals"][j]))
        self._deps(eng, reads, writes, False)
        p["vals"][j] += 16
        tok = (sem, p["vals"][j])
        self.stream[eng].append(("inst", lambda e, o=out, i=in_: e.dma_start(out=o, in_=i), sem, 16))
        self._record(tok, reads, writes)
        self.n_inst += 1
        return tok

    def custom(self, eng, fn, sem, newval, reads=(), writes=()):
        self._deps(eng, reads, writes, False)
        self.stream[eng].append(("inst", fn, sem, 1))
        tok = (sem, newval)
        self.customs[eng] = tok
        self._record(tok, reads, writes)
        return tok

    def barrier(self):
        self.nbar += 1
        for e in ENGS:
            if e in self.dpool:
                p = self.dpool[e]
                for s, v in zip(p["sems"], p["vals"]):
                    self._wait(e, (s, v))
            if e in self.customs:
                self._wait(e, self.customs[e])
            self._wait(e, (self.ctr[e], self.count[e]))
            self.stream[e].append(("inst", lambda en: en.nop(), self.bar[e], 1))
        for e in ENGS:
            for e2 in ENGS:
                if e2 != e:
                    self._wait(e, (self.bar[e2], self.nbar))
        self.keys = {}

    def emit(self):
        nc = self.nc
        engobj = {"pe": "tensor", "act": "scalar", "dve": "vector", "pool": "gpsimd", "sp": "sync"}
        with nc.Block() as block:
            for e in ENGS:
                stream = self.stream[e]

                def body(eng, stream=stream):
                    for it in stream:
                        if it[0] == "wait":
                            eng.wait_ge(it[1], it[2])
                        else:
                            it[1](eng).then_inc(it[2], it[3])

                getattr(block, engobj[e])(body)


def _dsize(dt):
    return {F32: 4, BF16: 2, I32: 4}[dt]


def rev_ap(ap, n):
    a = ap.ap
    assert len(a) == 2 and a[1][0] == 1 and a[1][1] == n, a
    return bass.AP(ap.tensor, ap.offset + (n - 1), [[a[0][0], a[0][1]], [-1, n]])


class Builder:
    def __init__(self, S, L, debug=()):
        self.S, self.L = S, L
        self.T = S // 4
        self.NT = self.T // 128
        self.debug = set(debug)
        self.nc = bass.Bass("TRN2", target_bir_lowering=False)
        self.fw = FW(self.nc)
        self.pers_off = SB_BASE
        self.arena_off = SB_BASE
        self.uid = 0
        self.inputs = {}
        self.dbg_out = []
        self.ps = [self.nc.alloc_psum_tensor(f"ps{i}", [128, 512], F32).ap() for i in range(8)]
        self.psi = 0
        self.cc_sem = self.nc.alloc_semaphore(name="cc_sem")
        self.cc_n = 0
        self.stage_i = 0
        self._cache = {}

    def _alloc(self, name, shape, dt, off):
        nbytes = int(np.prod(shape[1:])) * _dsize(dt)
        nbytes = (nbytes + 31) // 32 * 32
        assert off + nbytes <= SB_END, f"SBUF overflow allocating {name} {shape}: {off + nbytes - SB_END} bytes over"
        self.uid += 1
        h = self.nc.alloc_sbuf_tensor_at(f"{name}_{self.uid}", list(shape), dt, offset=off)
        return h.ap(), off + nbytes

    def pers(self, name, shape, dt=F32):
        ap, self.pers_off = self._alloc(name, shape, dt, self.pers_off)
        self.arena_off = max(self.arena_off, self.pers_off)
        return ap

    def sb(self, name, shape, dt=F32):
        ap, self.arena_off = self._alloc(name, shape, dt, self.arena_off)
        return ap

    def phase(self, name):
        self.fw.barrier()
        self.arena_off = self.pers_off
        self.pname = name

    def subphase(self, mark):
        self.fw.barrier()
        self.arena_off = mark

    def inp(self, name, shape, dt=F32):
        ap = self.nc.dram_tensor(name, list(shape), dt, kind="ExternalInput").ap()
        self.inputs[name] = (tuple(shape), dt)
        return ap

    def dram(self, name, shape, dt=F32):
        return self.nc.dram_tensor(name, list(shape), dt).ap()

    def nextps(self, cyc=None):
        cyc = cyc or list(range(8))
        i = cyc[self.psi % len(cyc)]
        self.psi += 1
        return self.ps[i], f"ps{i}"

    def op(self, eng, meth, r=(), w=(), **kw):
        return self.fw.op(eng, lambda e: getattr(e, meth)(**kw), reads=list(r), writes=list(w))

    def dma(self, out, in_, r=(), w=(), q="sp"):
        return self.fw.dma(q, out, in_, reads=list(r), writes=list(w))

    def mm(self, out, lhsT, rhs, start, stop, r, w):
        return self.fw.op("pe", lambda e: e.matmul(out, lhsT=lhsT, rhs=rhs, start=start, stop=stop), reads=list(r), writes=list(w))

    def tr(self, out, in_, r, w):
        p = in_.shape[0]
        idn = self.ident[0:p, 0:p]
        return self.fw.op("pe", lambda e: e.transpose(out=out, in_=in_, identity=idn), reads=list(r) + ["ident"], writes=list(w))

    def act(self, out, in_, func, r, w, bias=0.0, scale=1.0, **kw):
        return self.fw.op("act", lambda e: e.activation(out=out, in_=in_, func=func, bias=bias, scale=scale, **kw), reads=list(r), writes=list(w))

    def allgather(self, out_ap, in_ap, r, w):
        self.cc_n += 1
        n = self.cc_n
        return self.fw.custom("pool", lambda e: e.collective_compute("AllGather", ALU.bypass, replica_groups=[[0, 1, 2, 3], [4, 5, 6, 7]],
                                                                     ins=[in_ap.opt()], outs=[out_ap.opt()]),
                              self.cc_sem, n, reads=list(r), writes=list(w))

    def dbg(self, name, dram_ap, key):
        if name in self.debug:
            o = self.nc.dram_tensor("dbg_" + name, list(dram_ap.shape), dram_ap.dtype, kind="ExternalOutput").ap()
            self.dma(o, dram_ap, r=[key])
            self.dbg_out.append("dbg_" + name)

    def dbg_sb(self, name, sb_ap, key):
        if name in self.debug:
            o = self.nc.dram_tensor("dbg_" + name, list(sb_ap.shape), sb_ap.dtype, kind="ExternalOutput").ap()
            self.dma(o, sb_ap, r=[key])
            self.dbg_out.append("dbg_" + name)

    def wload(self, dst, src, dst_key, mul=None, mul_key=None, src_key=None):
        shp = list(src.shape)
        rows = shp[0]
        cols = int(np.prod(shp[1:]))
        si = self.stage_i % len(self.wstage)
        self.stage_i += 1
        skey = f"wst{si}"
        st = self.wstage[si][0:rows, 0:cols]
        if len(shp) == 3:
            st = st.rearrange("p (k n) -> p k n", k=shp[1])
        self.dma(st, src, r=[src_key] if src_key else [], w=[skey])
        if mul is None:
            self.op("pool", "tensor_copy", r=[skey], w=[dst_key], out=dst, in_=st)
        else:
            self.op("pool", "tensor_tensor", r=[skey, mul_key], w=[dst_key], out=dst, in0=st, in1=mul, op=ALU.mult)

    def mk_wstage(self, cols, n=3):
        self.wstage = [self.sb(f"wst{i}", [128, cols], F32) for i in range(n)]
        self.stage_i = 0

    def declare_inputs(self):
        L, T, NT, S = self.L, self.T, self.NT, self.S
        I = self.inp
        self.x_in = I("x", [T, D])
        self.cT_in = I("cT", [128, 8])
        self.pos_in = I("pos", [128, NT], I32)
        self.invf16_in = I("invf16", [128, 16])
        self.invf32_in = I("invf32", [128, 32])
        self.cmask_in = I("cmask", [4, 128, 128])
        self.ek_in = I("ek", [128, 4])
        self.etab_in = I("etab", [128, NT])
        self.io_in = I("io", [128, 512])
        self.dist_in = I("dist", [128, 4])
        self.sel_in = I("sel", [128, 4])
        self.selH_in = I("selH", [8, 2])
        self.flag_in = I("flag", [2, 1])
        self.w_in = I("w_in", [L, D, 5536])
        self.qnorm_in = I("qnorm", [L, 128, 2])
        self.w_uq = I("w_uq", [L, 256, 768])
        self.kvnorm_in = I("kvnorm", [L, 128, 1])
        self.w_ukv = I("w_ukv", [L, 128, 1024])
        self.ldc_in = I("ldc", [L, 8])
        self.lamS_re = I("lamS_re", [L, 2, 128, 4])
        self.lamS_im = I("lamS_im", [L, 2, 128, 4])
        self.lsS = I("lsS", [L, 2, 128, 4])
        self.lamB_re = I("lamB_re", [L, 2, 512])
        self.lamB_im = I("lamB_im", [L, 2, 512])
        self.lsB = I("lsB", [L, 2, 512])
        self.Bblk_re = I("Bblk_re", [L, 2, 128, 512])
        self.Bblk_im = I("Bblk_im", [L, 2, 128, 512])
        self.Cblk_re = I("Cblk_re", [L, 2, 128, 512])
        self.Cblk_im = I("Cblk_im", [L, 2, 128, 512])
        self.dskip_in = I("dskip", [L, 128, 4])
        self.w_glu = I("w_glu", [L, 512, 512])
        self.w_br = [I("w_br_mla", [L, 512, D]), I("w_br_ret", [L, 512, D]), I("w_br_s5", [L, 512, D])]
        self.w_o = I("w_o", [L, D, D])
        self.w_up = I("w_up", [L, D, 2 * DFF])
        self.convw_in = I("convw", [L, 128, 44 * 3])
        self.convb_in = I("convb", [L, 128, 44])
        self.w_down = I("w_down", [L, DFF, D])
        self.ln_in = I("ln", [L, 4, D])
        self.w_ada = I("w_ada", [L, D, 6 * D])
        self.b_ada = I("b_ada", [L, 6 * D])
        self.out = self.nc.dram_tensor("out", [T, D], F32, kind="ExternalOutput").ap()
        Dm = self.dram
        self.lat_loc = Dm("lat_loc", [160, T], BF16)
        self.lat_all = Dm("lat_all", [4 * 160, T], BF16)
        self.qT_d = Dm("qT_d", [NH, 96, T], BF16)
        self.rq_d = Dm("rq_d", [T, 256], BF16)
        self.rk_d = Dm("rk_d", [T, 256], BF16)
        self.rv_d = Dm("rv_d", [T, 512], BF16)
        self.rg_d = Dm("rg_d", [T, 512], BF16)
        self.u_loc = Dm("u_loc", [512, T], F32)
        self.u_all = Dm("u_all", [16 * 128, T], F32)
        self.agg_loc = Dm("agg_loc", [4 * 128, 128], F32)
        self.agg_all = Dm("agg_all", [16 * 128, 128], F32)
        self.ys_loc = Dm("ys_loc", [4 * 128, T], F32)
        self.ys_all = Dm("ys_all", [16 * 128, T], F32)
        self.ob_d = [Dm(f"ob{i}_d", [512, T], BF16) for i in range(3)]
        self.mT_d = Dm("mT_d", [D, T], BF16)
        self.halo_loc = Dm("halo_loc", [2, D], F32)
        self.halo_all = Dm("halo_all", [8, D], F32)

    def setup(self):
        T, NT = self.T, self.NT
        P = self.pers
        self.x_res = P("x_res", [128, NT, D])
        self.ident = P("ident", [128, 128], BF16)
        self.ones_f = P("ones_f", [128, 128])
        self.zeros_f = P("zeros_f", [128, 128])
        self.condrep = P("condrep", [128, 8, 128])
        self.cos16 = P("cos16", [128, NT, 16]); self.sin16 = P("sin16", [128, NT, 16])
        self.cos32 = P("cos32", [128, NT, 32]); self.sin32 = P("sin32", [128, NT, 32])
        self.cos32q = P("cos32q", [128, NT, 32]); self.sin32q = P("sin32q", [128, NT, 32])
        self.sel = P("sel", [128, 4]); self.dist = P("dist", [128, 4])
        self.selH = P("selH", [8, 2]); self.flag = P("flag", [2, 1])
        self.ek = P("ek", [128, 4]); self.etab = P("etab", [128, NT])
        self.constc = P("constc", [128, 4])
        self.epsc = self.constc[:, 0:1]; self.rmseps = self.constc[:, 1:2]; self.halfpi = self.constc[:, 2:3]
        self.fw.barrier()
        self.arena_off = self.pers_off
        op, dma = self.op, self.dma
        dma(self.x_res, self.x_in.rearrange("(n p) d -> p n d", p=128), w=["x_res"])
        op("pool", "memset", w=["ident"], ap=self.ident, constant=0.0)
        op("pool", "affine_select", r=["ident"], w=["ident"], out=self.ident, in_=self.ident, pattern=[[-1, 128]],
           compare_op=ALU.not_equal, fill=1.0, base=0, channel_multiplier=1)
        op("dve", "memset", w=["ones_f"], ap=self.ones_f, constant=1.0)
        op("dve", "memset", w=["zeros_f"], ap=self.zeros_f, constant=0.0)
        op("dve", "memset", w=["constc"], ap=self.constc[:, 0:1], constant=LN_EPS)
        op("dve", "memset", w=["constc"], ap=self.constc[:, 1:2], constant=RMS_EPS)
        op("dve", "memset", w=["constc"], ap=self.constc[:, 2:3], constant=math.pi / 2)
        op("dve", "memset", w=["constc"], ap=self.constc[:, 3:4], constant=GN_EPS)
        for nm, src in (("sel", self.sel_in), ("dist", self.dist_in), ("selH", self.selH_in), ("flag", self.flag_in),
                        ("ek", self.ek_in), ("etab", self.etab_in)):
            dma(getattr(self, nm), src, w=[nm])
        cT = self.sb("cT", [128, 8]); cs = self.sb("cs", [128, 8])
        dma(cT, self.cT_in, w=["cT"])
        self.act(cs, cT, AF.Silu, r=["cT"], w=["cs"])
        op("dve", "tensor_copy", r=["cs"], w=["condrep"], out=self.condrep, in_=cs.unsqueeze(2).to_broadcast([128, 8, 128]))
        posi = self.sb("posi", [128, NT], I32); posf = self.sb("posf", [128, NT])
        dma(posi, self.pos_in, w=["posi"])
        op("dve", "tensor_copy", r=["posi"], w=["posf"], out=posf, in_=posi)
        for nf, src, cosT, sinT in ((16, self.invf16_in, self.cos16, self.sin16), (32, self.invf32_in, self.cos32, self.sin32)):
            inv = self.sb(f"inv{nf}", [128, nf]); u = self.sb(f"u{nf}", [128, NT, nf]); ki = self.sb(f"ki{nf}", [128, NT, nf], I32)
            kf = self.sb(f"kf{nf}", [128, NT, nf]); fr = self.sb(f"fr{nf}", [128, NT, nf]); ab = self.sb(f"ab{nf}", [128, NT, nf])
            k = f"rp{nf}"
            dma(inv, src, w=[k + "inv"])
            op("dve", "tensor_scalar", r=[k + "inv"], w=[k + "inv"], out=inv, in0=inv, scalar1=1.0 / TWO_PI, scalar2=None, op0=ALU.mult)
            op("dve", "tensor_tensor", r=["posf", k + "inv"], w=[k + "u"], out=u, in0=posf.unsqueeze(2).to_broadcast([128, NT, nf]),
               in1=inv.unsqueeze(1).to_broadcast([128, NT, nf]), op=ALU.mult)
            op("dve", "tensor_copy", r=[k + "u"], w=[k + "ki"], out=ki, in_=u)
            op("dve", "tensor_copy", r=[k + "ki"], w=[k + "kf"], out=kf, in_=ki)
            op("dve", "tensor_tensor", r=[k + "u", k + "kf"], w=[k + "fr"], out=fr, in0=u, in1=kf, op=ALU.subtract)
            self.act(sinT, fr, AF.Sin, r=[k + "fr"], w=[k + "sin"], scale=TWO_PI)
            self.act(ab, fr, AF.Abs, r=[k + "fr"], w=[k + "ab"])
            self.act(cosT, ab, AF.Sin, r=[k + "ab"], w=[k + "cos"], scale=-TWO_PI, bias=self.halfpi)
        op("dve", "tensor_scalar", r=["rp32cos"], w=["c32q"], out=self.cos32q, in0=self.cos32, scalar1=0.125, scalar2=None, op0=ALU.mult)
        op("dve", "tensor_scalar", r=["rp32sin"], w=["s32q"], out=self.sin32q, in0=self.sin32, scalar1=0.125, scalar2=None, op0=ALU.mult)

    def modvec(self, l, idx, dst, dkey, plus_one=False):
        for qd in range(4):
            c0 = idx * D + qd * 256
            si = self.stage_i % len(self.wstage)
            self.stage_i += 1
            wst = self.wstage[si][:, 0:2048].rearrange("p (k n) -> p k n", k=8)
            brow = self.wstage[si][0:1, 2048:2304]
            sk = f"wst{si}"
            self.dma(wst, self.w_ada[l, :, c0:c0 + 256].rearrange("(k p) n -> p k n", p=128), w=[sk])
            self.dma(brow, self.b_ada[l:l + 1, c0:c0 + 256], w=[sk])
            ps, pk = self.nextps()
            for k in range(8):
                self.mm(ps[:, 0:256], self.condrep[:, k, :], wst[:, k, :], k == 0, False, r=["condrep", sk], w=[pk])
            self.mm(ps[:, 0:256], self.ones_f[0:1, :], brow, False, True, r=["ones_f", sk], w=[pk])
            self.act(dst[:, qd * 256:(qd + 1) * 256], ps[:, 0:256], AF.Identity, r=[], w=[pk, dkey], bias=1.0 if plus_one else 0.0)

    def mk_lnscr(self, tag):
        return dict(st=self.sb(f"lnst{tag}", [128, 2, 6]), mv=self.sb(f"lnmv{tag}", [128, 2]), rs=self.sb(f"lnrs{tag}", [128, 2]), k=f"ln{tag}")

    def ln_stats(self, xt, xkey, p, scr):
        st, mv, rs, k = scr["st"], scr["mv"], scr["rs"], scr["k"]
        for c in range(2):
            self.op("dve", "bn_stats", r=[xkey], w=[k + "st"], out=st[0:p, c, :], in_=xt[:, c * 512:(c + 1) * 512])
        self.op("dve", "bn_aggr", r=[k + "st"], w=[k + "mv"], out=mv[0:p, :], in_=st[0:p].rearrange("p a b -> p (a b)"))
        self.act(rs[0:p, 0:1], mv[0:p, 1:2], AF.Ln, r=[k + "mv"], w=[k + "rs"], bias=self.epsc[0:p, 0:1])
        self.act(rs[0:p, 0:1], rs[0:p, 0:1], AF.Exp, r=[k + "rs"], w=[k + "rs"], scale=-0.5)
        self.op("dve", "scalar_tensor_tensor", r=[k + "mv", k + "rs"], w=[k + "rs2"], out=rs[0:p, 1:2], in0=mv[0:p, 0:1], scalar=-1.0,
                in1=rs[0:p, 0:1], op0=ALU.mult, op1=ALU.mult)
        return rs, [k + "rs", k + "rs2"]

    def phase_A(self, l, which, hT, col0):
        NT = self.NT
        shB = self.sb("shB", [128, D]); scB = self.sb("scB", [128, D])
        self.modvec(l, 3 * which + 0, shB, "shB")
        self.modvec(l, 3 * which + 1, scB, "scB", plus_one=True)
        xn1 = self.sb("xn0", [128, D])
        xn = [xn1, xn1]
        hb = [self.sb(f"hb{i}", [128, D], BF16) for i in range(2)]
        lns = [self.mk_lnscr(i) for i in range(2)]
        for i in range(NT):
            j = i % 2
            xt = self.x_res[:, i, :]
            rs, rk = self.ln_stats(xt, "x_res", 128, lns[j])
            self.act(xn[j], xt, AF.Identity, r=["x_res"] + rk, w=["xn0"], scale=rs[:, 0:1], bias=rs[:, 1:2])
            self.op("dve", "tensor_tensor", r=["xn0", "scB"], w=["xn0"], out=xn[j], in0=xn[j], in1=scB, op=ALU.mult)
            self.op("dve", "tensor_tensor", r=["xn0", "shB"], w=[f"hb{j}"], out=hb[j], in0=xn[j], in1=shB, op=ALU.add)
            ps, pk = self.nextps()
            psb = ps.bitcast(BF16)
            for k in range(8):
                self.tr(psb[:, k * 128:(k + 1) * 128], hb[j][:, k * 128:(k + 1) * 128], r=[f"hb{j}"], w=[pk])
            self.op("dve", "tensor_copy", r=[], w=[pk, f"hT{i}"], out=hT[:, :, col0 + i * 128:col0 + (i + 1) * 128],
                    in_=psb.rearrange("p (k t) -> p k t", k=8))
        return shB, scB

    def rope(self, x1, x2, cosT, sinT, d1, d2, H, n, pk, dkey, tmp):
        c = cosT.unsqueeze(1).to_broadcast([128, H, n]); s = sinT.unsqueeze(1).to_broadcast([128, H, n])
        t1, t2 = tmp[0][:, 0:H * n].rearrange("p (h n) -> p h n", h=H), tmp[1][:, 0:H * n].rearrange("p (h n) -> p h n", h=H)
        o = self.op
        o("dve", "tensor_tensor", r=["rope"], w=[pk, "rt1"], out=t1, in0=x1, in1=c, op=ALU.mult)
        o("dve", "tensor_tensor", r=["rope"], w=[pk, "rt2"], out=t2, in0=x2, in1=s, op=ALU.mult)
        o("dve", "tensor_tensor", r=["rt1", "rt2"], w=[dkey], out=d1, in0=t1, in1=t2, op=ALU.subtract)
        o("dve", "tensor_tensor", r=["rope"], w=[pk, "rt1"], out=t1, in0=x1, in1=s, op=ALU.mult)
        o("dve", "tensor_tensor", r=["rope"], w=[pk, "rt2"], out=t2, in0=x2, in1=c, op=ALU.mult)
        o("dve", "tensor_tensor", r=["rt1", "rt2"], w=[dkey], out=d2, in0=t1, in1=t2, op=ALU.add)

    def phase_B(self, l):
        T, NT = self.T, self.NT
        op, dma, mm, tr, act = self.op, self.dma, self.mm, self.tr, self.act
        self.phase("B")
        hT = self.sb("hT", [128, 8, T], BF16)
        W1 = self.sb("W1", [128, 8, C_S5], BF16)
        Ws5 = self.sb("Ws5", [128, 8, 512], BF16)
        Wuq = self.sb("Wuq", [128, 2, 768], BF16)
        qg = self.sb("qg", [128, 2])
        self.mk_wstage(C_S5 + 512, n=2)
        dma(qg, self.qnorm_in[l], w=["qg"])
        for k in range(8):
            st = self.wstage[k % 2]
            dma(st, self.w_in[l, k * 128:(k + 1) * 128, 0:C_G], w=[f"wst{k % 2}"])
            op("pool", "tensor_copy", r=[f"wst{k % 2}"], w=["W1"], out=W1[:, k, :], in_=st[:, 0:C_S5])
            op("pool", "tensor_copy", r=[f"wst{k % 2}"], w=["Ws5"], out=Ws5[:, k, :], in_=st[:, C_S5:C_G])
        for k in range(2):
            st = self.wstage[k % 2]
            dma(st[:, 0:768], self.w_uq[l, k * 128:(k + 1) * 128, :], w=[f"wst{k % 2}"])
            op("dve", "tensor_scalar", r=[f"wst{k % 2}", "qg"], w=["Wuq"], out=Wuq[:, k, :], in0=st[:, 0:768], scalar1=qg[:, k:k + 1],
               scalar2=None, op0=ALU.mult)
        self.phase_A(l, 0, hT, 0)
        sq = self.sb("sq", [128, 384]); ssq = self.sb("ssq", [128, 2]); rr = self.sb("rr", [128, 2])
        qcn = self.sb("qcn", [128, 256], BF16); lat = self.sb("lat", [128, 160], BF16)
        qcnT = self.sb("qcnT", [128, 2, 128], BF16)
        latT = [self.sb(f"latT{i}", [128, 128], BF16) for i in range(2)]
        krT = [self.sb(f"krT{i}", [32, 128], BF16) for i in range(2)]
        Qr = self.sb("Qr", [128, 8, 96], BF16)
        QT = [self.sb(f"QT{i}", [96, 8, 128], BF16) for i in range(2)]
        rtmp = [self.sb(f"rtmp{i}", [128, 128]) for i in range(2)]
        rqk = [self.sb(f"rqk{i}", [128, 512], BF16) for i in range(2)]
        rvb = [self.sb(f"rvb{i}", [128, 512], BF16) for i in range(1)] * 2
        rgb = [self.sb(f"rgb{i}", [128, 512], BF16) for i in range(1)] * 2
        TB = min(512, T)
        ub = [self.sb(f"ub{i}", [128, TB]) for i in range(2)]
        for i in range(NT):
            j = i % 2
            tsl = slice(i * 128, (i + 1) * 128)
            hk = f"hT{i}"
            ps, pk = self.nextps()
            for k in range(8):
                mm(ps[:, 0:416], hT[:, k, tsl], W1[:, k, 0:416], k == 0, k == 7, r=[hk, "W1"], w=[pk])
            ps2, pk2 = self.nextps(); ps3, pk3 = self.nextps(); ps4, pk4 = self.nextps()
            for k in range(8):
                mm(ps2, hT[:, k, tsl], W1[:, k, C_RQ:C_RV], k == 0, k == 7, r=[hk, "W1"], w=[pk2])
            for k in range(8):
                mm(ps3, hT[:, k, tsl], W1[:, k, C_RV:C_RG], k == 0, k == 7, r=[hk, "W1"], w=[pk3])
            for k in range(8):
                mm(ps4, hT[:, k, tsl], W1[:, k, C_RG:C_S5], k == 0, k == 7, r=[hk, "W1"], w=[pk4])
            act(sq[:, 0:384], ps[:, 0:384], AF.Square, r=[], w=[pk, "sq"])
            op("dve", "tensor_reduce", r=["sq"], w=["ssq"], out=ssq[:, 0:1], in_=sq[:, 0:256], axis=mybir.AxisListType.X, op=ALU.add)
            op("dve", "tensor_reduce", r=["sq"], w=["ssq"], out=ssq[:, 1:2], in_=sq[:, 256:384], axis=mybir.AxisListType.X, op=ALU.add)
            act(rr[:, 0:1], ssq[:, 0:1], AF.Ln, r=["ssq"], w=["rr"], scale=1.0 / 256, bias=self.rmseps)
            act(rr[:, 1:2], ssq[:, 1:2], AF.Ln, r=["ssq"], w=["rr"], scale=1.0 / 128, bias=self.rmseps)
            act(rr, rr, AF.Exp, r=["rr"], w=["rr"], scale=-0.5)
            op("dve", "tensor_scalar", r=["rr"], w=[pk, "qcn"], out=qcn, in0=ps[:, 0:256], scalar1=rr[:, 0:1], scalar2=None, op0=ALU.mult)
            op("dve", "tensor_scalar", r=["rr"], w=[pk, "lat"], out=lat[:, 0:128], in0=ps[:, 256:384], scalar1=rr[:, 1:2], scalar2=None, op0=ALU.mult)
            self.rope(ps[:, 384:400].unsqueeze(1), ps[:, 400:416].unsqueeze(1), self.cos16[:, i, :], self.sin16[:, i, :],
                      lat[:, 128:144].unsqueeze(1), lat[:, 144:160].unsqueeze(1), 1, 16, pk, "lat", rtmp)
            pt, ptk = self.nextps()
            ptb = pt.bitcast(BF16)
            tr(ptb[:, 0:128], qcn[:, 0:128], r=["qcn"], w=[ptk])
            tr(ptb[:, 128:256], qcn[:, 128:256], r=["qcn"], w=[ptk])
            tr(ptb[:, 256:384], lat[:, 0:128], r=["lat"], w=[ptk])
            tr(ptb[0:32, 384:512], lat[:, 128:160], r=["lat"], w=[ptk])
            op("dve", "tensor_copy", r=[], w=[ptk, "qcnT"], out=qcnT, in_=ptb[:, 0:256].rearrange("p (k t) -> p k t", k=2))
            act(latT[j], ptb[:, 256:384], AF.Identity, r=[], w=[ptk, f"latT{j}"])
            act(krT[j], ptb[0:32, 384:512], AF.Identity, r=[], w=[ptk, f"krT{j}"])
            dma(self.lat_loc[0:128, tsl], latT[j], r=[f"latT{j}"], w=["lat_loc"])
            dma(self.lat_loc[128:160, tsl], krT[j], r=[f"krT{j}"], w=["lat_loc"])
            for g in range(2):
                ps, pk = self.nextps()
                for k in range(2):
                    mm(ps[:, 0:384], qcnT[:, k, :], Wuq[:, k, g * 384:(g + 1) * 384], k == 0, k == 1, r=["qcnT", "Wuq"], w=[pk])
                pv = ps[:, 0:384].rearrange("p (h d) -> p h d", h=4)
                act(Qr[:, g * 4:(g + 1) * 4, 0:64], pv[:, :, 0:64], AF.Identity, r=[], w=[pk, "Qr"])
                self.rope(pv[:, :, 64:80], pv[:, :, 80:96], self.cos16[:, i, :], self.sin16[:, i, :],
                          Qr[:, g * 4:(g + 1) * 4, 64:80], Qr[:, g * 4:(g + 1) * 4, 80:96], 4, 16, pk, "Qr", rtmp)
            pt, ptk = self.nextps()
            ptb = pt.bitcast(BF16)
            for h in range(8):
                tr(ptb[0:96, h * 128:(h + 1) * 128], Qr[:, h, :], r=["Qr"], w=[ptk])
            op("dve", "tensor_copy", r=[], w=[ptk, f"QT{j}"], out=QT[j], in_=ptb[0:96, :].rearrange("p (h t) -> p h t", h=8))
            dma(self.qT_d[:, :, tsl].rearrange("h p t -> p h t"), QT[j], r=[f"QT{j}"], w=["qT_d"])
            ps, pk = ps2, pk2
            for qk in range(2):
                pv = ps[:, qk * 256:(qk + 1) * 256].rearrange("p (h d) -> p h d", h=4)
                dv = rqk[j][:, qk * 256:(qk + 1) * 256].rearrange("p (h d) -> p h d", h=4)
                cT_, sT_ = (self.cos32q, self.sin32q) if qk == 0 else (self.cos32, self.sin32)
                self.rope(pv[:, :, 0:32], pv[:, :, 32:64], cT_[:, i, :], sT_[:, i, :], dv[:, :, 0:32], dv[:, :, 32:64], 4, 32, pk, f"rqk{j}", rtmp)
            dma(self.rq_d[tsl, :], rqk[j][:, 0:256], r=[f"rqk{j}"], w=["rq_d"])
            dma(self.rk_d[tsl, :], rqk[j][:, 256:512], r=[f"rqk{j}"], w=["rk_d"])
            ps, pk = ps3, pk3
            act(rvb[j], ps, AF.Identity, r=[], w=[pk, "rvb0"])
            dma(self.rv_d[tsl, :], rvb[j], r=["rvb0"], w=["rv_d"])
            ps, pk = ps4, pk4
            act(rgb[j], ps, AF.Silu, r=[], w=[pk, "rgb0"])
            dma(self.rg_d[tsl, :], rgb[j], r=["rgb0"], w=["rg_d"])
        n = 0
        for b in range(T // TB):
            bsl = slice(b * TB, (b + 1) * TB)
            hks = [f"hT{i}" for i in range(b * TB // 128, (b + 1) * TB // 128)]
            for fc in range(4):
                ps, pk = self.nextps()
                for k in range(8):
                    mm(ps[:, 0:TB], Ws5[:, k, fc * 128:(fc + 1) * 128], hT[:, k, bsl], k == 0, k == 7, r=hks + ["Ws5"], w=[pk])
                j = n % 2; n += 1
                act(ub[j], ps[:, 0:TB], AF.Identity, r=[], w=[pk, f"ub{j}"])
                dma(self.u_loc[fc * 128:(fc + 1) * 128, bsl], ub[j], r=[f"ub{j}"], w=["u_loc"])
        self.allgather(self.lat_all, self.lat_loc, r=["lat_loc"], w=["lat_all"])
        for c in range(4):
            self.allgather(self.u_all[c * 512:(c + 1) * 512, :], self.u_loc[c * 128:(c + 1) * 128, :], r=["u_loc"], w=["u_all"])
        for nm in ("lat_loc", "qT_d", "rq_d", "rk_d", "rv_d", "rg_d", "u_loc", "lat_all"):
            self.dbg(nm, getattr(self, nm), nm)

    def phase_R(self, l):
        T, NT = self.T, self.NT
        op, dma, mm, tr, act = self.op, self.dma, self.mm, self.tr, self.act
        self.phase("R")
        rq = self.sb("rq", [128, NT, 256], BF16); rk = self.sb("rk", [128, NT, 256], BF16)
        rv = self.sb("rv", [128, NT, 512], BF16); sg = self.sb("sg", [128, NT, 512], BF16)
        dma(rq, self.rq_d.rearrange("(n p) f -> p n f", p=128), w=["rq"])
        dma(rk, self.rk_d.rearrange("(n p) f -> p n f", p=128), w=["rk"])
        dma(rv, self.rv_d.rearrange("(n p) f -> p n f", p=128), w=["rv"])
        dma(sg, self.rg_d.rearrange("(n p) f -> p n f", p=128), w=["sg"])
        lgall = self.sb("lgall", [128, 8]); lgsel = self.sb("lgsel", [128, 4])
        dma(lgall, self.ldc_in[l:l + 1, :].partition_broadcast(128), w=["lgall"])
        dma(lgsel[0:64, :], self.ldc_in[l:l + 1, 0:4].partition_broadcast(64), w=["lgsel"])
        dma(lgsel[64:128, :], self.ldc_in[l:l + 1, 4:8].partition_broadcast(64), w=["lgsel"])
        for t_, k_ in ((lgall, "lgall"), (lgsel, "lgsel")):
            act(t_, t_, AF.Exp, r=[k_], w=[k_])
            act(t_, t_, AF.Ln, r=[k_], w=[k_], scale=-1.0, bias=1.0)
        zx = self.sb("zx", [128, 4, 4])
        for kind, (ecol, lo) in enumerate(((0, 0), (1, 4), (2, 0), (3, 4))):
            act(zx[:, kind, :], lgall[:, lo:lo + 4], AF.Exp, r=["lgall", "ek"], w=["zx"], scale=self.ek[:, ecol:ecol + 1])
        cm = self.sb("cm", [128, 4, 128])
        dma(cm, self.cmask_in.rearrange("a p f -> p a f"), w=["cm"])
        DT = self.sb("DT", [128, 4, 128]); dtmp = self.sb("dtmp", [128, 2, 128])
        for h in range(4):
            act(dtmp[:, 0, :], cm[:, 0, :], AF.Exp, r=["cm", "lgall"], w=["dtmp0"], scale=lgall[:, h:h + 1])
            act(dtmp[:, 1, :], cm[:, 1, :], AF.Exp, r=["cm", "lgall"], w=["dtmp1"], scale=lgall[:, 4 + h:5 + h])
            op("dve", "tensor_tensor", r=["dtmp0", "cm"], w=["dtmp0"], out=dtmp[:, 0, :], in0=dtmp[:, 0, :], in1=cm[:, 2, :], op=ALU.mult)
            op("dve", "tensor_tensor", r=["dtmp1", "cm"], w=["dtmp1"], out=dtmp[:, 1, :], in0=dtmp[:, 1, :], in1=cm[:, 3, :], op=ALU.mult)
            op("dve", "tensor_tensor", r=["dtmp0", "dtmp1"], w=["DT"], out=DT[:, h, :], in0=dtmp[:, 0, :], in1=dtmp[:, 1, :], op=ALU.add)
        lg128 = self.sb("lg128", [128, 4]); lgT = self.sb("lgT", [128, 4])
        op("dve", "tensor_scalar", r=["lgsel"], w=["lg128"], out=lg128, in0=lgsel, scalar1=128.0, scalar2=None, op0=ALU.mult)
        op("dve", "tensor_scalar", r=["lgsel"], w=["lgT"], out=lgT, in0=lgsel, scalar1=float(T), scalar2=None, op0=ALU.mult)
        cdec = self.sb("cdec", [128, 4])
        act(cdec, lg128, AF.Exp, r=["lg128"], w=["cdec"])
        coefn = self.sb("coefn", [128, 4, NT]); coefr = self.sb("coefr", [128, 4, 4])
        for h in range(4):
            act(coefn[:, h, :], self.etab, AF.Exp, r=["lg128"], w=["coefn"], scale=lg128[:, h:h + 1])
            act(coefr[:, h, :], self.dist, AF.Exp, r=["lgT"], w=["coefr"], scale=lgT[:, h:h + 1])
        E = self.sb("E", [128, 4, NT, 128])
        kk = [self.sb(f"kk{i}", [128, 128], BF16) for i in range(2)]
        n_ = 0
        for n in range(NT):
            for h in range(4):
                j = n_ % 2; n_ += 1
                kh = rk[:, n, h * 64:(h + 1) * 64]
                op("dve", "tensor_scalar", r=["rk", "zx"], w=[f"kk{j}"], out=kk[j][:, 0:64], in0=kh, scalar1=zx[:, 0, h:h + 1], scalar2=None, op0=ALU.mult)
                op("dve", "tensor_scalar", r=["rk", "zx"], w=[f"kk{j}"], out=kk[j][:, 64:128], in0=kh, scalar1=zx[:, 1, h:h + 1], scalar2=None, op0=ALU.mult)
                ps, pk = self.nextps()
                mm(ps[:, 0:128], kk[j], rv[:, n, h * 128:(h + 1) * 128], True, True, r=[f"kk{j}", "rv"], w=[pk])
                act(E[:, h, n, :], ps[:, 0:128], AF.Identity, r=[], w=[pk, f"E{h}_{n}"])
        for h in range(4):
            for n in range(1, NT):
                op("dve", "scalar_tensor_tensor", r=[f"E{h}_{n - 1}"], w=[f"E{h}_{n}"], out=E[0:64, h, n, :], in0=E[0:64, h, n - 1, :],
                   scalar=cdec[0:64, h:h + 1], in1=E[0:64, h, n, :], op0=ALU.mult, op1=ALU.add)
            for n in range(NT - 2, -1, -1):
                op("dve", "scalar_tensor_tensor", r=[f"E{h}_{n + 1}"], w=[f"E{h}_{n}"], out=E[64:128, h, n, :], in0=E[64:128, h, n + 1, :],
                   scalar=cdec[64:128, h:h + 1], in1=E[64:128, h, n, :], op0=ALU.mult, op1=ALU.add)
        ekeys = [f"E{h}_{n}" for h in range(4) for n in range(NT)]
        dma(self.agg_loc.rearrange("(h p) e -> p h e", p=128)[0:64], E[0:64, :, NT - 1, :], r=ekeys, w=["agg_loc"])
        dma(self.agg_loc.rearrange("(h p) e -> p h e", p=128)[64:128], E[64:128, :, 0, :], r=ekeys, w=["agg_loc"])
        self.allgather(self.agg_all, self.agg_loc, r=["agg_loc"], w=["agg_all"])
        agg = self.sb("agg", [128, 4, 4, 128])
        dma(agg, self.agg_all.rearrange("(r h p) e -> p r h e", r=4, h=4), r=["agg_all"], w=["agg"])
        Cin = self.sb("Cin", [128, 4, 128])
        for h in range(4):
            op("dve", "tensor_scalar", r=["agg", "coefr"], w=[f"Cin{h}"], out=Cin[:, h, :], in0=agg[:, 0, h, :], scalar1=coefr[:, h, 0:1], scalar2=None, op0=ALU.mult)
            for r_ in range(1, 4):
                op("dve", "scalar_tensor_tensor", r=["agg", "coefr"], w=[f"Cin{h}"], out=Cin[:, h, :], in0=agg[:, r_, h, :], scalar=coefr[:, h, r_:r_ + 1],
                   in1=Cin[:, h, :], op0=ALU.mult, op1=ALU.add)
        Pb = self.sb("Pb", [128, 4, NT, 128], BF16)
        for h in range(4):
            for n in range(NT):
                src_f = E[0:64, h, n - 1, :] if n >= 1 else self.zeros_f[0:64, :]
                src_b = E[64:128, h, n + 1, :] if n <= NT - 2 else self.zeros_f[64:128, :]
                op("dve", "scalar_tensor_tensor", r=ekeys + [f"Cin{h}", "coefn"], w=[f"Pb{n}"], out=Pb[0:64, h, n, :], in0=Cin[0:64, h, :],
                   scalar=coefn[0:64, h, n:n + 1], in1=src_f, op0=ALU.mult, op1=ALU.add)
                op("dve", "scalar_tensor_tensor", r=ekeys + [f"Cin{h}", "coefn"], w=[f"Pb{n}"], out=Pb[64:128, h, n, :], in0=Cin[64:128, h, :],
                   scalar=coefn[64:128, h, n:n + 1], in1=src_b, op0=ALU.mult, op1=ALU.add)
        for nm_, ap_, k_ in (("zx", zx, ["zx"]), ("DT", DT, ["DT"]), ("E", E, ekeys), ("Cin", Cin, [f"Cin{h}" for h in range(4)]), ("coefr", coefr, ["coefr"]), ("lgall", lgall, ["lgall"])):
            if nm_ in self.debug:
                o_ = self.nc.dram_tensor("dbg_" + nm_, list(ap_.shape), ap_.dtype, kind="ExternalOutput").ap()
                self.dma(o_, ap_, r=k_)
                self.dbg_out.append("dbg_" + nm_)
        qq = [self.sb(f"qq{i}", [128, 128], BF16) for i in range(2)]
        tqk = [self.sb(f"tqk{i}", [64, 256], BF16) for i in range(2)]
        tqq = [self.sb(f"tqq{i}", [128, 128], BF16) for i in range(2)]
        AT = [self.sb(f"AT{i}", [128, 128], BF16) for i in range(2)]
        gst = self.sb("gst", [128, 4, 6]); gmv = self.sb("gmv", [128, 4, 2]); grs = self.sb("grs", [128, 4, 2])
        yn = self.sb("yn", [128, 512]); orb = self.sb("orb", [128, 512], BF16)
        oT = [self.sb(f"oT{i}", [128, 4, 128], BF16) for i in range(2)]
        n_ = 0

        def heads_part(n):
            nonlocal n_
            yps, ypk = self.ps[6 + n % 2], f"ps{6 + n % 2}"
            for h in range(4):
                j = n_ % 2; n_ += 1
                qh = rq[:, n, h * 64:(h + 1) * 64]
                op("dve", "tensor_scalar", r=["rq", "zx"], w=[f"qq{j}"], out=qq[j][:, 0:64], in0=qh, scalar1=zx[:, 2, h:h + 1], scalar2=None, op0=ALU.mult)
                op("dve", "tensor_scalar", r=["rq", "zx"], w=[f"qq{j}"], out=qq[j][:, 64:128], in0=qh, scalar1=zx[:, 3, h:h + 1], scalar2=None, op0=ALU.mult)
                pt, ptk = self.nextps([0, 1, 2, 3, 4, 5])
                ptb = pt.bitcast(BF16)
                tr(ptb[0:64, 0:128], qh, r=["rq"], w=[ptk])
                tr(ptb[0:64, 128:256], rk[:, n, h * 64:(h + 1) * 64], r=["rk"], w=[ptk])
                tr(ptb[:, 256:384], qq[j], r=[f"qq{j}"], w=[ptk])
                op("dve", "tensor_copy", r=[], w=[ptk, f"tqk{j}"], out=tqk[j], in_=ptb[0:64, 0:256])
                act(tqq[j], ptb[:, 256:384], AF.Identity, r=[], w=[ptk, f"tqq{j}"])
                sps, spk = self.nextps([0, 1, 2, 3, 4, 5])
                mm(sps[:, 0:128], tqk[j][:, 128:256], tqk[j][:, 0:128], True, True, r=[f"tqk{j}"], w=[spk])
                op("dve", "tensor_tensor", r=["DT"], w=[spk, f"AT{j}"], out=AT[j], in0=sps[:, 0:128], in1=DT[:, h, :], op=ALU.mult)
                ysl = yps[:, h * 128:(h + 1) * 128]
                mm(ysl, AT[j], rv[:, n, h * 128:(h + 1) * 128], True, False, r=[f"AT{j}", "rv"], w=[ypk])
                mm(ysl, tqq[j], Pb[:, h, n, :], False, True, r=[f"tqq{j}", f"Pb{n}"], w=[ypk])
            return yps, ypk

        def epilogue(n, yps, ypk):
            for h in range(4):
                op("dve", "bn_stats", r=[], w=[ypk, "gst"], out=gst[:, h, :], in_=yps[:, h * 128:(h + 1) * 128])
            for h in range(4):
                op("dve", "bn_aggr", r=["gst"], w=["gmv"], out=gmv[:, h, :], in_=gst[:, h, :])
            act(grs[:, :, 0], gmv[:, :, 1], AF.Ln, r=["gmv"], w=["grs"], bias=self.constc[:, 3:4])
            act(grs[:, :, 0], grs[:, :, 0], AF.Exp, r=["grs"], w=["grs"], scale=-0.5)
            op("dve", "scalar_tensor_tensor", r=["gmv", "grs"], w=["grs2"], out=grs[:, :, 1], in0=gmv[:, :, 0], scalar=-1.0, in1=grs[:, :, 0],
               op0=ALU.mult, op1=ALU.mult)
            for h in range(4):
                act(yn[:, h * 128:(h + 1) * 128], yps[:, h * 128:(h + 1) * 128], AF.Identity, r=["grs", "grs2"], w=[ypk, "yn"],
                    scale=grs[:, h, 0:1], bias=grs[:, h, 1:2])
            op("dve", "tensor_tensor", r=["yn", "sg"], w=["orb"], out=orb, in0=yn, in1=sg[:, n, :], op=ALU.mult)
            pt, ptk = self.nextps([0, 1, 2, 3, 4, 5])
            ptb = pt.bitcast(BF16)
            for h in range(4):
                tr(ptb[:, h * 128:(h + 1) * 128], orb[:, h * 128:(h + 1) * 128], r=["orb"], w=[ptk])
            jo = n % 2
            op("dve", "tensor_copy", r=[], w=[ptk, f"oT{jo}"], out=oT[jo], in_=ptb[:, 0:512].rearrange("p (h t) -> p h t", h=4))
            dma(self.ob_d[1][:, n * 128:(n + 1) * 128].rearrange("(h p) t -> p h t", p=128), oT[jo], r=[f"oT{jo}"], w=["ob1_d"])

        prev = None
        for n in range(NT):
            cur = (n,) + heads_part(n)
            if prev is not None:
                epilogue(*prev)
            prev = cur
        epilogue(*prev)
        self.dbg("ob1_d", self.ob_d[1], "ob1_d")

    def angle_tables(self, u, ukey, shape, tag, sinT, cosT, skey, ckey, negsin=False):
        op, act = self.op, self.act
        ki = self.cached(f"at_ki{tag}", shape, I32); kf = self.cached(f"at_kf{tag}", shape); ab = self.cached(f"at_ab{tag}", shape)
        k = f"at{tag}"
        op("dve", "tensor_copy", r=[ukey], w=[k + "ki"], out=ki, in_=u)
        op("dve", "tensor_copy", r=[k + "ki"], w=[k + "kf"], out=kf, in_=ki)
        op("dve", "tensor_tensor", r=[ukey, k + "kf"], w=[ukey], out=u, in0=u, in1=kf, op=ALU.subtract)
        act(sinT, u, AF.Sin, r=[ukey], w=[skey], scale=-TWO_PI if negsin else TWO_PI)
        act(ab, u, AF.Abs, r=[ukey], w=[k + "ab"])
        act(cosT, ab, AF.Sin, r=[k + "ab"], w=[ckey], scale=-TWO_PI, bias=self.halfpi[0:shape[0], :])

    def cached(self, name, shape, dt=F32):
        key = (self.fw.nbar, name)
        if key not in self._cache:
            self._cache[key] = self.sb(name, shape, dt)
        return self._cache[key]

    def phase_S(self, l):
        S, T = self.S, self.T
        op, dma, mm, act = self.op, self.dma, self.mm, self.act
        self.phase("S")
        SBk = min(512, S)
        NB = S // SBk
        TB = min(2048, T)
        uT = self.sb("uT", [128, S], BF16)
        ys = self.sb("ys", [128, S])
        io = self.sb("io", [128, 512])
        lamS = self.sb("lamS", [128, 2, 3, 4])
        stepS = self.sb("stepS", [128, 2, 4]); magS = self.sb("magS", [128, 2, 4]); thS = self.sb("thS", [128, 2, 4])
        Bb = self.sb("Bb", [128, 2, 2, 512], BF16)
        Cb = self.sb("Cb", [128, 2, 2, 512], BF16)
        wc = self.sb("wc", [128, 2, 4, 2])
        mark = self.arena_off
        ust = [self.sb(f"ust{i}", [128, TB]) for i in range(3)]
        uacc = self.sb("uacc", [128, TB])
        n_ = 0
        for q in range(4):
            for tb in range(T // TB):
                for c in range(4):
                    j = n_ % 3; n_ += 1
                    dma(ust[j], self.u_all[(c * 4 + q) * 128:(c * 4 + q + 1) * 128, tb * TB:(tb + 1) * TB], w=[f"ust{j}"])
                    dst = uT[:, q * T + tb * TB:q * T + (tb + 1) * TB] if c == 3 else uacc
                    dk = "uT" if c == 3 else "uacc"
                    if c == 0:
                        op("dve", "tensor_scalar", r=[f"ust{j}"], w=["uacc"], out=uacc, in0=ust[j], scalar1=self.sel[:, 0:1], scalar2=None, op0=ALU.mult)
                    else:
                        op("dve", "scalar_tensor_tensor", r=[f"ust{j}", "uacc"], w=[dk], out=dst, in0=ust[j], scalar=self.sel[:, c:c + 1], in1=uacc,
                           op0=ALU.mult, op1=ALU.add)
        dma(io, self.io_in, w=["io"])
        for d in range(2):
            dma(lamS[:, d, 0, :], self.lamS_re[l, d], w=["lamS"])
            dma(lamS[:, d, 1, :], self.lamS_im[l, d], w=["lamS"])
            dma(lamS[:, d, 2, :], self.lsS[l, d], w=["lamS"])
        act(stepS, lamS[:, :, 2, :], AF.Exp, r=["lamS"], w=["stepS"])
        op("dve", "tensor_tensor", r=["lamS", "stepS"], w=["magS"], out=magS, in0=lamS[:, :, 0, :], in1=stepS, op=ALU.mult)
        act(magS, magS, AF.Exp, r=["magS"], w=["magS"])
        op("dve", "tensor_tensor", r=["lamS", "stepS"], w=["thS"], out=thS, in0=lamS[:, :, 1, :], in1=stepS, op=ALU.mult)
        op("dve", "tensor_scalar", r=["thS"], w=["thS"], out=thS, in0=thS, scalar1=1.0 / TWO_PI, scalar2=None, op0=ALU.mult)
        for d in range(2):
            self.subphase(mark)
            lB = self.sb("lB", [128, 3, 512])
            dma(lB[:, 0, :], self.lamB_re[l, d:d + 1, :].partition_broadcast(128), w=["lB"])
            dma(lB[:, 1, :], self.lamB_im[l, d:d + 1, :].partition_broadcast(128), w=["lB"])
            dma(lB[:, 2, :], self.lsB[l, d:d + 1, :].partition_broadcast(128), w=["lB"])
            sh = [128, 512]
            stB = self.sb("stB", sh); mgB = self.sb("mgB", sh); tu = self.sb("tu", sh); sn = self.sb("sn", sh); cs = self.sb("cs", sh)
            are = self.sb("are", sh); aim = self.sb("aim", sh); den = self.sb("den", sh); fre = self.sb("fre", sh); fim = self.sb("fim", sh)
            t1 = self.sb("t1s", sh)
            lre, lim = lB[:, 0, :], lB[:, 1, :]
            act(stB, lB[:, 2, :], AF.Exp, r=["lB"], w=["stB"])
            op("dve", "tensor_tensor", r=["lB", "stB"], w=["mgB"], out=mgB, in0=lre, in1=stB, op=ALU.mult)
            act(mgB, mgB, AF.Exp, r=["mgB"], w=["mgB"])
            op("dve", "tensor_tensor", r=["lB", "stB"], w=["tu"], out=tu, in0=lim, in1=stB, op=ALU.mult)
            op("dve", "tensor_scalar", r=["tu"], w=["tu"], out=tu, in0=tu, scalar1=1.0 / TWO_PI, scalar2=None, op0=ALU.mult)
            self.angle_tables(tu, "tu", sh, "B", sn, cs, "sn", "cs")
            op("dve", "tensor_tensor", r=["mgB", "cs"], w=["are"], out=are, in0=mgB, in1=cs, op=ALU.mult)
            op("dve", "tensor_tensor", r=["mgB", "sn"], w=["aim"], out=aim, in0=mgB, in1=sn, op=ALU.mult)
            op("dve", "tensor_scalar", r=["are"], w=["are"], out=are, in0=are, scalar1=-1.0, scalar2=None, op0=ALU.add)
            op("dve", "tensor_tensor", r=["lB"], w=["den"], out=den, in0=lre, in1=lre, op=ALU.mult)
            op("dve", "tensor_tensor", r=["lB"], w=["t1s"], out=t1, in0=lim, in1=lim, op=ALU.mult)
            op("dve", "tensor_tensor", r=["den", "t1s"], w=["den"], out=den, in0=den, in1=t1, op=ALU.add)
            op("dve", "reciprocal", r=["den"], w=["den"], out=den, in_=den)
            op("dve", "tensor_tensor", r=["are", "lB"], w=["fre"], out=fre, in0=are, in1=lre, op=ALU.mult)
            op("dve", "tensor_tensor", r=["aim", "lB"], w=["t1s"], out=t1, in0=aim, in1=lim, op=ALU.mult)
            op("dve", "tensor_tensor", r=["fre", "t1s"], w=["fre"], out=fre, in0=fre, in1=t1, op=ALU.add)
            op("dve", "tensor_tensor", r=["fre", "den"], w=["fre"], out=fre, in0=fre, in1=den, op=ALU.mult)
            op("dve", "tensor_tensor", r=["aim", "lB"], w=["fim"], out=fim, in0=aim, in1=lre, op=ALU.mult)
            op("dve", "tensor_tensor", r=["are", "lB"], w=["t1s"], out=t1, in0=are, in1=lim, op=ALU.mult)
            op("dve", "tensor_tensor", r=["fim", "t1s"], w=["fim"], out=fim, in0=fim, in1=t1, op=ALU.subtract)
            op("dve", "tensor_tensor", r=["fim", "den"], w=["fim"], out=fim, in0=fim, in1=den, op=ALU.mult)
            braw = self.sb("braw", [128, 2, 512]); t2 = self.sb("t2s", [128, 512]); t3 = self.sb("t3s", [128, 512])
            dma(braw[:, 0, :], self.Bblk_re[l, d], w=["braw"])
            dma(braw[:, 1, :], self.Bblk_im[l, d], w=["braw"])
            op("dve", "tensor_tensor", r=["braw", "fre"], w=["t2s"], out=t2, in0=braw[:, 0, :], in1=fre, op=ALU.mult)
            op("dve", "tensor_tensor", r=["braw", "fim"], w=["t3s"], out=t3, in0=braw[:, 1, :], in1=fim, op=ALU.mult)
            op("dve", "tensor_tensor", r=["t2s", "t3s"], w=["Bb"], out=Bb[:, d, 0, :], in0=t2, in1=t3, op=ALU.subtract)
            op("dve", "tensor_tensor", r=["braw", "fre"], w=["t2s"], out=t2, in0=braw[:, 1, :], in1=fre, op=ALU.mult)
            op("dve", "tensor_tensor", r=["braw", "fim"], w=["t3s"], out=t3, in0=braw[:, 0, :], in1=fim, op=ALU.mult)
            op("dve", "tensor_tensor", r=["t2s", "t3s"], w=["Bb"], out=Bb[:, d, 1, :], in0=t2, in1=t3, op=ALU.add)
            dma(braw[:, 0, :], self.Cblk_re[l, d], r=["Bb"], w=["braw"])
            dma(braw[:, 1, :], self.Cblk_im[l, d], r=["Bb"], w=["braw"])
            op("pool", "tensor_copy", r=["braw"], w=["Cb"], out=Cb[:, d, :, :], in_=braw)
        self.subphase(mark)
        op("dve", "memset", w=["wc"], ap=wc, constant=0.0)
        W = [128, SBk]
        tun = [self.sb(f"tun{i}", W) for i in range(2)]
        nsin = [self.sb(f"nsin{i}", W) for i in range(2)]; cosb = [self.sb(f"cosb{i}", W) for i in range(2)]
        ta = self.sb("ta", W); tb_ = self.sb("tb", W); tc = self.sb("tc", W); td = self.sb("td", W)
        vre = self.sb("vre", W); vim = self.sb("vim", W); wre = self.sb("wre", W); wim = self.sb("wim", W)
        xre = [self.sb(f"xre{i}", W, BF16) for i in range(2)]; nxi = [self.sb(f"nxi{i}", W, BF16) for i in range(2)]
        its = [(d, sbi, st) for d in range(2) for sbi in range(NB) for st in range(4)]

        def front(i):
            d, sbi, st = its[i]
            j = i % 2
            tau0 = sbi * SBk
            t0 = tau0 if d == 0 else S - tau0 - SBk
            op("dve", "tensor_scalar", r=["io", "thS"], w=[f"tun{j}"], out=tun[j], in0=io[:, 0:SBk], scalar1=float(tau0), scalar2=thS[:, d, st:st + 1],
               op0=ALU.add, op1=ALU.mult)
            self.angle_tables(tun[j], f"tun{j}", W, "M", nsin[j], cosb[j], f"nsin{j}", f"cosb{j}", negsin=True)
            bi = (2 * i) % 6
            pre, prk = self.ps[bi], f"ps{bi}"
            pim, pik = self.ps[bi + 1], f"ps{bi + 1}"
            mm(pre[:, 0:SBk], Bb[:, d, 0, st * 128:(st + 1) * 128], uT[:, t0:t0 + SBk], True, True, r=["Bb", "uT"], w=[prk])
            mm(pim[:, 0:SBk], Bb[:, d, 1, st * 128:(st + 1) * 128], uT[:, t0:t0 + SBk], True, True, r=["Bb", "uT"], w=[pik])
            return pre, prk, pim, pik

        def back(i, pre, prk, pim, pik, aps, apk):
            d, sbi, st = its[i]
            j = i % 2
            bre = pre[:, 0:SBk] if d == 0 else rev_ap(pre[:, 0:SBk], SBk)
            bim = pim[:, 0:SBk] if d == 0 else rev_ap(pim[:, 0:SBk], SBk)
            ns, cb = nsin[j], cosb[j]
            op("dve", "tensor_tensor", r=[f"cosb{j}"], w=[prk, "ta"], out=ta, in0=bre, in1=cb, op=ALU.mult)
            op("dve", "tensor_tensor", r=[f"nsin{j}"], w=[pik, "tb"], out=tb_, in0=bim, in1=ns, op=ALU.mult)
            op("dve", "tensor_tensor", r=["ta", "tb"], w=["vre"], out=vre, in0=ta, in1=tb_, op=ALU.subtract)
            op("dve", "tensor_tensor", r=[f"cosb{j}"], w=[pik, "tc"], out=tc, in0=bim, in1=cb, op=ALU.mult)
            op("dve", "tensor_tensor", r=[f"nsin{j}"], w=[prk, "td"], out=td, in0=bre, in1=ns, op=ALU.mult)
            op("dve", "tensor_tensor", r=["tc", "td"], w=["vim"], out=vim, in0=tc, in1=td, op=ALU.add)
            mg = magS[:, d, st:st + 1].to_broadcast(W)
            op("dve", "tensor_tensor_scan", r=["vre", "magS", "wc"], w=["wre"], out=wre, data0=mg, data1=vre, initial=wc[:, d, st, 0:1], op0=ALU.mult, op1=ALU.add)
            op("dve", "tensor_tensor_scan", r=["vim", "magS", "wc"], w=["wim"], out=wim, data0=mg, data1=vim, initial=wc[:, d, st, 1:2], op0=ALU.mult, op1=ALU.add)
            act(wc[:, d, st, 0:1], wre[:, SBk - 1:SBk], AF.Identity, r=["wre"], w=["wc"])
            act(wc[:, d, st, 1:2], wim[:, SBk - 1:SBk], AF.Identity, r=["wim"], w=["wc"])
            op("dve", "tensor_tensor", r=["wre", f"cosb{j}"], w=["ta"], out=ta, in0=wre, in1=cb, op=ALU.mult)
            op("dve", "tensor_tensor", r=["wim", f"nsin{j}"], w=["tb"], out=tb_, in0=wim, in1=ns, op=ALU.mult)
            op("dve", "tensor_tensor", r=["ta", "tb"], w=[f"xre{j}"], out=xre[j], in0=ta, in1=tb_, op=ALU.add)
            op("dve", "tensor_tensor", r=["wre", f"nsin{j}"], w=["tc"], out=tc, in0=wre, in1=ns, op=ALU.mult)
            op("dve", "tensor_tensor", r=["wim", f"cosb{j}"], w=["td"], out=td, in0=wim, in1=cb, op=ALU.mult)
            op("dve", "tensor_tensor", r=["tc", "td"], w=[f"nxi{j}"], out=nxi[j], in0=tc, in1=td, op=ALU.subtract)
            mm(aps[:, 0:SBk], Cb[:, d, 0, st * 128:(st + 1) * 128], xre[j], st == 0, False, r=["Cb", f"xre{j}"], w=[apk])
            mm(aps[:, 0:SBk], Cb[:, d, 1, st * 128:(st + 1) * 128], nxi[j], False, st == 3, r=["Cb", f"nxi{j}"], w=[apk])
            if st == 3:
                tau0 = sbi * SBk
                t0 = tau0 if d == 0 else S - tau0 - SBk
                if d == 0:
                    act(ys[:, t0:t0 + SBk], aps[:, 0:SBk], AF.Identity, r=[], w=[apk, "ys"])
                else:
                    yr = rev_ap(ys[:, t0:t0 + SBk], SBk)
                    op("dve", "tensor_tensor", r=[], w=[apk, "ys"], out=yr, in0=aps[:, 0:SBk], in1=yr, op=ALU.add)

        nxt = front(0)
        aps = apk = None
        for i in range(len(its)):
            cur = nxt
            if i + 1 < len(its):
                nxt = front(i + 1)
            if its[i][2] == 0:
                ai = 6 + (i // 4) % 2
                aps, apk = self.ps[ai], f"ps{ai}"
            back(i, *cur, aps, apk)
        for q in range(4):
            dma(self.ys_loc[q * 128:(q + 1) * 128, :], ys[:, q * T:(q + 1) * T], r=["ys"], w=["ys_loc"])
        for q in range(4):
            self.allgather(self.ys_all[q * 512:(q + 1) * 512, :], self.ys_loc[q * 128:(q + 1) * 128, :], r=["ys_loc"], w=["ys_all"])
        self.dbg("ys_loc", self.ys_loc, "ys_loc")

    def phase_AT(self, l):
        S, T = self.S, self.T
        op, dma, mm, act = self.op, self.dma, self.mm, self.act
        self.phase("AT")
        NKT = S // 128
        QB = min(512, T)
        KB = min(512, S)
        kvnT = self.sb("kvnT", [128, S], BF16); krT = self.sb("krT", [32, S], BF16)
        for r_ in range(4):
            dma(kvnT[:, r_ * T:(r_ + 1) * T], self.lat_all[r_ * 160:r_ * 160 + 128, :], r=["lat_all"], w=["kvnT"])
            dma(krT[:, r_ * T:(r_ + 1) * T], self.lat_all[r_ * 160 + 128:r_ * 160 + 160, :], r=["lat_all"], w=["krT"])
        kvg = self.sb("kvg", [128, 1]); wst = self.sb("wukv_st", [128, 1024])
        Wk = self.sb("Wk", [128, 8, 96], BF16); Wv = self.sb("Wv", [128, 8, 64], BF16); Sel = self.sb("Sel", [32, 96], BF16)
        dma(kvg, self.kvnorm_in[l], w=["kvg"])
        dma(wst, self.w_ukv[l], w=["wukv_st"])
        op("dve", "memset", w=["Wk"], ap=Wk, constant=0.0)
        op("dve", "memset", w=["Sel"], ap=Sel, constant=0.0)
        wv_ = wst.rearrange("p (h d) -> p h d", h=8)
        op("dve", "tensor_scalar", r=["wukv_st", "kvg", "Wk"], w=["Wk"], out=Wk[:, :, 0:64], in0=wv_[:, :, 0:64], scalar1=kvg[:, 0:1], scalar2=None, op0=ALU.mult)
        op("dve", "tensor_scalar", r=["wukv_st", "kvg"], w=["Wv"], out=Wv, in0=wv_[:, :, 64:128], scalar1=kvg[:, 0:1], scalar2=None, op0=ALU.mult)
        op("dve", "tensor_copy", r=["Sel", "ident"], w=["Sel"], out=Sel[:, 64:96], in_=self.ident[0:32, 0:32])
        KT = [self.sb(f"KT{i}", [96, S], BF16) for i in range(2)]
        Vp = [self.sb(f"Vp{i}", [128, NKT, 65], BF16) for i in range(2)]
        QTh = [self.sb(f"QTh{i}", [96, T], BF16) for i in range(2)]
        pT = [self.sb(f"pT{i}", [128, QB], BF16) for i in range(3)]
        rec = self.sb("rec", [128, QB]); Rb = self.sb("Rb", [64, QB])
        oh = [self.sb(f"oh{i}", [64, QB], BF16) for i in range(2)]
        for i in range(2):
            op("dve", "memset", w=[f"Vp{i}"], ap=Vp[i], constant=1.0)
        scale = 96.0 ** -0.5
        np_ = 0; no_ = 0
        for h in range(NH):
            hb = h % 2
            dma(QTh[hb], self.qT_d[h], r=["qT_d"], w=[f"QTh{hb}"])
            for kb in range(S // KB):
                ksl = slice(kb * KB, (kb + 1) * KB)
                ps, pk = self.nextps([0, 1])
                mm(ps[0:96, 0:KB], Wk[:, h, :], kvnT[:, ksl], True, False, r=["Wk", "kvnT"], w=[pk])
                mm(ps[0:96, 0:KB], Sel, krT[:, ksl], False, True, r=["Sel", "krT"], w=[pk])
                op("dve", "tensor_copy", r=[], w=[pk, f"KT{hb}"], out=KT[hb][:, ksl], in_=ps[0:96, 0:KB])
            for g in range(0, NKT, 8):
                ng = min(8, NKT - g)
                ps, pk = self.nextps([0, 1])
                for t in range(ng):
                    mm(ps[:, t * 64:(t + 1) * 64], kvnT[:, (g + t) * 128:(g + t + 1) * 128], Wv[:, h, :], True, True, r=["kvnT", "Wv"], w=[pk])
                act(Vp[hb][:, g:g + ng, 0:64], ps[:, 0:ng * 64].rearrange("p (t d) -> p t d", t=ng), AF.Identity, r=[], w=[pk, f"Vp{hb}"])
            for qb in range(T // QB):
                qsl = slice(qb * QB, (qb + 1) * QB)
                acc, ak = self.nextps([6, 7])
                LOOK = 2

                def issue_s(kt_):
                    sps_, spk_ = self.nextps([2, 3, 4])
                    mm(sps_[:, 0:QB], KT[hb][:, kt_ * 128:(kt_ + 1) * 128], QTh[hb][:, qsl], True, True, r=[f"KT{hb}", f"QTh{hb}"], w=[spk_])
                    return sps_, spk_
                pend = [issue_s(k_) for k_ in range(min(LOOK, NKT))]
                for kt in range(NKT):
                    sps, spk = pend.pop(0)
                    j = np_ % 3; np_ += 1
                    act(pT[j], sps[:, 0:QB], AF.Exp, r=[], w=[spk, f"pT{j}"], scale=scale)
                    if kt + LOOK < NKT:
                        pend.append(issue_s(kt + LOOK))
                    mm(acc[0:65, 0:QB], Vp[hb][:, kt, :], pT[j], kt == 0, kt == NKT - 1, r=[f"Vp{hb}", f"pT{j}"], w=[ak])
                op("dve", "reciprocal", r=[], w=[ak, "rec"], out=rec[64:65, :], in_=acc[64:65, 0:QB])
                rp, rpk = self.nextps([5])
                mm(rp[0:64, 0:QB], self.ones_f[64:65, 0:64], rec[64:65, :], True, True, r=["rec", "ones_f"], w=[rpk])
                act(Rb, rp[0:64, 0:QB], AF.Identity, r=[], w=[rpk, "Rb"])
                jo = no_ % 2; no_ += 1
                op("dve", "tensor_tensor", r=["Rb"], w=[ak, f"oh{jo}"], out=oh[jo], in0=acc[0:64, 0:QB], in1=Rb, op=ALU.mult)
                dma(self.ob_d[0][h * 64:(h + 1) * 64, qsl], oh[jo], r=[f"oh{jo}"], w=["ob0_d"])
        self.dbg("ob0_d", self.ob_d[0], "ob0_d")

    def phase_M0(self, l):
        S, T = self.S, self.T
        op, dma, mm, act = self.op, self.dma, self.mm, self.act
        self.phase("M0")
        TB = min(512, T)
        Wg = self.sb("Wglu", [128, 4, 512], BF16)
        self.mk_wstage(512, n=2)
        for k in range(4):
            self.wload(Wg[:, k, :], self.w_glu[l, k * 128:(k + 1) * 128, :], "Wglu")
        dsk = self.sb("dsk", [128, 4])
        dma(dsk, self.dskip_in[l], w=["dsk"])
        cand = [self.sb(f"cand{i}", [128, TB]) for i in range(3)]
        ub = [self.sb(f"ub{i}", [128, TB]) for i in range(2)]
        yacc = self.sb("yacc", [128, TB]); y = self.sb("yM0", [128, 4, TB]); t1 = self.sb("t1m", [128, TB]); t2 = self.sb("t2m", [128, TB])
        yg = self.sb("yg", [128, 4, TB]); ygb = self.sb("ygb", [128, 4, TB], BF16); sgm = self.sb("sgm", [128, TB])
        ob = [self.sb(f"obs{i}", [128, TB], BF16) for i in range(2)]
        n_ = 0; nu = 0; no_ = 0
        for tb in range(T // TB):
            bsl = slice(tb * TB, (tb + 1) * TB)
            for fc in range(4):
                for q in range(4):
                    j = n_ % 3; n_ += 1
                    dma(cand[j], self.ys_all[(q * 4 + fc) * 128:(q * 4 + fc + 1) * 128, tb * TB:(tb + 1) * TB], r=["ys_all"], w=[f"cand{j}"])
                    if q == 0:
                        op("dve", "tensor_scalar", r=[f"cand{j}"], w=["yacc"], out=yacc, in0=cand[j], scalar1=self.sel[:, 0:1], scalar2=None, op0=ALU.mult)
                    else:
                        op("dve", "scalar_tensor_tensor", r=[f"cand{j}", "yacc"], w=["yacc"], out=yacc, in0=cand[j], scalar=self.sel[:, q:q + 1], in1=yacc,
                           op0=ALU.mult, op1=ALU.add)
                ju = nu % 2; nu += 1
                dma(ub[ju], self.u_loc[fc * 128:(fc + 1) * 128, bsl], r=["u_loc"], w=[f"ub{ju}"])
                yk = f"y{fc}"
                op("dve", "scalar_tensor_tensor", r=[f"ub{ju}", "yacc", "dsk"], w=[yk], out=y[:, fc, :], in0=ub[ju], scalar=dsk[:, fc:fc + 1], in1=yacc,
                   op0=ALU.mult, op1=ALU.add)
                op("dve", "tensor_tensor", r=[yk], w=["t1m"], out=t1, in0=y[:, fc, :], in1=y[:, fc, :], op=ALU.mult)
                op("dve", "tensor_scalar", r=["t1m"], w=["t1m"], out=t1, in0=t1, scalar1=0.044715, scalar2=1.0, op0=ALU.mult, op1=ALU.add)
                op("dve", "tensor_tensor", r=["t1m", yk], w=["t2m"], out=t2, in0=t1, in1=y[:, fc, :], op=ALU.mult)
                act(t2, t2, AF.Sigmoid, r=["t2m"], w=["t2m"], scale=2.0 * math.sqrt(2.0 / math.pi))
                op("dve", "tensor_tensor", r=["t2m", yk], w=[f"yg{fc}"], out=yg[:, fc, :], in0=t2, in1=y[:, fc, :], op=ALU.mult)
                act(ygb[:, fc, :], yg[:, fc, :], AF.Identity, r=[f"yg{fc}"], w=[f"ygb{fc}"])
            for fo in range(4):
                ps, pk = self.nextps()
                for k in range(4):
                    mm(ps[:, 0:TB], Wg[:, k, fo * 128:(fo + 1) * 128], ygb[:, k, :], k == 0, k == 3, r=["Wglu"] + [f"ygb{k}"], w=[pk])
                act(sgm, ps[:, 0:TB], AF.Sigmoid, r=[], w=[pk, "sgm"])
                jo = no_ % 2; no_ += 1
                op("dve", "tensor_tensor", r=["sgm", f"yg{fo}"], w=[f"obs{jo}"], out=ob[jo], in0=sgm, in1=yg[:, fo, :], op=ALU.mult)
                dma(self.ob_d[2][fo * 128:(fo + 1) * 128, bsl], ob[jo], r=[f"obs{jo}"], w=["ob2_d"])
        self.dbg("ob2_d", self.ob_d[2], "ob2_d")

    def phase_M1(self, l):
        T = self.T
        op, dma, mm, act = self.op, self.dma, self.mm, self.act
        self.phase("M1")
        TB = min(512, T)
        hT = self.sb("hT", [128, 8, T], BF16)
        self.mk_wstage(2304, n=3)
        self.phase_A(l, 0, hT, 0)
        hks = [f"hT{i}" for i in range(self.NT)]
        obT = [self.sb(f"obT{i}", [128, 4, TB], BF16) for i in range(3)]
        nob = 0
        Wg = [self.sb(f"Wg{i}", [128, 3, 8, 128], BF16) for i in range(2)]
        Wb = [self.sb(f"Wb{i}", [128, 3, 4, 128], BF16) for i in range(2)]
        sg = [self.sb(f"sgt{i}", [128, TB]) for i in range(2)]
        m = self.sb("macc", [128, TB]); tm = self.sb("tmM", [128, TB])
        mo = [self.sb(f"mo{i}", [128, TB], BF16) for i in range(2)]
        ns = 0; no_ = 0
        for f in range(8):
            jw = f % 2
            for b in range(3):
                c0 = C_G + b * D + f * 128
                self.wload(Wg[jw][:, b, :, :], self.w_in[l, :, c0:c0 + 128].rearrange("(k p) n -> p k n", p=128), f"Wg{jw}")
                self.wload(Wb[jw][:, b, :, :], self.w_br[b][l, :, f * 128:(f + 1) * 128].rearrange("(k p) n -> p k n", p=128), f"Wb{jw}")
            for tb in range(T // TB):
                bsl = slice(tb * TB, (tb + 1) * TB)
                for b in range(3):
                    gp, gk = self.nextps()
                    for k in range(8):
                        mm(gp[:, 0:TB], Wg[jw][:, b, k, :], hT[:, k, bsl], k == 0, k == 7, r=[f"Wg{jw}"] + hks, w=[gk])
                    js = ns % 2; ns += 1
                    act(sg[js], gp[:, 0:TB], AF.Sigmoid, r=[], w=[gk, f"sgt{js}"])
                    jb = nob % 3; nob += 1
                    dma(obT[jb], self.ob_d[b][:, bsl].rearrange("(k p) t -> p k t", p=128), r=[f"ob{b}_d"], w=[f"obT{jb}"])
                    bp, bk = self.nextps()
                    for k in range(4):
                        mm(bp[:, 0:TB], Wb[jw][:, b, k, :], obT[jb][:, k, :], k == 0, k == 3, r=[f"Wb{jw}", f"obT{jb}"], w=[bk])
                    if b == 0:
                        op("dve", "tensor_tensor", r=[f"sgt{js}"], w=[bk, "macc"], out=m, in0=bp[:, 0:TB], in1=sg[js], op=ALU.mult)
                    else:
                        op("dve", "tensor_tensor", r=[f"sgt{js}"], w=[bk, "tmM"], out=tm, in0=bp[:, 0:TB], in1=sg[js], op=ALU.mult)
                        if b == 1:
                            op("dve", "tensor_tensor", r=["tmM", "macc"], w=["macc"], out=m, in0=m, in1=tm, op=ALU.add)
                        else:
                            jo = no_ % 2; no_ += 1
                            op("dve", "tensor_tensor", r=["tmM", "macc"], w=[f"mo{jo}"], out=mo[jo], in0=m, in1=tm, op=ALU.add)
                            dma(self.mT_d[f * 128:(f + 1) * 128, bsl], mo[jo], r=[f"mo{jo}"], w=["mT_d"])
        self.dbg("mT_d", self.mT_d, "mT_d")

    def phase_M2(self, l):
        T, NT = self.T, self.NT
        op, dma, mm, act = self.op, self.dma, self.mm, self.act
        self.phase("M2")
        self.mk_wstage(2304, n=2)
        gB = self.sb("gateB", [128, D])
        self.modvec(l, 2, gB, "gateB")
        Wo = self.sb("Wo", [128, 8, D], BF16)
        for k in range(8):
            self.wload(Wo[:, k, :], self.w_o[l, k * 128:(k + 1) * 128, :], "Wo", mul=gB, mul_key="gateB")
        mT = self.sb("mT", [128, 8, T], BF16)
        dma(mT, self.mT_d.rearrange("(k p) t -> p k t", p=128), r=["mT_d"], w=["mT"])
        for i in range(NT):
            for half in range(2):
                ps, pk = self.nextps()
                for k in range(8):
                    mm(ps, mT[:, k, i * 128:(i + 1) * 128], Wo[:, k, half * 512:(half + 1) * 512], k == 0, k == 7, r=["mT", "Wo"], w=[pk])
                xs = self.x_res[:, i, half * 512:(half + 1) * 512]
                op("dve", "scalar_tensor_tensor", r=[], w=[pk, f"x{i}"], out=xs, in0=xs, scalar=ALPHA, in1=ps, op0=ALU.mult, op1=ALU.add)
        self.post_norm(l, 0)

    def post_norm(self, l, which):
        NT = self.NT
        op, dma, act = self.op, self.dma, self.act
        gB = self.sb("lngB", [128, D]); bB = self.sb("lnbB", [128, D])
        dma(gB, self.ln_in[l, 2 * which:2 * which + 1, :].partition_broadcast(128), w=["lngB"])
        dma(bB, self.ln_in[l, 2 * which + 1:2 * which + 2, :].partition_broadcast(128), w=["lnbB"])
        lns = [self.mk_lnscr(f"p{i}") for i in range(2)]
        xn = [self.sb(f"pxn{i}", [128, D]) for i in range(2)]
        for i in range(NT):
            j = i % 2
            xt = self.x_res[:, i, :]
            rs, rk = self.ln_stats(xt, f"x{i}", 128, lns[j])
            act(xn[j], xt, AF.Identity, r=[f"x{i}"] + rk, w=[f"pxn{j}"], scale=rs[:, 0:1], bias=rs[:, 1:2])
            op("dve", "tensor_tensor", r=[f"pxn{j}", "lngB"], w=[f"pxn{j}"], out=xn[j], in0=xn[j], in1=gB, op=ALU.mult)
            op("dve", "tensor_tensor", r=[f"pxn{j}", "lnbB"], w=[f"x{i}"], out=xt, in0=xn[j], in1=bB, op=ALU.add)

    def phase_F(self, l, last):
        T, NT = self.T, self.NT
        op, dma, mm, act, tr = self.op, self.dma, self.mm, self.act, self.tr
        self.phase("H")
        dma(self.halo_loc[0:1, :], self.x_res[0:1, 0, :], w=["halo_loc"])
        dma(self.halo_loc[1:2, :], self.x_res[127:128, NT - 1, :], w=["halo_loc"])
        self.allgather(self.halo_all, self.halo_loc, r=["halo_loc"], w=["halo_all"])
        self.phase("F")
        hT = self.sb("hT2", [128, 8, T + 2], BF16)
        self.mk_wstage(2304, n=3)
        shB, scB = self.phase_A(l, 1, hT, 1)
        rows = self.sb("hrows", [8, D]); xh = self.sb("xh", [2, D]); hh = self.sb("hh", [2, D], BF16)
        dma(rows, self.halo_all, r=["halo_all"], w=["hrows"])
        for half in range(2):
            ps, pk = self.nextps()
            mm(ps[0:2, :], self.selH, rows[:, half * 512:(half + 1) * 512], True, True, r=["hrows"], w=[pk])
            act(xh[:, half * 512:(half + 1) * 512], ps[0:2, :], AF.Identity, r=[], w=[pk, "xh"])
        lsc = self.mk_lnscr("h")
        rs, rk = self.ln_stats(xh, "xh", 2, lsc)
        act(xh, xh, AF.Identity, r=["xh"] + rk, w=["xh"], scale=rs[0:2, 0:1], bias=rs[0:2, 1:2])
        op("dve", "tensor_tensor", r=["xh", "scB"], w=["xh"], out=xh, in0=xh, in1=scB[0:2, :], op=ALU.mult)
        op("dve", "tensor_tensor", r=["xh", "shB"], w=["xh"], out=xh, in0=xh, in1=shB[0:2, :], op=ALU.add)
        op("dve", "tensor_scalar", r=["xh"], w=["hh"], out=hh, in0=xh, scalar1=self.flag[0:2, 0:1], scalar2=None, op0=ALU.mult)
        ps, pk = self.nextps()
        psb = ps.bitcast(BF16)
        for k in range(8):
            tr(psb[:, k * 2:(k + 1) * 2], hh[:, k * 128:(k + 1) * 128], r=["hh"], w=[pk])
        pv = psb[:, 0:16].rearrange("p (k t) -> p k t", k=8)
        op("dve", "tensor_copy", r=[], w=[pk, "hTh0"], out=hT[:, :, 0:1], in_=pv[:, :, 0:1])
        op("dve", "tensor_copy", r=[], w=[pk, "hTh1"], out=hT[:, :, T + 1:T + 2], in_=pv[:, :, 1:2])
        hks = [f"hT{i}" for i in range(NT)] + ["hTh0", "hTh1"]
        gB = self.sb("gate2B", [128, D])
        self.modvec(l, 5, gB, "gate2B")
        cw = self.sb("cw", [128, 44, 3]); cb = self.sb("cb", [128, 44])
        dma(cw, self.convw_in[l].rearrange("p (c k) -> p c k", k=3), w=["cw"])
        dma(cb, self.convb_in[l], w=["cb"])
        FB = min(256, T)
        Wa = [self.sb(f"Wa{i}", [128, 8, 128], BF16) for i in range(2)]
        Wgt = [self.sb(f"Wgt{i}", [128, 8, 128], BF16) for i in range(2)]
        Wd = [self.sb(f"Wd{i}", [128, D], BF16) for i in range(2)]
        ca_ = [self.sb(f"ca{i}", [128, FB]) for i in range(2)]; cg_ = [self.sb(f"cg{i}", [128, FB]) for i in range(2)]
        av = [self.sb(f"av{i}", [128, FB], BF16) for i in range(2)]
        na = 0
        NTB = T // FB

        def issue_up(jf, tb):
            jw = jf % 2
            if tb == 0:
                self.wload(Wa[jw], self.w_up[l, :, jf * 128:(jf + 1) * 128].rearrange("(k p) n -> p k n", p=128), f"Wa{jw}")
                self.wload(Wgt[jw], self.w_up[l, :, DFF + jf * 128:DFF + (jf + 1) * 128].rearrange("(k p) n -> p k n", p=128), f"Wgt{jw}")
                self.wload(Wd[jw], self.w_down[l, jf * 128:(jf + 1) * 128, :], f"Wd{jw}", mul=gB, mul_key="gate2B")
            c0 = tb * FB
            ui = (jf * NTB + tb) % 2
            pa, pak = self.ps[ui], f"ps{ui}"
            pg, pgk = self.ps[2 + ui], f"ps{2 + ui}"
            for k in range(8):
                mm(pa[:, 0:FB + 2], Wa[jw][:, k, :], hT[:, k, c0:c0 + FB + 2], k == 0, k == 7, r=[f"Wa{jw}"] + hks, w=[pak])
            for k in range(8):
                mm(pg[:, 0:FB + 2], Wgt[jw][:, k, :], hT[:, k, c0:c0 + FB + 2], k == 0, k == 7, r=[f"Wgt{jw}"] + hks, w=[pgk])
            return pa, pak, pg, pgk

        def finish(jf, tb, pa, pak, pg, pgk):
            nonlocal na
            jw = jf % 2
            c0 = tb * FB
            jc = na % 2
            ca, cg = ca_[jc], cg_[jc]
            kca, kcg = f"ca{jc}", f"cg{jc}"
            for (pp, ppk, dst, dk, ch) in ((pa, pak, ca, kca, jf), (pg, pgk, cg, kcg, 22 + jf)):
                act(dst, pp[:, 1:FB + 1], AF.Identity, r=["cw", "cb"], w=[ppk, dk], scale=cw[:, ch, 1:2], bias=cb[:, ch:ch + 1])
                op("dve", "scalar_tensor_tensor", r=["cw"], w=[ppk, dk], out=dst, in0=pp[:, 0:FB], scalar=cw[:, ch, 0:1], in1=dst, op0=ALU.mult, op1=ALU.add)
                op("dve", "scalar_tensor_tensor", r=["cw"], w=[ppk, dk], out=dst, in0=pp[:, 2:FB + 2], scalar=cw[:, ch, 2:3], in1=dst, op0=ALU.mult, op1=ALU.add)
            act(cg, cg, AF.Silu, r=[kcg], w=[kcg])
            ja = na % 2; na += 1
            op("dve", "tensor_tensor", r=[kca, kcg], w=[f"av{ja}"], out=av[ja], in0=ca, in1=cg, op=ALU.mult)
            for ti in range(FB // 128):
                i = (c0 // 128) + ti
                for half in range(2):
                    ps, pk = self.nextps([4, 5, 6, 7])
                    mm(ps, av[ja][:, ti * 128:(ti + 1) * 128], Wd[jw][:, half * 512:(half + 1) * 512], True, True, r=[f"av{ja}", f"Wd{jw}"], w=[pk])
                    xs = self.x_res[:, i, half * 512:(half + 1) * 512]
                    if jf == 0:
                        op("dve", "scalar_tensor_tensor", r=[], w=[pk, f"x{i}"], out=xs, in0=xs, scalar=ALPHA, in1=ps, op0=ALU.mult, op1=ALU.add)
                    else:
                        op("dve", "tensor_tensor", r=[], w=[pk, f"x{i}"], out=xs, in0=xs, in1=ps, op=ALU.add)

        prev = None
        for jf in range(DFF // 128):
            for tb in range(NTB):
                cur = (jf, tb) + issue_up(jf, tb)
                if prev is not None:
                    finish(*prev)
                prev = cur
        finish(*prev)
        self.post_norm(l, 1)
        if last:
            dma(self.out.rearrange("(n p) d -> p n d", p=128), self.x_res, r=[f"x{i}" for i in range(NT)], w=["out"])

    def build(self, stop=None):
        self.declare_inputs()
        self.setup()
        for l in range(self.L):
            self.phase_B(l)
            self.phase_R(l)
            self.phase_S(l)
            self.phase_AT(l)
            self.phase_M0(l)
            self.phase_M1(l)
            self.phase_M2(l)
            self.phase_F(l, last=(l == self.L - 1))
        self.fw.barrier()
        self.fw.emit()
        return self.nc


def _consts(S):
    T = S // 4
    NT = T // 128
    f32 = np.float32
    p = np.arange(128, dtype=f32)
    c = {}
    c["invf16"] = np.broadcast_to((f32(10000.0) ** (-np.arange(16, dtype=f32) / f32(16))).astype(f32), (128, 16)).copy()
    c["invf32"] = np.broadcast_to((f32(10000.0) ** (-np.arange(32, dtype=f32) / f32(32))).astype(f32), (128, 32)).copy()
    jj, ii = np.meshgrid(p, p, indexing="ij")
    EF = np.maximum(ii - jj, 0); EB = np.maximum(jj - ii, 0)
    MF = (ii >= jj).astype(f32); MB = (jj > ii).astype(f32)
    c["cmask"] = np.stack([EF, EB, MF, MB]).astype(f32)
    c["ek"] = np.stack([127 - p, p, p + 1, 128 - p], 1).astype(f32)
    n = np.arange(NT, dtype=f32)
    et = np.zeros((128, NT), f32); et[:64] = n[None, :]; et[64:] = (NT - 1 - n)[None, :]
    c["etab"] = et
    c["io"] = np.broadcast_to(np.arange(1, 513, dtype=f32), (128, 512)).copy()
    return c


def prep_inputs(inp, S, L):
    T = S // 4
    NT = T // 128
    f32 = np.float32
    A = lambda a: np.ascontiguousarray(np.asarray(a))
    cst = _consts(S)
    sh = {}
    sh["w_in"] = A(inp["w_in"][:L]); sh["w_uq"] = A(inp["mla_w_uq"][:L]); sh["w_ukv"] = A(inp["mla_w_ukv"][:L])
    sh["qnorm"] = A(np.asarray(inp["mla_q_norm"])[:L].reshape(L, 2, 128).transpose(0, 2, 1))
    sh["kvnorm"] = A(np.asarray(inp["mla_kv_norm"])[:L].reshape(L, 128, 1))
    sh["ldc"] = A(np.asarray(inp["ret_log_decay"])[:L].reshape(L, 8))
    sh["dskip"] = A(np.asarray(inp["s5_d"])[:L].reshape(L, 4, 128).transpose(0, 2, 1))
    sh["w_glu"] = A(inp["s5_w_glu"][:L])
    sh["w_br_mla"] = A(inp["w_branch_mla"][:L]); sh["w_br_ret"] = A(inp["w_branch_ret"][:L]); sh["w_br_s5"] = A(inp["w_branch_s5"][:L])
    sh["w_o"] = A(inp["w_o"][:L]); sh["w_up"] = A(inp["ffn_w_up"][:L]); sh["w_down"] = A(inp["ffn_w_down"][:L])
    cw = np.asarray(inp["ffn_conv_w"])[:L]
    sh["convw"] = A(cw.reshape(L, 3, 44, 128).transpose(0, 3, 2, 1).reshape(L, 128, 132))
    sh["convb"] = A(np.asarray(inp["ffn_conv_b"])[:L].reshape(L, 44, 128).transpose(0, 2, 1))
    sh["ln"] = A(np.stack([np.asarray(inp[k])[:L] for k in ("ln1_g", "ln1_b", "ln2_g", "ln2_b")], 1))
    sh["w_ada"] = A(inp["w_ada"][:L]); sh["b_ada"] = A(inp["b_ada"][:L])
    sh.update(cst)
    x = np.asarray(inp["x"]); cc = np.asarray(inp["c"]); pos = np.asarray(inp["positions"])
    s5 = {k: np.asarray(inp[k])[:L] for k in ("s5_lam_re", "s5_lam_im", "s5_log_step", "s5_b_re", "s5_b_im", "s5_c_re", "s5_c_im")}
    maps = []
    for core in range(8):
        b, r = core // 4, core % 4
        m = dict(sh)
        m["x"] = A(x[b, r * T:(r + 1) * T, :])
        m["cT"] = A(cc[b].reshape(8, 128).T)
        m["pos"] = A(pos[b, r * T:(r + 1) * T].reshape(NT, 128).T.astype(np.int32))
        dist = np.full((128, 4), BIGD, f32)
        for rr in range(4):
            if rr < r:
                dist[:64, rr] = r - 1 - rr
            if rr > r:
                dist[64:, rr] = rr - 1 - r
        m["dist"] = dist
        sel = np.zeros((128, 4), f32); sel[:, r] = 1.0
        m["sel"] = sel
        selH = np.zeros((8, 2), f32); flag = np.zeros((2, 1), f32)
        if r > 0:
            selH[2 * (r - 1) + 1, 0] = 1.0; flag[0, 0] = 1.0
        if r < 3:
            selH[2 * (r + 1), 1] = 1.0; flag[1, 0] = 1.0
        m["selH"] = selH; m["flag"] = flag
        g0 = 8 * r
        lamS_re = np.zeros((L, 2, 128, 4), f32); lamS_im = np.zeros_like(lamS_re); lsS = np.zeros_like(lamS_re)
        Bre = np.zeros((L, 2, 128, 4, 128), f32); Bim = np.zeros_like(Bre); Cre = np.zeros_like(Bre); Cim = np.zeros_like(Bre)
        for st in range(4):
            for gg in range(2):
                g = g0 + 2 * st + gg
                gl = 2 * st + gg
                lamS_re[:, :, gg * 64:(gg + 1) * 64, st] = s5["s5_lam_re"][:, :, g, :]
                lamS_im[:, :, gg * 64:(gg + 1) * 64, st] = s5["s5_lam_im"][:, :, g, :]
                lsS[:, :, gg * 64:(gg + 1) * 64, st] = s5["s5_log_step"][:, :, g, None]
                Bre[:, :, gl * 16:(gl + 1) * 16, st, gg * 64:(gg + 1) * 64] = s5["s5_b_re"][:, :, g].transpose(0, 1, 3, 2)
                Bim[:, :, gl * 16:(gl + 1) * 16, st, gg * 64:(gg + 1) * 64] = s5["s5_b_im"][:, :, g].transpose(0, 1, 3, 2)
                Cre[:, :, gg * 64:(gg + 1) * 64, st, gl * 16:(gl + 1) * 16] = s5["s5_c_re"][:, :, g].transpose(0, 1, 3, 2)
                Cim[:, :, gg * 64:(gg + 1) * 64, st, gl * 16:(gl + 1) * 16] = s5["s5_c_im"][:, :, g].transpose(0, 1, 3, 2)
        m["lamS_re"] = lamS_re; m["lamS_im"] = lamS_im; m["lsS"] = lsS
        m["lamB_re"] = A(lamS_re.transpose(0, 1, 3, 2).reshape(L, 2, 512))
        m["lamB_im"] = A(lamS_im.transpose(0, 1, 3, 2).reshape(L, 2, 512))
        m["lsB"] = A(lsS.transpose(0, 1, 3, 2).reshape(L, 2, 512))
        m["Bblk_re"] = Bre.reshape(L, 2, 128, 512); m["Bblk_im"] = Bim.reshape(L, 2, 128, 512)
        m["Cblk_re"] = Cre.reshape(L, 2, 128, 512); m["Cblk_im"] = Cim.reshape(L, 2, 128, 512)
        maps.append(m)
    return maps


_CACHE = {}


def kernel(**inputs):
    S = int(np.asarray(inputs["x"]).shape[1])
    L = int(np.asarray(inputs["w_in"]).shape[0])
    T = S // 4
    key = (S, L)
    if key not in _CACHE:
        b = Builder(S, L)
        b.build()
        _CACHE[key] = b
    b = _CACHE[key]
    maps = prep_inputs(inputs, S, L)
    maps = [{k: m[k] for k in b.inputs} for m in maps]
    res = run_bass_kernel_spmd(b.nc, maps, core_ids=list(range(8)))
    out = np.zeros((2, S, D), np.float32)
    for c in range(8):
        out[c // 4, (c % 4) * T:(c % 4 + 1) * T, :] = np.asarray(res.results[c]["out"]).reshape(T, D)
    return out
```

```python
import math
import numpy as np
import ml_dtypes
import concourse.bass as bass
import concourse.mybir as mybir
from concourse.bass_utils import run_bass_kernel_spmd

F32 = mybir.dt.float32
BF16 = mybir.dt.bfloat16
I32 = mybir.dt.int32
ALU = mybir.AluOpType
AF = mybir.ActivationFunctionType

ENGS = ("pe", "act", "dve", "pool", "sp")
D = 1024
NH = 8
DFF = 2816
C_QC, C_KV, C_KR, C_RQ, C_RK, C_RV, C_RG, C_S5, C_G = 0, 256, 384, 416, 672, 928, 1440, 1952, 2464
ALPHA = float((2 * 4) ** 0.25)
LN_EPS, RMS_EPS, GN_EPS = 1e-5, 1e-6, 1e-5
TWO_PI = 2.0 * math.pi
SB_BASE, SB_END = 16512, 229376
BIGD = 40.0


class FW:
    def __init__(self, nc, n_dma_sems=(("sp", 40), ("pool", 16))):
        self.nc = nc
        self.stream = {e: [] for e in ENGS}
        self.ctr = {e: nc.alloc_semaphore(name=f"ctr_{e}") for e in ENGS}
        self.bar = {e: nc.alloc_semaphore(name=f"bar_{e}") for e in ENGS}
        self.count = {e: 0 for e in ENGS}
        self.known = {e: {} for e in ENGS}
        self.keys = {}
        self.dpool = {}
        for e, n in n_dma_sems:
            self.dpool[e] = dict(sems=[nc.alloc_semaphore(name=f"dq_{e}_{i}") for i in range(n)], vals=[0] * n, nxt=0)
        self.nbar = 0
        self.n_inst = 0
        self.customs = {}

    def _wait(self, eng, tok):
        sem, val = tok
        if val <= 0:
            return
        k = self.known[eng]
        if k.get(id(sem), 0) >= val:
            return
        k[id(sem)] = val
        self.stream[eng].append(("wait", sem, val))

    def _deps(self, eng, reads, writes, skip_same):
        toks = []
        for key in reads:
            st = self.keys.get(key)
            if st and st["w"] is not None:
                toks.append(st["w"])
        for key in writes:
            st = self.keys.get(key)
            if st:
                if st["w"] is not None:
                    toks.append(st["w"])
                toks.extend(st["r"])
        own = self.ctr[eng]
        for t in toks:
            if t[0] is own and skip_same:
                continue
            self._wait(eng, t)

    def _record(self, tok, reads, writes):
        for key in reads:
            st = self.keys.setdefault(key, {"w": None, "r": []})
            st["r"] = [t for t in st["r"] if t[0] is not tok[0]] + [tok]
        for key in writes:
            self.keys[key] = {"w": tok, "r": []}

    def op(self, eng, fn, reads=(), writes=(), skip_same=False):
        if eng == "pe":
            skip_same = True
        self._deps(eng, reads, writes, skip_same)
        self.count[eng] += 1
        tok = (self.ctr[eng], self.count[eng])
        self.stream[eng].append(("inst", fn, self.ctr[eng], 1))
        self._record(tok, reads, writes)
        self.n_inst += 1
        return tok

    def dma(self, eng, out, in_, reads=(), writes=()):
        p = self.dpool[eng]
        j = p["nxt"]
        p["nxt"] = (j + 1) % len(p["sems"])
        sem = p["sems"][j]
        self._wait(eng, (sem, p["vals"][j]))
        self._deps(eng, reads, writes, False)
        p["vals"][j] += 16
        tok = (sem, p["vals"][j])
        self.stream[eng].append(("inst", lambda e, o=out, i=in_: e.dma_start(out=o, in_=i), sem, 16))
        self._record(tok, reads, writes)
        self.n_inst += 1
        return tok

    def custom(self, eng, fn, sem, newval, reads=(), writes=()):
        self._deps(eng, reads, writes, False)
        self.stream[eng].append(("inst", fn, sem, 1))
        tok = (sem, newval)
        self.customs[eng] = tok
        self._record(tok, reads, writes)
        return tok

    def barrier(self):
        self.nbar += 1
        for e in ENGS:
            if e in self.dpool:
                p = self.dpool[e]
                for s, v in zip(p["sems"], p["vals"]):
                    self._wait(e, (s, v))
            if e in self.customs:
                self._wait(e, self.customs[e])
            self._wait(e, (self.ctr[e], self.count[e]))
            self.stream[e].append(("inst", lambda en: en.nop(), self.bar[e], 1))
        for e in ENGS:
            for e2 in ENGS:
                if e2 != e:
                    self._wait(e, (self.bar[e2], self.nbar))
        self.keys = {}

    def emit(self):
        nc = self.nc
        engobj = {"pe": "tensor", "act": "scalar", "dve": "vector", "pool": "gpsimd", "sp": "sync"}
        with nc.Block() as block:
            for e in ENGS:
                stream = self.stream[e]

                def body(eng, stream=stream):
                    for it in stream:
                        if it[0] == "wait":
                            eng.wait_ge(it[1], it[2])
                        else:
                            it[1](eng).then_inc(it[2], it[3])

                getattr(block, engobj[e])(body)


def _dsize(dt):
    return {F32: 4, BF16: 2, I32: 4}[dt]


def rev_ap(ap, n):
    a = ap.ap
    assert len(a) == 2 and a[1][0] == 1 and a[1][1] == n, a
    return bass.AP(ap.tensor, ap.offset + (n - 1), [[a[0][0], a[0][1]], [-1, n]])


class Builder:
    def __init__(self, S, L, debug=()):
        self.S, self.L = S, L
        self.T = S // 4
        self.NT = self.T // 128
        self.debug = set(debug)
        self.nc = bass.Bass("TRN2", target_bir_lowering=False)
        self.fw = FW(self.nc)
        self.pers_off = SB_BASE
        self.arena_off = SB_BASE
        self.uid = 0
        self.inputs = {}
        self.dbg_out = []
        self.ps = [self.nc.alloc_psum_tensor(f"ps{i}", [128, 512], F32).ap() for i in range(8)]
        self.psi = 0
        self.cc_sem = self.nc.alloc_semaphore(name="cc_sem")
        self.cc_n = 0
        self.stage_i = 0
        self._cache = {}

    def _alloc(self, name, shape, dt, off):
        nbytes = int(np.prod(shape[1:])) * _dsize(dt)
        nbytes = (nbytes + 31) // 32 * 32
        assert off + nbytes <= SB_END, f"SBUF overflow allocating {name} {shape}: {off + nbytes - SB_END} bytes over"
        self.uid += 1
        h = self.nc.alloc_sbuf_tensor_at(f"{name}_{self.uid}", list(shape), dt, offset=off)
        return h.ap(), off + nbytes

    def pers(self, name, shape, dt=F32):
        ap, self.pers_off = self._alloc(name, shape, dt, self.pers_off)
        self.arena_off = max(self.arena_off, self.pers_off)
        return ap

    def sb(self, name, shape, dt=F32):
        ap, self.arena_off = self._alloc(name, shape, dt, self.arena_off)
        return ap

    def phase(self, name):
        self.fw.barrier()
        self.arena_off = self.pers_off
        self.pname = name

    def subphase(self, mark):
        self.fw.barrier()
        self.arena_off = mark

    def inp(self, name, shape, dt=F32):
        ap = self.nc.dram_tensor(name, list(shape), dt, kind="ExternalInput").ap()
        self.inputs[name] = (tuple(shape), dt)
        return ap

    def dram(self, name, shape, dt=F32):
        return self.nc.dram_tensor(name, list(shape), dt).ap()

    def nextps(self, cyc=None):
        cyc = cyc or list(range(8))
        i = cyc[self.psi % len(cyc)]
        self.psi += 1
        return self.ps[i], f"ps{i}"

    def op(self, eng, meth, r=(), w=(), **kw):
        return self.fw.op(eng, lambda e: getattr(e, meth)(**kw), reads=list(r), writes=list(w))

    def dma(self, out, in_, r=(), w=(), q="sp"):
        return self.fw.dma(q, out, in_, reads=list(r), writes=list(w))

    def mm(self, out, lhsT, rhs, start, stop, r, w):
        return self.fw.op("pe", lambda e: e.matmul(out, lhsT=lhsT, rhs=rhs, start=start, stop=stop), reads=list(r), writes=list(w))

    def tr(self, out, in_, r, w):
        p = in_.shape[0]
        idn = self.ident[0:p, 0:p]
        return self.fw.op("pe", lambda e: e.transpose(out=out, in_=in_, identity=idn), reads=list(r) + ["ident"], writes=list(w))

    def act(self, out, in_, func, r, w, bias=0.0, scale=1.0, **kw):
        return self.fw.op("act", lambda e: e.activation(out=out, in_=in_, func=func, bias=bias, scale=scale, **kw), reads=list(r), writes=list(w))

    def allgather(self, out_ap, in_ap, r, w):
        self.cc_n += 1
        n = self.cc_n
        return self.fw.custom("pool", lambda e: e.collective_compute("AllGather", ALU.bypass, replica_groups=[[0, 1, 2, 3], [4, 5, 6, 7]],
                                                                     ins=[in_ap.opt()], outs=[out_ap.opt()]),
                              self.cc_sem, n, reads=list(r), writes=list(w))

    def dbg(self, name, dram_ap, key):
        if name in self.debug:
            o = self.nc.dram_tensor("dbg_" + name, list(dram_ap.shape), dram_ap.dtype, kind="ExternalOutput").ap()
            self.dma(o, dram_ap, r=[key])
            self.dbg_out.append("dbg_" + name)

    def dbg_sb(self, name, sb_ap, key):
        if name in self.debug:
            o = self.nc.dram_tensor("dbg_" + name, list(sb_ap.shape), sb_ap.dtype, kind="ExternalOutput").ap()
            self.dma(o, sb_ap, r=[key])
            self.dbg_out.append("dbg_" + name)

    def wload(self, dst, src, dst_key, mul=None, mul_key=None, src_key=None):
        shp = list(src.shape)
        rows = shp[0]
        cols = int(np.prod(shp[1:]))
        si = self.stage_i % len(self.wstage)
        self.stage_i += 1
        skey = f"wst{si}"
        st = self.wstage[si][0:rows, 0:cols]
        if len(shp) == 3:
            st = st.rearrange("p (k n) -> p k n", k=shp[1])
        self.dma(st, src, r=[src_key] if src_key else [], w=[skey])
        if mul is None:
            self.op("pool", "tensor_copy", r=[skey], w=[dst_key], out=dst, in_=st)
        else:
            self.op("pool", "tensor_tensor", r=[skey, mul_key], w=[dst_key], out=dst, in0=st, in1=mul, op=ALU.mult)

    def mk_wstage(self, cols, n=3):
        self.wstage = [self.sb(f"wst{i}", [128, cols], F32) for i in range(n)]
        self.stage_i = 0

    def declare_inputs(self):
        L, T, NT, S = self.L, self.T, self.NT, self.S
        I = self.inp
        self.x_in = I("x", [T, D])
        self.cT_in = I("cT", [128, 8])
        self.pos_in = I("pos", [128, NT], I32)
        self.invf16_in = I("invf16", [128, 16])
        self.invf32_in = I("invf32", [128, 32])
        self.cmask_in = I("cmask", [4, 128, 128])
        self.ek_in = I("ek", [128, 4])
        self.etab_in = I("etab", [128, NT])
        self.io_in = I("io", [128, 512])
        self.dist_in = I("dist", [128, 4])
        self.sel_in = I("sel", [128, 4])
        self.selH_in = I("selH", [8, 2])
        self.flag_in = I("flag", [2, 1])
        self.w_in = I("w_in", [L, D, 5536])
        self.qnorm_in = I("qnorm", [L, 128, 2])
        self.w_uq = I("w_uq", [L, 256, 768])
        self.kvnorm_in = I("kvnorm", [L, 128, 1])
        self.w_ukv = I("w_ukv", [L, 128, 1024])
        self.ldc_in = I("ldc", [L, 8])
        self.lamS_re = I("lamS_re", [L, 2, 128, 4])
        self.lamS_im = I("lamS_im", [L, 2, 128, 4])
        self.lsS = I("lsS", [L, 2, 128, 4])
        self.lamB_re = I("lamB_re", [L, 2, 512])
        self.lamB_im = I("lamB_im", [L, 2, 512])
        self.lsB = I("lsB", [L, 2, 512])
        self.Bblk_re = I("Bblk_re", [L, 2, 128, 512])
        self.Bblk_im = I("Bblk_im", [L, 2, 128, 512])
        self.Cblk_re = I("Cblk_re", [L, 2, 128, 512])
        self.Cblk_im = I("Cblk_im", [L, 2, 128, 512])
        self.dskip_in = I("dskip", [L, 128, 4])
        self.w_glu = I("w_glu", [L, 512, 512])
        self.w_br = [I("w_br_mla", [L, 512, D]), I("w_br_ret", [L, 512, D]), I("w_br_s5", [L, 512, D])]
        self.w_o = I("w_o", [L, D, D])
        self.w_up = I("w_up", [L, D, 2 * DFF])
        self.convw_in = I("convw", [L, 128, 44 * 3])
        self.convb_in = I("convb", [L, 128, 44])
        self.w_down = I("w_down", [L, DFF, D])
        self.ln_in = I("ln", [L, 4, D])
        self.w_ada = I("w_ada", [L, D, 6 * D])
        self.b_ada = I("b_ada", [L, 6 * D])
        self.out = self.nc.dram_tensor("out", [T, D], F32, kind="ExternalOutput").ap()
        Dm = self.dram
        self.lat_loc = Dm("lat_loc", [160, T], BF16)
        self.lat_all = Dm("lat_all", [4 * 160, T], BF16)
        self.qT_d = Dm("qT_d", [NH, 96, T], BF16)
        self.rq_d = Dm("rq_d", [T, 256], BF16)
        self.rk_d = Dm("rk_d", [T, 256], BF16)
        self.rv_d = Dm("rv_d", [T, 512], BF16)
        self.rg_d = Dm("rg_d", [T, 512], BF16)
        self.u_loc = Dm("u_loc", [512, T], F32)
        self.u_all = Dm("u_all", [16 * 128, T], F32)
        self.agg_loc = Dm("agg_loc", [4 * 128, 128], F32)
        self.agg_all = Dm("agg_all", [16 * 128, 128], F32)
        self.ys_loc = Dm("ys_loc", [4 * 128, T], F32)
        self.ys_all = Dm("ys_all", [16 * 128, T], F32)
        self.ob_d = [Dm(f"ob{i}_d", [512, T], BF16) for i in range(3)]
        self.mT_d = Dm("mT_d", [D, T], BF16)
        self.halo_loc = Dm("halo_loc", [2, D], F32)
        self.halo_all = Dm("halo_all", [8, D], F32)

    def setup(self):
        T, NT = self.T, self.NT
        P = self.pers
        self.x_res = P("x_res", [128, NT, D])
        self.ident = P("ident", [128, 128], BF16)
        self.ones_f = P("ones_f", [128, 128])
        self.zeros_f = P("zeros_f", [128, 128])
        self.condrep = P("condrep", [128, 8, 128])
        self.cos16 = P("cos16", [128, NT, 16]); self.sin16 = P("sin16", [128, NT, 16])
        self.cos32 = P("cos32", [128, NT, 32]); self.sin32 = P("sin32", [128, NT, 32])
        self.cos32q = P("cos32q", [128, NT, 32]); self.sin32q = P("sin32q", [128, NT, 32])
        self.sel = P("sel", [128, 4]); self.dist = P("dist", [128, 4])
        self.selH = P("selH", [8, 2]); self.flag = P("flag", [2, 1])
        self.ek = P("ek", [128, 4]); self.etab = P("etab", [128, NT])
        self.constc = P("constc", [128, 4])
        self.epsc = self.constc[:, 0:1]; self.rmseps = self.constc[:, 1:2]; self.halfpi = self.constc[:, 2:3]
        self.fw.barrier()
        self.arena_off = self.pers_off
        op, dma = self.op, self.dma
        dma(self.x_res, self.x_in.rearrange("(n p) d -> p n d", p=128), w=["x_res"])
        op("pool", "memset", w=["ident"], ap=self.ident, constant=0.0)
        op("pool", "affine_select", r=["ident"], w=["ident"], out=self.ident, in_=self.ident, pattern=[[-1, 128]],
           compare_op=ALU.not_equal, fill=1.0, base=0, channel_multiplier=1)
        op("dve", "memset", w=["ones_f"], ap=self.ones_f, constant=1.0)
        op("dve", "memset", w=["zeros_f"], ap=self.zeros_f, constant=0.0)
        op("dve", "memset", w=["constc"], ap=self.constc[:, 0:1], constant=LN_EPS)
        op("dve", "memset", w=["constc"], ap=self.constc[:, 1:2], constant=RMS_EPS)
        op("dve", "memset", w=["constc"], ap=self.constc[:, 2:3], constant=math.pi / 2)
        op("dve", "memset", w=["constc"], ap=self.constc[:, 3:4], constant=GN_EPS)
        for nm, src in (("sel", self.sel_in), ("dist", self.dist_in), ("selH", self.selH_in), ("flag", self.flag_in),
                        ("ek", self.ek_in), ("etab", self.etab_in)):
            dma(getattr(self, nm), src, w=[nm])
        cT = self.sb("cT", [128, 8]); cs = self.sb("cs", [128, 8])
        dma(cT, self.cT_in, w=["cT"])
        self.act(cs, cT, AF.Silu, r=["cT"], w=["cs"])
        op("dve", "tensor_copy", r=["cs"], w=["condrep"], out=self.condrep, in_=cs.unsqueeze(2).to_broadcast([128, 8, 128]))
        posi = self.sb("posi", [128, NT], I32); posf = self.sb("posf", [128, NT])
        dma(posi, self.pos_in, w=["posi"])
        op("dve", "tensor_copy", r=["posi"], w=["posf"], out=posf, in_=posi)
        for nf, src, cosT, sinT in ((16, self.invf16_in, self.cos16, self.sin16), (32, self.invf32_in, self.cos32, self.sin32)):
            inv = self.sb(f"inv{nf}", [128, nf]); u = self.sb(f"u{nf}", [128, NT, nf]); ki = self.sb(f"ki{nf}", [128, NT, nf], I32)
            kf = self.sb(f"kf{nf}", [128, NT, nf]); fr = self.sb(f"fr{nf}", [128, NT, nf]); ab = self.sb(f"ab{nf}", [128, NT, nf])
            k = f"rp{nf}"
            dma(inv, src, w=[k + "inv"])
            op("dve", "tensor_scalar", r=[k + "inv"], w=[k + "inv"], out=inv, in0=inv, scalar1=1.0 / TWO_PI, scalar2=None, op0=ALU.mult)
            op("dve", "tensor_tensor", r=["posf", k + "inv"], w=[k + "u"], out=u, in0=posf.unsqueeze(2).to_broadcast([128, NT, nf]),
               in1=inv.unsqueeze(1).to_broadcast([128, NT, nf]), op=ALU.mult)
            op("dve", "tensor_copy", r=[k + "u"], w=[k + "ki"], out=ki, in_=u)
            op("dve", "tensor_copy", r=[k + "ki"], w=[k + "kf"], out=kf, in_=ki)
            op("dve", "tensor_tensor", r=[k + "u", k + "kf"], w=[k + "fr"], out=fr, in0=u, in1=kf, op=ALU.subtract)
            self.act(sinT, fr, AF.Sin, r=[k + "fr"], w=[k + "sin"], scale=TWO_PI)
            self.act(ab, fr, AF.Abs, r=[k + "fr"], w=[k + "ab"])
            self.act(cosT, ab, AF.Sin, r=[k + "ab"], w=[k + "cos"], scale=-TWO_PI, bias=self.halfpi)
        op("dve", "tensor_scalar", r=["rp32cos"], w=["c32q"], out=self.cos32q, in0=self.cos32, scalar1=0.125, scalar2=None, op0=ALU.mult)
        op("dve", "tensor_scalar", r=["rp32sin"], w=["s32q"], out=self.sin32q, in0=self.sin32, scalar1=0.125, scalar2=None, op0=ALU.mult)

    def modvec(self, l, idx, dst, dkey, plus_one=False):
        for qd in range(4):
            c0 = idx * D + qd * 256
            si = self.stage_i % len(self.wstage)
            self.stage_i += 1
            wst = self.wstage[si][:, 0:2048].rearrange("p (k n) -> p k n", k=8)
            brow = self.wstage[si][0:1, 2048:2304]
            sk = f"wst{si}"
            self.dma(wst, self.w_ada[l, :, c0:c0 + 256].rearrange("(k p) n -> p k n", p=128), w=[sk])
            self.dma(brow, self.b_ada[l:l + 1, c0:c0 + 256], w=[sk])
            ps, pk = self.nextps()
            for k in range(8):
                self.mm(ps[:, 0:256], self.condrep[:, k, :], wst[:, k, :], k == 0, False, r=["condrep", sk], w=[pk])
            self.mm(ps[:, 0:256], self.ones_f[0:1, :], brow, False, True, r=["ones_f", sk], w=[pk])
            self.act(dst[:, qd * 256:(qd + 1) * 256], ps[:, 0:256], AF.Identity, r=[], w=[pk, dkey], bias=1.0 if plus_one else 0.0)

    def mk_lnscr(self, tag):
        return dict(st=self.sb(f"lnst{tag}", [128, 2, 6]), mv=self.sb(f"lnmv{tag}", [128, 2]), rs=self.sb(f"lnrs{tag}", [128, 2]), k=f"ln{tag}")

    def ln_stats(self, xt, xkey, p, scr):
        st, mv, rs, k = scr["st"], scr["mv"], scr["rs"], scr["k"]
        for c in range(2):
            self.op("dve", "bn_stats", r=[xkey], w=[k + "st"], out=st[0:p, c, :], in_=xt[:, c * 512:(c + 1) * 512])
        self.op("dve", "bn_aggr", r=[k + "st"], w=[k + "mv"], out=mv[0:p, :], in_=st[0:p].rearrange("p a b -> p (a b)"))
        self.act(rs[0:p, 0:1], mv[0:p, 1:2], AF.Ln, r=[k + "mv"], w=[k + "rs"], bias=self.epsc[0:p, 0:1])
        self.act(rs[0:p, 0:1], rs[0:p, 0:1], AF.Exp, r=[k + "rs"], w=[k + "rs"], scale=-0.5)
        self.op("dve", "scalar_tensor_tensor", r=[k + "mv", k + "rs"], w=[k + "rs2"], out=rs[0:p, 1:2], in0=mv[0:p, 0:1], scalar=-1.0,
                in1=rs[0:p, 0:1], op0=ALU.mult, op1=ALU.mult)
        return rs, [k + "rs", k + "rs2"]

    def phase_A(self, l, which, hT, col0):
        NT = self.NT
        shB = self.sb("shB", [128, D]); scB = self.sb("scB", [128, D])
        self.modvec(l, 3 * which + 0, shB, "shB")
        self.modvec(l, 3 * which + 1, scB, "scB", plus_one=True)
        xn1 = self.sb("xn0", [128, D])
        xn = [xn1, xn1]
        hb = [self.sb(f"hb{i}", [128, D], BF16) for i in range(2)]
        lns = [self.mk_lnscr(i) for i in range(2)]
        for i in range(NT):
            j = i % 2
            xt = self.x_res[:, i, :]
            rs, rk = self.ln_stats(xt, "x_res", 128, lns[j])
            self.act(xn[j], xt, AF.Identity, r=["x_res"] + rk, w=["xn0"], scale=rs[:, 0:1], bias=rs[:, 1:2])
            self.op("dve", "tensor_tensor", r=["xn0", "scB"], w=["xn0"], out=xn[j], in0=xn[j], in1=scB, op=ALU.mult)
            self.op("dve", "tensor_tensor", r=["xn0", "shB"], w=[f"hb{j}"], out=hb[j], in0=xn[j], in1=shB, op=ALU.add)
            ps, pk = self.nextps()
            psb = ps.bitcast(BF16)
            for k in range(8):
                self.tr(psb[:, k * 128:(k + 1) * 128], hb[j][:, k * 128:(k + 1) * 128], r=[f"hb{j}"], w=[pk])
            self.op("dve", "tensor_copy", r=[], w=[pk, f"hT{i}"], out=hT[:, :, col0 + i * 128:col0 + (i + 1) * 128],
                    in_=psb.rearrange("p (k t) -> p k t", k=8))
        return shB, scB

    def rope(self, x1, x2, cosT, sinT, d1, d2, H, n, pk, dkey, tmp):
        c = cosT.unsqueeze(1).to_broadcast([128, H, n]); s = sinT.unsqueeze(1).to_broadcast([128, H, n])
        t1, t2 = tmp[0][:, 0:H * n].rearrange("p (h n) -> p h n", h=H), tmp[1][:, 0:H * n].rearrange("p (h n) -> p h n", h=H)
        o = self.op
        o("dve", "tensor_tensor", r=["rope"], w=[pk, "rt1"], out=t1, in0=x1, in1=c, op=ALU.mult)
        o("dve", "tensor_tensor", r=["rope"], w=[pk, "rt2"], out=t2, in0=x2, in1=s, op=ALU.mult)
        o("dve", "tensor_tensor", r=["rt1", "rt2"], w=[dkey], out=d1, in0=t1, in1=t2, op=ALU.subtract)
        o("dve", "tensor_tensor", r=["rope"], w=[pk, "rt1"], out=t1, in0=x1, in1=s, op=ALU.mult)
        o("dve", "tensor_tensor", r=["rope"], w=[pk, "rt2"], out=t2, in0=x2, in1=c, op=ALU.mult)
        o("dve", "tensor_tensor", r=["rt1", "rt2"], w=[dkey], out=d2, in0=t1, in1=t2, op=ALU.add)

    def phase_B(self, l):
        T, NT = self.T, self.NT
        op, dma, mm, tr, act = self.op, self.dma, self.mm, self.tr, self.act
        self.phase("B")
        hT = self.sb("hT", [128, 8, T], BF16)
        W1 = self.sb("W1", [128, 8, C_S5], BF16)
        Ws5 = self.sb("Ws5", [128, 8, 512], BF16)
        Wuq = self.sb("Wuq", [128, 2, 768], BF16)
        qg = self.sb("qg", [128, 2])
        self.mk_wstage(C_S5 + 512, n=2)
        dma(qg, self.qnorm_in[l], w=["qg"])
        for k in range(8):
            st = self.wstage[k % 2]
            dma(st, self.w_in[l, k * 128:(k + 1) * 128, 0:C_G], w=[f"wst{k % 2}"])
            op("pool", "tensor_copy", r=[f"wst{k % 2}"], w=["W1"], out=W1[:, k, :], in_=st[:, 0:C_S5])
            op("pool", "tensor_copy", r=[f"wst{k % 2}"], w=["Ws5"], out=Ws5[:, k, :], in_=st[:, C_S5:C_G])
        for k in range(2):
            st = self.wstage[k % 2]
            dma(st[:, 0:768], self.w_uq[l, k * 128:(k + 1) * 128, :], w=[f"wst{k % 2}"])
            op("dve", "tensor_scalar", r=[f"wst{k % 2}", "qg"], w=["Wuq"], out=Wuq[:, k, :], in0=st[:, 0:768], scalar1=qg[:, k:k + 1],
               scalar2=None, op0=ALU.mult)
        self.phase_A(l, 0, hT, 0)
        sq = self.sb("sq", [128, 384]); ssq = self.sb("ssq", [128, 2]); rr = self.sb("rr", [128, 2])
        qcn = self.sb("qcn", [128, 256], BF16); lat = self.sb("lat", [128, 160], BF16)
        qcnT = self.sb("qcnT", [128, 2, 128], BF16)
        latT = [self.sb(f"latT{i}", [128, 128], BF16) for i in range(2)]
        krT = [self.sb(f"krT{i}", [32, 128], BF16) for i in range(2)]
        Qr = self.sb("Qr", [128, 8, 96], BF16)
        QT = [self.sb(f"QT{i}", [96, 8, 128], BF16) for i in range(2)]
        rtmp = [self.sb(f"rtmp{i}", [128, 128]) for i in range(2)]
        rqk = [self.sb(f"rqk{i}", [128, 512], BF16) for i in range(2)]
        rvb = [self.sb(f"rvb{i}", [128, 512], BF16) for i in range(1)] * 2
        rgb = [self.sb(f"rgb{i}", [128, 512], BF16) for i in range(1)] * 2
        TB = min(512, T)
        ub = [self.sb(f"ub{i}", [128, TB]) for i in range(2)]
        for i in range(NT):
            j = i % 2
            tsl = slice(i * 128, (i + 1) * 128)
            hk = f"hT{i}"
            ps, pk = self.nextps()
            for k in range(8):
                mm(ps[:, 0:416], hT[:, k, tsl], W1[:, k, 0:416], k == 0, k == 7, r=[hk, "W1"], w=[pk])
            ps2, pk2 = self.nextps(); ps3, pk3 = self.nextps(); ps4, pk4 = self.nextps()
            for k in range(8):
                mm(ps2, hT[:, k, tsl], W1[:, k, C_RQ:C_RV], k == 0, k == 7, r=[hk, "W1"], w=[pk2])
            for k in range(8):
                mm(ps3, hT[:, k, tsl], W1[:, k, C_RV:C_RG], k == 0, k == 7, r=[hk, "W1"], w=[pk3])
            for k in range(8):
                mm(ps4, hT[:, k, tsl], W1[:, k, C_RG:C_S5], k == 0, k == 7, r=[hk, "W1"], w=[pk4])
            act(sq[:, 0:384], ps[:, 0:384], AF.Square, r=[], w=[pk, "sq"])
            op("dve", "tensor_reduce", r=["sq"], w=["ssq"], out=ssq[:, 0:1], in_=sq[:, 0:256], axis=mybir.AxisListType.X, op=ALU.add)
            op("dve", "tensor_reduce", r=["sq"], w=["ssq"], out=ssq[:, 1:2], in_=sq[:, 256:384], axis=mybir.AxisListType.X, op=ALU.add)
            act(rr[:, 0:1], ssq[:, 0:1], AF.Ln, r=["ssq"], w=["rr"], scale=1.0 / 256, bias=self.rmseps)
            act(rr[:, 1:2], ssq[:, 1:2], AF.Ln, r=["ssq"], w=["rr"], scale=1.0 / 128, bias=self.rmseps)
            act(rr, rr, AF.Exp, r=["rr"], w=["rr"], scale=-0.5)
            op("dve", "tensor_scalar", r=["rr"], w=[pk, "qcn"], out=qcn, in0=ps[:, 0:256], scalar1=rr[:, 0:1], scalar2=None, op0=ALU.mult)
            op("dve", "tensor_scalar", r=["rr"], w=[pk, "lat"], out=lat[:, 0:128], in0=ps[:, 256:384], scalar1=rr[:, 1:2], scalar2=None, op0=ALU.mult)
            self.rope(ps[:, 384:400].unsqueeze(1), ps[:, 400:416].unsqueeze(1), self.cos16[:, i, :], self.sin16[:, i, :],
                      lat[:, 128:144].unsqueeze(1), lat[:, 144:160].unsqueeze(1), 1, 16, pk, "lat", rtmp)
            pt, ptk = self.nextps()
            ptb = pt.bitcast(BF16)
            tr(ptb[:, 0:128], qcn[:, 0:128], r=["qcn"], w=[ptk])
            tr(ptb[:, 128:256], qcn[:, 128:256], r=["qcn"], w=[ptk])
            tr(ptb[:, 256:384], lat[:, 0:128], r=["lat"], w=[ptk])
            tr(ptb[0:32, 384:512], lat[:, 128:160], r=["lat"], w=[ptk])
            op("dve", "tensor_copy", r=[], w=[ptk, "qcnT"], out=qcnT, in_=ptb[:, 0:256].rearrange("p (k t) -> p k t", k=2))
            act(latT[j], ptb[:, 256:384], AF.Identity, r=[], w=[ptk, f"latT{j}"])
            act(krT[j], ptb[0:32, 384:512], AF.Identity, r=[], w=[ptk, f"krT{j}"])
            dma(self.lat_loc[0:128, tsl], latT[j], r=[f"latT{j}"], w=["lat_loc"])
            dma(self.lat_loc[128:160, tsl], krT[j], r=[f"krT{j}"], w=["lat_loc"])
            for g in range(2):
                ps, pk = self.nextps()
                for k in range(2):
                    mm(ps[:, 0:384], qcnT[:, k, :], Wuq[:, k, g * 384:(g + 1) * 384], k == 0, k == 1, r=["qcnT", "Wuq"], w=[pk])
                pv = ps[:, 0:384].rearrange("p (h d) -> p h d", h=4)
                act(Qr[:, g * 4:(g + 1) * 4, 0:64], pv[:, :, 0:64], AF.Identity, r=[], w=[pk, "Qr"])
                self.rope(pv[:, :, 64:80], pv[:, :, 80:96], self.cos16[:, i, :], self.sin16[:, i, :],
                          Qr[:, g * 4:(g + 1) * 4, 64:80], Qr[:, g * 4:(g + 1) * 4, 80:96], 4, 16, pk, "Qr", rtmp)
            pt, ptk = self.nextps()
            ptb = pt.bitcast(BF16)
            for h in range(8):
                tr(ptb[0:96, h * 128:(h + 1) * 128], Qr[:, h, :], r=["Qr"], w=[ptk])
            op("dve", "tensor_copy", r=[], w=[ptk, f"QT{j}"], out=QT[j], in_=ptb[0:96, :].rearrange("p (h t) -> p h t", h=8))
            dma(self.qT_d[:, :, tsl].rearrange("h p t -> p h t"), QT[j], r=[f"QT{j}"], w=["qT_d"])
            ps, pk = ps2, pk2
            for qk in range(2):
                pv = ps[:, qk * 256:(qk + 1) * 256].rearrange("p (h d) -> p h d", h=4)
                dv = rqk[j][:, qk * 256:(qk + 1) * 256].rearrange("p (h d) -> p h d", h=4)
                cT_, sT_ = (self.cos32q, self.sin32q) if qk == 0 else (self.cos32, self.sin32)
                self.rope(pv[:, :, 0:32], pv[:, :, 32:64], cT_[:, i, :], sT_[:, i, :], dv[:, :, 0:32], dv[:, :, 32:64], 4, 32, pk, f"rqk{j}", rtmp)
            dma(self.rq_d[tsl, :], rqk[j][:, 0:256], r=[f"rqk{j}"], w=["rq_d"])
            dma(self.rk_d[tsl, :], rqk[j][:, 256:512], r=[f"rqk{j}"], w=["rk_d"])
            ps, pk = ps3, pk3
            act(rvb[j], ps, AF.Identity, r=[], w=[pk, "rvb0"])
            dma(self.rv_d[tsl, :], rvb[j], r=["rvb0"], w=["rv_d"])
            ps, pk = ps4, pk4
            act(rgb[j], ps, AF.Silu, r=[], w=[pk, "rgb0"])
            dma(self.rg_d[tsl, :], rgb[j], r=["rgb0"], w=["rg_d"])
        n = 0
        for b in range(T // TB):
            bsl = slice(b * TB, (b + 1) * TB)
            hks = [f"hT{i}" for i in range(b * TB // 128, (b + 1) * TB // 128)]
            for fc in range(4):
                ps, pk = self.nextps()
                for k in range(8):
                    mm(ps[:, 0:TB], Ws5[:, k, fc * 128:(fc + 1) * 128], hT[:, k, bsl], k == 0, k == 7, r=hks + ["Ws5"], w=[pk])
                j = n % 2; n += 1
                act(ub[j], ps[:, 0:TB], AF.Identity, r=[], w=[pk, f"ub{j}"])
                dma(self.u_loc[fc * 128:(fc + 1) * 128, bsl], ub[j], r=[f"ub{j}"], w=["u_loc"])
        self.allgather(self.lat_all, self.lat_loc, r=["lat_loc"], w=["lat_all"])
        for c in range(4):
            self.allgather(self.u_all[c * 512:(c + 1) * 512, :], self.u_loc[c * 128:(c + 1) * 128, :], r=["u_loc"], w=["u_all"])
        for nm in ("lat_loc", "qT_d", "rq_d", "rk_d", "rv_d", "rg_d", "u_loc", "lat_all"):
            self.dbg(nm, getattr(self, nm), nm)

    def phase_R(self, l):
        T, NT = self.T, self.NT
        op, dma, mm, tr, act = self.op, self.dma, self.mm, self.tr, self.act
        self.phase("R")
        rq = self.sb("rq", [128, NT, 256], BF16); rk = self.sb("rk", [128, NT, 256], BF16)
        rv = self.sb("rv", [128, NT, 512], BF16); sg = self.sb("sg", [128, NT, 512], BF16)
        dma(rq, self.rq_d.rearrange("(n p) f -> p n f", p=128), w=["rq"])
        dma(rk, self.rk_d.rearrange("(n p) f -> p n f", p=128), w=["rk"])
        dma(rv, self.rv_d.rearrange("(n p) f -> p n f", p=128), w=["rv"])
        dma(sg, self.rg_d.rearrange("(n p) f -> p n f", p=128), w=["sg"])
        lgall = self.sb("lgall", [128, 8]); lgsel = self.sb("lgsel", [128, 4])
        dma(lgall, self.ldc_in[l:l + 1, :].partition_broadcast(128), w=["lgall"])
        dma(lgsel[0:64, :], self.ldc_in[l:l + 1, 0:4].partition_broadcast(64), w=["lgsel"])
        dma(lgsel[64:128, :], self.ldc_in[l:l + 1, 4:8].partition_broadcast(64), w=["lgsel"])
        for t_, k_ in ((lgall, "lgall"), (lgsel, "lgsel")):
            act(t_, t_, AF.Exp, r=[k_], w=[k_])
            act(t_, t_, AF.Ln, r=[k_], w=[k_], scale=-1.0, bias=1.0)
        zx = self.sb("zx", [128, 4, 4])
        for kind, (ecol, lo) in enumerate(((0, 0), (1, 4), (2, 0), (3, 4))):
            act(zx[:, kind, :], lgall[:, lo:lo + 4], AF.Exp, r=["lgall", "ek"], w=["zx"], scale=self.ek[:, ecol:ecol + 1])
        cm = self.sb("cm", [128, 4, 128])
        dma(cm, self.cmask_in.rearrange("a p f -> p a f"), w=["cm"])
        DT = self.sb("DT", [128, 4, 128]); dtmp = self.sb("dtmp", [128, 2, 128])
        for h in range(4):
            act(dtmp[:, 0, :], cm[:, 0, :], AF.Exp, r=["cm", "lgall"], w=["dtmp0"], scale=lgall[:, h:h + 1])
            act(dtmp[:, 1, :], cm[:, 1, :], AF.Exp, r=["cm", "lgall"], w=["dtmp1"], scale=lgall[:, 4 + h:5 + h])
            op("dve", "tensor_tensor", r=["dtmp0", "cm"], w=["dtmp0"], out=dtmp[:, 0, :], in0=dtmp[:, 0, :], in1=cm[:, 2, :], op=ALU.mult)
            op("dve", "tensor_tensor", r=["dtmp1", "cm"], w=["dtmp1"], out=dtmp[:, 1, :], in0=dtmp[:, 1, :], in1=cm[:, 3, :], op=ALU.mult)
            op("dve", "tensor_tensor", r=["dtmp0", "dtmp1"], w=["DT"], out=DT[:, h, :], in0=dtmp[:, 0, :], in1=dtmp[:, 1, :], op=ALU.add)
        lg128 = self.sb("lg128", [128, 4]); lgT = self.sb("lgT", [128, 4])
        op("dve", "tensor_scalar", r=["lgsel"], w=["lg128"], out=lg128, in0=lgsel, scalar1=128.0, scalar2=None, op0=ALU.mult)
        op("dve", "tensor_scalar", r=["lgsel"], w=["lgT"], out=lgT, in0=lgsel, scalar1=float(T), scalar2=None, op0=ALU.mult)
        cdec = self.sb("cdec", [128, 4])
        act(cdec, lg128, AF.Exp, r=["lg128"], w=["cdec"])
        coefn = self.sb("coefn", [128, 4, NT]); coefr = self.sb("coefr", [128, 4, 4])
        for h in range(4):
            act(coefn[:, h, :], self.etab, AF.Exp, r=["lg128"], w=["coefn"], scale=lg128[:, h:h + 1])
            act(coefr[:, h, :], self.dist, AF.Exp, r=["lgT"], w=["coefr"], scale=lgT[:, h:h + 1])
        E = self.sb("E", [128, 4, NT, 128])
        kk = [self.sb(f"kk{i}", [128, 128], BF16) for i in range(2)]
        n_ = 0
        for n in range(NT):
            for h in range(4):
                j = n_ % 2; n_ += 1
                kh = rk[:, n, h * 64:(h + 1) * 64]
                op("dve", "tensor_scalar", r=["rk", "zx"], w=[f"kk{j}"], out=kk[j][:, 0:64], in0=kh, scalar1=zx[:, 0, h:h + 1], scalar2=None, op0=ALU.mult)
                op("dve", "tensor_scalar", r=["rk", "zx"], w=[f"kk{j}"], out=kk[j][:, 64:128], in0=kh, scalar1=zx[:, 1, h:h + 1], scalar2=None, op0=ALU.mult)
                ps, pk = self.nextps()
                mm(ps[:, 0:128], kk[j], rv[:, n, h * 128:(h + 1) * 128], True, True, r=[f"kk{j}", "rv"], w=[pk])
                act(E[:, h, n, :], ps[:, 0:128], AF.Identity, r=[], w=[pk, f"E{h}_{n}"])
        for h in range(4):
            for n in range(1, NT):
                op("dve", "scalar_tensor_tensor", r=[f"E{h}_{n - 1}"], w=[f"E{h}_{n}"], out=E[0:64, h, n, :], in0=E[0:64, h, n - 1, :],
                   scalar=cdec[0:64, h:h + 1], in1=E[0:64, h, n, :], op0=ALU.mult, op1=ALU.add)
            for n in range(NT - 2, -1, -1):
                op("dve", "scalar_tensor_tensor", r=[f"E{h}_{n + 1}"], w=[f"E{h}_{n}"], out=E[64:128, h, n, :], in0=E[64:128, h, n + 1, :],
                   scalar=cdec[64:128, h:h + 1], in1=E[64:128, h, n, :], op0=ALU.mult, op1=ALU.add)
        ekeys = [f"E{h}_{n}" for h in range(4) for n in range(NT)]
        dma(self.agg_loc.rearrange("(h p) e -> p h e", p=128)[0:64], E[0:64, :, NT - 1, :], r=ekeys, w=["agg_loc"])
        dma(self.agg_loc.rearrange("(h p) e -> p h e", p=128)[64:128], E[64:128, :, 0, :], r=ekeys, w=["agg_loc"])
        self.allgather(self.agg_all, self.agg_loc, r=["agg_loc"], w=["agg_all"])
        agg = self.sb("agg", [128, 4, 4, 128])
        dma(agg, self.agg_all.rearrange("(r h p) e -> p r h e", r=4, h=4), r=["agg_all"], w=["agg"])
        Cin = self.sb("Cin", [128, 4, 128])
        for h in range(4):
            op("dve", "tensor_scalar", r=["agg", "coefr"], w=[f"Cin{h}"], out=Cin[:, h, :], in0=agg[:, 0, h, :], scalar1=coefr[:, h, 0:1], scalar2=None, op0=ALU.mult)
            for r_ in range(1, 4):
                op("dve", "scalar_tensor_tensor", r=["agg", "coefr"], w=[f"Cin{h}"], out=Cin[:, h, :], in0=agg[:, r_, h, :], scalar=coefr[:, h, r_:r_ + 1],
                   in1=Cin[:, h, :], op0=ALU.mult, op1=ALU.add)
        Pb = self.sb("Pb", [128, 4, NT, 128], BF16)
        for h in range(4):
            for n in range(NT):
                src_f = E[0:64, h, n - 1, :] if n >= 1 else self.zeros_f[0:64, :]
                src_b = E[64:128, h, n + 1, :] if n <= NT - 2 else self.zeros_f[64:128, :]
                op("dve", "scalar_tensor_tensor", r=ekeys + [f"Cin{h}", "coefn"], w=[f"Pb{n}"], out=Pb[0:64, h, n, :], in0=Cin[0:64, h, :],
                   scalar=coefn[0:64, h, n:n + 1], in1=src_f, op0=ALU.mult, op1=ALU.add)
                op("dve", "scalar_tensor_tensor", r=ekeys + [f"Cin{h}", "coefn"], w=[f"Pb{n}"], out=Pb[64:128, h, n, :], in0=Cin[64:128, h, :],
                   scalar=coefn[64:128, h, n:n + 1], in1=src_b, op0=ALU.mult, op1=ALU.add)
        for nm_, ap_, k_ in (("zx", zx, ["zx"]), ("DT", DT, ["DT"]), ("E", E, ekeys), ("Cin", Cin, [f"Cin{h}" for h in range(4)]), ("coefr", coefr, ["coefr"]), ("lgall", lgall, ["lgall"])):
            if nm_ in self.debug:
                o_ = self.nc.dram_tensor("dbg_" + nm_, list(ap_.shape), ap_.dtype, kind="ExternalOutput").ap()
                self.dma(o_, ap_, r=k_)
                self.dbg_out.append("dbg_" + nm_)
        qq = [self.sb(f"qq{i}", [128, 128], BF16) for i in range(2)]
        tqk = [self.sb(f"tqk{i}", [64, 256], BF16) for i in range(2)]
        tqq = [self.sb(f"tqq{i}", [128, 128], BF16) for i in range(2)]
        AT = [self.sb(f"AT{i}", [128, 128], BF16) for i in range(2)]
        gst = self.sb("gst", [128, 4, 6]); gmv = self.sb("gmv", [128, 4, 2]); grs = self.sb("grs", [128, 4, 2])
        yn = self.sb("yn", [128, 512]); orb = self.sb("orb", [128, 512], BF16)
        oT = [self.sb(f"oT{i}", [128, 4, 128], BF16) for i in range(2)]
        n_ = 0

        def heads_part(n):
            nonlocal n_
            yps, ypk = self.ps[6 + n % 2], f"ps{6 + n % 2}"
            for h in range(4):
                j = n_ % 2; n_ += 1
                qh = rq[:, n, h * 64:(h + 1) * 64]
                op("dve", "tensor_scalar", r=["rq", "zx"], w=[f"qq{j}"], out=qq[j][:, 0:64], in0=qh, scalar1=zx[:, 2, h:h + 1], scalar2=None, op0=ALU.mult)
                op("dve", "tensor_scalar", r=["rq", "zx"], w=[f"qq{j}"], out=qq[j][:, 64:128], in0=qh, scalar1=zx[:, 3, h:h + 1], scalar2=None, op0=ALU.mult)
                pt, ptk = self.nextps([0, 1, 2, 3, 4, 5])
                ptb = pt.bitcast(BF16)
                tr(ptb[0:64, 0:128], qh, r=["rq"], w=[ptk])
                tr(ptb[0:64, 128:256], rk[:, n, h * 64:(h + 1) * 64], r=["rk"], w=[ptk])
                tr(ptb[:, 256:384], qq[j], r=[f"qq{j}"], w=[ptk])
                op("dve", "tensor_copy", r=[], w=[ptk, f"tqk{j}"], out=tqk[j], in_=ptb[0:64, 0:256])
                act(tqq[j], ptb[:, 256:384], AF.Identity, r=[], w=[ptk, f"tqq{j}"])
                sps, spk = self.nextps([0, 1, 2, 3, 4, 5])
                mm(sps[:, 0:128], tqk[j][:, 128:256], tqk[j][:, 0:128], True, True, r=[f"tqk{j}"], w=[spk])
                op("dve", "tensor_tensor", r=["DT"], w=[spk, f"AT{j}"], out=AT[j], in0=sps[:, 0:128], in1=DT[:, h, :], op=ALU.mult)
                ysl = yps[:, h * 128:(h + 1) * 128]
                mm(ysl, AT[j], rv[:, n, h * 128:(h + 1) * 128], True, False, r=[f"AT{j}", "rv"], w=[ypk])
                mm(ysl, tqq[j], Pb[:, h, n, :], False, True, r=[f"tqq{j}", f"Pb{n}"], w=[ypk])
            return yps, ypk

        def epilogue(n, yps, ypk):
            for h in range(4):
                op("dve", "bn_stats", r=[], w=[ypk, "gst"], out=gst[:, h, :], in_=yps[:, h * 128:(h + 1) * 128])
            for h in range(4):
                op("dve", "bn_aggr", r=["gst"], w=["gmv"], out=gmv[:, h, :], in_=gst[:, h, :])
            act(grs[:, :, 0], gmv[:, :, 1], AF.Ln, r=["gmv"], w=["grs"], bias=self.constc[:, 3:4])
            act(grs[:, :, 0], grs[:, :, 0], AF.Exp, r=["grs"], w=["grs"], scale=-0.5)
            op("dve", "scalar_tensor_tensor", r=["gmv", "grs"], w=["grs2"], out=grs[:, :, 1], in0=gmv[:, :, 0], scalar=-1.0, in1=grs[:, :, 0],
               op0=ALU.mult, op1=ALU.mult)
            for h in range(4):
                act(yn[:, h * 128:(h + 1) * 128], yps[:, h * 128:(h + 1) * 128], AF.Identity, r=["grs", "grs2"], w=[ypk, "yn"],
                    scale=grs[:, h, 0:1], bias=grs[:, h, 1:2])
            op("dve", "tensor_tensor", r=["yn", "sg"], w=["orb"], out=orb, in0=yn, in1=sg[:, n, :], op=ALU.mult)
            pt, ptk = self.nextps([0, 1, 2, 3, 4, 5])
            ptb = pt.bitcast(BF16)
            for h in range(4):
                tr(ptb[:, h * 128:(h + 1) * 128], orb[:, h * 128:(h + 1) * 128], r=["orb"], w=[ptk])
            jo = n % 2
            op("dve", "tensor_copy", r=[], w=[ptk, f"oT{jo}"], out=oT[jo], in_=ptb[:, 0:512].rearrange("p (h t) -> p h t", h=4))
            dma(self.ob_d[1][:, n * 128:(n + 1) * 128].rearrange("(h p) t -> p h t", p=128), oT[jo], r=[f"oT{jo}"], w=["ob1_d"])

        prev = None
        for n in range(NT):
            cur = (n,) + heads_part(n)
            if prev is not None:
                epilogue(*prev)
            prev = cur
        epilogue(*prev)
        self.dbg("ob1_d", self.ob_d[1], "ob1_d")

    def angle_tables(self, u, ukey, shape, tag, sinT, cosT, skey, ckey, negsin=False):
        op, act = self.op, self.act
        ki = self.cached(f"at_ki{tag}", shape, I32); kf = self.cached(f"at_kf{tag}", shape); ab = self.cached(f"at_ab{tag}", shape)
        k = f"at{tag}"
        op("dve", "tensor_copy", r=[ukey], w=[k + "ki"], out=ki, in_=u)
        op("dve", "tensor_copy", r=[k + "ki"], w=[k + "kf"], out=kf, in_=ki)
        op("dve", "tensor_tensor", r=[ukey, k + "kf"], w=[ukey], out=u, in0=u, in1=kf, op=ALU.subtract)
        act(sinT, u, AF.Sin, r=[ukey], w=[skey], scale=-TWO_PI if negsin else TWO_PI)
        act(ab, u, AF.Abs, r=[ukey], w=[k + "ab"])
        act(cosT, ab, AF.Sin, r=[k + "ab"], w=[ckey], scale=-TWO_PI, bias=self.halfpi[0:shape[0], :])

    def cached(self, name, shape, dt=F32):
        key = (self.fw.nbar, name)
        if key not in self._cache:
            self._cache[key] = self.sb(name, shape, dt)
        return self._cache[key]

    def phase_S(self, l):
        S, T = self.S, self.T
        op, dma, mm, act = self.op, self.dma, self.mm, self.act
        self.phase("S")
        SBk = min(512, S)
        NB = S // SBk
        TB = min(2048, T)
        uT = self.sb("uT", [128, S], BF16)
        ys = self.sb("ys", [128, S])
        io = self.sb("io", [128, 512])
        lamS = self.sb("lamS", [128, 2, 3, 4])
        stepS = self.sb("stepS", [128, 2, 4]); magS = self.sb("magS", [128, 2, 4]); thS = self.sb("thS", [128, 2, 4])
        Bb = self.sb("Bb", [128, 2, 2, 512], BF16)
        Cb = self.sb("Cb", [128, 2, 2, 512], BF16)
        wc = self.sb("wc", [128, 2, 4, 2])
        mark = self.arena_off
        ust = [self.sb(f"ust{i}", [128, TB]) for i in range(3)]
        uacc = self.sb("uacc", [128, TB])
        n_ = 0
        for q in range(4):
            for tb in range(T // TB):
                for c in range(4):
                    j = n_ % 3; n_ += 1
                    dma(ust[j], self.u_all[(c * 4 + q) * 128:(c * 4 + q + 1) * 128, tb * TB:(tb + 1) * TB], w=[f"ust{j}"])
                    dst = uT[:, q * T + tb * TB:q * T + (tb + 1) * TB] if c == 3 else uacc
                    dk = "uT" if c == 3 else "uacc"
                    if c == 0:
                        op("dve", "tensor_scalar", r=[f"ust{j}"], w=["uacc"], out=uacc, in0=ust[j], scalar1=self.sel[:, 0:1], scalar2=None, op0=ALU.mult)
                    else:
                        op("dve", "scalar_tensor_tensor", r=[f"ust{j}", "uacc"], w=[dk], out=dst, in0=ust[j], scalar=self.sel[:, c:c + 1], in1=uacc,
                           op0=ALU.mult, op1=ALU.add)
        dma(io, self.io_in, w=["io"])
        for d in range(2):
            dma(lamS[:, d, 0, :], self.lamS_re[l, d], w=["lamS"])
            dma(lamS[:, d, 1, :], self.lamS_im[l, d], w=["lamS"])
            dma(lamS[:, d, 2, :], self.lsS[l, d], w=["lamS"])
        act(stepS, lamS[:, :, 2, :], AF.Exp, r=["lamS"], w=["stepS"])
        op("dve", "tensor_tensor", r=["lamS", "stepS"], w=["magS"], out=magS, in0=lamS[:, :, 0, :], in1=stepS, op=ALU.mult)
        act(magS, magS, AF.Exp, r=["magS"], w=["magS"])
        op("dve", "tensor_tensor", r=["lamS", "stepS"], w=["thS"], out=thS, in0=lamS[:, :, 1, :], in1=stepS, op=ALU.mult)
        op("dve", "tensor_scalar", r=["thS"], w=["thS"], out=thS, in0=thS, scalar1=1.0 / TWO_PI, scalar2=None, op0=ALU.mult)
        for d in range(2):
            self.subphase(mark)
            lB = self.sb("lB", [128, 3, 512])
            dma(lB[:, 0, :], self.lamB_re[l, d:d + 1, :].partition_broadcast(128), w=["lB"])
            dma(lB[:, 1, :], self.lamB_im[l, d:d + 1, :].partition_broadcast(128), w=["lB"])
            dma(lB[:, 2, :], self.lsB[l, d:d + 1, :].partition_broadcast(128), w=["lB"])
            sh = [128, 512]
            stB = self.sb("stB", sh); mgB = self.sb("mgB", sh); tu = self.sb("tu", sh); sn = self.sb("sn", sh); cs = self.sb("cs", sh)
            are = self.sb("are", sh); aim = self.sb("aim", sh); den = self.sb("den", sh); fre = self.sb("fre", sh); fim = self.sb("fim", sh)
            t1 = self.sb("t1s", sh)
            lre, lim = lB[:, 0, :], lB[:, 1, :]
            act(stB, lB[:, 2, :], AF.Exp, r=["lB"], w=["stB"])
            op("dve", "tensor_tensor", r=["lB", "stB"], w=["mgB"], out=mgB, in0=lre, in1=stB, op=ALU.mult)
            act(mgB, mgB, AF.Exp, r=["mgB"], w=["mgB"])
            op("dve", "tensor_tensor", r=["lB", "stB"], w=["tu"], out=tu, in0=lim, in1=stB, op=ALU.mult)
            op("dve", "tensor_scalar", r=["tu"], w=["tu"], out=tu, in0=tu, scalar1=1.0 / TWO_PI, scalar2=None, op0=ALU.mult)
            self.angle_tables(tu, "tu", sh, "B", sn, cs, "sn", "cs")
            op("dve", "tensor_tensor", r=["mgB", "cs"], w=["are"], out=are, in0=mgB, in1=cs, op=ALU.mult)
            op("dve", "tensor_tensor", r=["mgB", "sn"], w=["aim"], out=aim, in0=mgB, in1=sn, op=ALU.mult)
            op("dve", "tensor_scalar", r=["are"], w=["are"], out=are, in0=are, scalar1=-1.0, scalar2=None, op0=ALU.add)
            op("dve", "tensor_tensor", r=["lB"], w=["den"], out=den, in0=lre, in1=lre, op=ALU.mult)
            op("dve", "tensor_tensor", r=["lB"], w=["t1s"], out=t1, in0=lim, in1=lim, op=ALU.mult)
            op("dve", "tensor_tensor", r=["den", "t1s"], w=["den"], out=den, in0=den, in1=t1, op=ALU.add)
            op("dve", "reciprocal", r=["den"], w=["den"], out=den, in_=den)
            op("dve", "tensor_tensor", r=["are", "lB"], w=["fre"], out=fre, in0=are, in1=lre, op=ALU.mult)
            op("dve", "tensor_tensor", r=["aim", "lB"], w=["t1s"], out=t1, in0=aim, in1=lim, op=ALU.mult)
            op("dve", "tensor_tensor", r=["fre", "t1s"], w=["fre"], out=fre, in0=fre, in1=t1, op=ALU.add)
            op("dve", "tensor_tensor", r=["fre", "den"], w=["fre"], out=fre, in0=fre, in1=den, op=ALU.mult)
            op("dve", "tensor_tensor", r=["aim", "lB"], w=["fim"], out=fim, in0=aim, in1=lre, op=ALU.mult)
            op("dve", "tensor_tensor", r=["are", "lB"], w=["t1s"], out=t1, in0=are, in1=lim, op=ALU.mult)
            op("dve", "tensor_tensor", r=["fim", "t1s"], w=["fim"], out=fim, in0=fim, in1=t1, op=ALU.subtract)
            op("dve", "tensor_tensor", r=["fim", "den"], w=["fim"], out=fim, in0=fim, in1=den, op=ALU.mult)
            braw = self.sb("braw", [128, 2, 512]); t2 = self.sb("t2s", [128, 512]); t3 = self.sb("t3s", [128, 512])
            dma(braw[:, 0, :], self.Bblk_re[l, d], w=["braw"])
            dma(braw[:, 1, :], self.Bblk_im[l, d], w=["braw"])
            op("dve", "tensor_tensor", r=["braw", "fre"], w=["t2s"], out=t2, in0=braw[:, 0, :], in1=fre, op=ALU.mult)
            op("dve", "tensor_tensor", r=["braw", "fim"], w=["t3s"], out=t3, in0=braw[:, 1, :], in1=fim, op=ALU.mult)
            op("dve", "tensor_tensor", r=["t2s", "t3s"], w=["Bb"], out=Bb[:, d, 0, :], in0=t2, in1=t3, op=ALU.subtract)
            op("dve", "tensor_tensor", r=["braw", "fre"], w=["t2s"], out=t2, in0=braw[:, 1, :], in1=fre, op=ALU.mult)
            op("dve", "tensor_tensor", r=["braw", "fim"], w=["t3s"], out=t3, in0=braw[:, 0, :], in1=fim, op=ALU.mult)
            op("dve", "tensor_tensor", r=["t2s", "t3s"], w=["Bb"], out=Bb[:, d, 1, :], in0=t2, in1=t3, op=ALU.add)
            dma(braw[:, 0, :], self.Cblk_re[l, d], r=["Bb"], w=["braw"])
            dma(braw[:, 1, :], self.Cblk_im[l, d], r=["Bb"], w=["braw"])
            op("pool", "tensor_copy", r=["braw"], w=["Cb"], out=Cb[:, d, :, :], in_=braw)
        self.subphase(mark)
        op("dve", "memset", w=["wc"], ap=wc, constant=0.0)
        W = [128, SBk]
        tun = [self.sb(f"tun{i}", W) for i in range(2)]
        nsin = [self.sb(f"nsin{i}", W) for i in range(2)]; cosb = [self.sb(f"cosb{i}", W) for i in range(2)]
        ta = self.sb("ta", W); tb_ = self.sb("tb", W); tc = self.sb("tc", W); td = self.sb("td", W)
        vre = self.sb("vre", W); vim = self.sb("vim", W); wre = self.sb("wre", W); wim = self.sb("wim", W)
        xre = [self.sb(f"xre{i}", W, BF16) for i in range(2)]; nxi = [self.sb(f"nxi{i}", W, BF16) for i in range(2)]
        its = [(d, sbi, st) for d in range(2) for sbi in range(NB) for st in range(4)]

        def front(i):
            d, sbi, st = its[i]
            j = i % 2
            tau0 = sbi * SBk
            t0 = tau0 if d == 0 else S - tau0 - SBk
            op("dve", "tensor_scalar", r=["io", "thS"], w=[f"tun{j}"], out=tun[j], in0=io[:, 0:SBk], scalar1=float(tau0), scalar2=thS[:, d, st:st + 1],
               op0=ALU.add, op1=ALU.mult)
            self.angle_tables(tun[j], f"tun{j}", W, "M", nsin[j], cosb[j], f"nsin{j}", f"cosb{j}", negsin=True)
            bi = (2 * i) % 6
            pre, prk = self.ps[bi], f"ps{bi}"
            pim, pik = self.ps[bi + 1], f"ps{bi + 1}"
            mm(pre[:, 0:SBk], Bb[:, d, 0, st * 128:(st + 1) * 128], uT[:, t0:t0 + SBk], True, True, r=["Bb", "uT"], w=[prk])
            mm(pim[:, 0:SBk], Bb[:, d, 1, st * 128:(st + 1) * 128], uT[:, t0:t0 + SBk], True, True, r=["Bb", "uT"], w=[pik])
            return pre, prk, pim, pik

        def back(i, pre, prk, pim, pik, aps, apk):
            d, sbi, st = its[i]
            j = i % 2
            bre = pre[:, 0:SBk] if d == 0 else rev_ap(pre[:, 0:SBk], SBk)
            bim = pim[:, 0:SBk] if d == 0 else rev_ap(pim[:, 0:SBk], SBk)
            ns, cb = nsin[j], cosb[j]
            op("dve", "tensor_tensor", r=[f"cosb{j}"], w=[prk, "ta"], out=ta, in0=bre, in1=cb, op=ALU.mult)
            op("dve", "tensor_tensor", r=[f"nsin{j}"], w=[pik, "tb"], out=tb_, in0=bim, in1=ns, op=ALU.mult)
            op("dve", "tensor_tensor", r=["ta", "tb"], w=["vre"], out=vre, in0=ta, in1=tb_, op=ALU.subtract)
            op("dve", "tensor_tensor", r=[f"cosb{j}"], w=[pik, "tc"], out=tc, in0=bim, in1=cb, op=ALU.mult)
            op("dve", "tensor_tensor", r=[f"nsin{j}"], w=[prk, "td"], out=td, in0=bre, in1=ns, op=ALU.mult)
            op("dve", "tensor_tensor", r=["tc", "td"], w=["vim"], out=vim, in0=tc, in1=td, op=ALU.add)
            mg = magS[:, d, st:st + 1].to_broadcast(W)
            op("dve", "tensor_tensor_scan", r=["vre", "magS", "wc"], w=["wre"], out=wre, data0=mg, data1=vre, initial=wc[:, d, st, 0:1], op0=ALU.mult, op1=ALU.add)
            op("dve", "tensor_tensor_scan", r=["vim", "magS", "wc"], w=["wim"], out=wim, data0=mg, data1=vim, initial=wc[:, d, st, 1:2], op0=ALU.mult, op1=ALU.add)
            act(wc[:, d, st, 0:1], wre[:, SBk - 1:SBk], AF.Identity, r=["wre"], w=["wc"])
            act(wc[:, d, st, 1:2], wim[:, SBk - 1:SBk], AF.Identity, r=["wim"], w=["wc"])
            op("dve", "tensor_tensor", r=["wre", f"cosb{j}"], w=["ta"], out=ta, in0=wre, in1=cb, op=ALU.mult)
            op("dve", "tensor_tensor", r=["wim", f"nsin{j}"], w=["tb"], out=tb_, in0=wim, in1=ns, op=ALU.mult)
            op("dve", "tensor_tensor", r=["ta", "tb"], w=[f"xre{j}"], out=xre[j], in0=ta, in1=tb_, op=ALU.add)
            op("dve", "tensor_tensor", r=["wre", f"nsin{j}"], w=["tc"], out=tc, in0=wre, in1=ns, op=ALU.mult)
            op("dve", "tensor_tensor", r=["wim", f"cosb{j}"], w=["td"], out=td, in0=wim, in1=cb, op=ALU.mult)
            op("dve", "tensor_tensor", r=["tc", "td"], w=[f"nxi{j}"], out=nxi[j], in0=tc, in1=td, op=ALU.subtract)
            mm(aps[:, 0:SBk], Cb[:, d, 0, st * 128:(st + 1) * 128], xre[j], st == 0, False, r=["Cb", f"xre{j}"], w=[apk])
            mm(aps[:, 0:SBk], Cb[:, d, 1, st * 128:(st + 1) * 128], nxi[j], False, st == 3, r=["Cb", f"nxi{j}"], w=[apk])
            if st == 3:
                tau0 = sbi * SBk
                t0 = tau0 if d == 0 else S - tau0 - SBk
                if d == 0:
                    act(ys[:, t0:t0 + SBk], aps[:, 0:SBk], AF.Identity, r=[], w=[apk, "ys"])
                else:
                    yr = rev_ap(ys[:, t0:t0 + SBk], SBk)
                    op("dve", "tensor_tensor", r=[], w=[apk, "ys"], out=yr, in0=aps[:, 0:SBk], in1=yr, op=ALU.add)

        nxt = front(0)
        aps = apk = None
        for i in range(len(its)):
            cur = nxt
            if i + 1 < len(its):
                nxt = front(i + 1)
            if its[i][2] == 0:
                ai = 6 + (i // 4) % 2
                aps, apk = self.ps[ai], f"ps{ai}"
            back(i, *cur, aps, apk)
        for q in range(4):
            dma(self.ys_loc[q * 128:(q + 1) * 128, :], ys[:, q * T:(q + 1) * T], r=["ys"], w=["ys_loc"])
        for q in range(4):
            self.allgather(self.ys_all[q * 512:(q + 1) * 512, :], self.ys_loc[q * 128:(q + 1) * 128, :], r=["ys_loc"], w=["ys_all"])
        self.dbg("ys_loc", self.ys_loc, "ys_loc")

    def phase_AT(self, l):
        S, T = self.S, self.T
        op, dma, mm, act = self.op, self.dma, self.mm, self.act
        self.phase("AT")
        NKT = S // 128
        QB = min(512, T)
        KB = min(512, S)
        kvnT = self.sb("kvnT", [128, S], BF16); krT = self.sb("krT", [32, S], BF16)
        for r_ in range(4):
            dma(kvnT[:, r_ * T:(r_ + 1) * T], self.lat_all[r_ * 160:r_ * 160 + 128, :], r=["lat_all"], w=["kvnT"])
            dma(krT[:, r_ * T:(r_ + 1) * T], self.lat_all[r_ * 160 + 128:r_ * 160 + 160, :], r=["lat_all"], w=["krT"])
        kvg = self.sb("kvg", [128, 1]); wst = self.sb("wukv_st", [128, 1024])
        Wk = self.sb("Wk", [128, 8, 96], BF16); Wv = self.sb("Wv", [128, 8, 64], BF16); Sel = self.sb("Sel", [32, 96], BF16)
        dma(kvg, self.kvnorm_in[l], w=["kvg"])
        dma(wst, self.w_ukv[l], w=["wukv_st"])
        op("dve", "memset", w=["Wk"], ap=Wk, constant=0.0)
        op("dve", "memset", w=["Sel"], ap=Sel, constant=0.0)
        wv_ = wst.rearrange("p (h d) -> p h d", h=8)
        op("dve", "tensor_scalar", r=["wukv_st", "kvg", "Wk"], w=["Wk"], out=Wk[:, :, 0:64], in0=wv_[:, :, 0:64], scalar1=kvg[:, 0:1], scalar2=None, op0=ALU.mult)
        op("dve", "tensor_scalar", r=["wukv_st", "kvg"], w=["Wv"], out=Wv, in0=wv_[:, :, 64:128], scalar1=kvg[:, 0:1], scalar2=None, op0=ALU.mult)
        op("dve", "tensor_copy", r=["Sel", "ident"], w=["Sel"], out=Sel[:, 64:96], in_=self.ident[0:32, 0:32])
        KT = [self.sb(f"KT{i}", [96, S], BF16) for i in range(2)]
        Vp = [self.sb(f"Vp{i}", [128, NKT, 65], BF16) for i in range(2)]
        QTh = [self.sb(f"QTh{i}", [96, T], BF16) for i in range(2)]
        pT = [self.sb(f"pT{i}", [128, QB], BF16) for i in range(3)]
        rec = self.sb("rec", [128, QB]); Rb = self.sb("Rb", [64, QB])
        oh = [self.sb(f"oh{i}", [64, QB], BF16) for i in range(2)]
        for i in range(2):
            op("dve", "memset", w=[f"Vp{i}"], ap=Vp[i], constant=1.0)
        scale = 96.0 ** -0.5
        np_ = 0; no_ = 0
        for h in range(NH):
            hb = h % 2
            dma(QTh[hb], self.qT_d[h], r=["qT_d"], w=[f"QTh{hb}"])
            for kb in range(S // KB):
                ksl = slice(kb * KB, (kb + 1) * KB)
                ps, pk = self.nextps([0, 1])
                mm(ps[0:96, 0:KB], Wk[:, h, :], kvnT[:, ksl], True, False, r=["Wk", "kvnT"], w=[pk])
                mm(ps[0:96, 0:KB], Sel, krT[:, ksl], False, True, r=["Sel", "krT"], w=[pk])
                op("dve", "tensor_copy", r=[], w=[pk, f"KT{hb}"], out=KT[hb][:, ksl], in_=ps[0:96, 0:KB])
            for g in range(0, NKT, 8):
                ng = min(8, NKT - g)
                ps, pk = self.nextps([0, 1])
                for t in range(ng):
                    mm(ps[:, t * 64:(t + 1) * 64], kvnT[:, (g + t) * 128:(g + t + 1) * 128], Wv[:, h, :], True, True, r=["kvnT", "Wv"], w=[pk])
                op("dve", "tensor_copy", r=[], w=[pk, f"Vp{hb}"], out=Vp[hb][:, g:g + ng, 0:64], in_=ps[:, 0:ng * 64].rearrange("p (t d) -> p t d", t=ng))
            for qb in range(T // QB):
                qsl = slice(qb * QB, (qb + 1) * QB)
                acc, ak = self.nextps([6, 7])
                LOOK = 2

                def issue_s(kt_):
                    sps_, spk_ = self.nextps([2, 3, 4])
                    mm(sps_[:, 0:QB], KT[hb][:, kt_ * 128:(kt_ + 1) * 128], QTh[hb][:, qsl], True, True, r=[f"KT{hb}", f"QTh{hb}"], w=[spk_])
                    return sps_, spk_
                pend = [issue_s(k_) for k_ in range(min(LOOK, NKT))]
                for kt in range(NKT):
                    sps, spk = pend.pop(0)
                    j = np_ % 3; np_ += 1
                    act(pT[j], sps[:, 0:QB], AF.Exp, r=[], w=[spk, f"pT{j}"], scale=scale)
                    if kt + LOOK < NKT:
                        pend.append(issue_s(kt + LOOK))
                    mm(acc[0:65, 0:QB], Vp[hb][:, kt, :], pT[j], kt == 0, kt == NKT - 1, r=[f"Vp{hb}", f"pT{j}"], w=[ak])
                op("dve", "reciprocal", r=[], w=[ak, "rec"], out=rec[64:65, :], in_=acc[64:65, 0:QB])
                rp, rpk = self.nextps([5])
                mm(rp[0:64, 0:QB], self.ones_f[64:65, 0:64], rec[64:65, :], True, True, r=["rec", "ones_f"], w=[rpk])
                op("dve", "tensor_copy", r=[], w=[rpk, "Rb"], out=Rb, in_=rp[0:64, 0:QB])
                jo = no_ % 2; no_ += 1
                op("dve", "tensor_tensor", r=["Rb"], w=[ak, f"oh{jo}"], out=oh[jo], in0=acc[0:64, 0:QB], in1=Rb, op=ALU.mult)
                dma(self.ob_d[0][h * 64:(h + 1) * 64, qsl], oh[jo], r=[f"oh{jo}"], w=["ob0_d"])
        self.dbg("ob0_d", self.ob_d[0], "ob0_d")

    def phase_M0(self, l):
        S, T = self.S, self.T
        op, dma, mm, act = self.op, self.dma, self.mm, self.act
        self.phase("M0")
        TB = min(512, T)
        Wg = self.sb("Wglu", [128, 4, 512], BF16)
        self.mk_wstage(512, n=2)
        for k in range(4):
            self.wload(Wg[:, k, :], self.w_glu[l, k * 128:(k + 1) * 128, :], "Wglu")
        dsk = self.sb("dsk", [128, 4])
        dma(dsk, self.dskip_in[l], w=["dsk"])
        cand = [self.sb(f"cand{i}", [128, TB]) for i in range(3)]
        ub = [self.sb(f"ub{i}", [128, TB]) for i in range(2)]
        yacc = self.sb("yacc", [128, TB]); y = self.sb("yM0", [128, 4, TB]); t1 = self.sb("t1m", [128, TB]); t2 = self.sb("t2m", [128, TB])
        yg = self.sb("yg", [128, 4, TB]); ygb = self.sb("ygb", [128, 4, TB], BF16); sgm = self.sb("sgm", [128, TB])
        ob = [self.sb(f"obs{i}", [128, TB], BF16) for i in range(2)]
        n_ = 0; nu = 0; no_ = 0
        for tb in range(T // TB):
            bsl = slice(tb * TB, (tb + 1) * TB)
            for fc in range(4):
                for q in range(4):
                    j = n_ % 3; n_ += 1
                    dma(cand[j], self.ys_all[(q * 4 + fc) * 128:(q * 4 + fc + 1) * 128, tb * TB:(tb + 1) * TB], r=["ys_all"], w=[f"cand{j}"])
                    if q == 0:
                        op("dve", "tensor_scalar", r=[f"cand{j}"], w=["yacc"], out=yacc, in0=cand[j], scalar1=self.sel[:, 0:1], scalar2=None, op0=ALU.mult)
                    else:
                        op("dve", "scalar_tensor_tensor", r=[f"cand{j}", "yacc"], w=["yacc"], out=yacc, in0=cand[j], scalar=self.sel[:, q:q + 1], in1=yacc,
                           op0=ALU.mult, op1=ALU.add)
                ju = nu % 2; nu += 1
                dma(ub[ju], self.u_loc[fc * 128:(fc + 1) * 128, bsl], r=["u_loc"], w=[f"ub{ju}"])
                yk = f"y{fc}"
                op("dve", "scalar_tensor_tensor", r=[f"ub{ju}", "yacc", "dsk"], w=[yk], out=y[:, fc, :], in0=ub[ju], scalar=dsk[:, fc:fc + 1], in1=yacc,
                   op0=ALU.mult, op1=ALU.add)
                op("dve", "tensor_tensor", r=[yk], w=["t1m"], out=t1, in0=y[:, fc, :], in1=y[:, fc, :], op=ALU.mult)
                op("dve", "tensor_scalar", r=["t1m"], w=["t1m"], out=t1, in0=t1, scalar1=0.044715, scalar2=1.0, op0=ALU.mult, op1=ALU.add)
                op("dve", "tensor_tensor", r=["t1m", yk], w=["t2m"], out=t2, in0=t1, in1=y[:, fc, :], op=ALU.mult)
                act(t2, t2, AF.Sigmoid, r=["t2m"], w=["t2m"], scale=2.0 * math.sqrt(2.0 / math.pi))
                op("dve", "tensor_tensor", r=["t2m", yk], w=[f"yg{fc}"], out=yg[:, fc, :], in0=t2, in1=y[:, fc, :], op=ALU.mult)
                act(ygb[:, fc, :], yg[:, fc, :], AF.Identity, r=[f"yg{fc}"], w=[f"ygb{fc}"])
            for fo in range(4):
                ps, pk = self.nextps()
                for k in range(4):
                    mm(ps[:, 0:TB], Wg[:, k, fo * 128:(fo + 1) * 128], ygb[:, k, :], k == 0, k == 3, r=["Wglu"] + [f"ygb{k}"], w=[pk])
                act(sgm, ps[:, 0:TB], AF.Sigmoid, r=[], w=[pk, "sgm"])
                jo = no_ % 2; no_ += 1
                op("dve", "tensor_tensor", r=["sgm", f"yg{fo}"], w=[f"obs{jo}"], out=ob[jo], in0=sgm, in1=yg[:, fo, :], op=ALU.mult)
                dma(self.ob_d[2][fo * 128:(fo + 1) * 128, bsl], ob[jo], r=[f"obs{jo}"], w=["ob2_d"])
        self.dbg("ob2_d", self.ob_d[2], "ob2_d")

    def phase_M1(self, l):
        T = self.T
        op, dma, mm, act = self.op, self.dma, self.mm, self.act
        self.phase("M1")
        TB = min(512, T)
        hT = self.sb("hT", [128, 8, T], BF16)
        self.mk_wstage(2304, n=3)
        self.phase_A(l, 0, hT, 0)
        hks = [f"hT{i}" for i in range(self.NT)]
        obT = [self.sb(f"obT{i}", [128, 4, TB], BF16) for i in range(3)]
        nob = 0
        Wg = [self.sb(f"Wg{i}", [128, 3, 8, 128], BF16) for i in range(2)]
        Wb = [self.sb(f"Wb{i}", [128, 3, 4, 128], BF16) for i in range(2)]
        sg = [self.sb(f"sgt{i}", [128, TB]) for i in range(2)]
        m = self.sb("macc", [128, TB]); tm = self.sb("tmM", [128, TB])
        mo = [self.sb(f"mo{i}", [128, TB], BF16) for i in range(2)]
        ns = 0; no_ = 0
        for f in range(8):
            jw = f % 2
            for b in range(3):
                c0 = C_G + b * D + f * 128
                self.wload(Wg[jw][:, b, :, :], self.w_in[l, :, c0:c0 + 128].rearrange("(k p) n -> p k n", p=128), f"Wg{jw}")
                self.wload(Wb[jw][:, b, :, :], self.w_br[b][l, :, f * 128:(f + 1) * 128].rearrange("(k p) n -> p k n", p=128), f"Wb{jw}")
            for tb in range(T // TB):
                bsl = slice(tb * TB, (tb + 1) * TB)
                for b in range(3):
                    gp, gk = self.nextps()
                    for k in range(8):
                        mm(gp[:, 0:TB], Wg[jw][:, b, k, :], hT[:, k, bsl], k == 0, k == 7, r=[f"Wg{jw}"] + hks, w=[gk])
                    js = ns % 2; ns += 1
                    act(sg[js], gp[:, 0:TB], AF.Sigmoid, r=[], w=[gk, f"sgt{js}"])
                    jb = nob % 3; nob += 1
                    dma(obT[jb], self.ob_d[b][:, bsl].rearrange("(k p) t -> p k t", p=128), r=[f"ob{b}_d"], w=[f"obT{jb}"])
                    bp, bk = self.nextps()
                    for k in range(4):
                        mm(bp[:, 0:TB], Wb[jw][:, b, k, :], obT[jb][:, k, :], k == 0, k == 3, r=[f"Wb{jw}", f"obT{jb}"], w=[bk])
                    if b == 0:
                        op("dve", "tensor_tensor", r=[f"sgt{js}"], w=[bk, "macc"], out=m, in0=bp[:, 0:TB], in1=sg[js], op=ALU.mult)
                    else:
                        op("dve", "tensor_tensor", r=[f"sgt{js}"], w=[bk, "tmM"], out=tm, in0=bp[:, 0:TB], in1=sg[js], op=ALU.mult)
                        if b == 1:
                            op("dve", "tensor_tensor", r=["tmM", "macc"], w=["macc"], out=m, in0=m, in1=tm, op=ALU.add)
                        else:
                            jo = no_ % 2; no_ += 1
                            op("dve", "tensor_tensor", r=["tmM", "macc"], w=[f"mo{jo}"], out=mo[jo], in0=m, in1=tm, op=ALU.add)
                            dma(self.mT_d[f * 128:(f + 1) * 128, bsl], mo[jo], r=[f"mo{jo}"], w=["mT_d"])
        self.dbg("mT_d", self.mT_d, "mT_d")

    def phase_M2(self, l):
        T, NT = self.T, self.NT
        op, dma, mm, act = self.op, self.dma, self.mm, self.act
        self.phase("M2")
        self.mk_wstage(2304, n=2)
        gB = self.sb("gateB", [128, D])
        self.modvec(l, 2, gB, "gateB")
        Wo = self.sb("Wo", [128, 8, D], BF16)
        for k in range(8):
            self.wload(Wo[:, k, :], self.w_o[l, k * 128:(k + 1) * 128, :], "Wo", mul=gB, mul_key="gateB")
        mT = self.sb("mT", [128, 8, T], BF16)
        dma(mT, self.mT_d.rearrange("(k p) t -> p k t", p=128), r=["mT_d"], w=["mT"])
        for i in range(NT):
            for half in range(2):
                ps, pk = self.nextps()
                for k in range(8):
                    mm(ps, mT[:, k, i * 128:(i + 1) * 128], Wo[:, k, half * 512:(half + 1) * 512], k == 0, k == 7, r=["mT", "Wo"], w=[pk])
                xs = self.x_res[:, i, half * 512:(half + 1) * 512]
                op("dve", "scalar_tensor_tensor", r=[], w=[pk, f"x{i}"], out=xs, in0=xs, scalar=ALPHA, in1=ps, op0=ALU.mult, op1=ALU.add)
        self.post_norm(l, 0)

    def post_norm(self, l, which):
        NT = self.NT
        op, dma, act = self.op, self.dma, self.act
        gB = self.sb("lngB", [128, D]); bB = self.sb("lnbB", [128, D])
        dma(gB, self.ln_in[l, 2 * which:2 * which + 1, :].partition_broadcast(128), w=["lngB"])
        dma(bB, self.ln_in[l, 2 * which + 1:2 * which + 2, :].partition_broadcast(128), w=["lnbB"])
        lns = [self.mk_lnscr(f"p{i}") for i in range(2)]
        xn = [self.sb(f"pxn{i}", [128, D]) for i in range(2)]
        for i in range(NT):
            j = i % 2
            xt = self.x_res[:, i, :]
            rs, rk = self.ln_stats(xt, f"x{i}", 128, lns[j])
            act(xn[j], xt, AF.Identity, r=[f"x{i}"] + rk, w=[f"pxn{j}"], scale=rs[:, 0:1], bias=rs[:, 1:2])
            op("dve", "tensor_tensor", r=[f"pxn{j}", "lngB"], w=[f"pxn{j}"], out=xn[j], in0=xn[j], in1=gB, op=ALU.mult)
            op("dve", "tensor_tensor", r=[f"pxn{j}", "lnbB"], w=[f"x{i}"], out=xt, in0=xn[j], in1=bB, op=ALU.add)

    def phase_F(self, l, last):
        T, NT = self.T, self.NT
        op, dma, mm, act, tr = self.op, self.dma, self.mm, self.act, self.tr
        self.phase("H")
        dma(self.halo_loc[0:1, :], self.x_res[0:1, 0, :], w=["halo_loc"])
        dma(self.halo_loc[1:2, :], self.x_res[127:128, NT - 1, :], w=["halo_loc"])
        self.allgather(self.halo_all, self.halo_loc, r=["halo_loc"], w=["halo_all"])
        self.phase("F")
        hT = self.sb("hT2", [128, 8, T + 2], BF16)
        self.mk_wstage(2304, n=3)
        shB, scB = self.phase_A(l, 1, hT, 1)
        rows = self.sb("hrows", [8, D]); xh = self.sb("xh", [2, D]); hh = self.sb("hh", [2, D], BF16)
        dma(rows, self.halo_all, r=["halo_all"], w=["hrows"])
        for half in range(2):
            ps, pk = self.nextps()
            mm(ps[0:2, :], self.selH, rows[:, half * 512:(half + 1) * 512], True, True, r=["hrows"], w=[pk])
            act(xh[:, half * 512:(half + 1) * 512], ps[0:2, :], AF.Identity, r=[], w=[pk, "xh"])
        lsc = self.mk_lnscr("h")
        rs, rk = self.ln_stats(xh, "xh", 2, lsc)
        act(xh, xh, AF.Identity, r=["xh"] + rk, w=["xh"], scale=rs[0:2, 0:1], bias=rs[0:2, 1:2])
        op("dve", "tensor_tensor", r=["xh", "scB"], w=["xh"], out=xh, in0=xh, in1=scB[0:2, :], op=ALU.mult)
        op("dve", "tensor_tensor", r=["xh", "shB"], w=["xh"], out=xh, in0=xh, in1=shB[0:2, :], op=ALU.add)
        op("dve", "tensor_scalar", r=["xh"], w=["hh"], out=hh, in0=xh, scalar1=self.flag[0:2, 0:1], scalar2=None, op0=ALU.mult)
        ps, pk = self.nextps()
        psb = ps.bitcast(BF16)
        for k in range(8):
            tr(psb[:, k * 2:(k + 1) * 2], hh[:, k * 128:(k + 1) * 128], r=["hh"], w=[pk])
        pv = psb[:, 0:16].rearrange("p (k t) -> p k t", k=8)
        op("dve", "tensor_copy", r=[], w=[pk, "hTh0"], out=hT[:, :, 0:1], in_=pv[:, :, 0:1])
        op("dve", "tensor_copy", r=[], w=[pk, "hTh1"], out=hT[:, :, T + 1:T + 2], in_=pv[:, :, 1:2])
        hks = [f"hT{i}" for i in range(NT)] + ["hTh0", "hTh1"]
        gB = self.sb("gate2B", [128, D])
        self.modvec(l, 5, gB, "gate2B")
        cw = self.sb("cw", [128, 44, 3]); cb = self.sb("cb", [128, 44])
        dma(cw, self.convw_in[l].rearrange("p (c k) -> p c k", k=3), w=["cw"])
        dma(cb, self.convb_in[l], w=["cb"])
        FB = min(256, T)
        Wa = [self.sb(f"Wa{i}", [128, 8, 128], BF16) for i in range(2)]
        Wgt = [self.sb(f"Wgt{i}", [128, 8, 128], BF16) for i in range(2)]
        Wd = [self.sb(f"Wd{i}", [128, D], BF16) for i in range(2)]
        ca_ = [self.sb(f"ca{i}", [128, FB]) for i in range(2)]; cg_ = [self.sb(f"cg{i}", [128, FB]) for i in range(2)]
        av = [self.sb(f"av{i}", [128, FB], BF16) for i in range(2)]
        na = 0
        NTB = T // FB

        def issue_up(jf, tb):
            jw = jf % 2
            if tb == 0:
                self.wload(Wa[jw], self.w_up[l, :, jf * 128:(jf + 1) * 128].rearrange("(k p) n -> p k n", p=128), f"Wa{jw}")
                self.wload(Wgt[jw], self.w_up[l, :, DFF + jf * 128:DFF + (jf + 1) * 128].rearrange("(k p) n -> p k n", p=128), f"Wgt{jw}")
                self.wload(Wd[jw], self.w_down[l, jf * 128:(jf + 1) * 128, :], f"Wd{jw}", mul=gB, mul_key="gate2B")
            c0 = tb * FB
            ui = (jf * NTB + tb) % 2
            pa, pak = self.ps[ui], f"ps{ui}"
            pg, pgk = self.ps[2 + ui], f"ps{2 + ui}"
            for k in range(8):
                mm(pa[:, 0:FB + 2], Wa[jw][:, k, :], hT[:, k, c0:c0 + FB + 2], k == 0, k == 7, r=[f"Wa{jw}"] + hks, w=[pak])
            for k in range(8):
                mm(pg[:, 0:FB + 2], Wgt[jw][:, k, :], hT[:, k, c0:c0 + FB + 2], k == 0, k == 7, r=[f"Wgt{jw}"] + hks, w=[pgk])
            return pa, pak, pg, pgk

        def finish(jf, tb, pa, pak, pg, pgk):
            nonlocal na
            jw = jf % 2
            c0 = tb * FB
            jc = na % 2
            ca, cg = ca_[jc], cg_[jc]
            kca, kcg = f"ca{jc}", f"cg{jc}"
            for (pp, ppk, dst, dk, ch) in ((pa, pak, ca, kca, jf), (pg, pgk, cg, kcg, 22 + jf)):
                act(dst, pp[:, 1:FB + 1], AF.Identity, r=["cw", "cb"], w=[ppk, dk], scale=cw[:, ch, 1:2], bias=cb[:, ch:ch + 1])
                op("dve", "scalar_tensor_tensor", r=["cw"], w=[ppk, dk], out=dst, in0=pp[:, 0:FB], scalar=cw[:, ch, 0:1], in1=dst, op0=ALU.mult, op1=ALU.add)
                op("dve", "scalar_tensor_tensor", r=["cw"], w=[ppk, dk], out=dst, in0=pp[:, 2:FB + 2], scalar=cw[:, ch, 2:3], in1=dst, op0=ALU.mult, op1=ALU.add)
            act(cg, cg, AF.Silu, r=[kcg], w=[kcg])
            ja = na % 2; na += 1
            op("dve", "tensor_tensor", r=[kca, kcg], w=[f"av{ja}"], out=av[ja], in0=ca, in1=cg, op=ALU.mult)
            for ti in range(FB // 128):
                i = (c0 // 128) + ti
                for half in range(2):
                    ps, pk = self.nextps([4, 5, 6, 7])
                    mm(ps, av[ja][:, ti * 128:(ti + 1) * 128], Wd[jw][:, half * 512:(half + 1) * 512], True, True, r=[f"av{ja}", f"Wd{jw}"], w=[pk])
                    xs = self.x_res[:, i, half * 512:(half + 1) * 512]
                    if jf == 0:
                        op("dve", "scalar_tensor_tensor", r=[], w=[pk, f"x{i}"], out=xs, in0=xs, scalar=ALPHA, in1=ps, op0=ALU.mult, op1=ALU.add)
                    else:
                        op("dve", "tensor_tensor", r=[], w=[pk, f"x{i}"], out=xs, in0=xs, in1=ps, op=ALU.add)

        prev = None
        for jf in range(DFF // 128):
            for tb in range(NTB):
                cur = (jf, tb) + issue_up(jf, tb)
                if prev is not None:
                    finish(*prev)
                prev = cur
        finish(*prev)
        self.post_norm(l, 1)
        if last:
            dma(self.out.rearrange("(n p) d -> p n d", p=128), self.x_res, r=[f"x{i}" for i in range(NT)], w=["out"])

    def build(self, stop=None):
        self.declare_inputs()
        self.setup()
        for l in range(self.L):
            self.phase_B(l)
            self.phase_R(l)
            self.phase_S(l)
            self.phase_AT(l)
            self.phase_M0(l)
            self.phase_M1(l)
            self.phase_M2(l)
            self.phase_F(l, last=(l == self.L - 1))
        self.fw.barrier()
        self.fw.emit()
        return self.nc


def _consts(S):
    T = S // 4
    NT = T // 128
    f32 = np.float32
    p = np.arange(128, dtype=f32)
    c = {}
    c["invf16"] = np.broadcast_to((f32(10000.0) ** (-np.arange(16, dtype=f32) / f32(16))).astype(f32), (128, 16)).copy()
    c["invf32"] = np.broadcast_to((f32(10000.0) ** (-np.arange(32, dtype=f32) / f32(32))).astype(f32), (128, 32)).copy()
    jj, ii = np.meshgrid(p, p, indexing="ij")
    EF = np.maximum(ii - jj, 0); EB = np.maximum(jj - ii, 0)
    MF = (ii >= jj).astype(f32); MB = (jj > ii).astype(f32)
    c["cmask"] = np.stack([EF, EB, MF, MB]).astype(f32)
    c["ek"] = np.stack([127 - p, p, p + 1, 128 - p], 1).astype(f32)
    n = np.arange(NT, dtype=f32)
    et = np.zeros((128, NT), f32); et[:64] = n[None, :]; et[64:] = (NT - 1 - n)[None, :]
    c["etab"] = et
    c["io"] = np.broadcast_to(np.arange(1, 513, dtype=f32), (128, 512)).copy()
    return c


def prep_inputs(inp, S, L):
    T = S // 4
    NT = T // 128
    f32 = np.float32
    A = lambda a: np.ascontiguousarray(np.asarray(a))
    cst = _consts(S)
    sh = {}
    sh["w_in"] = A(inp["w_in"][:L]); sh["w_uq"] = A(inp["mla_w_uq"][:L]); sh["w_ukv"] = A(inp["mla_w_ukv"][:L])
    sh["qnorm"] = A(np.asarray(inp["mla_q_norm"])[:L].reshape(L, 2, 128).transpose(0, 2, 1))
    sh["kvnorm"] = A(np.asarray(inp["mla_kv_norm"])[:L].reshape(L, 128, 1))
    sh["ldc"] = A(np.asarray(inp["ret_log_decay"])[:L].reshape(L, 8))
    sh["dskip"] = A(np.asarray(inp["s5_d"])[:L].reshape(L, 4, 128).transpose(0, 2, 1))
    sh["w_glu"] = A(inp["s5_w_glu"][:L])
    sh["w_br_mla"] = A(inp["w_branch_mla"][:L]); sh["w_br_ret"] = A(inp["w_branch_ret"][:L]); sh["w_br_s5"] = A(inp["w_branch_s5"][:L])
    sh["w_o"] = A(inp["w_o"][:L]); sh["w_up"] = A(inp["ffn_w_up"][:L]); sh["w_down"] = A(inp["ffn_w_down"][:L])
    cw = np.asarray(inp["ffn_conv_w"])[:L]
    sh["convw"] = A(cw.reshape(L, 3, 44, 128).transpose(0, 3, 2, 1).reshape(L, 128, 132))
    sh["convb"] = A(np.asarray(inp["ffn_conv_b"])[:L].reshape(L, 44, 128).transpose(0, 2, 1))
    sh["ln"] = A(np.stack([np.asarray(inp[k])[:L] for k in ("ln1_g", "ln1_b", "ln2_g", "ln2_b")], 1))
    sh["w_ada"] = A(inp["w_ada"][:L]); sh["b_ada"] = A(inp["b_ada"][:L])
    sh.update(cst)
    x = np.asarray(inp["x"]); cc = np.asarray(inp["c"]); pos = np.asarray(inp["positions"])
    s5 = {k: np.asarray(inp[k])[:L] for k in ("s5_lam_re", "s5_lam_im", "s5_log_step", "s5_b_re", "s5_b_im", "s5_c_re", "s5_c_im")}
    maps = []
    for core in range(8):
        b, r = core // 4, core % 4
        m = dict(sh)
        m["x"] = A(x[b, r * T:(r + 1) * T, :])
        m["cT"] = A(cc[b].reshape(8, 128).T)
        m["pos"] = A(pos[b, r * T:(r + 1) * T].reshape(NT, 128).T.astype(np.int32))
        dist = np.full((128, 4), BIGD, f32)
        for rr in range(4):
            if rr < r:
                dist[:64, rr] = r - 1 - rr
            if rr > r:
                dist[64:, rr] = rr - 1 - r
        m["dist"] = dist
        sel = np.zeros((128, 4), f32); sel[:, r] = 1.0
        m["sel"] = sel
        selH = np.zeros((8, 2), f32); flag = np.zeros((2, 1), f32)
        if r > 0:
            selH[2 * (r - 1) + 1, 0] = 1.0; flag[0, 0] = 1.0
        if r < 3:
            selH[2 * (r + 1), 1] = 1.0; flag[1, 0] = 1.0
        m["selH"] = selH; m["flag"] = flag
        g0 = 8 * r
        lamS_re = np.zeros((L, 2, 128, 4), f32); lamS_im = np.zeros_like(lamS_re); lsS = np.zeros_like(lamS_re)
        Bre = np.zeros((L, 2, 128, 4, 128), f32); Bim = np.zeros_like(Bre); Cre = np.zeros_like(Bre); Cim = np.zeros_like(Bre)
        for st in range(4):
            for gg in range(2):
                g = g0 + 2 * st + gg
                gl = 2 * st + gg
                lamS_re[:, :, gg * 64:(gg + 1) * 64, st] = s5["s5_lam_re"][:, :, g, :]
                lamS_im[:, :, gg * 64:(gg + 1) * 64, st] = s5["s5_lam_im"][:, :, g, :]
                lsS[:, :, gg * 64:(gg + 1) * 64, st] = s5["s5_log_step"][:, :, g, None]
                Bre[:, :, gl * 16:(gl + 1) * 16, st, gg * 64:(gg + 1) * 64] = s5["s5_b_re"][:, :, g].transpose(0, 1, 3, 2)
                Bim[:, :, gl * 16:(gl + 1) * 16, st, gg * 64:(gg + 1) * 64] = s5["s5_b_im"][:, :, g].transpose(0, 1, 3, 2)
                Cre[:, :, gg * 64:(gg + 1) * 64, st, gl * 16:(gl + 1) * 16] = s5["s5_c_re"][:, :, g].transpose(0, 1, 3, 2)
                Cim[:, :, gg * 64:(gg + 1) * 64, st, gl * 16:(gl + 1) * 16] = s5["s5_c_im"][:, :, g].transpose(0, 1, 3, 2)
        m["lamS_re"] = lamS_re; m["lamS_im"] = lamS_im; m["lsS"] = lsS
        m["lamB_re"] = A(lamS_re.transpose(0, 1, 3, 2).reshape(L, 2, 512))
        m["lamB_im"] = A(lamS_im.transpose(0, 1, 3, 2).reshape(L, 2, 512))
        m["lsB"] = A(lsS.transpose(0, 1, 3, 2).reshape(L, 2, 512))
        m["Bblk_re"] = Bre.reshape(L, 2, 128, 512); m["Bblk_im"] = Bim.reshape(L, 2, 128, 512)
        m["Cblk_re"] = Cre.reshape(L, 2, 128, 512); m["Cblk_im"] = Cim.reshape(L, 2, 128, 512)
        maps.append(m)
    return maps


_CACHE = {}


def kernel(**inputs):
    S = int(np.asarray(inputs["x"]).shape[1])
    L = int(np.asarray(inputs["w_in"]).shape[0])
    T = S // 4
    key = (S, L)
    if key not in _CACHE:
        b = Builder(S, L)
        b.build()
        _CACHE[key] = b
    b = _CACHE[key]
    maps = prep_inputs(inputs, S, L)
    maps = [{k: m[k] for k in b.inputs} for m in maps]
    res = run_bass_kernel_spmd(b.nc, maps, core_ids=list(range(8)))
    out = np.zeros((2, S, D), np.float32)
    for c in range(8):
        out[c // 4, (c % 4) * T:(c % 4 + 1) * T, :] = np.asarray(res.results[c]["out"]).reshape(T, D)
    return out
```
